# Optimizing a Trainium2 kernel written in Bass

```python
import math
import jax
import jax.numpy as jnp
from jax import lax
import numpy as np


D_MODEL = 1024
BATCH = 2
SEQ = 8192
DEPTH = 2

GRID_W = 64
CTX_LEN = 256
N_MIXERS = 2
RMS_EPS = 1e-6

RW_HEAD = 64
RW_WIDTH = D_MODEL
RW_HEADS = RW_WIDTH // RW_HEAD
W_LORA = 64
A_LORA = 64
RW_IN_COLS = 4 * RW_WIDTH + 2 * (W_LORA + A_LORA)
RW_MIX_COLS = 3 * RW_WIDTH + 2 * (W_LORA + A_LORA)
GN_EPS = 64e-5

DA_HEADS = 8
DA_HEAD = 64
DA_VHEAD = 2 * DA_HEAD
DA_QW = DA_HEADS * 2 * DA_HEAD
DA_VW = DA_HEADS * DA_VHEAD
DA_GW = DA_VW
DA_IN_COLS = 2 * DA_QW + DA_VW + DA_GW
SUBLN_EPS = 1e-5
ROPE_THETA = 10000.0
ROPE_FREQS = DA_HEAD // 4
Q_BLOCK = 128

N_A = (DEPTH + 1) // 2
N_B = DEPTH // 2

kernel_name = "hybrid_rwkv7_diffattn_prefix_flow_block"


def rms_norm(x, g, eps=RMS_EPS):
    xf = x.astype(jnp.float32)
    y = xf * lax.rsqrt(jnp.mean(xf * xf, axis=-1, keepdims=True) + eps)
    return (y * g.astype(jnp.float32)).astype(x.dtype)


def centred_token_shift(u, mu):
    prev = jnp.pad(u[:, :-1], ((0, 0), (1, 0), (0, 0)))
    nxt = jnp.pad(u[:, 1:], ((0, 0), (0, 1), (0, 0)))
    return u + mu * (0.5 * (prev + nxt) - u)


def rwkv7_streams(h, w_in, mu, w0, w2, a0, a2, k_k, k_a):
    B, T, _ = h.shape
    C = RW_WIDTH
    heads = lambda t: t.reshape(t.shape[:-1] + (RW_HEADS, RW_HEAD))
    proj = (h @ w_in).astype(jnp.float32)
    g = proj[..., 3 * C:4 * C]
    mixed = centred_token_shift(
        jnp.concatenate([proj[..., :3 * C], proj[..., 4 * C:]], axis=-1), mu.astype(jnp.float32))
    r, k, v = mixed[..., :C], mixed[..., C:2 * C], mixed[..., 2 * C:3 * C]
    lw = mixed[..., 3 * C:3 * C + 2 * W_LORA].reshape(B, T, 2, W_LORA)
    la = mixed[..., 3 * C + 2 * W_LORA:].reshape(B, T, 2, A_LORA)
    w_raw = w0 + jnp.einsum("btdr,drc->btdc", jnp.tanh(lw), w2)
    decay = jnp.exp(-jnp.exp(-jax.nn.softplus(-w_raw) - 0.5))
    a = jax.nn.sigmoid(a0 + jnp.einsum("btdr,drc->btdc", la, a2))
    kk = heads(k * k_k)
    kk = kk * lax.rsqrt(jnp.sum(kk * kk, axis=-1, keepdims=True) + 1e-12)
    kd = k[:, :, None, :] * (1.0 + (a - 1.0) * k_a)
    return heads(r), heads(v), kk, heads(decay), heads(a), heads(kd), g


def wkv_scan(r, w, k, v, kk, a, s0, reverse):
    xs = tuple(jnp.moveaxis(t, 1, 0) for t in (r, w, k, v, kk, a))

    def step(S, inp):
        r_t, w_t, k_t, v_t, kk_t, a_t = inp
        sa = jnp.einsum("bhvk,bhk->bhv", S, kk_t)
        S = (S * w_t[:, :, None, :]
             - sa[..., None] * (kk_t * a_t)[:, :, None, :]
             + v_t[..., None] * k_t[:, :, None, :])
        return S, jnp.einsum("bhvk,bhk->bhv", S, r_t)

    s_fin, o = lax.scan(step, s0, xs, reverse=reverse)
    return jnp.moveaxis(o, 0, 1), s_fin


def scan_direction(streams, d, s0, reverse):
    r, v, kk, decay, a, kd, _ = streams
    return wkv_scan(r, decay[:, :, d], kd[:, :, d], v, kk, a[:, :, d], s0, reverse)


def rwkv7_readout(o, streams, r_k, gn_g, gn_b, w_out, out_dtype):
    r, v, _, _, _, kd, g = streams
    B, T = o.shape[:2]
    mean = jnp.mean(o, axis=-1, keepdims=True)
    var = jnp.mean(jnp.square(o - mean), axis=-1, keepdims=True)
    y = ((o - mean) * lax.rsqrt(var + GN_EPS)).reshape(B, T, RW_WIDTH) * gn_g + gn_b
    bonus = jnp.sum(jnp.sum(r[:, :, None] * kd * r_k, axis=-1, keepdims=True) * v[:, :, None], axis=2)
    y = (y + bonus.reshape(B, T, RW_WIDTH)) * jax.nn.silu(g)
    return y.astype(out_dtype) @ w_out


def rwkv7_mixer(h, hc, w_in, mu, w0, w2, a0, a2, k_k, k_a, r_k, gn_g, gn_b, w_out, ctx_out):
    lat = rwkv7_streams(h, w_in, mu, w0, w2, a0, a2, k_k, k_a)
    con = rwkv7_streams(hc, w_in, mu, w0, w2, a0, a2, k_k, k_a)
    s0 = jnp.zeros((h.shape[0], RW_HEADS, RW_HEAD, RW_HEAD), jnp.float32)
    oc_f, sc_f = scan_direction(con, 0, s0, False)
    oc_b, sc_b = scan_direction(con, 1, s0, True)
    ol_f, _ = scan_direction(lat, 0, sc_f, False)
    ol_b, _ = scan_direction(lat, 1, sc_b, True)
    y = rwkv7_readout(ol_f + ol_b, lat, r_k, gn_g, gn_b, w_out, h.dtype)
    yc = rwkv7_readout(oc_f + oc_b, con, r_k, gn_g, gn_b, w_out, hc.dtype) if ctx_out else None
    return y, yc


def axial_rope_tables(rows):
    row_ids = jnp.repeat(jnp.arange(rows), GRID_W).astype(jnp.float32)
    col_ids = jnp.tile(jnp.arange(GRID_W), rows).astype(jnp.float32)
    inv_freq = ROPE_THETA ** (-jnp.arange(ROPE_FREQS, dtype=jnp.float32) / ROPE_FREQS)
    ang = jnp.stack([row_ids[:, None] * inv_freq, col_ids[:, None] * inv_freq], axis=1)
    return jnp.cos(ang), jnp.sin(ang)


def apply_axial_rope(x, cos, sin):
    xr = x.reshape(x.shape[:-1] + (2, 2, ROPE_FREQS))
    x1, x2 = xr[..., 0, :], xr[..., 1, :]
    c = cos[None, :, None, None].astype(x.dtype)
    s = sin[None, :, None, None].astype(x.dtype)
    out = jnp.stack([x1 * c - x2 * s, x2 * c + x1 * s], axis=-2)
    return out.reshape(x.shape)


def diff_attention_blocks(q, k_all, v_all, lam):
    B, T = q.shape[:2]
    nb = T // Q_BLOCK
    qb = jnp.moveaxis(q.reshape((B, nb, Q_BLOCK) + q.shape[2:]), 1, 0)
    scale = DA_HEAD ** -0.5

    def block(q_blk):
        s = jnp.einsum("bqhcd,bkhcd->bchqk", q_blk, k_all).astype(jnp.float32) * scale
        p = jax.nn.softmax(s, axis=-1)
        attn = p[:, 0] - lam * p[:, 1]
        return jnp.einsum("bhqk,bkhe->bqhe", attn.astype(v_all.dtype), v_all)

    o = lax.map(block, qb)
    return jnp.moveaxis(o, 0, 1).reshape(B, T, DA_HEADS, DA_VHEAD)


def diff_finish(o, g, subln, lam_init, w_out):
    B, T = o.shape[:2]
    y = rms_norm(o, subln, SUBLN_EPS) * (1.0 - lam_init)
    return (y.reshape(B, T, DA_VW) * jax.nn.silu(g)) @ w_out


def diff_mixer(h, hc, layer_idx, w_in, qn, kn, lam_vecs, subln, w_out, cos, sin, ctx_out):
    B, T, _ = h.shape
    Lc = hc.shape[1]
    lam_init = 0.8 - 0.6 * math.exp(-0.3 * layer_idx)
    lv = lam_vecs.astype(jnp.float32)
    lam = jnp.exp(jnp.sum(lv[0] * lv[1])) - jnp.exp(jnp.sum(lv[2] * lv[3])) + lam_init
    proj = h @ w_in
    q = apply_axial_rope(rms_norm(proj[..., :DA_QW].reshape(B, T, DA_HEADS, 2, DA_HEAD), qn), cos, sin)
    k = apply_axial_rope(rms_norm(proj[..., DA_QW:2 * DA_QW].reshape(B, T, DA_HEADS, 2, DA_HEAD), kn), cos, sin)
    v = proj[..., 2 * DA_QW:2 * DA_QW + DA_VW].reshape(B, T, DA_HEADS, DA_VHEAD)
    g = proj[..., 2 * DA_QW + DA_VW:]
    if ctx_out:
        pc = hc @ w_in
        kvc = pc[..., DA_QW:2 * DA_QW + DA_VW]
    else:
        kvc = hc @ w_in[:, DA_QW:2 * DA_QW + DA_VW]
    kc = rms_norm(kvc[..., :DA_QW].reshape(B, Lc, DA_HEADS, 2, DA_HEAD), kn)
    vc = kvc[..., DA_QW:].reshape(B, Lc, DA_HEADS, DA_VHEAD)
    k_all = jnp.concatenate([kc, k], axis=1)
    v_all = jnp.concatenate([vc, v], axis=1)
    y = diff_finish(diff_attention_blocks(q, k_all, v_all, lam), g, subln, lam_init, w_out)
    yc = None
    if ctx_out:
        qc = rms_norm(pc[..., :DA_QW].reshape(B, Lc, DA_HEADS, 2, DA_HEAD), qn)
        yc = diff_finish(diff_attention_blocks(qc, kc, vc, lam), pc[..., 2 * DA_QW + DA_VW:],
                         subln, lam_init, w_out)
    return y, yc


def setup_inputs(seed: int = 0) -> dict:
    key = jax.random.key(seed)
    ks = jax.random.split(key, 26)
    f32 = jnp.float32
    nrm = lambda k, shape, s: jax.random.normal(k, shape, f32) * s
    D = D_MODEL
    return {
        "x": nrm(ks[0], (BATCH, SEQ, D), 1.0),
        "c": nrm(ks[1], (BATCH, D), 1.0),
        "ctx": nrm(ks[2], (BATCH, CTX_LEN, D), 1.0),
        "c_ctx": nrm(ks[3], (D,), 1.0),
        "ada_w": nrm(ks[4], (DEPTH, D, 3 * D), 0.5 * D ** -0.5),
        "ada_b": nrm(ks[5], (DEPTH, 3 * D), 0.02),
        "norm_g": 1.0 + nrm(ks[6], (DEPTH, D), 0.02),
        "rw_in": nrm(ks[7], (N_A, D, RW_IN_COLS), D ** -0.5),
        "rw_mu": jax.random.uniform(ks[8], (N_A, RW_MIX_COLS), f32),
        "rw_w0": jax.random.uniform(ks[9], (N_A, 2, RW_WIDTH), f32, minval=-6.0, maxval=-1.0),
        "rw_w2": nrm(ks[10], (N_A, 2, W_LORA, RW_WIDTH), 0.1 * W_LORA ** -0.5),
        "rw_a0": nrm(ks[11], (N_A, 2, RW_WIDTH), 0.1),
        "rw_a2": nrm(ks[12], (N_A, 2, A_LORA, RW_WIDTH), 0.1 * A_LORA ** -0.5),
        "rw_kk": 0.85 + nrm(ks[13], (N_A, RW_WIDTH), 0.05),
        "rw_ka": 1.0 + nrm(ks[14], (N_A, RW_WIDTH), 0.05),
        "rw_rk": nrm(ks[15], (N_A, RW_HEADS, RW_HEAD), 0.1),
        "rw_gn_g": 1.0 + nrm(ks[16], (N_A, RW_WIDTH), 0.02),
        "rw_gn_b": nrm(ks[17], (N_A, RW_WIDTH), 0.02),
        "rw_out": nrm(ks[18], (N_A, RW_WIDTH, D), RW_WIDTH ** -0.5),
        "da_in": nrm(ks[19], (N_B, D, DA_IN_COLS), D ** -0.5),
        "da_qn": 1.0 + nrm(ks[20], (N_B, DA_HEAD), 0.02),
        "da_kn": 1.0 + nrm(ks[21], (N_B, DA_HEAD), 0.02),
        "da_lam": nrm(ks[22], (N_B, 4, DA_HEAD), 0.1),
        "da_subln": 1.0 + nrm(ks[23], (N_B, DA_VHEAD), 0.02),
        "da_out": nrm(ks[24], (N_B, DA_VW, D), DA_VW ** -0.5),
    }


def reference(x, c, ctx, c_ctx, ada_w, ada_b, norm_g, rw_in, rw_mu, rw_w0, rw_w2, rw_a0, rw_a2,
              rw_kk, rw_ka, rw_rk, rw_gn_g, rw_gn_b, rw_out, da_in, da_qn, da_kn, da_lam,
              da_subln, da_out):
    n_lat = x.shape[1]
    rows = n_lat // GRID_W
    cos, sin = axial_rope_tables(rows)
    xc = ctx
    for i in range(DEPTH):
        last = i == DEPTH - 1
        j = i // N_MIXERS
        mod = jax.nn.silu(c) @ ada_w[i] + ada_b[i]
        shift, scale, gate = jnp.split(mod[:, None, :], 3, axis=-1)
        modc = jax.nn.silu(c_ctx) @ ada_w[i] + ada_b[i]
        shift_c, scale_c, gate_c = jnp.split(modc, 3)
        h = rms_norm(x, norm_g[i]) * (1.0 + scale) + shift
        hc = rms_norm(xc, norm_g[i]) * (1.0 + scale_c) + shift_c
        if i % N_MIXERS == 0:
            y, yc = rwkv7_mixer(h, hc, rw_in[j], rw_mu[j], rw_w0[j], rw_w2[j], rw_a0[j], rw_a2[j],
                                rw_kk[j], rw_ka[j], rw_rk[j], rw_gn_g[j], rw_gn_b[j], rw_out[j],
                                not last)
        else:
            y, yc = diff_mixer(h, hc, i, da_in[j], da_qn[j], da_kn[j], da_lam[j], da_subln[j],
                               da_out[j], cos, sin, not last)
        x = x + gate * y
        if not last:
            xc = xc + gate_c * yc
    return x
```

```python
import numpy as np
import concourse.bass as bass
import concourse.mybir as mybir
from concourse.bass_utils import run_bass_kernel_spmd

F32 = mybir.dt.float32
BF16 = mybir.dt.bfloat16
AF = mybir.ActivationFunctionType
ALU = mybir.AluOpType
AX = mybir.AxisListType

ENGS = ("pe", "act", "dve", "pool", "sp")


class Buf:
    def __init__(self, prog, t, name, space):
        self.prog = prog
        self.t = t
        self.name = name
        self.space = space
        self.last_writer = None
        self.readers = []
        self.dma_sem = None
        self.dma_count = 0

    def __getitem__(self, idx):
        return View(self, self.t[idx])

    @property
    def v(self):
        return View(self, self.t[:])


class View:
    def __init__(self, buf, ap):
        self.buf = buf
        self.ap = ap

    def __getitem__(self, idx):
        return View(self.buf, self.ap[idx])


class Op:
    __slots__ = ("eng", "fn", "deps", "needs_inc", "seq", "dma_buf", "dma_val", "idx", "dma_inc")

    def __init__(self, eng, fn):
        self.eng = eng
        self.fn = fn
        self.deps = []
        self.needs_inc = False
        self.seq = None
        self.dma_buf = None
        self.dma_val = None
        self.dma_inc = 16


def _ap(x):
    return x.ap if isinstance(x, View) else x


class Prog:
    def __init__(self, nc):
        import contextlib
        self.nc = nc
        self.ops = {e: [] for e in ENGS}
        self.bufs = []
        self.dram = {}
        self.same_engine_sync = True
        self.stack = contextlib.ExitStack()
        self.phase = 0
        self.phase_sem = None
        self.uid = 0

    def end_phase(self):
        import contextlib
        self.stack.close()
        self.stack = contextlib.ExitStack()
        self.bufs = []

    def sb(self, name, shape, dtype):
        self.uid += 1
        t = self.stack.enter_context(self.nc.sbuf_tensor("sb%d_%s" % (self.uid, name), list(shape), dtype))
        b = Buf(self, t, name, "sb")
        self.bufs.append(b)
        return b

    def ps(self, name, shape, dtype=F32):
        self.uid += 1
        t = self.stack.enter_context(self.nc.psum_tensor("pp%d_%s" % (self.uid, name), list(shape), dtype))
        b = Buf(self, t, name, "ps")
        self.bufs.append(b)
        return b

    def dram_in(self, name, shape, dtype):
        t = self.nc.dram_tensor(name, list(shape), dtype, kind="ExternalInput")
        b = Buf(self, t, name, "dram")
        self.dram[name] = b
        return b

    def dram_out(self, name, shape, dtype):
        t = self.nc.dram_tensor(name, list(shape), dtype, kind="ExternalOutput")
        b = Buf(self, t, name, "dram")
        self.dram[name] = b
        return b

    def dram_tmp(self, name, shape, dtype, shared=False):
        if shared:
            t = self.nc.dram_tensor(name, list(shape), dtype, addr_space="Shared")
        else:
            t = self.nc.dram_tensor(name, list(shape), dtype)
        b = Buf(self, t, name, "dram")
        self.dram[name] = b
        return b

    def _record(self, eng, fn, reads, writes):
        op = Op(eng, fn)
        deps = []
        for v in reads:
            b = v.buf if isinstance(v, View) else v
            if b.last_writer is not None:
                deps.append(b.last_writer)
        for v in writes:
            b = v.buf if isinstance(v, View) else v
            if b.last_writer is not None:
                deps.append(b.last_writer)
            deps.extend(b.readers)
        seen = set()
        for d in deps:
            if id(d) in seen or d is op:
                continue
            seen.add(id(d))
            if d.dma_buf is None and d.eng == eng and (eng == "pe" or not self.same_engine_sync):
                continue
            op.deps.append(d)
            if d.dma_buf is None:
                d.needs_inc = True
        for v in writes:
            b = v.buf if isinstance(v, View) else v
            b.last_writer = op
            b.readers = []
        for v in reads:
            b = v.buf if isinstance(v, View) else v
            if b.last_writer is not op:
                b.readers.append(op)
        self.ops[eng].append(op)
        return op

    def op(self, eng, fn, reads=(), writes=()):
        return self._record(eng, fn, list(reads), list(writes))

    def matmul(self, out, lhsT, rhs, start=True, stop=True, extra_reads=(), **kw):
        o, l, r = _ap(out), _ap(lhsT), _ap(rhs)
        return self._record("pe", lambda e: e.matmul(o, l, r, start=start, stop=stop, **kw),
                            [lhsT, rhs] + list(extra_reads), [out])

    def transpose(self, out, in_, ident):
        o, i, d = _ap(out), _ap(in_), _ap(ident)
        return self._record("pe", lambda e: e.transpose(o, i, d), [in_, ident], [out])

    def act(self, out, in_, func, bias=None, scale=None, eng="act", accum_out=None):
        o, i = _ap(out), _ap(in_)
        kw = {}
        reads = [in_]
        writes = [out]
        if bias is not None:
            kw["bias"] = _ap(bias)
            if isinstance(bias, View):
                reads.append(bias)
        if scale is not None:
            kw["scale"] = _ap(scale)
            if isinstance(scale, View):
                reads.append(scale)
        if accum_out is not None:
            kw["accum_out"] = _ap(accum_out)
            writes.append(accum_out)
        return self._record("act", lambda e: e.activation(o, i, func, **kw), reads, writes)

    def tt(self, eng, out, in0, in1, op):
        o, a, b = _ap(out), _ap(in0), _ap(in1)
        return self._record(eng, lambda e: e.tensor_tensor(o, a, b, op), [in0, in1], [out])

    def ts(self, eng, out, in0, s1, s2, op0, op1=None, accum_out=None):
        o, a = _ap(out), _ap(in0)
        reads = [in0]
        writes = [out]
        for s in (s1, s2):
            if isinstance(s, View):
                reads.append(s)
        x1, x2 = _ap(s1), _ap(s2)
        kw = {}
        if op1 is not None:
            kw["op1"] = op1
        if accum_out is not None:
            kw["accum_out"] = _ap(accum_out)
            writes.append(accum_out)
        return self._record(eng, lambda e: e.tensor_scalar(o, a, x1, x2, op0, **kw), reads, writes)

    def stt(self, out, in0, scalar, in1, op0, op1, eng="dve"):
        o, a, b = _ap(out), _ap(in0), _ap(in1)
        reads = [in0, in1]
        if isinstance(scalar, View):
            reads.append(scalar)
        s = _ap(scalar)
        return self._record(eng, lambda e: e.scalar_tensor_tensor(o, a, s, b, op0, op1), reads, [out])

    def copy(self, eng, out, in_):
        o, i = _ap(out), _ap(in_)
        if eng == "act":
            return self._record(eng, lambda e: e.copy(o, i), [in_], [out])
        return self._record(eng, lambda e: e.tensor_copy(o, i), [in_], [out])

    def scan(self, out, d0, d1, initial, op0, op1):
        o, a, b = _ap(out), _ap(d0), _ap(d1)
        reads = [d0, d1]
        if isinstance(initial, View):
            reads.append(initial)
        ini = _ap(initial)
        return self._record("dve", lambda e: e.tensor_tensor_scan(o, a, b, ini, op0, op1), reads, [out])

    def recip(self, out, in_):
        o, i = _ap(out), _ap(in_)
        return self._record("dve", lambda e: e.reciprocal(o, i), [in_], [out])

    def memset(self, eng, out, val):
        o = _ap(out)
        return self._record(eng, lambda e: e.memset(o, val), [], [out])

    def reduce(self, out, in_, axis, op, eng="dve"):
        o, i = _ap(out), _ap(in_)
        return self._record(eng, lambda e: e.tensor_reduce(o, i, axis, op), [in_], [out])

    def dma(self, out, in_, queue="sp", **kw):
        o, i = _ap(out), _ap(in_)
        ob = out.buf
        ib = in_.buf
        key = ob if ob.space != "dram" else ib
        op = self._record(queue, lambda e: e.dma_start(out=o, in_=i, **kw), [in_], [out])
        key.dma_count += 1
        op.dma_buf = key
        op.dma_val = 16 * key.dma_count
        return op

    def emit(self, final_waits=()):
        nc = self.nc
        for e in ENGS:
            n = 0
            for op in self.ops[e]:
                if op.dma_buf is None and op.needs_inc:
                    n += 1
                    op.seq = n
        SEMCAP = 30000
        nsem = {e: 1 + max([op.seq or 0 for op in self.ops[e]] + [0]) // SEMCAP for e in ENGS}
        ph = self.phase
        sems = {e: [nc.alloc_semaphore("s%d_%s_%d" % (ph, e, i)) for i in range(nsem[e])] for e in ENGS}
        if self.phase_sem is None:
            self.phase_sem = nc.alloc_semaphore("phase_done")
        phase_sem = self.phase_sem
        dummy_sb = self.sb("phdummy", [128, 8], F32)

        for b in self.bufs + list(self.dram.values()):
            if b.dma_count > 0:
                b.dma_sem = nc.alloc_semaphore("d%d_%s" % (ph, b.name))
        engmap = {"pe": "tensor", "act": "scalar", "dve": "vector", "pool": "gpsimd", "sp": "sync"}
        all_dma = []
        for e in ENGS:
            for op in self.ops[e]:
                if op.dma_buf is not None:
                    all_dma.append(op)

        def gen(ename):
            def body(eng):
                waited = {}
                if ph > 0:
                    eng.wait_ge(phase_sem, 4 * ph)
                for op in self.ops[ename]:
                    need = {}
                    for d in op.deps:
                        if d.dma_buf is not None:
                            k = ("dma", id(d.dma_buf))
                            sem = d.dma_buf.dma_sem
                            val = d.dma_val
                        else:
                            si = (d.seq - 1) // SEMCAP
                            k = ("eng", d.eng, si)
                            sem = sems[d.eng][si]
                            val = d.seq - si * SEMCAP
                        if k not in need or need[k][1] < val:
                            need[k] = (sem, val)
                    for k, (sem, val) in need.items():
                        if waited.get(k, 0) >= val:
                            continue
                        eng.wait_ge(sem, val)
                        waited[k] = val
                    inst = op.fn(eng)
                    if op.dma_buf is not None:
                        if op.dma_inc == 16:
                            inst.then_inc(op.dma_buf.dma_sem, 16)
                        else:
                            inst.then_inc(op.dma_buf.dma_sem)
                    elif op.needs_inc:
                        inst.then_inc(sems[ename][(op.seq - 1) // SEMCAP], 1)
                if ename == "sp":
                    finals = {}
                    for op in all_dma:
                        b = op.dma_buf
                        finals[id(b)] = (b.dma_sem, op.dma_val if op.dma_inc != 16 else 16 * b.dma_count)
                    for sem, val in finals.values():
                        eng.wait_ge(sem, val)
                    eng.sem_inc(phase_sem, 1)
                elif ename == "act":
                    eng.copy(dummy_sb.t[:, 2:3], dummy_sb.t[:, 3:4]).then_inc(phase_sem, 1)
                elif ename == "dve":
                    eng.memset(dummy_sb.t[:, 4:5], 0.0).then_inc(phase_sem, 1)
                elif ename == "pool":
                    eng.memset(dummy_sb.t[:, 6:7], 0.0).then_inc(phase_sem, 1)
            return body

        with nc.Block() as block:
            block.tensor(gen("pe"))
            block.scalar(gen("act"))
            block.vector(gen("dve"))
            block.gpsimd(gen("pool"))
            block.sync(gen("sp"))
        self.phase += 1
        self.ops = {e: [] for e in ENGS}
        for b in self.bufs + list(self.dram.values()):
            b.last_writer = None
            b.readers = []
            b.dma_count = 0
            b.dma_sem = None


def _collective(self, kind, out, in_, groups, op=None):
    o, i = _ap(out), _ap(in_)
    alu = op if op is not None else ALU.bypass
    rec = self._record("pool", lambda e: e.collective_compute(kind, alu, replica_groups=groups, ins=[i], outs=[o]),
                       [in_], [out])
    key = out.buf
    key.dma_count += 1
    rec.dma_buf = key
    rec.dma_inc = 1
    rec.dma_val = key.dma_count
    return rec


Prog.collective = _collective


D = 1024
TC = 256
TL = 8192
TA = TC + TL
W = 256
WH = W + 2
NBLK_L = TL // W
XA_COLS = 1 + TC + 1 + 1 + TL + 1
NSMALL = 50
EXPM05 = 0.6065306597126334
RMS_EPS = 1e-6
GN_EPS = 64e-5


def l0_consts():
    ident = np.eye(128, dtype=np.float32)
    bones = np.kron(np.eye(2, dtype=np.float32), np.ones((64, 64), np.float32))
    idx = np.arange(128)
    same = (idx[:, None] // 64) == (idx[None, :] // 64)
    strict_f = (same & (idx[None, :] < idx[:, None])).astype(np.float32)
    incl_f = (same & (idx[None, :] <= idx[:, None])).astype(np.float32)
    strict_b = (same & (idx[None, :] > idx[:, None])).astype(np.float32)
    incl_b = (same & (idx[None, :] >= idx[:, None])).astype(np.float32)
    out = {}
    for nm, st, inc in (("f", strict_f, incl_f), ("b", strict_b, incl_b)):
        m1 = np.concatenate([st, st], axis=1)
        m2h = np.concatenate([st.T, inc.T], axis=1)
        m2 = np.concatenate([m2h, m2h], axis=1)
        out["m1" + nm] = np.ascontiguousarray(m1)
        out["m2" + nm] = np.ascontiguousarray(m2)
    ident2 = np.concatenate([np.eye(64, dtype=np.float32)] * 2, axis=0)
    cst = np.concatenate([ident, bones, out["m1f"], out["m2f"], out["m1b"], out["m2b"], ident2,
                          np.ones((128, 64), np.float32)], axis=1)
    return np.ascontiguousarray(cst)


C_ID, C_BO, C_M1F, C_M2F, C_M1B, C_M2B, C_ID2, C_ONE = 0, 128, 256, 512, 1024, 1280, 1792, 1856
C_TOT = 1920


def V2(view):
    return View(view.buf, view.ap.rearrange("p a b -> p (a b)"))


def build_l0(debug_out=False, stop=None, ctx=None):
    if ctx is None:
        nc = bass.Bass("TRN2", target_bir_lowering=False)
        P = Prog(nc)
        xa_d = P.dram_in("xa", [D, XA_COLS], F32)
        ct_d = P.dram_in("ct", [128, 16], F32)
        adaw_d = P.dram_in("adaw", [D, 2048], F32)
        sm_d = P.dram_in("smalls", [128, NSMALL], F32)
        win_d = P.dram_in("win", [D, 1280], F32)
        w2_d = P.dram_in("w2", [128, 256], F32)
        a2_d = P.dram_in("a2", [128, 256], F32)
        cst_d = P.dram_in("cst", [128, C_TOT], F32)
        y0_d = P.dram_out("y0", [256, TA], BF16)
        of_d = P.dram_tmp("of_scratch", [256, TA], F32)
    else:
        nc, P = ctx["nc"], ctx["P"]
        xa_d, ct_d, adaw_d, sm_d, win_d, w2_d, a2_d, cst_d, y0_d, of_d = [ctx[k] for k in (
            "a_xa", "ct", "a_adaw", "a_smalls", "a_win", "a_w2", "a_a2", "a_cst", "y0loc", "of_scratch")]

    cst = P.sb("cst", [128, C_TOT], F32)
    P.dma(cst.v, cst_d.v)
    ident = cst[:, C_ID:C_ID + 128]
    bones = cst[:, C_BO:C_BO + 128]
    ident2 = cst[:, C_ID2:C_ID2 + 64]
    ones64 = cst[:, C_ONE:C_ONE + 64]
    masks = {0: (cst[:, C_M1F:C_M1F + 256], cst[:, C_M2F:C_M2F + 512]),
             1: (cst[:, C_M1B:C_M1B + 256], cst[:, C_M2B:C_M2B + 512])}
    sm = P.sb("sm", [128, NSMALL], F32)
    P.dma(sm.v, sm_d.v)
    S_NG, S_ADAB, S_MU, S_W0, S_A0, S_KK, S_KA, S_RK, S_GG, S_GB = 0, 8, 24, 32, 36, 40, 42, 44, 46, 48
    w2 = P.sb("w2", [128, 256], F32)
    a2 = P.sb("a2", [128, 256], F32)
    P.dma(w2.v, w2_d.v)
    P.dma(a2.v, a2_d.v)
    ones128 = P.sb("ones128", [128, 128], F32)
    P.memset("pool", ones128.v, 1.0)

    der = P.sb("der", [128, 32], F32)
    P.ts("pool", der[:, 0:8], sm[:, S_MU:S_MU + 8], -1.0, 1.0, ALU.mult, ALU.add)
    P.ts("pool", der[:, 8:16], sm[:, S_MU:S_MU + 8], 0.5, None, ALU.mult)
    P.ts("pool", der[:, 16:18], sm[:, S_KA:S_KA + 2], -1.0, 1.0, ALU.mult, ALU.add)
    omu = lambda ci: der[:, ci:ci + 1]
    hmu = lambda ci: der[:, 8 + ci:9 + ci]
    omka = lambda p: der[:, 16 + p:17 + p]

    def dbg_stop(views):
        tot = max(64, sum(n for _, n in views))
        dbg = P.dram_out("dbg", [128, tot], F32)
        dsb = P.sb("dsb", [128, tot], F32)
        P.memset("dve", dsb.v, 0.0)
        c = 0
        for v, n in views:
            P.copy("dve", dsb[:, c:c + n], v)
            c += n
        P.dma(dbg.v, dsb.v)
        P.emit()
        return nc
    if stop == "pre0":
        return dbg_stop([(der[:, 0:18], 18)])
    ct = P.sb("ct", [128, 16], F32)
    P.dma(ct.v, ct_d.v)
    sct = P.sb("sct", [128, 16], F32)
    P.act(sct.v, ct.v, AF.Silu)
    modT = P.sb("modT", [128, 16, 2], F32)
    adaw = P.sb("adaw", [128, 8, 512], F32)
    ps_misc = P.ps("ps_misc", [128, 512])
    adaw_r = adaw_d.t[:].rearrange("(k p) m -> p k m", p=128)
    for piece in range(4):
        P.dma(adaw.v, View(adaw_d, adaw_r[:, :, piece * 512:(piece + 1) * 512]))
        for mcl in range(4):
            mc = piece * 4 + mcl
            for kc in range(8):
                P.matmul(ps_misc[:, 0:2], adaw[:, kc, mcl * 128:(mcl + 1) * 128], sct[:, kc * 2:kc * 2 + 2],
                         start=(kc == 0), stop=(kc == 7))
            P.ts("dve", modT[:, mc, :], ps_misc[:, 0:2], sm[:, S_ADAB + mc:S_ADAB + mc + 1], None, ALU.add)
    if stop == "pre1":
        return dbg_stop([(V2(modT.v), 32)])
    gmod = P.sb("gmod", [128, 8, 2], F32)
    for kc in range(8):
        P.ts("pool", gmod[:, kc, :], modT[:, 8 + kc, :], 1.0, sm[:, S_NG + kc:S_NG + kc + 1], ALU.add, ALU.mult)

    if stop == "pre2":
        return dbg_stop([(V2(modT.v), 32), (V2(gmod.v), 16)])
    Wb = P.sb("Wb", [128, 8, 1280], BF16)
    wst = [P.sb("wst%d" % i, [128, 1280], F32) for i in range(2)]
    for kc in range(8):
        P.dma(wst[kc % 2].v, win_d[kc * 128:(kc + 1) * 128, :])
        P.copy("pool", Wb[:, kc, :], wst[kc % 2].v)

    if stop == "pre":
        dbg = P.dram_out("dbg", [128, 64], F32)
        dsb = P.sb("dsb", [128, 64], F32)
        P.copy("dve", dsb[:, 0:32], V2(modT.v))
        P.copy("dve", dsb[:, 32:48], V2(gmod.v))
        P.copy("dve", dsb[:, 48:64], Wb[:, 7, 0:16])
        P.dma(dbg.v, dsb.v)
        P.emit()
        return nc
    xin = [P.sb("xin%d" % i, [128, 8, WH], F32) for i in range(2)]
    hT = P.sb("hT", [128, 8, WH], BF16)
    sqb = [P.sb("sqb%d" % i, [128, WH], F32) for i in range(2)]
    rstd = P.sb("rstd", [128, WH], F32)
    htmp = [P.sb("htmp%d" % i, [128, WH], F32) for i in range(2)]
    ps_proj = [P.ps("ps_proj%d" % i, [128, 512]) for i in range(2)]
    ps_a = P.ps("ps_a", [128, 512])
    u_sb = [P.sb("u_sb%d" % i, [128, WH], F32) for i in range(2)]
    s_sb = [P.sb("s_sb%d" % i, [128, W], F32) for i in range(2)]
    t_sb = [P.sb("t_sb%d" % i, [128, W], F32) for i in range(2)]

    def blk(name):
        return P.sb(name, [128, W], F32)

    Rb = [blk("R%d" % p) for p in range(2)]
    Kb = [blk("K%d" % p) for p in range(2)]
    Vb = [blk("V%d" % p) for p in range(2)]
    SG = [blk("SG%d" % p) for p in range(2)]
    LW = blk("LW")
    LA = blk("LA")
    TLW = blk("TLW")
    LOGW = [blk("LOGW%d" % p) for p in range(2)]
    Ab = [blk("A%d" % p) for p in range(2)]
    KQ = [blk("KQ%d" % p) for p in range(2)]
    KK = [blk("KK%d" % p) for p in range(2)]
    KD = [blk("KD%d" % p) for p in range(2)]
    KD0 = [blk("KD0%d" % p) for p in range(2)]
    T1 = [blk("T1%d" % p) for p in range(2)]
    T2 = [blk("T2%d" % p) for p in range(2)]
    CL = [blk("CL%d" % p) for p in range(2)]
    PRE = [blk("PRE%d" % p) for p in range(2)]
    E1 = [blk("E1%d" % p) for p in range(2)]
    E2 = [blk("E2%d" % p) for p in range(2)]
    E3 = [blk("E3%d" % p) for p in range(2)]
    AT = [blk("AT%d" % p) for p in range(2)]
    RT = [blk("RT%d" % p) for p in range(2)]
    BT = [blk("BT%d" % p) for p in range(2)]
    KT = [blk("KT%d" % p) for p in range(2)]
    BH = [blk("BH%d" % p) for p in range(2)]
    KH = [blk("KH%d" % p) for p in range(2)]
    DG = [P.sb("DG%d" % p, [128, 256], F32) for p in range(2)]
    OB = [blk("OB%d" % p) for p in range(2)]
    OF = [blk("OF%d" % p) for p in range(2)]
    YB = [P.sb("YB%d" % p, [128, W], BF16) for p in range(2)]

    def sbp(name, shape):
        return [P.sb("%s%d" % (name, p), shape, F32) for p in range(2)]

    Lm = sbp("Lm", [128, 256])
    NM = sbp("NM", [128, 512])
    KM = sbp("KM", [128, 512])
    Lk = [sbp("Lk%d_" % i, [128, 256]) for i in range(2)]
    Nk = [sbp("Nk%d_" % i, [128, 256]) for i in range(2)]
    Xk = [sbp("Xk%d_" % i, [128, 256]) for i in range(2)]
    Zb = sbp("Zb", [128, 256])
    TZ = sbp("TZ", [128, 256])
    VT = sbp("VT", [128, 128])
    BHT = sbp("BHT", [128, 128])
    KHT = sbp("KHT", [128, 128])
    RPT = sbp("RPT", [128, 128])
    PT = sbp("PT", [128, 128])
    ST = [sbp("ST%d_" % i, [128, 128]) for i in range(3)]
    BTbd = sbp("BTbd", [128, 512])
    KTbd = sbp("KTbd", [128, 512])
    DGd = sbp("DGd", [128, 512])
    BHTc = [sbp("BHTc%d_" % i, [128, 128]) for i in range(2)]
    KHTc = [sbp("KHTc%d_" % i, [128, 128]) for i in range(2)]
    RPTm = [sbp("RPTm%d_" % i, [128, 128]) for i in range(2)]
    PTbd = [sbp("PTbd%d_" % i, [128, 128]) for i in range(2)]
    T3 = sbp("T3", [128, 128])
    OTK = sbp("OTK", [128, 128])
    for p in range(2):
        for bb in (BTbd[p], KTbd[p], BHTc[0][p], BHTc[1][p], KHTc[0][p], KHTc[1][p], RPTm[0][p], RPTm[1][p]):
            P.memset("pool", bb.v, 0.0)
    psB = [P.ps("psB%d" % p, [128, 512]) for p in range(2)]
    psC = [P.ps("psC%d" % p, [128, 512]) for p in range(2)]

    def HH(buf, h):
        return buf[:, h * 128:(h + 1) * 128]

    def NMa(p, h):
        return NM[p][:, h * 256:h * 256 + 128]

    def NMb(p, h):
        return NM[p][:, h * 256 + 128:h * 256 + 256]

    def KMa(p, h):
        return KM[p][:, h * 256:h * 256 + 128]

    def KMb(p, h):
        return KM[p][:, h * 256 + 128:h * 256 + 256]

    def V3(view, h):
        return View(view.buf, view.ap.rearrange("p (h c) -> p h c", h=h))


    def x_cols(blk_id):
        if blk_id < 0:
            return 0
        return 258 + 256 * blk_id

    def tok0(blk_id):
        return 0 if blk_id < 0 else TC + 256 * blk_id

    xa_r = xa_d.t[:].rearrange("(k p) c -> p k c", p=128)

    def issue_x(blk_id, slot):
        c0 = x_cols(blk_id)
        P.dma(xin[slot].v, View(xa_d, xa_r[:, :, c0:c0 + WH]))

    evac_flip = [0]

    def evac(out, in_):
        evac_flip[0] ^= 1
        P.copy("act" if evac_flip[0] else "dve", out, in_)


    epsb = P.sb("epsb", [128, 4], F32)
    P.memset("pool", epsb[:, 0:1], RMS_EPS)
    P.memset("pool", epsb[:, 1:2], 1e-12)
    P.memset("pool", epsb[:, 2:3], GN_EPS)

    def stage_a(blk_id, slot, d):
        col = 1 if blk_id < 0 else 0
        xs = xin[slot]
        for kc in range(8):
            sq = sqb[kc % 2]
            P.act(sq.v, xs[:, kc, :], AF.Square)
            P.matmul(ps_a[:, 0:WH], ones128.v, sq.v, start=(kc == 0), stop=(kc == 7))
        P.act(rstd.v, ps_a[:, 0:WH], AF.Sqrt, scale=1.0 / D, bias=epsb[:, 0:1])
        P.recip(rstd.v, rstd.v)
        for kc in range(8):
            tmp = htmp[kc % 2]
            P.stt(tmp.v, xs[:, kc, :], gmod[:, kc, col:col + 1], rstd.v, ALU.mult, ALU.mult)
            P.act(hT[:, kc, :], tmp.v, AF.Identity, bias=modT[:, kc, col:col + 1])
        if blk_id < 0 or blk_id == 0:
            P.memset("pool", hT[:, :, 0:1], 0.0)
        if blk_id < 0 or blk_id == NBLK_L - 1:
            P.memset("pool", hT[:, :, WH - 1:WH], 0.0)
        mixed_dst = [Rb[0], Rb[1], Kb[0], Kb[1], Vb[0], Vb[1], None, None, LW, LA]
        mix_ci = [0, 1, 2, 3, 4, 5, None, None, 6, 7]
        n = 0
        for cc in range(10):
            if cc in (6, 7) and d == 0:
                continue
            pp = ps_proj[n % 2]
            for kc in range(8):
                P.matmul(pp[:, 0:WH], Wb[:, kc, cc * 128:(cc + 1) * 128], hT[:, kc, :],
                         start=(kc == 0), stop=(kc == 7))
            if cc in (6, 7):
                P.act(SG[cc - 6].v, pp[:, 1:W + 1], AF.Silu)
            else:
                ci = mix_ci[cc]
                u = u_sb[n % 2]
                s_ = s_sb[n % 2]
                t_ = t_sb[n % 2]
                P.copy("act", u.v, pp[:, 0:WH])
                P.tt("dve", s_.v, u[:, 0:W], u[:, 2:W + 2], ALU.add)
                P.ts("pool", t_.v, u[:, 1:W + 1], omu(ci), None, ALU.mult)
                P.stt(mixed_dst[cc].v, s_.v, hmu(ci), t_.v, ALU.mult, ALU.add)
            n += 1
        P.act(TLW.v, LW.v, AF.Tanh)
        for p in range(2):
            pc = slice(p * 128, (p + 1) * 128)
            P.matmul(ps_a[:, 0:W], w2[64 * d:64 * d + 64, pc], TLW[64 * d:64 * d + 64, :])
            P.act(LOGW[p].v, ps_a[:, 0:W], AF.Sigmoid, bias=sm[:, S_W0 + 2 * d + p:S_W0 + 2 * d + p + 1])
            P.ts("pool", LOGW[p].v, LOGW[p].v, -EXPM05, None, ALU.mult)
            P.matmul(ps_a[:, W:2 * W], a2[64 * d:64 * d + 64, pc], LA[64 * d:64 * d + 64, :])
            P.act(Ab[p].v, ps_a[:, W:2 * W], AF.Sigmoid, bias=sm[:, S_A0 + 2 * d + p:S_A0 + 2 * d + p + 1])
            P.ts("pool", KQ[p].v, Kb[p].v, sm[:, S_KK + p:S_KK + p + 1], None, ALU.mult)
            P.act(T1[p].v, KQ[p].v, AF.Square)
            P.matmul(ps_a[:, 0:W], bones, T1[p].v)
            P.act(T2[p].v, ps_a[:, 0:W], AF.Sqrt, bias=epsb[:, 1:2])
            P.recip(T2[p].v, T2[p].v)
            P.tt("pool", KK[p].v, KQ[p].v, T2[p].v, ALU.mult)
            P.ts("pool", T1[p].v, Ab[p].v, sm[:, S_KA + p:S_KA + p + 1], omka(p), ALU.mult, ALU.add)
            P.tt("pool", KD[p].v, Kb[p].v, T1[p].v, ALU.mult)
            if d == 1:
                P.matmul(ps_a[:, W:2 * W], a2[0:64, pc], LA[0:64, :])
                P.act(T2[p].v, ps_a[:, W:2 * W], AF.Sigmoid, bias=sm[:, S_A0 + p:S_A0 + p + 1])
                P.ts("pool", T2[p].v, T2[p].v, sm[:, S_KA + p:S_KA + p + 1], omka(p), ALU.mult, ALU.add)
                P.tt("pool", KD0[p].v, Kb[p].v, T2[p].v, ALU.mult)
            for ch in range(4):
                sl = slice(ch * 64, (ch + 1) * 64)
                P.scan(PRE[p][:, sl], ones64, LOGW[p][:, sl], 0.0, ALU.mult, ALU.add)
            if d == 0:
                clb = PRE[p]
            else:
                clb = CL[p]
                for ch in range(4):
                    sl = slice(ch * 64, (ch + 1) * 64)
                    P.ts("pool", CL[p][:, sl], PRE[p][:, sl], -1.0, PRE[p][:, ch * 64 + 63:ch * 64 + 64],
                         ALU.mult, ALU.add)
                P.tt("pool", CL[p].v, CL[p].v, LOGW[p].v, ALU.add)
            P.act(E1[p].v, clb.v, AF.Exp)
            P.act(E2[p].v, clb.v, AF.Exp, scale=-1.0)
            P.tt("pool", T1[p].v, clb.v, LOGW[p].v, ALU.subtract)
            P.act(E3[p].v, T1[p].v, AF.Exp)
            P.stt(AT[p].v, KK[p].v, -1.0, E3[p].v, ALU.mult, ALU.mult)
            P.tt("pool", RT[p].v, Rb[p].v, E1[p].v, ALU.mult)
            P.tt("pool", T1[p].v, KK[p].v, Ab[p].v, ALU.mult)
            P.tt("pool", BT[p].v, T1[p].v, E2[p].v, ALU.mult)
            P.tt("pool", KT[p].v, KD[p].v, E2[p].v, ALU.mult)
            for ch in range(4):
                sl = slice(ch * 64, (ch + 1) * 64)
                gc = ch * 64 + 63 if d == 0 else ch * 64
                gcol = E1[p][:, gc:gc + 1]
                P.ts("pool", BH[p][:, sl], BT[p][:, sl], gcol, None, ALU.mult)
                P.ts("pool", KH[p][:, sl], KT[p][:, sl], gcol, None, ALU.mult)
                P.ts("pool", DGd[p][:, ch * 128:(ch + 1) * 128], ident, gcol, None, ALU.mult)
            for h in range(2):
                hp = slice(64 * h, 64 * h + 64)
                for tl2 in range(2):
                    q = (tl2 * 2 + h) * 128
                    P.copy("pool", BTbd[p][hp, q:q + 128], BT[p][hp, tl2 * 128:(tl2 + 1) * 128])
                    P.copy("pool", KTbd[p][hp, q:q + 128], KT[p][hp, tl2 * 128:(tl2 + 1) * 128])

    def stage_b(p, tl, d, sw_state, upto=99):
        m1, m2 = masks[d]
        cs = slice(tl * 128, (tl + 1) * 128)
        pc = psC[p]
        pb = psB[p]
        btbd = lambda h: BTbd[p][:, (tl * 2 + h) * 128:(tl * 2 + h + 1) * 128]
        ktbd = lambda h: KTbd[p][:, (tl * 2 + h) * 128:(tl * 2 + h + 1) * 128]
        P.matmul(pc[:, 0:256], AT[p][:, cs], BTbd[p][:, tl * 256:(tl + 1) * 256])
        P.tt("dve", Lm[p].v, pc[:, 0:256], m1, ALU.mult)
        for h in range(2):
            P.matmul(pb[:, h * 256:h * 256 + 128], btbd(h), AT[p][:, cs])
            P.matmul(pb[:, h * 256 + 128:h * 256 + 256], btbd(h), RT[p][:, cs])
        P.tt("dve", NM[p].v, pb[:, 0:512], m2, ALU.mult)
        for h in range(2):
            P.matmul(pb[:, h * 256:h * 256 + 128], ktbd(h), AT[p][:, cs])
            P.matmul(pb[:, h * 256 + 128:h * 256 + 256], ktbd(h), RT[p][:, cs])
        P.tt("dve", KM[p].v, pb[:, 0:512], m2, ALU.mult)
        if upto <= 1:
            return
        X = Xk[0][p]
        for h in range(2):
            P.tt("pool", HH(X, h), NMa(p, h), ident, ALU.add)
        Lc = Lm[p]
        Nc_views = [NMa(p, h) for h in range(2)]
        xi = 0
        for k in range(1, 6):
            Ln = Lk[k % 2][p]
            for h in range(2):
                P.matmul(pc[:, h * 128:(h + 1) * 128], Nc_views[h], HH(Lc, h))
            P.copy("act", Ln.v, pc[:, 0:256])
            if k < 5:
                Nn = Nk[k % 2][p]
                for h in range(2):
                    P.matmul(pc[:, 256 + h * 128:256 + (h + 1) * 128], HH(Lc, h), Nc_views[h])
                P.copy("act", Nn.v, pc[:, 256:512])
            for h in range(2):
                P.matmul(pb[:, h * 128:(h + 1) * 128], HH(Ln, h), HH(Xk[xi][p], h))
            Xn = Xk[1 - xi][p]
            P.tt("dve", Xn.v, pb[:, 0:256], Xk[xi][p].v, ALU.add)
            xi = 1 - xi
            Lc = Ln
            if k < 5:
                Nc_views = [HH(Nn, h) for h in range(2)]
        X = Xk[xi][p]
        if upto <= 2:
            return
        P.transpose(pc[:, 0:128], AT[p][:, cs], ident)
        P.copy("act", Zb[p][:, 0:128], pc[:, 0:128])
        P.transpose(pc[:, 128:256], Vb[p][:, cs], ident)
        P.copy("dve", VT[p].v, pc[:, 128:256])
        P.transpose(pc[:, 256:384], BH[p][:, cs], ident)
        P.copy("act", BHTc[0][p][0:64, :], pc[0:64, 256:384])
        P.copy("dve", BHTc[1][p][64:128, :], pc[64:128, 256:384])
        P.transpose(pc[:, 384:512], KH[p][:, cs], ident)
        P.copy("act", KHTc[0][p][0:64, :], pc[0:64, 384:512])
        P.copy("dve", KHTc[1][p][64:128, :], pc[64:128, 384:512])
        for h in range(2):
            P.matmul(pb[:, h * 64:(h + 1) * 64], KMa(p, h), VT[p][:, h * 64:(h + 1) * 64])
        P.copy("act", Zb[p][:, 128:256], pb[:, 0:128])
        for part in range(2):
            for h in range(2):
                q = (part * 2 + h) * 64
                P.matmul(pb[:, 128 + q:128 + q + 64], HH(X, h), Zb[p][:, q:q + 64])
        P.copy("dve", TZ[p].v, pb[:, 128:384])
        if upto <= 3:
            return
        for h in range(2):
            P.matmul(pc[:, h * 128:(h + 1) * 128], TZ[p][:, 0:128], NMb(p, h))
        for h in range(2):
            hp = slice(64 * h, 64 * h + 64)
            P.tt("dve", RPT[p][hp, :], pc[hp, h * 128:(h + 1) * 128], RT[p][hp, cs], ALU.add)
        P.copy("pool", RPTm[0][p][:, 0:64], RPT[p][:, 0:64])
        P.copy("pool", RPTm[1][p][:, 64:128], RPT[p][:, 64:128])
        for c in range(2):
            P.matmul(pc[:, 256 + c * 128:256 + (c + 1) * 128], TZ[p][:, 0:128], BHTc[c][p].v)
        for c in range(2):
            ch = 2 * tl + c
            P.tt("dve", T3[p].v, pc[:, 256 + c * 128:256 + (c + 1) * 128], bones, ALU.mult)
            P.tt("pool", PTbd[c][p].v, T3[p].v, DGd[p][:, ch * 128:(ch + 1) * 128], ALU.add)
        if upto <= 5:
            return
        order = (0, 1) if d == 0 else (1, 0)
        s_at = {}
        for c in order:
            si = sw_state[p]
            S_in = ST[si][p]
            S_out = ST[(si + 1) % 3][p]
            s_at[c] = S_in
            P.matmul(pb[:, 0:128], PTbd[c][p].v, S_in.v, start=True, stop=False)
            P.matmul(pb[:, 0:128], BHTc[c][p].v, TZ[p][:, 128:256], start=False, stop=False)
            P.matmul(pb[:, 0:128], KHTc[c][p].v, VT[p].v, start=False, stop=True)
            P.tt("dve", S_out.v, pb[:, 0:128], bones, ALU.mult)
            sw_state[p] = (si + 1) % 3
        if upto <= 6:
            return
        P.matmul(pb[:, 128:256], RPTm[0][p].v, s_at[0].v, start=True, stop=False)
        P.matmul(pb[:, 128:256], RPTm[1][p].v, s_at[1].v, start=False, stop=False)
        for h in range(2):
            P.matmul(pb[:, 128 + h * 64:128 + (h + 1) * 64], NMb(p, h), TZ[p][:, 128 + h * 64:128 + (h + 1) * 64],
                     start=False, stop=False)
            P.matmul(pb[:, 128 + h * 64:128 + (h + 1) * 64], KMb(p, h), VT[p][:, h * 64:(h + 1) * 64],
                     start=False, stop=(h == 1))
        P.copy("act", OTK[p].v, pb[:, 128:256])
        P.transpose(pb[:, 256:384], OTK[p].v, ident)
        if d == 0:
            P.copy("dve", OB[p][:, cs], pb[:, 256:384])
        else:
            P.tt("dve", OB[p][:, cs], pb[:, 256:384], OF[p][:, cs], ALU.add)

    def readout(blk_id):
        t0 = tok0(blk_id)
        for p in range(2):
            P.matmul(ps_a[:, 0:W], bones, OB[p].v)
            P.stt(T1[p].v, ps_a[:, 0:W], -1.0 / 64, OB[p].v, ALU.mult, ALU.add)
            P.act(T2[p].v, T1[p].v, AF.Square)
            P.matmul(ps_a[:, W:2 * W], bones, T2[p].v)
            P.act(T2[p].v, ps_a[:, W:2 * W], AF.Sqrt, scale=1.0 / 64, bias=epsb[:, 2:3])
            P.recip(T2[p].v, T2[p].v)
            P.tt("pool", T1[p].v, T1[p].v, T2[p].v, ALU.mult)
            P.ts("pool", T1[p].v, T1[p].v, sm[:, S_GG + p:S_GG + p + 1], sm[:, S_GB + p:S_GB + p + 1],
                 ALU.mult, ALU.add)
            P.tt("pool", T2[p].v, KD[p].v, KD0[p].v, ALU.add)
            P.stt(T2[p].v, Rb[p].v, sm[:, S_RK + p:S_RK + p + 1], T2[p].v, ALU.mult, ALU.mult)
            P.matmul(ps_a[:, 0:W], bones, T2[p].v)
            P.tt("dve", T2[p].v, ps_a[:, 0:W], Vb[p].v, ALU.mult)
            P.tt("pool", T1[p].v, T1[p].v, T2[p].v, ALU.add)
            P.tt("pool", YB[p].v, T1[p].v, SG[p].v, ALU.mult)
            if isinstance(y0_d, list):
                pi, off = (0, t0) if t0 < TC else (1 + (t0 - TC) // 2048, (t0 - TC) % 2048)
                P.dma(y0_d[pi][p * 128:(p + 1) * 128, off:off + W], YB[p].v)
            else:
                P.dma(y0_d[p * 128:(p + 1) * 128, t0:t0 + W], YB[p].v)

    for p in range(2):
        P.memset("pool", ST[0][p].v, 0.0)
    for d in range(2):
        blocks = [-1] + (list(range(NBLK_L)) if d == 0 else list(range(NBLK_L - 1, -1, -1)))
        if debug_out and isinstance(debug_out, int) and debug_out > 1:
            blocks = blocks[:debug_out]
        sw_state = [0, 0]
        if d == 1:
            for p in range(2):
                P.memset("pool", ST[0][p].v, 0.0)
        issue_x(blocks[0], 0)
        for bi, b in enumerate(blocks):
            slot = bi % 2
            if bi + 1 < len(blocks):
                issue_x(blocks[bi + 1], 1 - slot)
            t0 = tok0(b)
            if d == 1:
                for p in range(2):
                    P.dma(OF[p].v, of_d[p * 128:(p + 1) * 128, t0:t0 + W])
            stage_a(b, slot, d)
            if stop == "a":
                return dbg_stop([(Rb[0].v, 256), (KK[1].v, 256), (LOGW[0].v, 256), (Ab[1].v, 256), (KD[0].v, 256),
                                 (AT[0].v, 256), (RT[0].v, 256), (BT[0].v, 256), (KT[0].v, 256), (BH[0].v, 256),
                                 (DG[0].v, 256), (Vb[1].v, 256)])
            tiles = (0, 1) if d == 0 else (1, 0)
            for tl in tiles:
                for p in range(2):
                    stage_b(p, tl, d, sw_state, upto=int(stop[1:]) if (stop and stop[0] == "b" and len(stop) > 1) else 99)
                    if stop and stop[0] == "b":
                        return dbg_stop([(OB[0][:, 0:128], 128), (ST[sw_state[0]][0].v, 128), (TZ[0].v, 256),
                                         (Lm[0].v, 256), (NM[0].v, 512), (Xk[1][0].v, 256), (RPT[0].v, 128), (PTbd[0][0].v, 128)])
            if d == 0:
                for p in range(2):
                    P.dma(of_d[p * 128:(p + 1) * 128, t0:t0 + W], OB[p].v)
            else:
                readout(b)
    if ctx is None:
        P.emit()
    return nc


def l0_inputs(inp, b, hg):
    f32 = np.float32
    x, ctx = inp["x"], inp["ctx"]
    z1 = np.zeros((D, 1), f32)
    xa = np.concatenate([z1, ctx[b].T, z1, z1, x[b].T, z1], axis=1)
    ct = np.zeros((128, 16), f32)
    cb = inp["c"][b].reshape(8, 128).T
    cc = inp["c_ctx"].reshape(8, 128).T
    ct[:, 0::2] = cb
    ct[:, 1::2] = cc
    adaw = np.ascontiguousarray(inp["ada_w"][0][:, 0:2048])
    hc = slice(hg * 256, (hg + 1) * 256)
    rw_in = inp["rw_in"][0]
    cols = []
    for X in range(4):
        cols.append(rw_in[:, X * 1024 + hg * 256: X * 1024 + (hg + 1) * 256])
    cols.append(rw_in[:, 4096:4352])
    win = np.ascontiguousarray(np.concatenate(cols, axis=1))
    sm = np.zeros((128, NSMALL), f32)
    sm[:, 0:8] = inp["norm_g"][0].reshape(8, 128).T
    sm[:, 8:24] = inp["ada_b"][0][:2048].reshape(16, 128).T
    mu = inp["rw_mu"][0]
    mus = []
    for X in range(3):
        for p in range(2):
            mus.append(mu[X * 1024 + hg * 256 + p * 128: X * 1024 + hg * 256 + (p + 1) * 128])
    mus.append(mu[3072:3200])
    mus.append(mu[3200:3328])
    sm[:, 24:32] = np.stack(mus, axis=1)
    for d in range(2):
        for p in range(2):
            sm[:, 32 + 2 * d + p] = inp["rw_w0"][0][d, hg * 256 + p * 128: hg * 256 + (p + 1) * 128]
            sm[:, 36 + 2 * d + p] = inp["rw_a0"][0][d, hg * 256 + p * 128: hg * 256 + (p + 1) * 128]
    for p in range(2):
        sl = slice(hg * 256 + p * 128, hg * 256 + (p + 1) * 128)
        sm[:, 40 + p] = inp["rw_kk"][0][sl]
        sm[:, 42 + p] = inp["rw_ka"][0][sl]
        sm[:, 44 + p] = inp["rw_rk"][0].reshape(-1)[sl]
        sm[:, 46 + p] = inp["rw_gn_g"][0][sl]
        sm[:, 48 + p] = inp["rw_gn_b"][0][sl]
    w2 = np.ascontiguousarray(inp["rw_w2"][0][:, :, hc].reshape(128, 256))
    a2 = np.ascontiguousarray(inp["rw_a2"][0][:, :, hc].reshape(128, 256))
    return {"xa": np.ascontiguousarray(xa), "ct": ct, "adaw": adaw, "smalls": sm, "win": win,
            "w2": w2, "a2": a2, "cst": l0_consts()}


SUBLN_EPS = 1e-5
LAM_INIT = 0.8 - 0.6 * float(np.exp(-0.3 * 1))
QSCALE = 0.125
NKT = TA // 128
NQB = TL // 256


def tok0(bi):
    return 0 if bi < 0 else TC + 256 * bi


def mod_compute(P, adaw_d, ncol, sct, adab_view, ps, modT, adaw_sb):
    adaw_r = adaw_d.t[:].rearrange("(k p) m -> p k m", p=128)
    for piece in range(ncol // 256):
        P.dma(adaw_sb.v, View(adaw_d, adaw_r[:, :, piece * 256:(piece + 1) * 256]))
        for mcl in range(2):
            mc = piece * 2 + mcl
            for kc in range(8):
                P.matmul(ps[:, 0:2], adaw_sb[:, kc, mcl * 128:(mcl + 1) * 128], sct[:, kc * 2:kc * 2 + 2],
                         start=(kc == 0), stop=(kc == 7))
            P.ts("dve", modT[:, mc, :], ps[:, 0:2], adab_view(mc), None, ALU.add)


def rope_tables():
    rows = TL // 64
    t = np.arange(TL)
    row = (t // 64).astype(np.float32)
    colid = (t % 64).astype(np.float32)
    inv = (10000.0 ** (-np.arange(16, dtype=np.float32) / 16)).astype(np.float32)
    ang_r = row[None, :] * inv[:, None]
    ang_c = colid[None, :] * inv[:, None]
    cos64 = np.concatenate([np.cos(ang_r), np.cos(ang_r), np.cos(ang_c), np.cos(ang_c)], axis=0)
    sin64 = np.concatenate([np.sin(ang_r), np.sin(ang_r), np.sin(ang_c), np.sin(ang_c)], axis=0)
    cosT = np.concatenate([cos64, cos64], axis=0).astype(np.float32)
    sinT = np.concatenate([sin64, sin64], axis=0).astype(np.float32)
    R = np.zeros((128, 128), np.float32)
    for base in range(0, 128, 32):
        for f in range(16):
            R[base + 16 + f, base + f] = -1.0
            R[base + f, base + 16 + f] = 1.0
    return cosT, sinT, R


def build_l1(stop=None, ctx=None):
    if ctx is None:
        nc = bass.Bass("TRN2", target_bir_lowering=False)
        P = Prog(nc)
        xa_d = P.dram_in("xa", [D, TA], F32)
        y0_d = P.dram_in("y0g", [D, TA], BF16)
        ct_d = P.dram_in("ct", [128, 16], F32)
        adaw0_d = P.dram_in("adaw0g", [D, 1024], F32)
        adaw1_d = P.dram_in("adaw1", [D, 2048], F32)
        sm_d = P.dram_in("smalls", [128, 48], F32)
        wo_d = P.dram_in("wo", [D, D], F32)
        wd_d = P.dram_in("wd", [D, 1024], F32)
        cst_d = P.dram_in("cst", [128, 384], F32)
        cos_d = P.dram_in("cosT", [128, TL], F32)
        sin_d = P.dram_in("sinT", [128, TL], F32)
        lam_d = P.dram_in("lamv", [128, 256], F32)
        subw_d = P.dram_in("subw", [128, 128], F32)
        y1_d = P.dram_out("y1", [256, TL], BF16)
        xn_d = P.dram_out("xn", [D, TL], F32)
    else:
        nc, P = ctx["nc"], ctx["P"]
        (xa_d, y0_d, ct_d, adaw0_d, adaw1_d, sm_d, wo_d, wd_d, cst_d, cos_d, sin_d, lam_d, subw_d, y1_d, xn_d) = [
            ctx[k] for k in ("b_xa", "y0g", "ct", "b_adaw0g", "b_adaw1", "b_smalls", "b_wo", "b_wd", "b_cst",
                             "b_cosT", "b_sinT", "b_lamv", "b_subw", "y1loc", "xn_s")]

    cst = P.sb("cst", [128, 384], F32)
    P.dma(cst.v, cst_d.v)
    ident = cst[:, 0:128]
    bones = cst[:, 128:256]
    rrot = cst[:, 256:384]
    sm = P.sb("sm", [128, 48], F32)
    P.dma(sm.v, sm_d.v)
    S_NG, S_B0, S_B1, S_QN, S_KN = 0, 8, 16, 32, 33
    ones128 = P.sb("ones128", [128, 128], F32)
    P.memset("pool", ones128.v, 1.0)
    epsb = P.sb("epsb", [128, 4], F32)
    P.memset("pool", epsb[:, 0:1], RMS_EPS)
    P.memset("pool", epsb[:, 1:2], SUBLN_EPS)
    banks = [P.ps("bank%d" % i, [128, 512]) for i in range(8)]

    ct = P.sb("ct", [128, 16], F32)
    P.dma(ct.v, ct_d.v)
    sct = P.sb("sct", [128, 16], F32)
    P.act(sct.v, ct.v, AF.Silu)
    adaw_sb = P.sb("adaw_sb", [128, 8, 256], F32)
    mod0 = P.sb("mod0", [128, 8, 2], F32)
    mod1 = P.sb("mod1", [128, 16, 2], F32)
    mod_compute(P, adaw0_d, 1024, sct, lambda mc: sm[:, S_B0 + mc:S_B0 + mc + 1], banks[0], mod0, adaw_sb)
    mod_compute(P, adaw1_d, 2048, sct, lambda mc: sm[:, S_B1 + mc:S_B1 + mc + 1], banks[0], mod1, adaw_sb)
    gmod = P.sb("gmod", [128, 8, 2], F32)
    for kc in range(8):
        P.ts("pool", gmod[:, kc, :], mod1[:, 8 + kc, :], 1.0, sm[:, S_NG + kc:S_NG + kc + 1], ALU.add, ALU.mult)

    lamv = P.sb("lamv", [128, 256], F32)
    P.dma(lamv.v, lam_d.v)
    lt = P.sb("lt", [128, 128], F32)
    lsc = P.sb("lsc", [128, 8], F32)
    P.tt("pool", lt[:, 0:64], lamv[:, 0:64], lamv[:, 64:128], ALU.mult)
    P.tt("pool", lt[:, 64:128], lamv[:, 128:192], lamv[:, 192:256], ALU.mult)
    P.reduce(lsc[:, 0:1], lt[:, 0:64], AX.X, ALU.add)
    P.reduce(lsc[:, 1:2], lt[:, 64:128], AX.X, ALU.add)
    P.act(lsc[:, 2:4], lsc[:, 0:2], AF.Exp)
    P.tt("pool", lsc[:, 4:5], lsc[:, 2:3], lsc[:, 3:4], ALU.subtract)
    P.ts("pool", lsc[:, 5:6], lsc[:, 4:5], LAM_INIT, -1.0, ALU.add, ALU.mult)
    neglam = lsc[:, 5:6]
    subw = P.sb("subw", [128, 128], F32)
    P.dma(subw.v, subw_d.v)
    P.ts("pool", subw.v, subw.v, 1.0 - LAM_INIT, None, ALU.mult)

    Wo = P.sb("Wo", [128, 8, 1024], BF16)
    Wd = P.sb("Wd", [128, 8, 1024], BF16)
    wst = [P.sb("wst%d" % i, [128, 1024], F32) for i in range(1)] * 2
    n = 0
    for src, dst in ((wo_d, Wo), (wd_d, Wd)):
        for kc in range(8):
            P.dma(wst[n % 2].v, src[kc * 128:(kc + 1) * 128, :])
            P.copy("pool", dst[:, kc, :], wst[n % 2].v)
            n += 1

    xin = [P.sb("xin%d" % i, [128, 8, W], F32) for i in range(2)]
    yin = [P.sb("yin%d" % i, [128, 8, W], BF16) for i in range(2)]
    xn = P.sb("xn", [128, 8, W], F32)
    hT = P.sb("hT", [128, 8, W], BF16)
    sqb = [P.sb("sqb%d" % i, [128, W], F32) for i in range(2)]
    rstd = P.sb("rstd", [128, W], F32)
    htmp = [P.sb("htmp%d" % i, [128, W], F32) for i in range(2)]
    cosb = P.sb("cosb", [128, W], F32)
    sinb = P.sb("sinb", [128, W], F32)
    qraw = P.sb("qraw", [128, W], F32)
    qsq = P.sb("qsq", [128, W], F32)
    qrs = P.sb("qrs", [128, W], F32)
    qn_ = P.sb("qn_", [128, W], F32)
    qt1 = P.sb("qt1", [128, W], F32)
    qt2 = P.sb("qt2", [128, W], F32)
    KTb = P.sb("KTb", [128, TA], BF16)
    QTb = P.sb("QTb", [128, NQB * 512], BF16)
    P.memset("pool", QTb.v, 0.0)
    Vx = P.sb("Vx", [128, NKT, 130], BF16)
    P.memset("pool", Vx[:, :, 129:130], 0.0)
    Gs = P.sb("Gs", [128, TL // 128, 128], BF16)
    P.memset("pool", Vx[:, :, 128:129], 1.0)
    pT = [P.sb("pT%d" % i, [128, 512], BF16) for i in range(2)]
    vg_sb = [P.sb("vg_sb%d" % i, [128, 512], F32) for i in range(2)]
    o_sb = P.sb("o_sb", [128, 128], F32)
    o_sq = P.sb("o_sq", [128, 128], F32)
    ybs = [P.sb("yb%d" % i, [128, 256], BF16) for i in range(2)]
    zs = P.sb("zs", [128, 8], F32)
    ps_bf = P.ps("ps_bf_unused", [128, 2], F32) if False else None

    xa_r = xa_d.t[:].rearrange("(k p) c -> p k c", p=128)
    y0_r = None if isinstance(y0_d, list) else y0_d.t[:].rearrange("(k p) c -> p k c", p=128)
    xn_r = xn_d.t[:].rearrange("(k p) c -> p k c", p=128)

    def issue(bi, slot):
        t0 = tok0(bi)
        P.dma(xin[slot].v, View(xa_d, xa_r[:, :, t0:t0 + W]))
        if isinstance(y0_d, list):
            pi, off = (0, t0) if t0 < TC else (1 + (t0 - TC) // 2048, (t0 - TC) % 2048)
            yr = y0_d[pi].t[:].rearrange("(k p) c -> p k c", p=128)
            P.dma(yin[slot].v, View(y0_d[pi], yr[:, :, off:off + W]))
        else:
            P.dma(yin[slot].v, View(y0_d, y0_r[:, :, t0:t0 + W]))

    def stage_a(bi, slot, h, hp_quarter, first_pass):
        col = 1 if bi < 0 else 0
        t0 = tok0(bi)
        lat0 = t0 - TC
        for oc in range(8):
            pp = banks[oc % 2]
            for kc in range(8):
                P.matmul(pp[:, 0:W], Wo[:, kc, oc * 128:(oc + 1) * 128], yin[slot][:, kc, :],
                         start=(kc == 0), stop=(kc == 7))
            P.stt(xn[:, oc, :], pp[:, 0:W], mod0[:, oc, col:col + 1], xin[slot][:, oc, :], ALU.mult, ALU.add)
        if first_pass and bi >= 0 and stop != 'noxn':
            P.dma(View(xn_d, xn_r[:, :, lat0:lat0 + W]), xn.v)
        if stop == 'a1':
            return
        for kc in range(8):
            sq = sqb[kc % 2]
            P.act(sq.v, xn[:, kc, :], AF.Square)
            P.matmul(banks[2][:, 0:W], ones128.v, sq.v, start=(kc == 0), stop=(kc == 7))
        P.act(rstd.v, banks[2][:, 0:W], AF.Sqrt, scale=1.0 / D, bias=epsb[:, 0:1])
        P.recip(rstd.v, rstd.v)
        for kc in range(8):
            tmp = htmp[kc % 2]
            P.stt(tmp.v, xn[:, kc, :], gmod[:, kc, col:col + 1], rstd.v, ALU.mult, ALU.mult)
            P.act(hT[:, kc, :], tmp.v, AF.Identity, bias=mod1[:, kc, col:col + 1])
        if stop == 'a2':
            return
        if bi >= 0:
            P.dma(cosb.v, cos_d[:, lat0:lat0 + W])
            P.dma(sinb.v, sin_d[:, lat0:lat0 + W])
        for which in (("q", "k") if bi >= 0 else ("k",)):
            cc = h if which == "q" else 2 + h
            pp = banks[3]
            for kc in range(8):
                P.matmul(pp[:, 0:W], Wd[:, kc, cc * 128:(cc + 1) * 128], hT[:, kc, :],
                         start=(kc == 0), stop=(kc == 7))
            P.copy("act", qraw.v, pp[:, 0:W])
            P.tt("pool", qsq.v, qraw.v, qraw.v, ALU.mult)
            P.matmul(banks[4][:, 0:W], bones, qsq.v)
            P.act(qrs.v, banks[4][:, 0:W], AF.Sqrt, scale=1.0 / 64, bias=epsb[:, 0:1])
            P.recip(qrs.v, qrs.v)
            wcol = sm[:, S_QN:S_QN + 1] if which == "q" else sm[:, S_KN:S_KN + 1]
            P.stt(qn_.v, qraw.v, wcol, qrs.v, ALU.mult, ALU.mult)
            if bi >= 0:
                P.matmul(banks[4][:, W:2 * W], rrot, qn_.v)
                P.tt("dve", qt1.v, banks[4][:, W:2 * W], sinb.v, ALU.mult)
                P.tt("pool", qt2.v, qn_.v, cosb.v, ALU.mult)
                if which == "q":
                    qb_ = lat0 // 256
                    P.tt("pool", QTb[0:64, qb_ * 512:qb_ * 512 + 256], qt1[0:64, :], qt2[0:64, :], ALU.add)
                    P.tt("pool", QTb[64:128, qb_ * 512 + 256:qb_ * 512 + 512], qt1[64:128, :], qt2[64:128, :], ALU.add)
                else:
                    P.tt("pool", KTb[:, t0:t0 + W], qt1.v, qt2.v, ALU.add)
            else:
                P.copy("pool", KTb[:, t0:t0 + W], qn_.v)
        if stop == 'a3':
            return
        for sub in range(2):
            pp = banks[5 + sub]
            for kc in range(8):
                P.matmul(pp[:, 0:512], hT[:, kc, sub * 128:(sub + 1) * 128], Wd[:, kc, 512:1024],
                         start=(kc == 0), stop=(kc == 7))
            kt = (t0 + sub * 128) // 128
            vg = vg_sb[sub]
            P.copy("dve", vg.v, pp[:, 0:512])
            P.copy("pool", Vx[:, kt, 0:128], vg[:, h * 128:(h + 1) * 128])
            if bi >= 0:
                qt = (lat0 + sub * 128) // 128
                P.act(Gs[:, qt, :], vg[:, 256 + h * 128:256 + (h + 1) * 128], AF.Silu)

    def attention(h, nqb=NQB):
        sbank = [(banks[0], banks[1]), (banks[2], banks[3])]
        acc = [[banks[4], banks[5]], [banks[6], banks[7]]]
        n = 0
        for qb in range(nqb):
            q0 = qb * 256
            for kt in range(NKT):
                sA = banks[n % 4]
                pt = pT[n % 2]
                P.matmul(sA[:, 0:512], KTb[:, kt * 128:(kt + 1) * 128], QTb[:, qb * 512:(qb + 1) * 512])
                P.act(pt.v, sA[:, 0:512], AF.Exp, scale=QSCALE)
                if stop == 's1':
                    n += 1
                    continue
                for comp in range(2):
                    for qs in range(2):
                        P.matmul(acc[comp][qs][:, 0:130], pt[:, comp * 256 + qs * 128:comp * 256 + (qs + 1) * 128],
                                 Vx[:, kt, :], start=(kt == 0), stop=(kt == NKT - 1))
                n += 1
            if stop in ('s1', 's2'):
                continue
            ytile = ybs[qb % 2]
            for qs in range(2):
                a0 = acc[0][qs]
                a1 = acc[1][qs]
                P.recip(zs[:, 0:1], a0[:, 128:129])
                P.recip(zs[:, 1:2], a1[:, 128:129])
                P.tt("pool", zs[:, 2:3], zs[:, 1:2], neglam, ALU.mult)
                P.ts("dve", o_sb.v, a0[:, 0:128], zs[:, 0:1], None, ALU.mult)
                P.stt(o_sb.v, a1[:, 0:128], zs[:, 2:3], o_sb.v, ALU.mult, ALU.add)
                P.tt("pool", o_sq.v, o_sb.v, o_sb.v, ALU.mult)
                P.reduce(zs[:, 3:4], o_sq.v, AX.X, ALU.add)
                P.act(zs[:, 4:5], zs[:, 3:4], AF.Sqrt, scale=1.0 / 128, bias=epsb[:, 1:2])
                P.recip(zs[:, 4:5], zs[:, 4:5])
                P.stt(o_sb.v, o_sb.v, zs[:, 4:5], subw.v, ALU.mult, ALU.mult)
                qt = (q0 + qs * 128) // 128
                P.tt("pool", o_sq.v, o_sb.v, Gs[:, qt, :], ALU.mult)
                P.transpose(a0[:, 256:384], o_sq.v, ident)
                P.copy("act", ytile[:, qs * 128:(qs + 1) * 128], a0[:, 256:384])
            if isinstance(y1_d, list):
                P.dma(y1_d[q0 // 2048][h * 128:(h + 1) * 128, q0 % 2048:q0 % 2048 + 256], ytile.v)
            else:
                P.dma(y1_d[h * 128:(h + 1) * 128, q0:q0 + 256], ytile.v)

    return nc, P, issue, stage_a, attention


def build_l1_full(hp_quarter, do_attn=True, nheads=2, stop=None, nblocks=None, nqb=NQB, ctx=None):
    nc, P, issue, stage_a, attention = build_l1(stop, ctx)
    blocks = [-1] + list(range(NBLK_L))
    if nblocks:
        blocks = blocks[:nblocks]
    if stop == 'pre':
        P.emit()
        return nc
    for h in range(nheads):
        issue(blocks[0], 0)
        for i, bi in enumerate(blocks):
            if i + 1 < len(blocks):
                issue(blocks[i + 1], (i + 1) % 2)
            stage_a(bi, i % 2, h, hp_quarter, h == 0)
        if do_attn:
            attention(h, nqb)
    if ctx is None:
        P.emit()
    return nc


def l1_inputs(inp, b, hp, y0g_b):
    f32 = np.float32
    xa = np.ascontiguousarray(np.concatenate([inp["ctx"][b].T, inp["x"][b].T], axis=1))
    ct = np.zeros((128, 16), f32)
    ct[:, 0::2] = inp["c"][b].reshape(8, 128).T
    ct[:, 1::2] = inp["c_ctx"].reshape(8, 128).T
    sm = np.zeros((128, 48), f32)
    sm[:, 0:8] = inp["norm_g"][1].reshape(8, 128).T
    sm[:, 8:16] = inp["ada_b"][0][2048:3072].reshape(8, 128).T
    sm[:, 16:32] = inp["ada_b"][1][0:2048].reshape(16, 128).T
    sm[:, 32] = np.tile(inp["da_qn"][0], 2)
    sm[:, 33] = np.tile(inp["da_kn"][0], 2)
    da_in = inp["da_in"][0]
    cols = []
    for X in range(4):
        for hh in range(2):
            base = X * 1024 + (hp * 2 + hh) * 128
            cols.append(da_in[:, base:base + 128])
    wd = np.ascontiguousarray(np.concatenate(cols, axis=1))
    cosT, sinT, R = rope_tables()
    ident = np.eye(128, dtype=f32)
    bones = np.kron(np.eye(2, dtype=f32), np.ones((64, 64), f32))
    cst = np.ascontiguousarray(np.concatenate([ident, bones, R], axis=1))
    lamv = np.ascontiguousarray(np.broadcast_to(inp["da_lam"][0].reshape(1, 256), (128, 256))).astype(f32)
    subw = np.ascontiguousarray(np.broadcast_to(inp["da_subln"][0].reshape(1, 128), (128, 128))).astype(f32)
    return {"xa": xa, "y0g": y0g_b, "ct": ct,
            "adaw0g": np.ascontiguousarray(inp["ada_w"][0][:, 2048:3072]),
            "adaw1": np.ascontiguousarray(inp["ada_w"][1][:, 0:2048]),
            "smalls": sm, "wo": np.ascontiguousarray(inp["rw_out"][0]), "wd": wd, "cst": cst,
            "cosT": cosT, "sinT": sinT, "lamv": lamv, "subw": subw}


def build_l2(ctx=None):
    if ctx is None:
        nc = bass.Bass("TRN2", target_bir_lowering=False)
        P = Prog(nc)
        NT = 2048
        xn_d = P.dram_in("xn", [D, NT], F32)
        y1_d = P.dram_in("y1T", [D, NT], BF16)
        ct_d = P.dram_in("ct", [128, 16], F32)
        adaw_d = P.dram_in("adaw1g", [D, 1024], F32)
        sm_d = P.dram_in("smalls", [128, 8], F32)
        w_d = P.dram_in("wda", [D, D], F32)
        out_d = P.dram_out("outT", [D, NT], F32)
    else:
        nc, P = ctx["nc"], ctx["P"]
        NT = TL
        xn_d, y1_d, ct_d, adaw_d, sm_d, w_d, out_d = [ctx[k] for k in (
            "xn_s", "y1g", "ct", "c_adaw1g", "c_smalls", "c_wda", "outT")]
    sm = P.sb("sm", [128, 8], F32)
    P.dma(sm.v, sm_d.v)
    banks = [P.ps("bank%d" % i, [128, 512]) for i in range(3)]
    ct = P.sb("ct", [128, 16], F32)
    P.dma(ct.v, ct_d.v)
    sct = P.sb("sct", [128, 16], F32)
    P.act(sct.v, ct.v, AF.Silu)
    adaw_sb = P.sb("adaw_sb", [128, 8, 256], F32)
    modg = P.sb("modg", [128, 8, 2], F32)
    mod_compute(P, adaw_d, 1024, sct, lambda mc: sm[:, mc:mc + 1], banks[0], modg, adaw_sb)
    Wa = P.sb("Wa", [128, 8, 1024], BF16)
    wst = [P.sb("wst%d" % i, [128, 1024], F32) for i in range(2)]
    for kc in range(8):
        P.dma(wst[kc % 2].v, w_d[kc * 128:(kc + 1) * 128, :])
        P.copy("pool", Wa[:, kc, :], wst[kc % 2].v)
    xin = [P.sb("xin%d" % i, [128, 8, W], F32) for i in range(2)]
    yin = [P.sb("yin%d" % i, [128, 8, W], BF16) for i in range(2)]
    ob = [P.sb("ob%d" % i, [128, 8, W], F32) for i in range(2)]
    xn_r = xn_d.t[:].rearrange("(k p) c -> p k c", p=128)
    y1_r = None if isinstance(y1_d, list) else y1_d.t[:].rearrange("(k p) c -> p k c", p=128)
    out_r = out_d.t[:].rearrange("(k p) c -> p k c", p=128)
    for bi in range(NT // W):
        s_ = bi % 2
        P.dma(xin[s_].v, View(xn_d, xn_r[:, :, bi * W:(bi + 1) * W]))
        if isinstance(y1_d, list):
            c0 = bi * W
            yr = y1_d[c0 // 2048].t[:].rearrange("(k p) c -> p k c", p=128)
            P.dma(yin[s_].v, View(y1_d[c0 // 2048], yr[:, :, c0 % 2048:c0 % 2048 + W]))
        else:
            P.dma(yin[s_].v, View(y1_d, y1_r[:, :, bi * W:(bi + 1) * W]))
        for oc in range(8):
            pp = banks[1 + oc % 2]
            for kc in range(8):
                P.matmul(pp[:, 0:W], Wa[:, kc, oc * 128:(oc + 1) * 128], yin[s_][:, kc, :],
                         start=(kc == 0), stop=(kc == 7))
            P.stt(ob[s_][:, oc, :], pp[:, 0:W], modg[:, oc, 0:1], xin[s_][:, oc, :], ALU.mult, ALU.add)
        P.dma(View(out_d, out_r[:, :, bi * W:(bi + 1) * W]), ob[s_].v)
    if ctx is None:
        P.emit()
    return nc


def l2_inputs(inp, b, tq, xn, y1T):
    f32 = np.float32
    ct = np.zeros((128, 16), f32)
    ct[:, 0::2] = inp["c"][b].reshape(8, 128).T
    ct[:, 1::2] = inp["c_ctx"].reshape(8, 128).T
    sm = np.ascontiguousarray(inp["ada_b"][1][2048:3072].reshape(8, 128).T).astype(f32)
    return {"xn": xn, "y1T": y1T, "ct": ct, "adaw1g": np.ascontiguousarray(inp["ada_w"][1][:, 2048:3072]),
            "smalls": sm, "wda": np.ascontiguousarray(inp["da_out"][0])}


GROUPS = [[0, 1, 2, 3], [4, 5, 6, 7]]


def build_fused(upto=None):
    nc = bass.Bass("TRN2", target_bir_lowering=False)
    P = Prog(nc)
    ctx = {"nc": nc, "P": P}
    decl = [("a_xa", [D, XA_COLS], F32), ("ct", [128, 16], F32), ("a_adaw", [D, 2048], F32),
            ("a_smalls", [128, NSMALL], F32), ("a_win", [D, 1280], F32), ("a_w2", [128, 256], F32),
            ("a_a2", [128, 256], F32), ("a_cst", [128, C_TOT], F32),
            ("b_xa", [D, TA], F32), ("b_adaw0g", [D, 1024], F32), ("b_adaw1", [D, 2048], F32),
            ("b_smalls", [128, 48], F32), ("b_wo", [D, D], F32), ("b_wd", [D, 1024], F32),
            ("b_cst", [128, 384], F32), ("b_cosT", [128, TL], F32), ("b_sinT", [128, TL], F32),
            ("b_lamv", [128, 256], F32), ("b_subw", [128, 128], F32),
            ("c_adaw1g", [D, 1024], F32), ("c_smalls", [128, 8], F32), ("c_wda", [D, D], F32)]
    for name, shape, dt_ in decl:
        if upto == "A" and name[0] in "bc" and name[1] == "_":
            continue
        if upto == "B" and name[0] == "c" and name[1] == "_":
            continue
        ctx[name] = P.dram_in(name, shape, dt_)
    ctx["outT"] = P.dram_out("outT", [D, TL], F32)
    pw0 = [TC, 2048, 2048, 2048, 2048]
    ctx["y0loc"] = [P.dram_tmp("y0loc%d" % i, [256, w], BF16) for i, w in enumerate(pw0)]
    ctx["y0g"] = [P.dram_tmp("y0g%d" % i, [D, w], BF16) for i, w in enumerate(pw0)]
    ctx["of_scratch"] = P.dram_tmp("of_scratch", [256, TA], F32)
    ctx["xn_s"] = P.dram_tmp("xn_s", [D, TL], F32)
    ctx["y1loc"] = [P.dram_tmp("y1loc%d" % i, [256, 2048], BF16) for i in range(4)]
    ctx["y1g"] = [P.dram_tmp("y1g%d" % i, [D, 2048], BF16) for i in range(4)]
    build_l0(ctx=ctx)
    P.emit()
    P.end_phase()
    for i in range(5):
        P.collective("AllGather", ctx["y0g"][i].v, ctx["y0loc"][i].v, GROUPS)
    if upto == "A":
        tb = P.sb("dbg_b", [128, 8, 512], BF16)
        tf = P.sb("dbg_f", [128, 8, 512], F32)
        orr = ctx["outT"].t[:].rearrange("(k p) c -> p k c", p=128)
        for i in range(4):
            pi, off = ((1, 0), (1, 512), (2, 1536), (4, 1536))[i]
            yr = ctx["y0g"][pi].t[:].rearrange("(k p) c -> p k c", p=128)
            P.dma(tb.v, View(ctx["y0g"][pi], yr[:, :, off:off + 512]))
            P.copy("dve", tf.v, tb.v)
            P.dma(View(ctx["outT"], orr[:, :, i * 512:(i + 1) * 512]), tf.v)
        P.emit()
        P.end_phase()
        return nc
    build_l1_full(0, ctx=ctx)
    P.emit()
    P.end_phase()
    for i in range(4):
        P.collective("AllGather", ctx["y1g"][i].v, ctx["y1loc"][i].v, GROUPS)
    if upto == "B":
        tb = P.sb("dbg_b", [128, 8, 512], BF16)
        tf = P.sb("dbg_f", [128, 8, 512], F32)
        xr = ctx["xn_s"].t[:].rearrange("(k p) c -> p k c", p=128)
        orr = ctx["outT"].t[:].rearrange("(k p) c -> p k c", p=128)
        for i in range(2):
            c0 = (0, TL - 512)[i]
            yr = ctx["y1g"][c0 // 2048].t[:].rearrange("(k p) c -> p k c", p=128)
            P.dma(tb.v, View(ctx["y1g"][c0 // 2048], yr[:, :, c0 % 2048:c0 % 2048 + 512]))
            P.copy("dve", tf.v, tb.v)
            P.dma(View(ctx["outT"], orr[:, :, i * 512:(i + 1) * 512]), tf.v)
            P.dma(tf.v, View(ctx["xn_s"], xr[:, :, c0:c0 + 512]))
            P.dma(View(ctx["outT"], orr[:, :, (2 + i) * 512:(3 + i) * 512]), tf.v)
        P.emit()
        P.end_phase()
        return nc
    build_l2(ctx=ctx)
    P.emit()
    P.end_phase()
    return nc


def fused_inputs(inp, b, g):
    m = {}
    a = l0_inputs(inp, b, g)
    for k in ("xa", "adaw", "smalls", "win", "w2", "a2", "cst"):
        m["a_" + k] = a[k]
    m["ct"] = a["ct"]
    bb = l1_inputs(inp, b, g, None)
    for k in ("xa", "adaw0g", "adaw1", "smalls", "wo", "wd", "cst", "cosT", "sinT", "lamv", "subw"):
        m["b_" + k] = bb[k]
    c = l2_inputs(inp, b, g, None, None)
    for k in ("adaw1g", "smalls", "wda"):
        m["c_" + k] = c[k]
    return m


def kernel(**inp):
    inp = {k: np.asarray(v) for k, v in inp.items()}
    cores = list(range(8))
    nc = build_fused()
    maps = [fused_inputs(inp, c // 4, c % 4) for c in cores]
    res = run_bass_kernel_spmd(nc, maps, core_ids=cores).results
    out = np.zeros((2, TL, D), np.float32)
    for c in cores:
        b, tq = c // 4, c % 4
        out[b, tq * 2048:(tq + 1) * 2048] = np.asarray(res[c]["outT"])[:, tq * 2048:(tq + 1) * 2048].T
    return out
```

```python
import numpy as np
import concourse.bass as bass
import concourse.mybir as mybir
from concourse.bass_utils import run_bass_kernel_spmd

F32 = mybir.dt.float32
BF16 = mybir.dt.bfloat16
AF = mybir.ActivationFunctionType
ALU = mybir.AluOpType
AX = mybir.AxisListType

ENGS = ("pe", "act", "dve", "pool", "sp")


class Buf:
    def __init__(self, prog, t, name, space):
        self.prog = prog
        self.t = t
        self.name = name
        self.space = space
        self.last_writer = None
        self.readers = []
        self.dma_sem = None
        self.dma_count = 0

    def __getitem__(self, idx):
        return View(self, self.t[idx])

    @property
    def v(self):
        return View(self, self.t[:])


class View:
    def __init__(self, buf, ap):
        self.buf = buf
        self.ap = ap

    def __getitem__(self, idx):
        return View(self.buf, self.ap[idx])


class Op:
    __slots__ = ("eng", "fn", "deps", "needs_inc", "seq", "dma_buf", "dma_val", "idx", "dma_inc")

    def __init__(self, eng, fn):
        self.eng = eng
        self.fn = fn
        self.deps = []
        self.needs_inc = False
        self.seq = None
        self.dma_buf = None
        self.dma_val = None
        self.dma_inc = 16


def _ap(x):
    return x.ap if isinstance(x, View) else x


class Prog:
    def __init__(self, nc):
        import contextlib
        self.nc = nc
        self.ops = {e: [] for e in ENGS}
        self.bufs = []
        self.dram = {}
        self.same_engine_sync = True
        self.stack = contextlib.ExitStack()
        self.phase = 0
        self.phase_sem = None
        self.uid = 0

    def end_phase(self):
        import contextlib
        self.stack.close()
        self.stack = contextlib.ExitStack()
        self.bufs = []

    def sb(self, name, shape, dtype):
        self.uid += 1
        t = self.stack.enter_context(self.nc.sbuf_tensor("sb%d_%s" % (self.uid, name), list(shape), dtype))
        b = Buf(self, t, name, "sb")
        self.bufs.append(b)
        return b

    def ps(self, name, shape, dtype=F32):
        self.uid += 1
        t = self.stack.enter_context(self.nc.psum_tensor("pp%d_%s" % (self.uid, name), list(shape), dtype))
        b = Buf(self, t, name, "ps")
        self.bufs.append(b)
        return b

    def dram_in(self, name, shape, dtype):
        t = self.nc.dram_tensor(name, list(shape), dtype, kind="ExternalInput")
        b = Buf(self, t, name, "dram")
        self.dram[name] = b
        return b

    def dram_out(self, name, shape, dtype):
        t = self.nc.dram_tensor(name, list(shape), dtype, kind="ExternalOutput")
        b = Buf(self, t, name, "dram")
        self.dram[name] = b
        return b

    def dram_tmp(self, name, shape, dtype, shared=False):
        if shared:
            t = self.nc.dram_tensor(name, list(shape), dtype, addr_space="Shared")
        else:
            t = self.nc.dram_tensor(name, list(shape), dtype)
        b = Buf(self, t, name, "dram")
        self.dram[name] = b
        return b

    def _record(self, eng, fn, reads, writes):
        op = Op(eng, fn)
        deps = []
        for v in reads:
            b = v.buf if isinstance(v, View) else v
            if b.last_writer is not None:
                deps.append(b.last_writer)
            if b.space == "ps":
                deps.extend(r for r in b.readers if r.eng != eng)
        for v in writes:
            b = v.buf if isinstance(v, View) else v
            if b.last_writer is not None:
                deps.append(b.last_writer)
            deps.extend(b.readers)
        seen = set()
        for d in deps:
            if id(d) in seen or d is op:
                continue
            seen.add(id(d))
            if d.dma_buf is None and d.eng == eng and (eng == "pe" or not self.same_engine_sync):
                continue
            op.deps.append(d)
            if d.dma_buf is None:
                d.needs_inc = True
        for v in writes:
            b = v.buf if isinstance(v, View) else v
            b.last_writer = op
            b.readers = []
        for v in reads:
            b = v.buf if isinstance(v, View) else v
            if b.last_writer is not op:
                b.readers.append(op)
        self.ops[eng].append(op)
        return op

    def op(self, eng, fn, reads=(), writes=()):
        return self._record(eng, fn, list(reads), list(writes))

    def matmul(self, out, lhsT, rhs, start=True, stop=True, extra_reads=(), **kw):
        o, l, r = _ap(out), _ap(lhsT), _ap(rhs)
        return self._record("pe", lambda e: e.matmul(o, l, r, start=start, stop=stop, **kw),
                            [lhsT, rhs] + list(extra_reads), [out])

    def transpose(self, out, in_, ident):
        o, i, d = _ap(out), _ap(in_), _ap(ident)
        return self._record("pe", lambda e: e.transpose(o, i, d), [in_, ident], [out])

    def act(self, out, in_, func, bias=None, scale=None, eng="act", accum_out=None):
        o, i = _ap(out), _ap(in_)
        kw = {}
        reads = [in_]
        writes = [out]
        if bias is not None:
            kw["bias"] = _ap(bias)
            if isinstance(bias, View):
                reads.append(bias)
        if scale is not None:
            kw["scale"] = _ap(scale)
            if isinstance(scale, View):
                reads.append(scale)
        if accum_out is not None:
            kw["accum_out"] = _ap(accum_out)
            writes.append(accum_out)
        return self._record("act", lambda e: e.activation(o, i, func, **kw), reads, writes)

    def tt(self, eng, out, in0, in1, op):
        o, a, b = _ap(out), _ap(in0), _ap(in1)
        return self._record(eng, lambda e: e.tensor_tensor(o, a, b, op), [in0, in1], [out])

    def ts(self, eng, out, in0, s1, s2, op0, op1=None, accum_out=None):
        o, a = _ap(out), _ap(in0)
        reads = [in0]
        writes = [out]
        for s in (s1, s2):
            if isinstance(s, View):
                reads.append(s)
        x1, x2 = _ap(s1), _ap(s2)
        kw = {}
        if op1 is not None:
            kw["op1"] = op1
        if accum_out is not None:
            kw["accum_out"] = _ap(accum_out)
            writes.append(accum_out)
        return self._record(eng, lambda e: e.tensor_scalar(o, a, x1, x2, op0, **kw), reads, writes)

    def stt(self, out, in0, scalar, in1, op0, op1, eng="dve"):
        o, a, b = _ap(out), _ap(in0), _ap(in1)
        reads = [in0, in1]
        if isinstance(scalar, View):
            reads.append(scalar)
        s = _ap(scalar)
        return self._record(eng, lambda e: e.scalar_tensor_tensor(o, a, s, b, op0, op1), reads, [out])

    def copy(self, eng, out, in_):
        o, i = _ap(out), _ap(in_)
        if eng == "act":
            return self._record(eng, lambda e: e.copy(o, i), [in_], [out])
        return self._record(eng, lambda e: e.tensor_copy(o, i), [in_], [out])

    def scan(self, out, d0, d1, initial, op0, op1):
        o, a, b = _ap(out), _ap(d0), _ap(d1)
        reads = [d0, d1]
        if isinstance(initial, View):
            reads.append(initial)
        ini = _ap(initial)
        return self._record("dve", lambda e: e.tensor_tensor_scan(o, a, b, ini, op0, op1), reads, [out])

    def recip(self, out, in_):
        o, i = _ap(out), _ap(in_)
        return self._record("dve", lambda e: e.reciprocal(o, i), [in_], [out])

    def memset(self, eng, out, val):
        o = _ap(out)
        return self._record(eng, lambda e: e.memset(o, val), [], [out])

    def reduce(self, out, in_, axis, op, eng="dve"):
        o, i = _ap(out), _ap(in_)
        return self._record(eng, lambda e: e.tensor_reduce(o, i, axis, op), [in_], [out])

    def dma(self, out, in_, queue="sp", **kw):
        o, i = _ap(out), _ap(in_)
        ob = out.buf
        ib = in_.buf
        key = ob if ob.space != "dram" else ib
        op = self._record(queue, lambda e: e.dma_start(out=o, in_=i, **kw), [in_], [out])
        key.dma_count += 1
        op.dma_buf = key
        op.dma_val = 16 * key.dma_count
        return op

    def emit(self, final_waits=()):
        nc = self.nc
        for e in ENGS:
            n = 0
            for op in self.ops[e]:
                if op.dma_buf is None and op.needs_inc:
                    n += 1
                    op.seq = n
        SEMCAP = 30000
        nsem = {e: 1 + max([op.seq or 0 for op in self.ops[e]] + [0]) // SEMCAP for e in ENGS}
        ph = self.phase
        sems = {e: [nc.alloc_semaphore("s%d_%s_%d" % (ph, e, i)) for i in range(nsem[e])] for e in ENGS}
        if self.phase_sem is None:
            self.phase_sem = nc.alloc_semaphore("phase_done")
        phase_sem = self.phase_sem
        dummy_sb = self.sb("phdummy", [128, 8], F32)

        for b in self.bufs + list(self.dram.values()):
            if b.dma_count > 0:
                b.dma_sem = nc.alloc_semaphore("d%d_%s" % (ph, b.name))
        engmap = {"pe": "tensor", "act": "scalar", "dve": "vector", "pool": "gpsimd", "sp": "sync"}
        all_dma = []
        for e in ENGS:
            for op in self.ops[e]:
                if op.dma_buf is not None:
                    all_dma.append(op)

        def gen(ename):
            def body(eng):
                waited = {}
                if ph > 0:
                    eng.wait_ge(phase_sem, 4 * ph)
                for op in self.ops[ename]:
                    need = {}
                    for d in op.deps:
                        if d.dma_buf is not None:
                            k = ("dma", id(d.dma_buf))
                            sem = d.dma_buf.dma_sem
                            val = d.dma_val
                        else:
                            si = (d.seq - 1) // SEMCAP
                            k = ("eng", d.eng, si)
                            sem = sems[d.eng][si]
                            val = d.seq - si * SEMCAP
                        if k not in need or need[k][1] < val:
                            need[k] = (sem, val)
                    for k, (sem, val) in need.items():
                        if waited.get(k, 0) >= val:
                            continue
                        eng.wait_ge(sem, val)
                        waited[k] = val
                    inst = op.fn(eng)
                    if op.dma_buf is not None:
                        if op.dma_inc == 16:
                            inst.then_inc(op.dma_buf.dma_sem, 16)
                        else:
                            inst.then_inc(op.dma_buf.dma_sem)
                    elif op.needs_inc:
                        inst.then_inc(sems[ename][(op.seq - 1) // SEMCAP], 1)
                if ename == "sp":
                    finals = {}
                    for op in all_dma:
                        b = op.dma_buf
                        finals[id(b)] = (b.dma_sem, op.dma_val if op.dma_inc != 16 else 16 * b.dma_count)
                    for sem, val in finals.values():
                        eng.wait_ge(sem, val)
                    eng.sem_inc(phase_sem, 1)
                elif ename == "act":
                    eng.copy(dummy_sb.t[:, 2:3], dummy_sb.t[:, 3:4]).then_inc(phase_sem, 1)
                elif ename == "dve":
                    eng.memset(dummy_sb.t[:, 4:5], 0.0).then_inc(phase_sem, 1)
                elif ename == "pool":
                    eng.memset(dummy_sb.t[:, 6:7], 0.0).then_inc(phase_sem, 1)
            return body

        with nc.Block() as block:
            block.tensor(gen("pe"))
            block.scalar(gen("act"))
            block.vector(gen("dve"))
            block.gpsimd(gen("pool"))
            block.sync(gen("sp"))
        self.phase += 1
        self.ops = {e: [] for e in ENGS}
        for b in self.bufs + list(self.dram.values()):
            b.last_writer = None
            b.readers = []
            b.dma_count = 0
            b.dma_sem = None


def _collective(self, kind, out, in_, groups, op=None):
    o, i = _ap(out), _ap(in_)
    alu = op if op is not None else ALU.bypass
    rec = self._record("pool", lambda e: e.collective_compute(kind, alu, replica_groups=groups, ins=[i], outs=[o]),
                       [in_], [out])
    key = out.buf
    key.dma_count += 1
    rec.dma_buf = key
    rec.dma_inc = 1
    rec.dma_val = key.dma_count
    return rec


Prog.collective = _collective


D = 1024
TC = 256
TL = 8192
TA = TC + TL
W = 256
WH = W + 2
NBLK_L = TL // W
XA_COLS = 1 + TC + 1 + 1 + TL + 1
NSMALL = 50
EXPM05 = 0.6065306597126334
RMS_EPS = 1e-6
GN_EPS = 64e-5


def l0_consts():
    ident = np.eye(128, dtype=np.float32)
    bones = np.kron(np.eye(2, dtype=np.float32), np.ones((64, 64), np.float32))
    idx = np.arange(128)
    same = (idx[:, None] // 64) == (idx[None, :] // 64)
    strict_f = (same & (idx[None, :] < idx[:, None])).astype(np.float32)
    incl_f = (same & (idx[None, :] <= idx[:, None])).astype(np.float32)
    strict_b = (same & (idx[None, :] > idx[:, None])).astype(np.float32)
    incl_b = (same & (idx[None, :] >= idx[:, None])).astype(np.float32)
    out = {}
    for nm, st, inc in (("f", strict_f, incl_f), ("b", strict_b, incl_b)):
        m1 = np.concatenate([st, st], axis=1)
        m2h = np.concatenate([st.T, inc.T], axis=1)
        m2 = np.concatenate([m2h, m2h], axis=1)
        out["m1" + nm] = np.ascontiguousarray(m1)
        out["m2" + nm] = np.ascontiguousarray(m2)
    ident2 = np.concatenate([np.eye(64, dtype=np.float32)] * 2, axis=0)
    cst = np.concatenate([ident, bones, out["m1f"], out["m2f"], out["m1b"], out["m2b"], ident2,
                          np.ones((128, 64), np.float32)], axis=1)
    return np.ascontiguousarray(cst)


C_ID, C_BO, C_M1F, C_M2F, C_M1B, C_M2B, C_ID2, C_ONE = 0, 128, 256, 512, 1024, 1280, 1792, 1856
C_TOT = 1920


def V2(view):
    return View(view.buf, view.ap.rearrange("p a b -> p (a b)"))


def build_l0(debug_out=False, stop=None, ctx=None):
    if ctx is None:
        nc = bass.Bass("TRN2", target_bir_lowering=False)
        P = Prog(nc)
        xa_d = P.dram_in("xa", [D, XA_COLS], F32)
        ct_d = P.dram_in("ct", [128, 16], F32)
        adaw_d = P.dram_in("adaw", [D, 2048], F32)
        sm_d = P.dram_in("smalls", [128, NSMALL], F32)
        win_d = P.dram_in("win", [D, 1280], F32)
        w2_d = P.dram_in("w2", [128, 256], F32)
        a2_d = P.dram_in("a2", [128, 256], F32)
        cst_d = P.dram_in("cst", [128, C_TOT], F32)
        y0_d = P.dram_out("y0", [256, TA], BF16)
        of_d = P.dram_tmp("of_scratch", [256, TA], F32)
    else:
        nc, P = ctx["nc"], ctx["P"]
        xa_d, ct_d, adaw_d, sm_d, win_d, w2_d, a2_d, cst_d, y0_d, of_d = [ctx[k] for k in (
            "a_xa", "ct", "a_adaw", "a_smalls", "a_win", "a_w2", "a_a2", "a_cst", "y0loc", "of_scratch")]

    cst = P.sb("cst", [128, C_TOT], F32)
    P.dma(cst.v, cst_d.v)
    ident = cst[:, C_ID:C_ID + 128]
    bones = cst[:, C_BO:C_BO + 128]
    ident2 = cst[:, C_ID2:C_ID2 + 64]
    ones64 = cst[:, C_ONE:C_ONE + 64]
    masks = {0: (cst[:, C_M1F:C_M1F + 256], cst[:, C_M2F:C_M2F + 512]),
             1: (cst[:, C_M1B:C_M1B + 256], cst[:, C_M2B:C_M2B + 512])}
    sm = P.sb("sm", [128, NSMALL], F32)
    P.dma(sm.v, sm_d.v)
    S_NG, S_ADAB, S_MU, S_W0, S_A0, S_KK, S_KA, S_RK, S_GG, S_GB = 0, 8, 24, 32, 36, 40, 42, 44, 46, 48
    w2 = P.sb("w2", [128, 256], F32)
    a2 = P.sb("a2", [128, 256], F32)
    P.dma(w2.v, w2_d.v)
    P.dma(a2.v, a2_d.v)
    ones128 = P.sb("ones128", [128, 128], F32)
    P.memset("pool", ones128.v, 1.0)

    der = P.sb("der", [128, 32], F32)
    P.ts("pool", der[:, 0:8], sm[:, S_MU:S_MU + 8], -1.0, 1.0, ALU.mult, ALU.add)
    P.ts("pool", der[:, 8:16], sm[:, S_MU:S_MU + 8], 0.5, None, ALU.mult)
    P.ts("pool", der[:, 16:18], sm[:, S_KA:S_KA + 2], -1.0, 1.0, ALU.mult, ALU.add)
    omu = lambda ci: der[:, ci:ci + 1]
    hmu = lambda ci: der[:, 8 + ci:9 + ci]
    omka = lambda p: der[:, 16 + p:17 + p]

    def dbg_stop(views):
        tot = max(64, sum(n for _, n in views))
        dbg = P.dram_out("dbg", [128, tot], F32)
        dsb = P.sb("dsb", [128, tot], F32)
        P.memset("dve", dsb.v, 0.0)
        c = 0
        for v, n in views:
            P.copy("dve", dsb[:, c:c + n], v)
            c += n
        P.dma(dbg.v, dsb.v)
        P.emit()
        return nc
    if stop == "pre0":
        return dbg_stop([(der[:, 0:18], 18)])
    ct = P.sb("ct", [128, 16], F32)
    P.dma(ct.v, ct_d.v)
    sct = P.sb("sct", [128, 16], F32)
    P.act(sct.v, ct.v, AF.Silu)
    modT = P.sb("modT", [128, 16, 2], F32)
    adaw = P.sb("adaw", [128, 8, 512], F32)
    ps_misc = P.ps("ps_misc", [128, 512])
    adaw_r = adaw_d.t[:].rearrange("(k p) m -> p k m", p=128)
    for piece in range(4):
        P.dma(adaw.v, View(adaw_d, adaw_r[:, :, piece * 512:(piece + 1) * 512]))
        for mcl in range(4):
            mc = piece * 4 + mcl
            for kc in range(8):
                P.matmul(ps_misc[:, 0:2], adaw[:, kc, mcl * 128:(mcl + 1) * 128], sct[:, kc * 2:kc * 2 + 2],
                         start=(kc == 0), stop=(kc == 7))
            P.ts("dve", modT[:, mc, :], ps_misc[:, 0:2], sm[:, S_ADAB + mc:S_ADAB + mc + 1], None, ALU.add)
    if stop == "pre1":
        return dbg_stop([(V2(modT.v), 32)])
    gmod = P.sb("gmod", [128, 8, 2], F32)
    for kc in range(8):
        P.ts("pool", gmod[:, kc, :], modT[:, 8 + kc, :], 1.0, sm[:, S_NG + kc:S_NG + kc + 1], ALU.add, ALU.mult)

    if stop == "pre2":
        return dbg_stop([(V2(modT.v), 32), (V2(gmod.v), 16)])
    Wb = P.sb("Wb", [128, 8, 1280], BF16)
    wst = [P.sb("wst%d" % i, [128, 1280], F32) for i in range(2)]
    for kc in range(8):
        P.dma(wst[kc % 2].v, win_d[kc * 128:(kc + 1) * 128, :])
        P.copy("pool", Wb[:, kc, :], wst[kc % 2].v)

    if stop == "pre":
        dbg = P.dram_out("dbg", [128, 64], F32)
        dsb = P.sb("dsb", [128, 64], F32)
        P.copy("dve", dsb[:, 0:32], V2(modT.v))
        P.copy("dve", dsb[:, 32:48], V2(gmod.v))
        P.copy("dve", dsb[:, 48:64], Wb[:, 7, 0:16])
        P.dma(dbg.v, dsb.v)
        P.emit()
        return nc
    xin = [P.sb("xin%d" % i, [128, 8, WH], F32) for i in range(2)]
    hT = P.sb("hT", [128, 8, WH], BF16)
    sqb = [P.sb("sqb%d" % i, [128, WH], F32) for i in range(2)]
    rstd = P.sb("rstd", [128, WH], F32)
    htmp = [P.sb("htmp%d" % i, [128, WH], F32) for i in range(2)]
    ps_proj = [P.ps("ps_proj%d" % i, [128, 512]) for i in range(2)]
    ps_a = P.ps("ps_a", [128, 512])
    u_sb = [P.sb("u_sb%d" % i, [128, WH], F32) for i in range(2)]
    s_sb = [P.sb("s_sb%d" % i, [128, W], F32) for i in range(2)]
    t_sb = [P.sb("t_sb%d" % i, [128, W], F32) for i in range(2)]

    def blk(name):
        return P.sb(name, [128, W], F32)

    Rb = [blk("R%d" % p) for p in range(2)]
    Kb = [blk("K%d" % p) for p in range(2)]
    Vb = [blk("V%d" % p) for p in range(2)]
    SG = [blk("SG%d" % p) for p in range(2)]
    LW = blk("LW")
    LA = blk("LA")
    TLW = blk("TLW")
    LOGW = [blk("LOGW%d" % p) for p in range(2)]
    Ab = [blk("A%d" % p) for p in range(2)]
    KQ = [blk("KQ%d" % p) for p in range(2)]
    KK = [blk("KK%d" % p) for p in range(2)]
    KD = [blk("KD%d" % p) for p in range(2)]
    KD0 = [blk("KD0%d" % p) for p in range(2)]
    T1 = [blk("T1%d" % p) for p in range(2)]
    T2 = [blk("T2%d" % p) for p in range(2)]
    CL = [blk("CL%d" % p) for p in range(2)]
    PRE = [blk("PRE%d" % p) for p in range(2)]
    E1 = [blk("E1%d" % p) for p in range(2)]
    E2 = [blk("E2%d" % p) for p in range(2)]
    E3 = [blk("E3%d" % p) for p in range(2)]
    AT = [blk("AT%d" % p) for p in range(2)]
    RT = [blk("RT%d" % p) for p in range(2)]
    BT = [blk("BT%d" % p) for p in range(2)]
    KT = [blk("KT%d" % p) for p in range(2)]
    BH = [blk("BH%d" % p) for p in range(2)]
    KH = [blk("KH%d" % p) for p in range(2)]
    DG = [P.sb("DG%d" % p, [128, 256], F32) for p in range(2)]
    OB = [blk("OB%d" % p) for p in range(2)]
    OF = [blk("OF%d" % p) for p in range(2)]
    YB = [P.sb("YB%d" % p, [128, W], BF16) for p in range(2)]

    def sbp(name, shape):
        return [P.sb("%s%d" % (name, p), shape, F32) for p in range(2)]

    Lm = sbp("Lm", [128, 256])
    NM = sbp("NM", [128, 512])
    KM = sbp("KM", [128, 512])
    Lk = [sbp("Lk%d_" % i, [128, 256]) for i in range(2)]
    Nk = [sbp("Nk%d_" % i, [128, 256]) for i in range(2)]
    Xk = [sbp("Xk%d_" % i, [128, 256]) for i in range(2)]
    Zb = sbp("Zb", [128, 256])
    TZ = sbp("TZ", [128, 256])
    VT = sbp("VT", [128, 128])
    BHT = sbp("BHT", [128, 128])
    KHT = sbp("KHT", [128, 128])
    RPT = sbp("RPT", [128, 128])
    PT = sbp("PT", [128, 128])
    ST = [sbp("ST%d_" % i, [128, 128]) for i in range(3)]
    BTbd = sbp("BTbd", [128, 512])
    KTbd = sbp("KTbd", [128, 512])
    DGd = sbp("DGd", [128, 512])
    BHTc = [sbp("BHTc%d_" % i, [128, 128]) for i in range(2)]
    KHTc = [sbp("KHTc%d_" % i, [128, 128]) for i in range(2)]
    RPTm = [sbp("RPTm%d_" % i, [128, 128]) for i in range(2)]
    PTbd = [sbp("PTbd%d_" % i, [128, 128]) for i in range(2)]
    T3 = sbp("T3", [128, 128])
    OTK = sbp("OTK", [128, 128])
    for p in range(2):
        for bb in (BTbd[p], KTbd[p], BHTc[0][p], BHTc[1][p], KHTc[0][p], KHTc[1][p], RPTm[0][p], RPTm[1][p]):
            P.memset("pool", bb.v, 0.0)
    psB = [P.ps("psB%d" % p, [128, 512]) for p in range(2)]
    psC = [P.ps("psC%d" % p, [128, 512]) for p in range(2)]

    def HH(buf, h):
        return buf[:, h * 128:(h + 1) * 128]

    def NMa(p, h):
        return NM[p][:, h * 256:h * 256 + 128]

    def NMb(p, h):
        return NM[p][:, h * 256 + 128:h * 256 + 256]

    def KMa(p, h):
        return KM[p][:, h * 256:h * 256 + 128]

    def KMb(p, h):
        return KM[p][:, h * 256 + 128:h * 256 + 256]

    def V3(view, h):
        return View(view.buf, view.ap.rearrange("p (h c) -> p h c", h=h))


    def x_cols(blk_id):
        if blk_id < 0:
            return 0
        return 258 + 256 * blk_id

    def tok0(blk_id):
        return 0 if blk_id < 0 else TC + 256 * blk_id

    xa_r = xa_d.t[:].rearrange("(k p) c -> p k c", p=128)

    def issue_x(blk_id, slot):
        c0 = x_cols(blk_id)
        P.dma(xin[slot].v, View(xa_d, xa_r[:, :, c0:c0 + WH]))

    evac_flip = [0]

    def evac(out, in_):
        evac_flip[0] ^= 1
        P.copy("act" if evac_flip[0] else "dve", out, in_)


    epsb = P.sb("epsb", [128, 4], F32)
    P.memset("pool", epsb[:, 0:1], RMS_EPS)
    P.memset("pool", epsb[:, 1:2], 1e-12)
    P.memset("pool", epsb[:, 2:3], GN_EPS)

    def stage_a(blk_id, slot, d):
        col = 1 if blk_id < 0 else 0
        xs = xin[slot]
        for kc in range(8):
            sq = sqb[kc % 2]
            P.act(sq.v, xs[:, kc, :], AF.Square)
            P.matmul(ps_a[:, 0:WH], ones128.v, sq.v, start=(kc == 0), stop=(kc == 7))
        P.act(rstd.v, ps_a[:, 0:WH], AF.Sqrt, scale=1.0 / D, bias=epsb[:, 0:1])
        P.recip(rstd.v, rstd.v)
        for kc in range(8):
            tmp = htmp[kc % 2]
            P.stt(tmp.v, xs[:, kc, :], gmod[:, kc, col:col + 1], rstd.v, ALU.mult, ALU.mult)
            P.act(hT[:, kc, :], tmp.v, AF.Identity, bias=modT[:, kc, col:col + 1])
        if blk_id < 0 or blk_id == 0:
            P.memset("pool", hT[:, :, 0:1], 0.0)
        if blk_id < 0 or blk_id == NBLK_L - 1:
            P.memset("pool", hT[:, :, WH - 1:WH], 0.0)
        mixed_dst = [Rb[0], Rb[1], Kb[0], Kb[1], Vb[0], Vb[1], None, None, LW, LA]
        mix_ci = [0, 1, 2, 3, 4, 5, None, None, 6, 7]
        n = 0
        for cc in range(10):
            if cc in (6, 7) and d == 0:
                continue
            pp = ps_proj[n % 2]
            for kc in range(8):
                P.matmul(pp[:, 0:WH], Wb[:, kc, cc * 128:(cc + 1) * 128], hT[:, kc, :],
                         start=(kc == 0), stop=(kc == 7))
            if cc in (6, 7):
                P.act(SG[cc - 6].v, pp[:, 1:W + 1], AF.Silu)
            else:
                ci = mix_ci[cc]
                u = u_sb[n % 2]
                s_ = s_sb[n % 2]
                t_ = t_sb[n % 2]
                P.copy("act", u.v, pp[:, 0:WH])
                P.tt("dve", s_.v, u[:, 0:W], u[:, 2:W + 2], ALU.add)
                P.ts("pool", t_.v, u[:, 1:W + 1], omu(ci), None, ALU.mult)
                P.stt(mixed_dst[cc].v, s_.v, hmu(ci), t_.v, ALU.mult, ALU.add)
            n += 1
        P.act(TLW.v, LW.v, AF.Tanh)
        def derive(p):
            pc = slice(p * 128, (p + 1) * 128)
            P.matmul(ps_a[:, p * W:(p + 1) * W], w2[64 * d:64 * d + 64, pc], TLW[64 * d:64 * d + 64, :])
            yield
            P.act(LOGW[p].v, ps_a[:, p * W:(p + 1) * W], AF.Sigmoid, bias=sm[:, S_W0 + 2 * d + p:S_W0 + 2 * d + p + 1])
            yield
            P.ts("pool", LOGW[p].v, LOGW[p].v, -EXPM05, None, ALU.mult)
            yield
            P.matmul(ps_a[:, p * W:(p + 1) * W], a2[64 * d:64 * d + 64, pc], LA[64 * d:64 * d + 64, :])
            yield
            P.act(Ab[p].v, ps_a[:, p * W:(p + 1) * W], AF.Sigmoid, bias=sm[:, S_A0 + 2 * d + p:S_A0 + 2 * d + p + 1])
            yield
            P.ts("pool", KQ[p].v, Kb[p].v, sm[:, S_KK + p:S_KK + p + 1], None, ALU.mult)
            yield
            P.act(T1[p].v, KQ[p].v, AF.Square)
            yield
            P.matmul(ps_a[:, p * W:(p + 1) * W], bones, T1[p].v)
            yield
            P.act(T2[p].v, ps_a[:, p * W:(p + 1) * W], AF.Sqrt, bias=epsb[:, 1:2])
            yield
            P.recip(T2[p].v, T2[p].v)
            yield
            P.tt("pool", KK[p].v, KQ[p].v, T2[p].v, ALU.mult)
            yield
            P.ts("pool", T1[p].v, Ab[p].v, sm[:, S_KA + p:S_KA + p + 1], omka(p), ALU.mult, ALU.add)
            yield
            P.tt("pool", KD[p].v, Kb[p].v, T1[p].v, ALU.mult)
            yield
            if d == 1:
                P.matmul(ps_a[:, p * W:(p + 1) * W], a2[0:64, pc], LA[0:64, :])
                yield
                P.act(T2[p].v, ps_a[:, p * W:(p + 1) * W], AF.Sigmoid, bias=sm[:, S_A0 + p:S_A0 + p + 1])
                yield
                P.ts("pool", T2[p].v, T2[p].v, sm[:, S_KA + p:S_KA + p + 1], omka(p), ALU.mult, ALU.add)
                yield
                P.tt("pool", KD0[p].v, Kb[p].v, T2[p].v, ALU.mult)
                yield
            for ch in range(4):
                sl = slice(ch * 64, (ch + 1) * 64)
                P.scan(PRE[p][:, sl], ones64, LOGW[p][:, sl], 0.0, ALU.mult, ALU.add)
                yield
            if d == 0:
                clb = PRE[p]
            else:
                clb = CL[p]
                for ch in range(4):
                    sl = slice(ch * 64, (ch + 1) * 64)
                    P.ts("pool", CL[p][:, sl], PRE[p][:, sl], -1.0, PRE[p][:, ch * 64 + 63:ch * 64 + 64],
                         ALU.mult, ALU.add)
                    yield
                P.tt("pool", CL[p].v, CL[p].v, LOGW[p].v, ALU.add)
                yield
            P.act(E1[p].v, clb.v, AF.Exp)
            yield
            P.act(E2[p].v, clb.v, AF.Exp, scale=-1.0)
            yield
            P.tt("pool", T1[p].v, clb.v, LOGW[p].v, ALU.subtract)
            yield
            P.act(E3[p].v, T1[p].v, AF.Exp)
            yield
            P.stt(AT[p].v, KK[p].v, -1.0, E3[p].v, ALU.mult, ALU.mult)
            yield
            P.tt("pool", RT[p].v, Rb[p].v, E1[p].v, ALU.mult)
            yield
            P.tt("pool", T1[p].v, KK[p].v, Ab[p].v, ALU.mult)
            yield
            P.tt("pool", BT[p].v, T1[p].v, E2[p].v, ALU.mult)
            yield
            P.tt("pool", KT[p].v, KD[p].v, E2[p].v, ALU.mult)
            yield
            for ch in range(4):
                sl = slice(ch * 64, (ch + 1) * 64)
                gc = ch * 64 + 63 if d == 0 else ch * 64
                gcol = E1[p][:, gc:gc + 1]
                P.ts("pool", BH[p][:, sl], BT[p][:, sl], gcol, None, ALU.mult)
                yield
                P.ts("pool", KH[p][:, sl], KT[p][:, sl], gcol, None, ALU.mult)
                yield
                P.ts("pool", DGd[p][:, ch * 128:(ch + 1) * 128], ident, gcol, None, ALU.mult)
                yield
            for h in range(2):
                hp = slice(64 * h, 64 * h + 64)
                for tl2 in range(2):
                    q = (tl2 * 2 + h) * 128
                    P.copy("pool", BTbd[p][hp, q:q + 128], BT[p][hp, tl2 * 128:(tl2 + 1) * 128])
                    yield
                    P.copy("pool", KTbd[p][hp, q:q + 128], KT[p][hp, tl2 * 128:(tl2 + 1) * 128])
                    yield

        gens = [derive(p) for p in range(2)]
        while gens:
            for g in list(gens):
                try:
                    next(g)
                except StopIteration:
                    gens.remove(g)

    def stage_b(p, tl, d, sw_state, upto=99):
        m1, m2 = masks[d]
        cs = slice(tl * 128, (tl + 1) * 128)
        pc = psC[p]
        pb = psB[p]
        btbd = lambda h: BTbd[p][:, (tl * 2 + h) * 128:(tl * 2 + h + 1) * 128]
        ktbd = lambda h: KTbd[p][:, (tl * 2 + h) * 128:(tl * 2 + h + 1) * 128]
        P.matmul(pc[:, 0:256], AT[p][:, cs], BTbd[p][:, tl * 256:(tl + 1) * 256])
        P.tt("dve", Lm[p].v, pc[:, 0:256], m1, ALU.mult)
        yield
        for h in range(2):
            P.matmul(pb[:, h * 256:h * 256 + 128], btbd(h), AT[p][:, cs])
            P.matmul(pb[:, h * 256 + 128:h * 256 + 256], btbd(h), RT[p][:, cs])
        P.tt("dve", NM[p].v, pb[:, 0:512], m2, ALU.mult)
        yield
        for h in range(2):
            P.matmul(pb[:, h * 256:h * 256 + 128], ktbd(h), AT[p][:, cs])
            P.matmul(pb[:, h * 256 + 128:h * 256 + 256], ktbd(h), RT[p][:, cs])
        P.tt("dve", KM[p].v, pb[:, 0:512], m2, ALU.mult)
        yield
        if upto <= 1:
            return
        X = Xk[0][p]
        for h in range(2):
            P.tt("pool", HH(X, h), NMa(p, h), ident, ALU.add)
        Lc = Lm[p]
        Nc_views = [NMa(p, h) for h in range(2)]
        xi = 0
        for k in range(1, 6):
            Ln = Lk[k % 2][p]
            for h in range(2):
                P.matmul(pc[:, h * 128:(h + 1) * 128], Nc_views[h], HH(Lc, h))
            P.copy("act", Ln.v, pc[:, 0:256])
            yield
            if k < 5:
                Nn = Nk[k % 2][p]
                for h in range(2):
                    P.matmul(pc[:, 256 + h * 128:256 + (h + 1) * 128], HH(Lc, h), Nc_views[h])
                P.copy("act", Nn.v, pc[:, 256:512])
            for h in range(2):
                P.matmul(pb[:, h * 128:(h + 1) * 128], HH(Ln, h), HH(Xk[xi][p], h))
            Xn = Xk[1 - xi][p]
            P.tt("dve", Xn.v, pb[:, 0:256], Xk[xi][p].v, ALU.add)
            yield
            xi = 1 - xi
            Lc = Ln
            if k < 5:
                Nc_views = [HH(Nn, h) for h in range(2)]
        X = Xk[xi][p]
        if upto <= 2:
            return
        P.transpose(pc[:, 0:128], AT[p][:, cs], ident)
        P.copy("act", Zb[p][:, 0:128], pc[:, 0:128])
        yield
        P.transpose(pc[:, 128:256], Vb[p][:, cs], ident)
        P.copy("dve", VT[p].v, pc[:, 128:256])
        yield
        P.transpose(pc[:, 256:384], BH[p][:, cs], ident)
        P.copy("act", BHTc[0][p][0:64, :], pc[0:64, 256:384])
        yield
        P.copy("dve", BHTc[1][p][64:128, :], pc[64:128, 256:384])
        yield
        P.transpose(pc[:, 384:512], KH[p][:, cs], ident)
        P.copy("act", KHTc[0][p][0:64, :], pc[0:64, 384:512])
        yield
        P.copy("dve", KHTc[1][p][64:128, :], pc[64:128, 384:512])
        yield
        for h in range(2):
            P.matmul(pb[:, h * 64:(h + 1) * 64], KMa(p, h), VT[p][:, h * 64:(h + 1) * 64])
        P.copy("act", Zb[p][:, 128:256], pb[:, 0:128])
        yield
        for part in range(2):
            for h in range(2):
                q = (part * 2 + h) * 64
                P.matmul(pb[:, 128 + q:128 + q + 64], HH(X, h), Zb[p][:, q:q + 64])
        P.copy("dve", TZ[p].v, pb[:, 128:384])
        yield
        if upto <= 3:
            return
        for h in range(2):
            P.matmul(pc[:, h * 128:(h + 1) * 128], TZ[p][:, 0:128], NMb(p, h))
        for h in range(2):
            hp = slice(64 * h, 64 * h + 64)
            P.tt("dve", RPT[p][hp, :], pc[hp, h * 128:(h + 1) * 128], RT[p][hp, cs], ALU.add)
            yield
        P.copy("pool", RPTm[0][p][:, 0:64], RPT[p][:, 0:64])
        P.copy("pool", RPTm[1][p][:, 64:128], RPT[p][:, 64:128])
        for c in range(2):
            P.matmul(pc[:, 256 + c * 128:256 + (c + 1) * 128], TZ[p][:, 0:128], BHTc[c][p].v)
        for c in range(2):
            ch = 2 * tl + c
            P.tt("dve", T3[p].v, pc[:, 256 + c * 128:256 + (c + 1) * 128], bones, ALU.mult)
            yield
            P.tt("pool", PTbd[c][p].v, T3[p].v, DGd[p][:, ch * 128:(ch + 1) * 128], ALU.add)
        if upto <= 5:
            return
        order = (0, 1) if d == 0 else (1, 0)
        s_at = {}
        for c in order:
            si = sw_state[p]
            S_in = ST[si][p]
            S_out = ST[(si + 1) % 3][p]
            s_at[c] = S_in
            P.matmul(pb[:, 0:128], PTbd[c][p].v, S_in.v, start=True, stop=False)
            P.matmul(pb[:, 0:128], BHTc[c][p].v, TZ[p][:, 128:256], start=False, stop=False)
            P.matmul(pb[:, 0:128], KHTc[c][p].v, VT[p].v, start=False, stop=True)
            P.tt("dve", S_out.v, pb[:, 0:128], bones, ALU.mult)
            yield
            sw_state[p] = (si + 1) % 3
        if upto <= 6:
            return
        P.matmul(pb[:, 128:256], RPTm[0][p].v, s_at[0].v, start=True, stop=False)
        P.matmul(pb[:, 128:256], RPTm[1][p].v, s_at[1].v, start=False, stop=False)
        for h in range(2):
            P.matmul(pb[:, 128 + h * 64:128 + (h + 1) * 64], NMb(p, h), TZ[p][:, 128 + h * 64:128 + (h + 1) * 64],
                     start=False, stop=False)
            P.matmul(pb[:, 128 + h * 64:128 + (h + 1) * 64], KMb(p, h), VT[p][:, h * 64:(h + 1) * 64],
                     start=False, stop=(h == 1))
        P.copy("act", OTK[p].v, pb[:, 128:256])
        yield
        P.transpose(pb[:, 256:384], OTK[p].v, ident)
        if d == 0:
            P.copy("dve", OB[p][:, cs], pb[:, 256:384])
            yield
        else:
            P.tt("dve", OB[p][:, cs], pb[:, 256:384], OF[p][:, cs], ALU.add)
            yield

    def readout(blk_id):
        t0 = tok0(blk_id)
        for p in range(2):
            P.matmul(ps_a[:, 0:W], bones, OB[p].v)
            P.stt(T1[p].v, ps_a[:, 0:W], -1.0 / 64, OB[p].v, ALU.mult, ALU.add)
            P.act(T2[p].v, T1[p].v, AF.Square)
            P.matmul(ps_a[:, W:2 * W], bones, T2[p].v)
            P.act(T2[p].v, ps_a[:, W:2 * W], AF.Sqrt, scale=1.0 / 64, bias=epsb[:, 2:3])
            P.recip(T2[p].v, T2[p].v)
            P.tt("pool", T1[p].v, T1[p].v, T2[p].v, ALU.mult)
            P.ts("pool", T1[p].v, T1[p].v, sm[:, S_GG + p:S_GG + p + 1], sm[:, S_GB + p:S_GB + p + 1],
                 ALU.mult, ALU.add)
            P.tt("pool", T2[p].v, KD[p].v, KD0[p].v, ALU.add)
            P.stt(T2[p].v, Rb[p].v, sm[:, S_RK + p:S_RK + p + 1], T2[p].v, ALU.mult, ALU.mult)
            P.matmul(ps_a[:, 0:W], bones, T2[p].v)
            P.tt("dve", T2[p].v, ps_a[:, 0:W], Vb[p].v, ALU.mult)
            P.tt("pool", T1[p].v, T1[p].v, T2[p].v, ALU.add)
            P.tt("pool", YB[p].v, T1[p].v, SG[p].v, ALU.mult)
            if isinstance(y0_d, list):
                pi, off = (0, t0) if t0 < TC else (1 + (t0 - TC) // 2048, (t0 - TC) % 2048)
                P.dma(y0_d[pi][p * 128:(p + 1) * 128, off:off + W], YB[p].v)
            else:
                P.dma(y0_d[p * 128:(p + 1) * 128, t0:t0 + W], YB[p].v)

    for p in range(2):
        P.memset("pool", ST[0][p].v, 0.0)
    for d in range(2):
        blocks = [-1] + (list(range(NBLK_L)) if d == 0 else list(range(NBLK_L - 1, -1, -1)))
        if debug_out and isinstance(debug_out, int) and debug_out > 1:
            blocks = blocks[:debug_out]
        sw_state = [0, 0]
        if d == 1:
            for p in range(2):
                P.memset("pool", ST[0][p].v, 0.0)
        issue_x(blocks[0], 0)
        for bi, b in enumerate(blocks):
            slot = bi % 2
            if bi + 1 < len(blocks):
                issue_x(blocks[bi + 1], 1 - slot)
            t0 = tok0(b)
            if d == 1:
                for p in range(2):
                    P.dma(OF[p].v, of_d[p * 128:(p + 1) * 128, t0:t0 + W])
            stage_a(b, slot, d)
            if stop == "a":
                return dbg_stop([(Rb[0].v, 256), (KK[1].v, 256), (LOGW[0].v, 256), (Ab[1].v, 256), (KD[0].v, 256),
                                 (AT[0].v, 256), (RT[0].v, 256), (BT[0].v, 256), (KT[0].v, 256), (BH[0].v, 256),
                                 (DG[0].v, 256), (Vb[1].v, 256)])
            tiles = (0, 1) if d == 0 else (1, 0)
            for tl in tiles:
                if not (stop and stop[0] == "b"):
                    gens = [stage_b(p, tl, d, sw_state) for p in range(2)]
                    while gens:
                        for g in list(gens):
                            try:
                                next(g)
                            except StopIteration:
                                gens.remove(g)
                    continue
                for p in range(2):
                    for _ in stage_b(p, tl, d, sw_state, upto=int(stop[1:]) if (stop and stop[0] == "b" and len(stop) > 1) else 99):
                        pass
                    if stop and stop[0] == "b":
                        return dbg_stop([(OB[0][:, 0:128], 128), (ST[sw_state[0]][0].v, 128), (TZ[0].v, 256),
                                         (Lm[0].v, 256), (NM[0].v, 512), (Xk[1][0].v, 256), (RPT[0].v, 128), (PTbd[0][0].v, 128)])
            if d == 0:
                for p in range(2):
                    P.dma(of_d[p * 128:(p + 1) * 128, t0:t0 + W], OB[p].v)
            else:
                readout(b)
    if ctx is None:
        P.emit()
    return nc


def l0_inputs(inp, b, hg):
    f32 = np.float32
    x, ctx = inp["x"], inp["ctx"]
    z1 = np.zeros((D, 1), f32)
    xa = np.concatenate([z1, ctx[b].T, z1, z1, x[b].T, z1], axis=1)
    ct = np.zeros((128, 16), f32)
    cb = inp["c"][b].reshape(8, 128).T
    cc = inp["c_ctx"].reshape(8, 128).T
    ct[:, 0::2] = cb
    ct[:, 1::2] = cc
    adaw = np.ascontiguousarray(inp["ada_w"][0][:, 0:2048])
    hc = slice(hg * 256, (hg + 1) * 256)
    rw_in = inp["rw_in"][0]
    cols = []
    for X in range(4):
        cols.append(rw_in[:, X * 1024 + hg * 256: X * 1024 + (hg + 1) * 256])
    cols.append(rw_in[:, 4096:4352])
    win = np.ascontiguousarray(np.concatenate(cols, axis=1))
    sm = np.zeros((128, NSMALL), f32)
    sm[:, 0:8] = inp["norm_g"][0].reshape(8, 128).T
    sm[:, 8:24] = inp["ada_b"][0][:2048].reshape(16, 128).T
    mu = inp["rw_mu"][0]
    mus = []
    for X in range(3):
        for p in range(2):
            mus.append(mu[X * 1024 + hg * 256 + p * 128: X * 1024 + hg * 256 + (p + 1) * 128])
    mus.append(mu[3072:3200])
    mus.append(mu[3200:3328])
    sm[:, 24:32] = np.stack(mus, axis=1)
    for d in range(2):
        for p in range(2):
            sm[:, 32 + 2 * d + p] = inp["rw_w0"][0][d, hg * 256 + p * 128: hg * 256 + (p + 1) * 128]
            sm[:, 36 + 2 * d + p] = inp["rw_a0"][0][d, hg * 256 + p * 128: hg * 256 + (p + 1) * 128]
    for p in range(2):
        sl = slice(hg * 256 + p * 128, hg * 256 + (p + 1) * 128)
        sm[:, 40 + p] = inp["rw_kk"][0][sl]
        sm[:, 42 + p] = inp["rw_ka"][0][sl]
        sm[:, 44 + p] = inp["rw_rk"][0].reshape(-1)[sl]
        sm[:, 46 + p] = inp["rw_gn_g"][0][sl]
        sm[:, 48 + p] = inp["rw_gn_b"][0][sl]
    w2 = np.ascontiguousarray(inp["rw_w2"][0][:, :, hc].reshape(128, 256))
    a2 = np.ascontiguousarray(inp["rw_a2"][0][:, :, hc].reshape(128, 256))
    return {"xa": np.ascontiguousarray(xa), "ct": ct, "adaw": adaw, "smalls": sm, "win": win,
            "w2": w2, "a2": a2, "cst": l0_consts()}


SUBLN_EPS = 1e-5
LAM_INIT = 0.8 - 0.6 * float(np.exp(-0.3 * 1))
QSCALE = 0.125
NKT = TA // 128
NQB = TL // 256


def tok0(bi):
    return 0 if bi < 0 else TC + 256 * bi


def mod_compute(P, adaw_d, ncol, sct, adab_view, ps, modT, adaw_sb):
    adaw_r = adaw_d.t[:].rearrange("(k p) m -> p k m", p=128)
    for piece in range(ncol // 256):
        P.dma(adaw_sb.v, View(adaw_d, adaw_r[:, :, piece * 256:(piece + 1) * 256]))
        for mcl in range(2):
            mc = piece * 2 + mcl
            for kc in range(8):
                P.matmul(ps[:, 0:2], adaw_sb[:, kc, mcl * 128:(mcl + 1) * 128], sct[:, kc * 2:kc * 2 + 2],
                         start=(kc == 0), stop=(kc == 7))
            P.ts("dve", modT[:, mc, :], ps[:, 0:2], adab_view(mc), None, ALU.add)


def rope_tables():
    rows = TL // 64
    t = np.arange(TL)
    row = (t // 64).astype(np.float32)
    colid = (t % 64).astype(np.float32)
    inv = (10000.0 ** (-np.arange(16, dtype=np.float32) / 16)).astype(np.float32)
    ang_r = row[None, :] * inv[:, None]
    ang_c = colid[None, :] * inv[:, None]
    cos64 = np.concatenate([np.cos(ang_r), np.cos(ang_r), np.cos(ang_c), np.cos(ang_c)], axis=0)
    sin64 = np.concatenate([np.sin(ang_r), np.sin(ang_r), np.sin(ang_c), np.sin(ang_c)], axis=0)
    cosT = np.concatenate([cos64, cos64], axis=0).astype(np.float32)
    sinT = np.concatenate([sin64, sin64], axis=0).astype(np.float32)
    R = np.zeros((128, 128), np.float32)
    for base in range(0, 128, 32):
        for f in range(16):
            R[base + 16 + f, base + f] = -1.0
            R[base + f, base + 16 + f] = 1.0
    return cosT, sinT, R


def build_l1(stop=None, ctx=None):
    if ctx is None:
        nc = bass.Bass("TRN2", target_bir_lowering=False)
        P = Prog(nc)
        xa_d = P.dram_in("xa", [D, TA], F32)
        y0_d = P.dram_in("y0g", [D, TA], BF16)
        ct_d = P.dram_in("ct", [128, 16], F32)
        adaw0_d = P.dram_in("adaw0g", [D, 1024], F32)
        adaw1_d = P.dram_in("adaw1", [D, 2048], F32)
        sm_d = P.dram_in("smalls", [128, 48], F32)
        wo_d = P.dram_in("wo", [D, D], F32)
        wd_d = P.dram_in("wd", [D, 1024], F32)
        cst_d = P.dram_in("cst", [128, 384], F32)
        cos_d = P.dram_in("cosT", [128, TL], F32)
        sin_d = P.dram_in("sinT", [128, TL], F32)
        lam_d = P.dram_in("lamv", [128, 256], F32)
        subw_d = P.dram_in("subw", [128, 128], F32)
        y1_d = P.dram_out("y1", [256, TL], BF16)
        xn_d = P.dram_out("xn", [D, TL], F32)
    else:
        nc, P = ctx["nc"], ctx["P"]
        (xa_d, y0_d, ct_d, adaw0_d, adaw1_d, sm_d, wo_d, wd_d, cst_d, cos_d, sin_d, lam_d, subw_d, y1_d, xn_d) = [
            ctx[k] for k in ("b_xa", "y0g", "ct", "b_adaw0g", "b_adaw1", "b_smalls", "b_wo", "b_wd", "b_cst",
                             "b_cosT", "b_sinT", "b_lamv", "b_subw", "y1loc", "xn_s")]

    cst = P.sb("cst", [128, 384], F32)
    P.dma(cst.v, cst_d.v)
    ident = cst[:, 0:128]
    bones = cst[:, 128:256]
    rrot = cst[:, 256:384]
    sm = P.sb("sm", [128, 48], F32)
    P.dma(sm.v, sm_d.v)
    S_NG, S_B0, S_B1, S_QN, S_KN = 0, 8, 16, 32, 33
    ones128 = P.sb("ones128", [128, 128], F32)
    P.memset("pool", ones128.v, 1.0)
    epsb = P.sb("epsb", [128, 4], F32)
    P.memset("pool", epsb[:, 0:1], RMS_EPS)
    P.memset("pool", epsb[:, 1:2], SUBLN_EPS)
    banks = [P.ps("bank%d" % i, [128, 512]) for i in range(8)]

    ct = P.sb("ct", [128, 16], F32)
    P.dma(ct.v, ct_d.v)
    sct = P.sb("sct", [128, 16], F32)
    P.act(sct.v, ct.v, AF.Silu)
    adaw_sb = P.sb("adaw_sb", [128, 8, 256], F32)
    mod0 = P.sb("mod0", [128, 8, 2], F32)
    mod1 = P.sb("mod1", [128, 16, 2], F32)
    mod_compute(P, adaw0_d, 1024, sct, lambda mc: sm[:, S_B0 + mc:S_B0 + mc + 1], banks[0], mod0, adaw_sb)
    mod_compute(P, adaw1_d, 2048, sct, lambda mc: sm[:, S_B1 + mc:S_B1 + mc + 1], banks[0], mod1, adaw_sb)
    gmod = P.sb("gmod", [128, 8, 2], F32)
    for kc in range(8):
        P.ts("pool", gmod[:, kc, :], mod1[:, 8 + kc, :], 1.0, sm[:, S_NG + kc:S_NG + kc + 1], ALU.add, ALU.mult)

    lamv = P.sb("lamv", [128, 256], F32)
    P.dma(lamv.v, lam_d.v)
    lt = P.sb("lt", [128, 128], F32)
    lsc = P.sb("lsc", [128, 8], F32)
    P.tt("pool", lt[:, 0:64], lamv[:, 0:64], lamv[:, 64:128], ALU.mult)
    P.tt("pool", lt[:, 64:128], lamv[:, 128:192], lamv[:, 192:256], ALU.mult)
    P.reduce(lsc[:, 0:1], lt[:, 0:64], AX.X, ALU.add)
    P.reduce(lsc[:, 1:2], lt[:, 64:128], AX.X, ALU.add)
    P.act(lsc[:, 2:4], lsc[:, 0:2], AF.Exp)
    P.tt("pool", lsc[:, 4:5], lsc[:, 2:3], lsc[:, 3:4], ALU.subtract)
    P.ts("pool", lsc[:, 5:6], lsc[:, 4:5], LAM_INIT, -1.0, ALU.add, ALU.mult)
    neglam = lsc[:, 5:6]
    subw = P.sb("subw", [128, 128], F32)
    P.dma(subw.v, subw_d.v)
    P.ts("pool", subw.v, subw.v, 1.0 - LAM_INIT, None, ALU.mult)

    Wo = P.sb("Wo", [128, 8, 1024], BF16)
    Wd = P.sb("Wd", [128, 8, 1024], BF16)
    wst = [P.sb("wst%d" % i, [128, 1024], F32) for i in range(1)] * 2
    n = 0
    for src, dst in ((wo_d, Wo), (wd_d, Wd)):
        for kc in range(8):
            P.dma(wst[n % 2].v, src[kc * 128:(kc + 1) * 128, :])
            P.copy("pool", dst[:, kc, :], wst[n % 2].v)
            n += 1

    xin = [P.sb("xin%d" % i, [128, 8, W], F32) for i in range(2)]
    yin = [P.sb("yin%d" % i, [128, 8, W], BF16) for i in range(2)]
    xn = P.sb("xn", [128, 8, W], F32)
    hT = P.sb("hT", [128, 8, W], BF16)
    sqb = [P.sb("sqb%d" % i, [128, W], F32) for i in range(2)]
    rstd = P.sb("rstd", [128, W], F32)
    htmp = [P.sb("htmp%d" % i, [128, W], F32) for i in range(2)]
    cosb = P.sb("cosb", [128, W], F32)
    sinb = P.sb("sinb", [128, W], F32)
    qraw = P.sb("qraw", [128, W], F32)
    qsq = P.sb("qsq", [128, W], F32)
    qrs = P.sb("qrs", [128, W], F32)
    qn_ = P.sb("qn_", [128, W], F32)
    qt1 = P.sb("qt1", [128, W], F32)
    qt2 = P.sb("qt2", [128, W], F32)
    KTb = P.sb("KTb", [128, TA], BF16)
    QTb = P.sb("QTb", [128, NQB * 512], BF16)
    P.memset("pool", QTb.v, 0.0)
    Vx = P.sb("Vx", [128, NKT, 130], BF16)
    P.memset("pool", Vx[:, :, 129:130], 0.0)
    Gs = P.sb("Gs", [128, TL // 128, 128], BF16)
    P.memset("pool", Vx[:, :, 128:129], 1.0)
    pT = [P.sb("pT%d" % i, [128, 512], BF16) for i in range(2)]
    vg_sb = [P.sb("vg_sb%d" % i, [128, 512], F32) for i in range(2)]
    o_sb = P.sb("o_sb", [128, 128], F32)
    o_sq = P.sb("o_sq", [128, 128], F32)
    ybs = [P.sb("yb%d" % i, [128, 256], BF16) for i in range(2)]
    zs = P.sb("zs", [128, 8], F32)
    ps_bf = P.ps("ps_bf_unused", [128, 2], F32) if False else None

    xa_r = xa_d.t[:].rearrange("(k p) c -> p k c", p=128)
    y0_r = None if isinstance(y0_d, list) else y0_d.t[:].rearrange("(k p) c -> p k c", p=128)
    xn_r = xn_d.t[:].rearrange("(k p) c -> p k c", p=128)

    def issue(bi, slot):
        t0 = tok0(bi)
        P.dma(xin[slot].v, View(xa_d, xa_r[:, :, t0:t0 + W]))
        if isinstance(y0_d, list):
            pi, off = (0, t0) if t0 < TC else (1 + (t0 - TC) // 2048, (t0 - TC) % 2048)
            yr = y0_d[pi].t[:].rearrange("(k p) c -> p k c", p=128)
            P.dma(yin[slot].v, View(y0_d[pi], yr[:, :, off:off + W]))
        else:
            P.dma(yin[slot].v, View(y0_d, y0_r[:, :, t0:t0 + W]))

    def stage_a(bi, slot, h, hp_quarter, first_pass):
        col = 1 if bi < 0 else 0
        t0 = tok0(bi)
        lat0 = t0 - TC
        for oc in range(8):
            pp = banks[oc % 2]
            for kc in range(8):
                P.matmul(pp[:, 0:W], Wo[:, kc, oc * 128:(oc + 1) * 128], yin[slot][:, kc, :],
                         start=(kc == 0), stop=(kc == 7))
            P.stt(xn[:, oc, :], pp[:, 0:W], mod0[:, oc, col:col + 1], xin[slot][:, oc, :], ALU.mult, ALU.add)
        if first_pass and bi >= 0 and stop != 'noxn':
            P.dma(View(xn_d, xn_r[:, :, lat0:lat0 + W]), xn.v)
        if stop == 'a1':
            return
        for kc in range(8):
            sq = sqb[kc % 2]
            P.act(sq.v, xn[:, kc, :], AF.Square)
            P.matmul(banks[2][:, 0:W], ones128.v, sq.v, start=(kc == 0), stop=(kc == 7))
        P.act(rstd.v, banks[2][:, 0:W], AF.Sqrt, scale=1.0 / D, bias=epsb[:, 0:1])
        P.recip(rstd.v, rstd.v)
        for kc in range(8):
            tmp = htmp[kc % 2]
            P.stt(tmp.v, xn[:, kc, :], gmod[:, kc, col:col + 1], rstd.v, ALU.mult, ALU.mult)
            P.act(hT[:, kc, :], tmp.v, AF.Identity, bias=mod1[:, kc, col:col + 1])
        if stop == 'a2':
            return
        if bi >= 0:
            P.dma(cosb.v, cos_d[:, lat0:lat0 + W])
            P.dma(sinb.v, sin_d[:, lat0:lat0 + W])
        for which in (("q", "k") if bi >= 0 else ("k",)):
            cc = h if which == "q" else 2 + h
            pp = banks[3]
            for kc in range(8):
                P.matmul(pp[:, 0:W], Wd[:, kc, cc * 128:(cc + 1) * 128], hT[:, kc, :],
                         start=(kc == 0), stop=(kc == 7))
            P.copy("act", qraw.v, pp[:, 0:W])
            P.tt("pool", qsq.v, qraw.v, qraw.v, ALU.mult)
            P.matmul(banks[4][:, 0:W], bones, qsq.v)
            P.act(qrs.v, banks[4][:, 0:W], AF.Sqrt, scale=1.0 / 64, bias=epsb[:, 0:1])
            P.recip(qrs.v, qrs.v)
            wcol = sm[:, S_QN:S_QN + 1] if which == "q" else sm[:, S_KN:S_KN + 1]
            P.stt(qn_.v, qraw.v, wcol, qrs.v, ALU.mult, ALU.mult)
            if bi >= 0:
                P.matmul(banks[4][:, W:2 * W], rrot, qn_.v)
                P.tt("dve", qt1.v, banks[4][:, W:2 * W], sinb.v, ALU.mult)
                P.tt("pool", qt2.v, qn_.v, cosb.v, ALU.mult)
                if which == "q":
                    qb_ = lat0 // 256
                    P.tt("pool", QTb[0:64, qb_ * 512:qb_ * 512 + 256], qt1[0:64, :], qt2[0:64, :], ALU.add)
                    P.tt("pool", QTb[64:128, qb_ * 512 + 256:qb_ * 512 + 512], qt1[64:128, :], qt2[64:128, :], ALU.add)
                else:
                    P.tt("pool", KTb[:, t0:t0 + W], qt1.v, qt2.v, ALU.add)
            else:
                P.copy("pool", KTb[:, t0:t0 + W], qn_.v)
        if stop == 'a3':
            return
        for sub in range(2):
            pp = banks[5 + sub]
            for kc in range(8):
                P.matmul(pp[:, 0:512], hT[:, kc, sub * 128:(sub + 1) * 128], Wd[:, kc, 512:1024],
                         start=(kc == 0), stop=(kc == 7))
            kt = (t0 + sub * 128) // 128
            vg = vg_sb[sub]
            P.copy("dve", vg.v, pp[:, 0:512])
            P.copy("pool", Vx[:, kt, 0:128], vg[:, h * 128:(h + 1) * 128])
            if bi >= 0:
                qt = (lat0 + sub * 128) // 128
                P.act(Gs[:, qt, :], vg[:, 256 + h * 128:256 + (h + 1) * 128], AF.Silu)

    def attention(h, nqb=NQB):
        sbank = [(banks[0], banks[1]), (banks[2], banks[3])]
        acc = [[banks[4], banks[5]], [banks[6], banks[7]]]
        n = 0

        def s_mm(qb_, kt_, n_):
            P.matmul(banks[n_ % 4][:, 0:512], KTb[:, kt_ * 128:(kt_ + 1) * 128], QTb[:, qb_ * 512:(qb_ + 1) * 512])

        s_mm(0, 0, 0)
        for qb in range(nqb):
            q0 = qb * 256
            for kt in range(NKT):
                sA = banks[n % 4]
                pt = pT[n % 2]
                if kt + 1 < NKT:
                    s_mm(qb, kt + 1, n + 1)
                elif qb + 1 < nqb:
                    s_mm(qb + 1, 0, n + 1)
                P.act(pt.v, sA[:, 0:512], AF.Exp, scale=QSCALE)
                if stop == 's1':
                    n += 1
                    continue
                for comp in range(2):
                    for qs in range(2):
                        P.matmul(acc[comp][qs][:, 0:130], pt[:, comp * 256 + qs * 128:comp * 256 + (qs + 1) * 128],
                                 Vx[:, kt, :], start=(kt == 0), stop=(kt == NKT - 1))
                n += 1
            if stop in ('s1', 's2'):
                continue
            ytile = ybs[qb % 2]
            for qs in range(2):
                a0 = acc[0][qs]
                a1 = acc[1][qs]
                P.recip(zs[:, 0:1], a0[:, 128:129])
                P.recip(zs[:, 1:2], a1[:, 128:129])
                P.tt("pool", zs[:, 2:3], zs[:, 1:2], neglam, ALU.mult)
                P.ts("dve", o_sb.v, a0[:, 0:128], zs[:, 0:1], None, ALU.mult)
                P.stt(o_sb.v, a1[:, 0:128], zs[:, 2:3], o_sb.v, ALU.mult, ALU.add)
                P.tt("pool", o_sq.v, o_sb.v, o_sb.v, ALU.mult)
                P.reduce(zs[:, 3:4], o_sq.v, AX.X, ALU.add)
                P.act(zs[:, 4:5], zs[:, 3:4], AF.Sqrt, scale=1.0 / 128, bias=epsb[:, 1:2])
                P.recip(zs[:, 4:5], zs[:, 4:5])
                P.stt(o_sb.v, o_sb.v, zs[:, 4:5], subw.v, ALU.mult, ALU.mult)
                qt = (q0 + qs * 128) // 128
                P.tt("pool", o_sq.v, o_sb.v, Gs[:, qt, :], ALU.mult)
                P.transpose(a0[:, 256:384], o_sq.v, ident)
                P.copy("act", ytile[:, qs * 128:(qs + 1) * 128], a0[:, 256:384])
            if isinstance(y1_d, list):
                P.dma(y1_d[q0 // 2048][h * 128:(h + 1) * 128, q0 % 2048:q0 % 2048 + 256], ytile.v)
            else:
                P.dma(y1_d[h * 128:(h + 1) * 128, q0:q0 + 256], ytile.v)

    return nc, P, issue, stage_a, attention


def build_l1_full(hp_quarter, do_attn=True, nheads=2, stop=None, nblocks=None, nqb=NQB, ctx=None):
    nc, P, issue, stage_a, attention = build_l1(stop, ctx)
    blocks = [-1] + list(range(NBLK_L))
    if nblocks:
        blocks = blocks[:nblocks]
    if stop == 'pre':
        P.emit()
        return nc
    for h in range(nheads):
        issue(blocks[0], 0)
        for i, bi in enumerate(blocks):
            if i + 1 < len(blocks):
                issue(blocks[i + 1], (i + 1) % 2)
            stage_a(bi, i % 2, h, hp_quarter, h == 0)
        if do_attn:
            attention(h, nqb)
    if ctx is None:
        P.emit()
    return nc


def l1_inputs(inp, b, hp, y0g_b):
    f32 = np.float32
    xa = np.ascontiguousarray(np.concatenate([inp["ctx"][b].T, inp["x"][b].T], axis=1))
    ct = np.zeros((128, 16), f32)
    ct[:, 0::2] = inp["c"][b].reshape(8, 128).T
    ct[:, 1::2] = inp["c_ctx"].reshape(8, 128).T
    sm = np.zeros((128, 48), f32)
    sm[:, 0:8] = inp["norm_g"][1].reshape(8, 128).T
    sm[:, 8:16] = inp["ada_b"][0][2048:3072].reshape(8, 128).T
    sm[:, 16:32] = inp["ada_b"][1][0:2048].reshape(16, 128).T
    sm[:, 32] = np.tile(inp["da_qn"][0], 2)
    sm[:, 33] = np.tile(inp["da_kn"][0], 2)
    da_in = inp["da_in"][0]
    cols = []
    for X in range(4):
        for hh in range(2):
            base = X * 1024 + (hp * 2 + hh) * 128
            cols.append(da_in[:, base:base + 128])
    wd = np.ascontiguousarray(np.concatenate(cols, axis=1))
    cosT, sinT, R = rope_tables()
    ident = np.eye(128, dtype=f32)
    bones = np.kron(np.eye(2, dtype=f32), np.ones((64, 64), f32))
    cst = np.ascontiguousarray(np.concatenate([ident, bones, R], axis=1))
    lamv = np.ascontiguousarray(np.broadcast_to(inp["da_lam"][0].reshape(1, 256), (128, 256))).astype(f32)
    subw = np.ascontiguousarray(np.broadcast_to(inp["da_subln"][0].reshape(1, 128), (128, 128))).astype(f32)
    return {"xa": xa, "y0g": y0g_b, "ct": ct,
            "adaw0g": np.ascontiguousarray(inp["ada_w"][0][:, 2048:3072]),
            "adaw1": np.ascontiguousarray(inp["ada_w"][1][:, 0:2048]),
            "smalls": sm, "wo": np.ascontiguousarray(inp["rw_out"][0]), "wd": wd, "cst": cst,
            "cosT": cosT, "sinT": sinT, "lamv": lamv, "subw": subw}


def build_l2(ctx=None):
    if ctx is None:
        nc = bass.Bass("TRN2", target_bir_lowering=False)
        P = Prog(nc)
        NT = 2048
        xn_d = P.dram_in("xn", [D, NT], F32)
        y1_d = P.dram_in("y1T", [D, NT], BF16)
        ct_d = P.dram_in("ct", [128, 16], F32)
        adaw_d = P.dram_in("adaw1g", [D, 1024], F32)
        sm_d = P.dram_in("smalls", [128, 8], F32)
        w_d = P.dram_in("wda", [D, D], F32)
        out_d = P.dram_out("outT", [D, NT], F32)
    else:
        nc, P = ctx["nc"], ctx["P"]
        NT = TL
        xn_d, y1_d, ct_d, adaw_d, sm_d, w_d, out_d = [ctx[k] for k in (
            "xn_s", "y1g", "ct", "c_adaw1g", "c_smalls", "c_wda", "outT")]
    sm = P.sb("sm", [128, 8], F32)
    P.dma(sm.v, sm_d.v)
    banks = [P.ps("bank%d" % i, [128, 512]) for i in range(3)]
    ct = P.sb("ct", [128, 16], F32)
    P.dma(ct.v, ct_d.v)
    sct = P.sb("sct", [128, 16], F32)
    P.act(sct.v, ct.v, AF.Silu)
    adaw_sb = P.sb("adaw_sb", [128, 8, 256], F32)
    modg = P.sb("modg", [128, 8, 2], F32)
    mod_compute(P, adaw_d, 1024, sct, lambda mc: sm[:, mc:mc + 1], banks[0], modg, adaw_sb)
    Wa = P.sb("Wa", [128, 8, 1024], BF16)
    wst = [P.sb("wst%d" % i, [128, 1024], F32) for i in range(2)]
    for kc in range(8):
        P.dma(wst[kc % 2].v, w_d[kc * 128:(kc + 1) * 128, :])
        P.copy("pool", Wa[:, kc, :], wst[kc % 2].v)
    xin = [P.sb("xin%d" % i, [128, 8, W], F32) for i in range(2)]
    yin = [P.sb("yin%d" % i, [128, 8, W], BF16) for i in range(2)]
    ob = [P.sb("ob%d" % i, [128, 8, W], F32) for i in range(2)]
    xn_r = xn_d.t[:].rearrange("(k p) c -> p k c", p=128)
    y1_r = None if isinstance(y1_d, list) else y1_d.t[:].rearrange("(k p) c -> p k c", p=128)
    out_r = out_d.t[:].rearrange("(k p) c -> p k c", p=128)
    for bi in range(NT // W):
        s_ = bi % 2
        P.dma(xin[s_].v, View(xn_d, xn_r[:, :, bi * W:(bi + 1) * W]))
        if isinstance(y1_d, list):
            c0 = bi * W
            yr = y1_d[c0 // 2048].t[:].rearrange("(k p) c -> p k c", p=128)
            P.dma(yin[s_].v, View(y1_d[c0 // 2048], yr[:, :, c0 % 2048:c0 % 2048 + W]))
        else:
            P.dma(yin[s_].v, View(y1_d, y1_r[:, :, bi * W:(bi + 1) * W]))
        for oc in range(8):
            pp = banks[1 + oc % 2]
            for kc in range(8):
                P.matmul(pp[:, 0:W], Wa[:, kc, oc * 128:(oc + 1) * 128], yin[s_][:, kc, :],
                         start=(kc == 0), stop=(kc == 7))
            P.stt(ob[s_][:, oc, :], pp[:, 0:W], modg[:, oc, 0:1], xin[s_][:, oc, :], ALU.mult, ALU.add)
        P.dma(View(out_d, out_r[:, :, bi * W:(bi + 1) * W]), ob[s_].v)
    if ctx is None:
        P.emit()
    return nc


def l2_inputs(inp, b, tq, xn, y1T):
    f32 = np.float32
    ct = np.zeros((128, 16), f32)
    ct[:, 0::2] = inp["c"][b].reshape(8, 128).T
    ct[:, 1::2] = inp["c_ctx"].reshape(8, 128).T
    sm = np.ascontiguousarray(inp["ada_b"][1][2048:3072].reshape(8, 128).T).astype(f32)
    return {"xn": xn, "y1T": y1T, "ct": ct, "adaw1g": np.ascontiguousarray(inp["ada_w"][1][:, 2048:3072]),
            "smalls": sm, "wda": np.ascontiguousarray(inp["da_out"][0])}


GROUPS = [[0, 1, 2, 3], [4, 5, 6, 7]]


def build_fused(upto=None):
    nc = bass.Bass("TRN2", target_bir_lowering=False)
    P = Prog(nc)
    ctx = {"nc": nc, "P": P}
    decl = [("a_xa", [D, XA_COLS], F32), ("ct", [128, 16], F32), ("a_adaw", [D, 2048], F32),
            ("a_smalls", [128, NSMALL], F32), ("a_win", [D, 1280], F32), ("a_w2", [128, 256], F32),
            ("a_a2", [128, 256], F32), ("a_cst", [128, C_TOT], F32),
            ("b_xa", [D, TA], F32), ("b_adaw0g", [D, 1024], F32), ("b_adaw1", [D, 2048], F32),
            ("b_smalls", [128, 48], F32), ("b_wo", [D, D], F32), ("b_wd", [D, 1024], F32),
            ("b_cst", [128, 384], F32), ("b_cosT", [128, TL], F32), ("b_sinT", [128, TL], F32),
            ("b_lamv", [128, 256], F32), ("b_subw", [128, 128], F32),
            ("c_adaw1g", [D, 1024], F32), ("c_smalls", [128, 8], F32), ("c_wda", [D, D], F32)]
    for name, shape, dt_ in decl:
        if upto == "A" and name[0] in "bc" and name[1] == "_":
            continue
        if upto == "B" and name[0] == "c" and name[1] == "_":
            continue
        ctx[name] = P.dram_in(name, shape, dt_)
    ctx["outT"] = P.dram_out("outT", [D, TL], F32)
    pw0 = [TC, 2048, 2048, 2048, 2048]
    ctx["y0loc"] = [P.dram_tmp("y0loc%d" % i, [256, w], BF16) for i, w in enumerate(pw0)]
    ctx["y0g"] = [P.dram_tmp("y0g%d" % i, [D, w], BF16) for i, w in enumerate(pw0)]
    ctx["of_scratch"] = P.dram_tmp("of_scratch", [256, TA], F32)
    ctx["xn_s"] = P.dram_tmp("xn_s", [D, TL], F32)
    ctx["y1loc"] = [P.dram_tmp("y1loc%d" % i, [256, 2048], BF16) for i in range(4)]
    ctx["y1g"] = [P.dram_tmp("y1g%d" % i, [D, 2048], BF16) for i in range(4)]
    build_l0(ctx=ctx)
    P.emit()
    P.end_phase()
    for i in range(5):
        P.collective("AllGather", ctx["y0g"][i].v, ctx["y0loc"][i].v, GROUPS)
    if upto == "A":
        tb = P.sb("dbg_b", [128, 8, 512], BF16)
        tf = P.sb("dbg_f", [128, 8, 512], F32)
        orr = ctx["outT"].t[:].rearrange("(k p) c -> p k c", p=128)
        for i in range(4):
            pi, off = ((1, 0), (1, 512), (2, 1536), (4, 1536))[i]
            yr = ctx["y0g"][pi].t[:].rearrange("(k p) c -> p k c", p=128)
            P.dma(tb.v, View(ctx["y0g"][pi], yr[:, :, off:off + 512]))
            P.copy("dve", tf.v, tb.v)
            P.dma(View(ctx["outT"], orr[:, :, i * 512:(i + 1) * 512]), tf.v)
        P.emit()
        P.end_phase()
        return nc
    build_l1_full(0, ctx=ctx)
    P.emit()
    P.end_phase()
    for i in range(4):
        P.collective("AllGather", ctx["y1g"][i].v, ctx["y1loc"][i].v, GROUPS)
    if upto == "B":
        tb = P.sb("dbg_b", [128, 8, 512], BF16)
        tf = P.sb("dbg_f", [128, 8, 512], F32)
        xr = ctx["xn_s"].t[:].rearrange("(k p) c -> p k c", p=128)
        orr = ctx["outT"].t[:].rearrange("(k p) c -> p k c", p=128)
        for i in range(2):
            c0 = (0, TL - 512)[i]
            yr = ctx["y1g"][c0 // 2048].t[:].rearrange("(k p) c -> p k c", p=128)
            P.dma(tb.v, View(ctx["y1g"][c0 // 2048], yr[:, :, c0 % 2048:c0 % 2048 + 512]))
            P.copy("dve", tf.v, tb.v)
            P.dma(View(ctx["outT"], orr[:, :, i * 512:(i + 1) * 512]), tf.v)
            P.dma(tf.v, View(ctx["xn_s"], xr[:, :, c0:c0 + 512]))
            P.dma(View(ctx["outT"], orr[:, :, (2 + i) * 512:(3 + i) * 512]), tf.v)
        P.emit()
        P.end_phase()
        return nc
    build_l2(ctx=ctx)
    P.emit()
    P.end_phase()
    return nc


def fused_inputs(inp, b, g):
    m = {}
    a = l0_inputs(inp, b, g)
    for k in ("xa", "adaw", "smalls", "win", "w2", "a2", "cst"):
        m["a_" + k] = a[k]
    m["ct"] = a["ct"]
    bb = l1_inputs(inp, b, g, None)
    for k in ("xa", "adaw0g", "adaw1", "smalls", "wo", "wd", "cst", "cosT", "sinT", "lamv", "subw"):
        m["b_" + k] = bb[k]
    c = l2_inputs(inp, b, g, None, None)
    for k in ("adaw1g", "smalls", "wda"):
        m["c_" + k] = c[k]
    return m


def kernel(**inp):
    inp = {k: np.asarray(v) for k, v in inp.items()}
    cores = list(range(8))
    nc = build_fused()
    maps = [fused_inputs(inp, c // 4, c % 4) for c in cores]
    res = run_bass_kernel_spmd(nc, maps, core_ids=cores).results
    out = np.zeros((2, TL, D), np.float32)
    for c in cores:
        b, tq = c // 4, c % 4
        out[b, tq * 2048:(tq + 1) * 2048] = np.asarray(res[c]["outT"])[:, tq * 2048:(tq + 1) * 2048].T
    return out
```

```python
import numpy as np
import concourse.bass as bass
import concourse.mybir as mybir
from concourse.bass_utils import run_bass_kernel_spmd

F32 = mybir.dt.float32
BF16 = mybir.dt.bfloat16
AF = mybir.ActivationFunctionType
ALU = mybir.AluOpType
AX = mybir.AxisListType

ENGS = ("pe", "act", "dve", "pool", "sp")


class Buf:
    def __init__(self, prog, t, name, space):
        self.prog = prog
        self.t = t
        self.name = name
        self.space = space
        self.last_writer = None
        self.readers = []
        self.dma_sem = None
        self.dma_count = 0

    def __getitem__(self, idx):
        return View(self, self.t[idx])

    @property
    def v(self):
        return View(self, self.t[:])


class View:
    def __init__(self, buf, ap):
        self.buf = buf
        self.ap = ap

    def __getitem__(self, idx):
        return View(self.buf, self.ap[idx])


class Op:
    __slots__ = ("eng", "fn", "deps", "needs_inc", "seq", "dma_buf", "dma_val", "idx", "dma_inc")

    def __init__(self, eng, fn):
        self.eng = eng
        self.fn = fn
        self.deps = []
        self.needs_inc = False
        self.seq = None
        self.dma_buf = None
        self.dma_val = None
        self.dma_inc = 16


def _ap(x):
    return x.ap if isinstance(x, View) else x


class Prog:
    def __init__(self, nc):
        import contextlib
        self.nc = nc
        self.ops = {e: [] for e in ENGS}
        self.bufs = []
        self.dram = {}
        self.same_engine_sync = True
        self.stack = contextlib.ExitStack()
        self.phase = 0
        self.phase_sem = None
        self.uid = 0

    def end_phase(self):
        import contextlib
        self.stack.close()
        self.stack = contextlib.ExitStack()
        self.bufs = []

    def sb(self, name, shape, dtype):
        self.uid += 1
        t = self.stack.enter_context(self.nc.sbuf_tensor("sb%d_%s" % (self.uid, name), list(shape), dtype))
        b = Buf(self, t, name, "sb")
        self.bufs.append(b)
        return b

    def ps(self, name, shape, dtype=F32):
        self.uid += 1
        t = self.stack.enter_context(self.nc.psum_tensor("pp%d_%s" % (self.uid, name), list(shape), dtype))
        b = Buf(self, t, name, "ps")
        self.bufs.append(b)
        return b

    def dram_in(self, name, shape, dtype):
        t = self.nc.dram_tensor(name, list(shape), dtype, kind="ExternalInput")
        b = Buf(self, t, name, "dram")
        self.dram[name] = b
        return b

    def dram_out(self, name, shape, dtype):
        t = self.nc.dram_tensor(name, list(shape), dtype, kind="ExternalOutput")
        b = Buf(self, t, name, "dram")
        self.dram[name] = b
        return b

    def dram_tmp(self, name, shape, dtype, shared=False):
        if shared:
            t = self.nc.dram_tensor(name, list(shape), dtype, addr_space="Shared")
        else:
            t = self.nc.dram_tensor(name, list(shape), dtype)
        b = Buf(self, t, name, "dram")
        self.dram[name] = b
        return b

    def _record(self, eng, fn, reads, writes):
        op = Op(eng, fn)
        deps = []
        for v in reads:
            b = v.buf if isinstance(v, View) else v
            if b.last_writer is not None:
                deps.append(b.last_writer)
            if b.space == "ps":
                deps.extend(r for r in b.readers if r.eng != eng)
        for v in writes:
            b = v.buf if isinstance(v, View) else v
            if b.last_writer is not None:
                deps.append(b.last_writer)
            deps.extend(b.readers)
        seen = set()
        for d in deps:
            if id(d) in seen or d is op:
                continue
            seen.add(id(d))
            if d.dma_buf is None and d.eng == eng and (eng == "pe" or not self.same_engine_sync):
                continue
            op.deps.append(d)
            if d.dma_buf is None:
                d.needs_inc = True
        for v in writes:
            b = v.buf if isinstance(v, View) else v
            b.last_writer = op
            b.readers = []
        for v in reads:
            b = v.buf if isinstance(v, View) else v
            if b.last_writer is not op:
                b.readers.append(op)
        self.ops[eng].append(op)
        return op

    def op(self, eng, fn, reads=(), writes=()):
        return self._record(eng, fn, list(reads), list(writes))

    def matmul(self, out, lhsT, rhs, start=True, stop=True, extra_reads=(), **kw):
        o, l, r = _ap(out), _ap(lhsT), _ap(rhs)
        return self._record("pe", lambda e: e.matmul(o, l, r, start=start, stop=stop, **kw),
                            [lhsT, rhs] + list(extra_reads), [out])

    def transpose(self, out, in_, ident):
        o, i, d = _ap(out), _ap(in_), _ap(ident)
        return self._record("pe", lambda e: e.transpose(o, i, d), [in_, ident], [out])

    def act(self, out, in_, func, bias=None, scale=None, eng="act", accum_out=None):
        o, i = _ap(out), _ap(in_)
        kw = {}
        reads = [in_]
        writes = [out]
        if bias is not None:
            kw["bias"] = _ap(bias)
            if isinstance(bias, View):
                reads.append(bias)
        if scale is not None:
            kw["scale"] = _ap(scale)
            if isinstance(scale, View):
                reads.append(scale)
        if accum_out is not None:
            kw["accum_out"] = _ap(accum_out)
            writes.append(accum_out)
        return self._record("act", lambda e: e.activation(o, i, func, **kw), reads, writes)

    def tt(self, eng, out, in0, in1, op):
        o, a, b = _ap(out), _ap(in0), _ap(in1)
        return self._record(eng, lambda e: e.tensor_tensor(o, a, b, op), [in0, in1], [out])

    def ts(self, eng, out, in0, s1, s2, op0, op1=None, accum_out=None):
        o, a = _ap(out), _ap(in0)
        reads = [in0]
        writes = [out]
        for s in (s1, s2):
            if isinstance(s, View):
                reads.append(s)
        x1, x2 = _ap(s1), _ap(s2)
        kw = {}
        if op1 is not None:
            kw["op1"] = op1
        if accum_out is not None:
            kw["accum_out"] = _ap(accum_out)
            writes.append(accum_out)
        return self._record(eng, lambda e: e.tensor_scalar(o, a, x1, x2, op0, **kw), reads, writes)

    def stt(self, out, in0, scalar, in1, op0, op1, eng="dve"):
        o, a, b = _ap(out), _ap(in0), _ap(in1)
        reads = [in0, in1]
        if isinstance(scalar, View):
            reads.append(scalar)
        s = _ap(scalar)
        return self._record(eng, lambda e: e.scalar_tensor_tensor(o, a, s, b, op0, op1), reads, [out])

    def copy(self, eng, out, in_):
        o, i = _ap(out), _ap(in_)
        if eng == "act":
            return self._record(eng, lambda e: e.copy(o, i), [in_], [out])
        return self._record(eng, lambda e: e.tensor_copy(o, i), [in_], [out])

    def scan(self, out, d0, d1, initial, op0, op1):
        o, a, b = _ap(out), _ap(d0), _ap(d1)
        reads = [d0, d1]
        if isinstance(initial, View):
            reads.append(initial)
        ini = _ap(initial)
        return self._record("dve", lambda e: e.tensor_tensor_scan(o, a, b, ini, op0, op1), reads, [out])

    def recip(self, out, in_):
        o, i = _ap(out), _ap(in_)
        return self._record("dve", lambda e: e.reciprocal(o, i), [in_], [out])

    def memset(self, eng, out, val):
        o = _ap(out)
        return self._record(eng, lambda e: e.memset(o, val), [], [out])

    def reduce(self, out, in_, axis, op, eng="dve"):
        o, i = _ap(out), _ap(in_)
        return self._record(eng, lambda e: e.tensor_reduce(o, i, axis, op), [in_], [out])

    def dma(self, out, in_, queue="sp", **kw):
        o, i = _ap(out), _ap(in_)
        ob = out.buf
        ib = in_.buf
        key = ob if ob.space != "dram" else ib
        op = self._record(queue, lambda e: e.dma_start(out=o, in_=i, **kw), [in_], [out])
        key.dma_count += 1
        op.dma_buf = key
        op.dma_val = 16 * key.dma_count
        return op

    def emit(self, final_waits=()):
        nc = self.nc
        for e in ENGS:
            n = 0
            for op in self.ops[e]:
                if op.dma_buf is None and op.needs_inc:
                    n += 1
                    op.seq = n
        SEMCAP = 30000
        nsem = {e: 1 + max([op.seq or 0 for op in self.ops[e]] + [0]) // SEMCAP for e in ENGS}
        ph = self.phase
        sems = {e: [nc.alloc_semaphore("s%d_%s_%d" % (ph, e, i)) for i in range(nsem[e])] for e in ENGS}
        if self.phase_sem is None:
            self.phase_sem = nc.alloc_semaphore("phase_done")
        phase_sem = self.phase_sem
        dummy_sb = self.sb("phdummy", [128, 8], F32)

        for b in self.bufs + list(self.dram.values()):
            if b.dma_count > 0:
                b.dma_sem = nc.alloc_semaphore("d%d_%s" % (ph, b.name))
        engmap = {"pe": "tensor", "act": "scalar", "dve": "vector", "pool": "gpsimd", "sp": "sync"}
        all_dma = []
        for e in ENGS:
            for op in self.ops[e]:
                if op.dma_buf is not None:
                    all_dma.append(op)

        def gen(ename):
            def body(eng):
                waited = {}
                if ph > 0:
                    eng.wait_ge(phase_sem, 4 * ph)
                for op in self.ops[ename]:
                    need = {}
                    for d in op.deps:
                        if d.dma_buf is not None:
                            k = ("dma", id(d.dma_buf))
                            sem = d.dma_buf.dma_sem
                            val = d.dma_val
                        else:
                            si = (d.seq - 1) // SEMCAP
                            k = ("eng", d.eng, si)
                            sem = sems[d.eng][si]
                            val = d.seq - si * SEMCAP
                        if k not in need or need[k][1] < val:
                            need[k] = (sem, val)
                    for k, (sem, val) in need.items():
                        if waited.get(k, 0) >= val:
                            continue
                        eng.wait_ge(sem, val)
                        waited[k] = val
                    inst = op.fn(eng)
                    if op.dma_buf is not None:
                        if op.dma_inc == 16:
                            inst.then_inc(op.dma_buf.dma_sem, 16)
                        else:
                            inst.then_inc(op.dma_buf.dma_sem)
                    elif op.needs_inc:
                        inst.then_inc(sems[ename][(op.seq - 1) // SEMCAP], 1)
                if ename == "sp":
                    finals = {}
                    for op in all_dma:
                        b = op.dma_buf
                        finals[id(b)] = (b.dma_sem, op.dma_val if op.dma_inc != 16 else 16 * b.dma_count)
                    for sem, val in finals.values():
                        eng.wait_ge(sem, val)
                    eng.sem_inc(phase_sem, 1)
                elif ename == "act":
                    eng.copy(dummy_sb.t[:, 2:3], dummy_sb.t[:, 3:4]).then_inc(phase_sem, 1)
                elif ename == "dve":
                    eng.memset(dummy_sb.t[:, 4:5], 0.0).then_inc(phase_sem, 1)
                elif ename == "pool":
                    eng.memset(dummy_sb.t[:, 6:7], 0.0).then_inc(phase_sem, 1)
            return body

        with nc.Block() as block:
            block.tensor(gen("pe"))
            block.scalar(gen("act"))
            block.vector(gen("dve"))
            block.gpsimd(gen("pool"))
            block.sync(gen("sp"))
        self.phase += 1
        self.ops = {e: [] for e in ENGS}
        for b in self.bufs + list(self.dram.values()):
            b.last_writer = None
            b.readers = []
            b.dma_count = 0
            b.dma_sem = None


def _collective(self, kind, out, in_, groups, op=None):
    o, i = _ap(out), _ap(in_)
    alu = op if op is not None else ALU.bypass
    rec = self._record("pool", lambda e: e.collective_compute(kind, alu, replica_groups=groups, ins=[i], outs=[o]),
                       [in_], [out])
    key = out.buf
    key.dma_count += 1
    rec.dma_buf = key
    rec.dma_inc = 1
    rec.dma_val = key.dma_count
    return rec


Prog.collective = _collective


D = 1024
TC = 256
TL = 8192
TA = TC + TL
W = 256
WH = W + 2
NBLK_L = TL // W
XA_COLS = 1 + TC + 1 + 1 + TL + 1
NSMALL = 50
EXPM05 = 0.6065306597126334
RMS_EPS = 1e-6
GN_EPS = 64e-5


def l0_consts():
    ident = np.eye(128, dtype=np.float32)
    bones = np.kron(np.eye(2, dtype=np.float32), np.ones((64, 64), np.float32))
    idx = np.arange(128)
    same = (idx[:, None] // 64) == (idx[None, :] // 64)
    strict_f = (same & (idx[None, :] < idx[:, None])).astype(np.float32)
    incl_f = (same & (idx[None, :] <= idx[:, None])).astype(np.float32)
    strict_b = (same & (idx[None, :] > idx[:, None])).astype(np.float32)
    incl_b = (same & (idx[None, :] >= idx[:, None])).astype(np.float32)
    out = {}
    for nm, st, inc in (("f", strict_f, incl_f), ("b", strict_b, incl_b)):
        m1 = np.concatenate([st, st], axis=1)
        m2h = np.concatenate([st.T, inc.T], axis=1)
        m2 = np.concatenate([m2h, m2h], axis=1)
        out["m1" + nm] = np.ascontiguousarray(m1)
        out["m2" + nm] = np.ascontiguousarray(m2)
    ident2 = np.concatenate([np.eye(64, dtype=np.float32)] * 2, axis=0)
    cst = np.concatenate([ident, bones, out["m1f"], out["m2f"], out["m1b"], out["m2b"], ident2,
                          np.ones((128, 64), np.float32)], axis=1)
    return np.ascontiguousarray(cst)


C_ID, C_BO, C_M1F, C_M2F, C_M1B, C_M2B, C_ID2, C_ONE = 0, 128, 256, 512, 1024, 1280, 1792, 1856
C_TOT = 1920


def V2(view):
    return View(view.buf, view.ap.rearrange("p a b -> p (a b)"))


def build_l0(debug_out=False, stop=None, ctx=None):
    if ctx is None:
        nc = bass.Bass("TRN2", target_bir_lowering=False)
        P = Prog(nc)
        xa_d = P.dram_in("xa", [D, XA_COLS], F32)
        ct_d = P.dram_in("ct", [128, 16], F32)
        adaw_d = P.dram_in("adaw", [D, 2048], F32)
        sm_d = P.dram_in("smalls", [128, NSMALL], F32)
        win_d = P.dram_in("win", [D, 1280], F32)
        w2_d = P.dram_in("w2", [128, 256], F32)
        a2_d = P.dram_in("a2", [128, 256], F32)
        cst_d = P.dram_in("cst", [128, C_TOT], F32)
        y0_d = P.dram_out("y0", [256, TA], BF16)
        of_d = P.dram_tmp("of_scratch", [256, TA], F32)
    else:
        nc, P = ctx["nc"], ctx["P"]
        xa_d, ct_d, adaw_d, sm_d, win_d, w2_d, a2_d, cst_d, y0_d, of_d = [ctx[k] for k in (
            "a_xa", "ct", "a_adaw", "a_smalls", "a_win", "a_w2", "a_a2", "a_cst", "y0loc", "of_scratch")]

    cst = P.sb("cst", [128, C_TOT], F32)
    P.dma(cst.v, cst_d.v)
    ident = cst[:, C_ID:C_ID + 128]
    bones = cst[:, C_BO:C_BO + 128]
    ident2 = cst[:, C_ID2:C_ID2 + 64]
    ones64 = cst[:, C_ONE:C_ONE + 64]
    masks = {0: (cst[:, C_M1F:C_M1F + 256], cst[:, C_M2F:C_M2F + 512]),
             1: (cst[:, C_M1B:C_M1B + 256], cst[:, C_M2B:C_M2B + 512])}
    sm = P.sb("sm", [128, NSMALL], F32)
    P.dma(sm.v, sm_d.v)
    S_NG, S_ADAB, S_MU, S_W0, S_A0, S_KK, S_KA, S_RK, S_GG, S_GB = 0, 8, 24, 32, 36, 40, 42, 44, 46, 48
    w2 = P.sb("w2", [128, 256], F32)
    a2 = P.sb("a2", [128, 256], F32)
    P.dma(w2.v, w2_d.v)
    P.dma(a2.v, a2_d.v)
    ones128 = P.sb("ones128", [128, 128], F32)
    P.memset("pool", ones128.v, 1.0)

    der = P.sb("der", [128, 32], F32)
    P.ts("pool", der[:, 0:8], sm[:, S_MU:S_MU + 8], -1.0, 1.0, ALU.mult, ALU.add)
    P.ts("pool", der[:, 8:16], sm[:, S_MU:S_MU + 8], 0.5, None, ALU.mult)
    P.ts("pool", der[:, 16:18], sm[:, S_KA:S_KA + 2], -1.0, 1.0, ALU.mult, ALU.add)
    omu = lambda ci: der[:, ci:ci + 1]
    hmu = lambda ci: der[:, 8 + ci:9 + ci]
    omka = lambda p: der[:, 16 + p:17 + p]

    def dbg_stop(views):
        tot = max(64, sum(n for _, n in views))
        dbg = P.dram_out("dbg", [128, tot], F32)
        dsb = P.sb("dsb", [128, tot], F32)
        P.memset("dve", dsb.v, 0.0)
        c = 0
        for v, n in views:
            P.copy("dve", dsb[:, c:c + n], v)
            c += n
        P.dma(dbg.v, dsb.v)
        P.emit()
        return nc
    if stop == "pre0":
        return dbg_stop([(der[:, 0:18], 18)])
    ct = P.sb("ct", [128, 16], F32)
    P.dma(ct.v, ct_d.v)
    sct = P.sb("sct", [128, 16], F32)
    P.act(sct.v, ct.v, AF.Silu)
    modT = P.sb("modT", [128, 16, 2], F32)
    adaw = P.sb("adaw", [128, 8, 512], F32)
    ps_misc = P.ps("ps_misc", [128, 512])
    adaw_r = adaw_d.t[:].rearrange("(k p) m -> p k m", p=128)
    for piece in range(4):
        P.dma(adaw.v, View(adaw_d, adaw_r[:, :, piece * 512:(piece + 1) * 512]))
        for mcl in range(4):
            mc = piece * 4 + mcl
            for kc in range(8):
                P.matmul(ps_misc[:, 0:2], adaw[:, kc, mcl * 128:(mcl + 1) * 128], sct[:, kc * 2:kc * 2 + 2],
                         start=(kc == 0), stop=(kc == 7))
            P.ts("dve", modT[:, mc, :], ps_misc[:, 0:2], sm[:, S_ADAB + mc:S_ADAB + mc + 1], None, ALU.add)
    if stop == "pre1":
        return dbg_stop([(V2(modT.v), 32)])
    gmod = P.sb("gmod", [128, 8, 2], F32)
    for kc in range(8):
        P.ts("pool", gmod[:, kc, :], modT[:, 8 + kc, :], 1.0, sm[:, S_NG + kc:S_NG + kc + 1], ALU.add, ALU.mult)

    if stop == "pre2":
        return dbg_stop([(V2(modT.v), 32), (V2(gmod.v), 16)])
    Wb = P.sb("Wb", [128, 8, 1280], BF16)
    wst = [P.sb("wst%d" % i, [128, 1280], F32) for i in range(2)]
    for kc in range(8):
        P.dma(wst[kc % 2].v, win_d[kc * 128:(kc + 1) * 128, :])
        P.copy("pool", Wb[:, kc, :], wst[kc % 2].v)

    if stop == "pre":
        dbg = P.dram_out("dbg", [128, 64], F32)
        dsb = P.sb("dsb", [128, 64], F32)
        P.copy("dve", dsb[:, 0:32], V2(modT.v))
        P.copy("dve", dsb[:, 32:48], V2(gmod.v))
        P.copy("dve", dsb[:, 48:64], Wb[:, 7, 0:16])
        P.dma(dbg.v, dsb.v)
        P.emit()
        return nc
    xin = [P.sb("xin%d" % i, [128, 8, WH], F32) for i in range(2)]
    hT = P.sb("hT", [128, 8, WH], BF16)
    sqb = [P.sb("sqb%d" % i, [128, WH], F32) for i in range(2)]
    rstd = P.sb("rstd", [128, WH], F32)
    htmp = [P.sb("htmp%d" % i, [128, WH], F32) for i in range(2)]
    ps_proj = [P.ps("ps_proj%d" % i, [128, 512]) for i in range(2)]
    ps_a = P.ps("ps_a", [128, 512])
    u_sb = [P.sb("u_sb%d" % i, [128, WH], F32) for i in range(2)]
    s_sb = [P.sb("s_sb%d" % i, [128, W], F32) for i in range(2)]
    t_sb = [P.sb("t_sb%d" % i, [128, W], F32) for i in range(2)]

    def blk(name):
        return P.sb(name, [128, W], F32)

    Rb = [blk("R%d" % p) for p in range(2)]
    Kb = [blk("K%d" % p) for p in range(2)]
    Vb = [blk("V%d" % p) for p in range(2)]
    SG = [blk("SG%d" % p) for p in range(2)]
    LW = blk("LW")
    LA = blk("LA")
    TLW = blk("TLW")
    LOGW = [blk("LOGW%d" % p) for p in range(2)]
    Ab = [blk("A%d" % p) for p in range(2)]
    KQ = [blk("KQ%d" % p) for p in range(2)]
    KK = [blk("KK%d" % p) for p in range(2)]
    KD = [blk("KD%d" % p) for p in range(2)]
    KD0 = [blk("KD0%d" % p) for p in range(2)]
    T1 = [blk("T1%d" % p) for p in range(2)]
    T2 = [blk("T2%d" % p) for p in range(2)]
    CL = [blk("CL%d" % p) for p in range(2)]
    PRE = [blk("PRE%d" % p) for p in range(2)]
    E1 = [blk("E1%d" % p) for p in range(2)]
    E2 = [blk("E2%d" % p) for p in range(2)]
    E3 = [blk("E3%d" % p) for p in range(2)]
    AT = [blk("AT%d" % p) for p in range(2)]
    RT = [blk("RT%d" % p) for p in range(2)]
    BT = [blk("BT%d" % p) for p in range(2)]
    KT = [blk("KT%d" % p) for p in range(2)]
    BH = [blk("BH%d" % p) for p in range(2)]
    KH = [blk("KH%d" % p) for p in range(2)]
    DG = [P.sb("DG%d" % p, [128, 256], F32) for p in range(2)]
    OB = [blk("OB%d" % p) for p in range(2)]
    OF = [blk("OF%d" % p) for p in range(2)]
    YB = [P.sb("YB%d" % p, [128, W], BF16) for p in range(2)]

    def sbp(name, shape):
        return [P.sb("%s%d" % (name, p), shape, F32) for p in range(2)]

    Lm = sbp("Lm", [128, 256])
    NM = sbp("NM", [128, 512])
    KM = sbp("KM", [128, 512])
    Lk = [sbp("Lk%d_" % i, [128, 256]) for i in range(2)]
    Nk = [sbp("Nk%d_" % i, [128, 256]) for i in range(2)]
    Xk = [sbp("Xk%d_" % i, [128, 256]) for i in range(2)]
    Zb = sbp("Zb", [128, 256])
    TZ = sbp("TZ", [128, 256])
    VT = sbp("VT", [128, 128])
    BHT = sbp("BHT", [128, 128])
    KHT = sbp("KHT", [128, 128])
    RPT = sbp("RPT", [128, 128])
    PT = sbp("PT", [128, 128])
    ST = [sbp("ST%d_" % i, [128, 128]) for i in range(3)]
    BTbd = sbp("BTbd", [128, 512])
    KTbd = sbp("KTbd", [128, 512])
    DGd = sbp("DGd", [128, 512])
    BHTc = [sbp("BHTc%d_" % i, [128, 128]) for i in range(2)]
    KHTc = [sbp("KHTc%d_" % i, [128, 128]) for i in range(2)]
    RPTm = [sbp("RPTm%d_" % i, [128, 128]) for i in range(2)]
    PTbd = [sbp("PTbd%d_" % i, [128, 128]) for i in range(2)]
    T3 = sbp("T3", [128, 128])
    OTK = sbp("OTK", [128, 128])
    for p in range(2):
        for bb in (BTbd[p], KTbd[p], BHTc[0][p], BHTc[1][p], KHTc[0][p], KHTc[1][p], RPTm[0][p], RPTm[1][p]):
            P.memset("pool", bb.v, 0.0)
    psB = [P.ps("psB%d" % p, [128, 512]) for p in range(2)]
    psC = [P.ps("psC%d" % p, [128, 512]) for p in range(2)]

    def HH(buf, h):
        return buf[:, h * 128:(h + 1) * 128]

    def NMa(p, h):
        return NM[p][:, h * 256:h * 256 + 128]

    def NMb(p, h):
        return NM[p][:, h * 256 + 128:h * 256 + 256]

    def KMa(p, h):
        return KM[p][:, h * 256:h * 256 + 128]

    def KMb(p, h):
        return KM[p][:, h * 256 + 128:h * 256 + 256]

    def V3(view, h):
        return View(view.buf, view.ap.rearrange("p (h c) -> p h c", h=h))


    def x_cols(blk_id):
        if blk_id < 0:
            return 0
        return 258 + 256 * blk_id

    def tok0(blk_id):
        return 0 if blk_id < 0 else TC + 256 * blk_id

    xa_r = xa_d.t[:].rearrange("(k p) c -> p k c", p=128)

    def issue_x(blk_id, slot):
        c0 = x_cols(blk_id)
        P.dma(xin[slot].v, View(xa_d, xa_r[:, :, c0:c0 + WH]))

    evac_flip = [0]

    def evac(out, in_):
        evac_flip[0] ^= 1
        P.copy("act" if evac_flip[0] else "dve", out, in_)


    epsb = P.sb("epsb", [128, 4], F32)
    P.memset("pool", epsb[:, 0:1], RMS_EPS)
    P.memset("pool", epsb[:, 1:2], 1e-12)
    P.memset("pool", epsb[:, 2:3], GN_EPS)

    def stage_a(blk_id, slot, d):
        col = 1 if blk_id < 0 else 0
        xs = xin[slot]
        for kc in range(8):
            sq = sqb[kc % 2]
            P.act(sq.v, xs[:, kc, :], AF.Square)
            P.matmul(ps_a[:, 0:WH], ones128.v, sq.v, start=(kc == 0), stop=(kc == 7))
        P.act(rstd.v, ps_a[:, 0:WH], AF.Sqrt, scale=1.0 / D, bias=epsb[:, 0:1])
        P.recip(rstd.v, rstd.v)
        for kc in range(8):
            tmp = htmp[kc % 2]
            P.stt(tmp.v, xs[:, kc, :], gmod[:, kc, col:col + 1], rstd.v, ALU.mult, ALU.mult)
            P.act(hT[:, kc, :], tmp.v, AF.Identity, bias=modT[:, kc, col:col + 1])
        if blk_id < 0 or blk_id == 0:
            P.memset("pool", hT[:, :, 0:1], 0.0)
        if blk_id < 0 or blk_id == NBLK_L - 1:
            P.memset("pool", hT[:, :, WH - 1:WH], 0.0)
        mixed_dst = [Rb[0], Rb[1], Kb[0], Kb[1], Vb[0], Vb[1], None, None, LW, LA]
        mix_ci = [0, 1, 2, 3, 4, 5, None, None, 6, 7]
        n = 0
        for cc in range(10):
            if cc in (6, 7) and d == 0:
                continue
            pp = ps_proj[n % 2]
            for kc in range(8):
                P.matmul(pp[:, 0:WH], Wb[:, kc, cc * 128:(cc + 1) * 128], hT[:, kc, :],
                         start=(kc == 0), stop=(kc == 7))
            if cc in (6, 7):
                P.act(SG[cc - 6].v, pp[:, 1:W + 1], AF.Silu)
            else:
                ci = mix_ci[cc]
                u = u_sb[n % 2]
                s_ = s_sb[n % 2]
                t_ = t_sb[n % 2]
                P.copy("act", u.v, pp[:, 0:WH])
                P.tt("dve", s_.v, u[:, 0:W], u[:, 2:W + 2], ALU.add)
                P.act(t_.v, u[:, 1:W + 1], AF.Identity, scale=omu(ci))
                P.stt(mixed_dst[cc].v, s_.v, hmu(ci), t_.v, ALU.mult, ALU.add)
            n += 1
        P.act(TLW.v, LW.v, AF.Tanh)
        def derive(p):
            pc = slice(p * 128, (p + 1) * 128)
            P.matmul(ps_a[:, p * W:(p + 1) * W], w2[64 * d:64 * d + 64, pc], TLW[64 * d:64 * d + 64, :])
            yield
            P.act(LOGW[p].v, ps_a[:, p * W:(p + 1) * W], AF.Sigmoid, bias=sm[:, S_W0 + 2 * d + p:S_W0 + 2 * d + p + 1])
            yield
            P.ts("dve", LOGW[p].v, LOGW[p].v, -EXPM05, None, ALU.mult)
            yield
            P.matmul(ps_a[:, p * W:(p + 1) * W], a2[64 * d:64 * d + 64, pc], LA[64 * d:64 * d + 64, :])
            yield
            P.act(Ab[p].v, ps_a[:, p * W:(p + 1) * W], AF.Sigmoid, bias=sm[:, S_A0 + 2 * d + p:S_A0 + 2 * d + p + 1])
            yield
            P.act(KQ[p].v, Kb[p].v, AF.Identity, scale=sm[:, S_KK + p:S_KK + p + 1])
            yield
            P.act(T1[p].v, KQ[p].v, AF.Square)
            yield
            P.matmul(ps_a[:, p * W:(p + 1) * W], bones, T1[p].v)
            yield
            P.act(T2[p].v, ps_a[:, p * W:(p + 1) * W], AF.Sqrt, bias=epsb[:, 1:2])
            yield
            P.recip(T2[p].v, T2[p].v)
            yield
            P.tt("pool", KK[p].v, KQ[p].v, T2[p].v, ALU.mult)
            yield
            P.act(T1[p].v, Ab[p].v, AF.Identity, scale=sm[:, S_KA + p:S_KA + p + 1], bias=omka(p))
            yield
            P.tt("pool", KD[p].v, Kb[p].v, T1[p].v, ALU.mult)
            yield
            if d == 1:
                P.matmul(ps_a[:, p * W:(p + 1) * W], a2[0:64, pc], LA[0:64, :])
                yield
                P.act(T2[p].v, ps_a[:, p * W:(p + 1) * W], AF.Sigmoid, bias=sm[:, S_A0 + p:S_A0 + p + 1])
                yield
                P.act(T2[p].v, T2[p].v, AF.Identity, scale=sm[:, S_KA + p:S_KA + p + 1], bias=omka(p))
                yield
                P.tt("pool", KD0[p].v, Kb[p].v, T2[p].v, ALU.mult)
                yield
            for ch in range(4):
                sl = slice(ch * 64, (ch + 1) * 64)
                P.scan(PRE[p][:, sl], ones64, LOGW[p][:, sl], 0.0, ALU.mult, ALU.add)
                yield
            if d == 0:
                clb = PRE[p]
            else:
                clb = CL[p]
                for ch in range(4):
                    sl = slice(ch * 64, (ch + 1) * 64)
                    P.act(CL[p][:, sl], PRE[p][:, sl], AF.Identity, scale=-1.0,
                          bias=PRE[p][:, ch * 64 + 63:ch * 64 + 64])
                    yield
                P.tt("pool", CL[p].v, CL[p].v, LOGW[p].v, ALU.add)
                yield
            P.act(E1[p].v, clb.v, AF.Exp)
            yield
            P.act(E2[p].v, clb.v, AF.Exp, scale=-1.0)
            yield
            P.tt("pool", T1[p].v, clb.v, LOGW[p].v, ALU.subtract)
            yield
            P.act(E3[p].v, T1[p].v, AF.Exp)
            yield
            P.stt(AT[p].v, KK[p].v, -1.0, E3[p].v, ALU.mult, ALU.mult)
            yield
            P.tt("pool", RT[p].v, Rb[p].v, E1[p].v, ALU.mult)
            yield
            P.tt("pool", T1[p].v, KK[p].v, Ab[p].v, ALU.mult)
            yield
            P.tt("pool", BT[p].v, T1[p].v, E2[p].v, ALU.mult)
            yield
            P.tt("pool", KT[p].v, KD[p].v, E2[p].v, ALU.mult)
            yield
            for ch in range(4):
                sl = slice(ch * 64, (ch + 1) * 64)
                gc = ch * 64 + 63 if d == 0 else ch * 64
                gcol = E1[p][:, gc:gc + 1]
                P.ts("dve", BH[p][:, sl], BT[p][:, sl], gcol, None, ALU.mult)
                yield
                P.act(KH[p][:, sl], KT[p][:, sl], AF.Identity, scale=gcol)
                yield
                P.ts("dve", DGd[p][:, ch * 128:(ch + 1) * 128], ident, gcol, None, ALU.mult)
                yield
            for h in range(2):
                hp = slice(64 * h, 64 * h + 64)
                for tl2 in range(2):
                    q = (tl2 * 2 + h) * 128
                    P.copy("pool", BTbd[p][hp, q:q + 128], BT[p][hp, tl2 * 128:(tl2 + 1) * 128])
                    yield
                    P.copy("pool", KTbd[p][hp, q:q + 128], KT[p][hp, tl2 * 128:(tl2 + 1) * 128])
                    yield

        gens = [derive(p) for p in range(2)]
        while gens:
            for g in list(gens):
                try:
                    next(g)
                except StopIteration:
                    gens.remove(g)

    def stage_b(p, tl, d, sw_state, upto=99):
        m1, m2 = masks[d]
        cs = slice(tl * 128, (tl + 1) * 128)
        pc = psC[p]
        pb = psB[p]
        btbd = lambda h: BTbd[p][:, (tl * 2 + h) * 128:(tl * 2 + h + 1) * 128]
        ktbd = lambda h: KTbd[p][:, (tl * 2 + h) * 128:(tl * 2 + h + 1) * 128]
        P.matmul(pc[:, 0:256], AT[p][:, cs], BTbd[p][:, tl * 256:(tl + 1) * 256])
        P.tt("dve", Lm[p].v, pc[:, 0:256], m1, ALU.mult)
        yield
        for h in range(2):
            P.matmul(pb[:, h * 256:h * 256 + 128], btbd(h), AT[p][:, cs])
            P.matmul(pb[:, h * 256 + 128:h * 256 + 256], btbd(h), RT[p][:, cs])
        P.tt("dve", NM[p].v, pb[:, 0:512], m2, ALU.mult)
        yield
        for h in range(2):
            P.matmul(pb[:, h * 256:h * 256 + 128], ktbd(h), AT[p][:, cs])
            P.matmul(pb[:, h * 256 + 128:h * 256 + 256], ktbd(h), RT[p][:, cs])
        P.tt("dve", KM[p].v, pb[:, 0:512], m2, ALU.mult)
        yield
        if upto <= 1:
            return
        X = Xk[0][p]
        for h in range(2):
            P.tt("pool", HH(X, h), NMa(p, h), ident, ALU.add)
        Lc = Lm[p]
        Nc_views = [NMa(p, h) for h in range(2)]
        xi = 0
        for k in range(1, 6):
            Ln = Lk[k % 2][p]
            for h in range(2):
                P.matmul(pc[:, h * 128:(h + 1) * 128], Nc_views[h], HH(Lc, h))
            P.copy("act", Ln.v, pc[:, 0:256])
            yield
            if k < 5:
                Nn = Nk[k % 2][p]
                for h in range(2):
                    P.matmul(pc[:, 256 + h * 128:256 + (h + 1) * 128], HH(Lc, h), Nc_views[h])
                P.copy("act", Nn.v, pc[:, 256:512])
            for h in range(2):
                P.matmul(pb[:, h * 128:(h + 1) * 128], HH(Ln, h), HH(Xk[xi][p], h))
            Xn = Xk[1 - xi][p]
            P.tt("dve", Xn.v, pb[:, 0:256], Xk[xi][p].v, ALU.add)
            yield
            xi = 1 - xi
            Lc = Ln
            if k < 5:
                Nc_views = [HH(Nn, h) for h in range(2)]
        X = Xk[xi][p]
        if upto <= 2:
            return
        P.transpose(pc[:, 0:128], AT[p][:, cs], ident)
        P.copy("act", Zb[p][:, 0:128], pc[:, 0:128])
        yield
        P.transpose(pc[:, 128:256], Vb[p][:, cs], ident)
        P.copy("dve", VT[p].v, pc[:, 128:256])
        yield
        P.transpose(pc[:, 256:384], BH[p][:, cs], ident)
        P.copy("act", BHTc[0][p][0:64, :], pc[0:64, 256:384])
        yield
        P.copy("dve", BHTc[1][p][64:128, :], pc[64:128, 256:384])
        yield
        P.transpose(pc[:, 384:512], KH[p][:, cs], ident)
        P.copy("act", KHTc[0][p][0:64, :], pc[0:64, 384:512])
        yield
        P.copy("dve", KHTc[1][p][64:128, :], pc[64:128, 384:512])
        yield
        for h in range(2):
            P.matmul(pb[:, h * 64:(h + 1) * 64], KMa(p, h), VT[p][:, h * 64:(h + 1) * 64])
        P.copy("act", Zb[p][:, 128:256], pb[:, 0:128])
        yield
        for part in range(2):
            for h in range(2):
                q = (part * 2 + h) * 64
                P.matmul(pb[:, 128 + q:128 + q + 64], HH(X, h), Zb[p][:, q:q + 64])
        P.copy("dve", TZ[p].v, pb[:, 128:384])
        yield
        if upto <= 3:
            return
        for h in range(2):
            P.matmul(pc[:, h * 128:(h + 1) * 128], TZ[p][:, 0:128], NMb(p, h))
        for h in range(2):
            hp = slice(64 * h, 64 * h + 64)
            P.tt("dve", RPT[p][hp, :], pc[hp, h * 128:(h + 1) * 128], RT[p][hp, cs], ALU.add)
            yield
        P.copy("pool", RPTm[0][p][:, 0:64], RPT[p][:, 0:64])
        P.copy("pool", RPTm[1][p][:, 64:128], RPT[p][:, 64:128])
        for c in range(2):
            P.matmul(pc[:, 256 + c * 128:256 + (c + 1) * 128], TZ[p][:, 0:128], BHTc[c][p].v)
        for c in range(2):
            ch = 2 * tl + c
            P.tt("dve", T3[p].v, pc[:, 256 + c * 128:256 + (c + 1) * 128], bones, ALU.mult)
            yield
            P.tt("pool", PTbd[c][p].v, T3[p].v, DGd[p][:, ch * 128:(ch + 1) * 128], ALU.add)
        if upto <= 5:
            return
        order = (0, 1) if d == 0 else (1, 0)
        s_at = {}
        for c in order:
            si = sw_state[p]
            S_in = ST[si][p]
            S_out = ST[(si + 1) % 3][p]
            s_at[c] = S_in
            P.matmul(pb[:, 0:128], PTbd[c][p].v, S_in.v, start=True, stop=False)
            P.matmul(pb[:, 0:128], BHTc[c][p].v, TZ[p][:, 128:256], start=False, stop=False)
            P.matmul(pb[:, 0:128], KHTc[c][p].v, VT[p].v, start=False, stop=True)
            P.tt("dve", S_out.v, pb[:, 0:128], bones, ALU.mult)
            yield
            sw_state[p] = (si + 1) % 3
        if upto <= 6:
            return
        P.matmul(pb[:, 128:256], RPTm[0][p].v, s_at[0].v, start=True, stop=False)
        P.matmul(pb[:, 128:256], RPTm[1][p].v, s_at[1].v, start=False, stop=False)
        for h in range(2):
            P.matmul(pb[:, 128 + h * 64:128 + (h + 1) * 64], NMb(p, h), TZ[p][:, 128 + h * 64:128 + (h + 1) * 64],
                     start=False, stop=False)
            P.matmul(pb[:, 128 + h * 64:128 + (h + 1) * 64], KMb(p, h), VT[p][:, h * 64:(h + 1) * 64],
                     start=False, stop=(h == 1))
        P.copy("act", OTK[p].v, pb[:, 128:256])
        yield
        P.transpose(pb[:, 256:384], OTK[p].v, ident)
        if d == 0:
            P.copy("dve", OB[p][:, cs], pb[:, 256:384])
            yield
        else:
            P.tt("dve", OB[p][:, cs], pb[:, 256:384], OF[p][:, cs], ALU.add)
            yield

    def readout(blk_id):
        t0 = tok0(blk_id)
        for p in range(2):
            P.matmul(ps_a[:, 0:W], bones, OB[p].v)
            P.stt(T1[p].v, ps_a[:, 0:W], -1.0 / 64, OB[p].v, ALU.mult, ALU.add)
            P.act(T2[p].v, T1[p].v, AF.Square)
            P.matmul(ps_a[:, W:2 * W], bones, T2[p].v)
            P.act(T2[p].v, ps_a[:, W:2 * W], AF.Sqrt, scale=1.0 / 64, bias=epsb[:, 2:3])
            P.recip(T2[p].v, T2[p].v)
            P.tt("pool", T1[p].v, T1[p].v, T2[p].v, ALU.mult)
            P.act(T1[p].v, T1[p].v, AF.Identity, scale=sm[:, S_GG + p:S_GG + p + 1],
                  bias=sm[:, S_GB + p:S_GB + p + 1])
            P.tt("pool", T2[p].v, KD[p].v, KD0[p].v, ALU.add)
            P.stt(T2[p].v, Rb[p].v, sm[:, S_RK + p:S_RK + p + 1], T2[p].v, ALU.mult, ALU.mult)
            P.matmul(ps_a[:, 0:W], bones, T2[p].v)
            P.tt("dve", T2[p].v, ps_a[:, 0:W], Vb[p].v, ALU.mult)
            P.tt("pool", T1[p].v, T1[p].v, T2[p].v, ALU.add)
            P.tt("pool", YB[p].v, T1[p].v, SG[p].v, ALU.mult)
            if isinstance(y0_d, list):
                pi, off = (0, t0) if t0 < TC else (1 + (t0 - TC) // 2048, (t0 - TC) % 2048)
                P.dma(y0_d[pi][p * 128:(p + 1) * 128, off:off + W], YB[p].v)
            else:
                P.dma(y0_d[p * 128:(p + 1) * 128, t0:t0 + W], YB[p].v)

    for p in range(2):
        P.memset("pool", ST[0][p].v, 0.0)
    for d in range(2):
        blocks = [-1] + (list(range(NBLK_L)) if d == 0 else list(range(NBLK_L - 1, -1, -1)))
        if debug_out and isinstance(debug_out, int) and debug_out > 1:
            blocks = blocks[:debug_out]
        sw_state = [0, 0]
        if d == 1:
            for p in range(2):
                P.memset("pool", ST[0][p].v, 0.0)
        issue_x(blocks[0], 0)
        for bi, b in enumerate(blocks):
            slot = bi % 2
            if bi + 1 < len(blocks):
                issue_x(blocks[bi + 1], 1 - slot)
            t0 = tok0(b)
            if d == 1:
                for p in range(2):
                    P.dma(OF[p].v, of_d[p * 128:(p + 1) * 128, t0:t0 + W])
            stage_a(b, slot, d)
            if stop == "a":
                return dbg_stop([(Rb[0].v, 256), (KK[1].v, 256), (LOGW[0].v, 256), (Ab[1].v, 256), (KD[0].v, 256),
                                 (AT[0].v, 256), (RT[0].v, 256), (BT[0].v, 256), (KT[0].v, 256), (BH[0].v, 256),
                                 (DG[0].v, 256), (Vb[1].v, 256)])
            tiles = (0, 1) if d == 0 else (1, 0)
            for tl in tiles:
                if not (stop and stop[0] == "b"):
                    gens = [stage_b(p, tl, d, sw_state) for p in range(2)]
                    while gens:
                        for g in list(gens):
                            try:
                                next(g)
                            except StopIteration:
                                gens.remove(g)
                    continue
                for p in range(2):
                    for _ in stage_b(p, tl, d, sw_state, upto=int(stop[1:]) if (stop and stop[0] == "b" and len(stop) > 1) else 99):
                        pass
                    if stop and stop[0] == "b":
                        return dbg_stop([(OB[0][:, 0:128], 128), (ST[sw_state[0]][0].v, 128), (TZ[0].v, 256),
                                         (Lm[0].v, 256), (NM[0].v, 512), (Xk[1][0].v, 256), (RPT[0].v, 128), (PTbd[0][0].v, 128)])
            if d == 0:
                for p in range(2):
                    P.dma(of_d[p * 128:(p + 1) * 128, t0:t0 + W], OB[p].v)
            else:
                readout(b)
    if ctx is None:
        P.emit()
    return nc


def l0_inputs(inp, b, hg):
    f32 = np.float32
    x, ctx = inp["x"], inp["ctx"]
    z1 = np.zeros((D, 1), f32)
    xa = np.concatenate([z1, ctx[b].T, z1, z1, x[b].T, z1], axis=1)
    ct = np.zeros((128, 16), f32)
    cb = inp["c"][b].reshape(8, 128).T
    cc = inp["c_ctx"].reshape(8, 128).T
    ct[:, 0::2] = cb
    ct[:, 1::2] = cc
    adaw = np.ascontiguousarray(inp["ada_w"][0][:, 0:2048])
    hc = slice(hg * 256, (hg + 1) * 256)
    rw_in = inp["rw_in"][0]
    cols = []
    for X in range(4):
        cols.append(rw_in[:, X * 1024 + hg * 256: X * 1024 + (hg + 1) * 256])
    cols.append(rw_in[:, 4096:4352])
    win = np.ascontiguousarray(np.concatenate(cols, axis=1))
    sm = np.zeros((128, NSMALL), f32)
    sm[:, 0:8] = inp["norm_g"][0].reshape(8, 128).T
    sm[:, 8:24] = inp["ada_b"][0][:2048].reshape(16, 128).T
    mu = inp["rw_mu"][0]
    mus = []
    for X in range(3):
        for p in range(2):
            mus.append(mu[X * 1024 + hg * 256 + p * 128: X * 1024 + hg * 256 + (p + 1) * 128])
    mus.append(mu[3072:3200])
    mus.append(mu[3200:3328])
    sm[:, 24:32] = np.stack(mus, axis=1)
    for d in range(2):
        for p in range(2):
            sm[:, 32 + 2 * d + p] = inp["rw_w0"][0][d, hg * 256 + p * 128: hg * 256 + (p + 1) * 128]
            sm[:, 36 + 2 * d + p] = inp["rw_a0"][0][d, hg * 256 + p * 128: hg * 256 + (p + 1) * 128]
    for p in range(2):
        sl = slice(hg * 256 + p * 128, hg * 256 + (p + 1) * 128)
        sm[:, 40 + p] = inp["rw_kk"][0][sl]
        sm[:, 42 + p] = inp["rw_ka"][0][sl]
        sm[:, 44 + p] = inp["rw_rk"][0].reshape(-1)[sl]
        sm[:, 46 + p] = inp["rw_gn_g"][0][sl]
        sm[:, 48 + p] = inp["rw_gn_b"][0][sl]
    w2 = np.ascontiguousarray(inp["rw_w2"][0][:, :, hc].reshape(128, 256))
    a2 = np.ascontiguousarray(inp["rw_a2"][0][:, :, hc].reshape(128, 256))
    return {"xa": np.ascontiguousarray(xa), "ct": ct, "adaw": adaw, "smalls": sm, "win": win,
            "w2": w2, "a2": a2, "cst": l0_consts()}


SUBLN_EPS = 1e-5
LAM_INIT = 0.8 - 0.6 * float(np.exp(-0.3 * 1))
QSCALE = 0.125
NKT = TA // 128
NQB = TL // 256


def tok0(bi):
    return 0 if bi < 0 else TC + 256 * bi


def mod_compute(P, adaw_d, ncol, sct, adab_view, ps, modT, adaw_sb):
    adaw_r = adaw_d.t[:].rearrange("(k p) m -> p k m", p=128)
    for piece in range(ncol // 256):
        P.dma(adaw_sb.v, View(adaw_d, adaw_r[:, :, piece * 256:(piece + 1) * 256]))
        for mcl in range(2):
            mc = piece * 2 + mcl
            for kc in range(8):
                P.matmul(ps[:, 0:2], adaw_sb[:, kc, mcl * 128:(mcl + 1) * 128], sct[:, kc * 2:kc * 2 + 2],
                         start=(kc == 0), stop=(kc == 7))
            P.ts("dve", modT[:, mc, :], ps[:, 0:2], adab_view(mc), None, ALU.add)


def rope_tables():
    rows = TL // 64
    t = np.arange(TL)
    row = (t // 64).astype(np.float32)
    colid = (t % 64).astype(np.float32)
    inv = (10000.0 ** (-np.arange(16, dtype=np.float32) / 16)).astype(np.float32)
    ang_r = row[None, :] * inv[:, None]
    ang_c = colid[None, :] * inv[:, None]
    cos64 = np.concatenate([np.cos(ang_r), np.cos(ang_r), np.cos(ang_c), np.cos(ang_c)], axis=0)
    sin64 = np.concatenate([np.sin(ang_r), np.sin(ang_r), np.sin(ang_c), np.sin(ang_c)], axis=0)
    cosT = np.concatenate([cos64, cos64], axis=0).astype(np.float32)
    sinT = np.concatenate([sin64, sin64], axis=0).astype(np.float32)
    R = np.zeros((128, 128), np.float32)
    for base in range(0, 128, 32):
        for f in range(16):
            R[base + 16 + f, base + f] = -1.0
            R[base + f, base + 16 + f] = 1.0
    return cosT, sinT, R


def build_l1(stop=None, ctx=None):
    if ctx is None:
        nc = bass.Bass("TRN2", target_bir_lowering=False)
        P = Prog(nc)
        xa_d = P.dram_in("xa", [D, TA], F32)
        y0_d = P.dram_in("y0g", [D, TA], BF16)
        ct_d = P.dram_in("ct", [128, 16], F32)
        adaw0_d = P.dram_in("adaw0g", [D, 1024], F32)
        adaw1_d = P.dram_in("adaw1", [D, 2048], F32)
        sm_d = P.dram_in("smalls", [128, 48], F32)
        wo_d = P.dram_in("wo", [D, D], F32)
        wd_d = P.dram_in("wd", [D, 1024], F32)
        cst_d = P.dram_in("cst", [128, 384], F32)
        cos_d = P.dram_in("cosT", [128, TL], F32)
        sin_d = P.dram_in("sinT", [128, TL], F32)
        lam_d = P.dram_in("lamv", [128, 256], F32)
        subw_d = P.dram_in("subw", [128, 128], F32)
        y1_d = P.dram_out("y1", [256, TL], BF16)
        xn_d = P.dram_out("xn", [D, TL], F32)
    else:
        nc, P = ctx["nc"], ctx["P"]
        (xa_d, y0_d, ct_d, adaw0_d, adaw1_d, sm_d, wo_d, wd_d, cst_d, cos_d, sin_d, lam_d, subw_d, y1_d, xn_d) = [
            ctx[k] for k in ("b_xa", "y0g", "ct", "b_adaw0g", "b_adaw1", "b_smalls", "b_wo", "b_wd", "b_cst",
                             "b_cosT", "b_sinT", "b_lamv", "b_subw", "y1loc", "xn_s")]

    cst = P.sb("cst", [128, 384], F32)
    P.dma(cst.v, cst_d.v)
    ident = cst[:, 0:128]
    bones = cst[:, 128:256]
    rrot = cst[:, 256:384]
    sm = P.sb("sm", [128, 48], F32)
    P.dma(sm.v, sm_d.v)
    S_NG, S_B0, S_B1, S_QN, S_KN = 0, 8, 16, 32, 33
    ones128 = P.sb("ones128", [128, 128], F32)
    P.memset("pool", ones128.v, 1.0)
    epsb = P.sb("epsb", [128, 4], F32)
    P.memset("pool", epsb[:, 0:1], RMS_EPS)
    P.memset("pool", epsb[:, 1:2], SUBLN_EPS)
    banks = [P.ps("bank%d" % i, [128, 512]) for i in range(8)]

    ct = P.sb("ct", [128, 16], F32)
    P.dma(ct.v, ct_d.v)
    sct = P.sb("sct", [128, 16], F32)
    P.act(sct.v, ct.v, AF.Silu)
    adaw_sb = P.sb("adaw_sb", [128, 8, 256], F32)
    mod0 = P.sb("mod0", [128, 8, 2], F32)
    mod1 = P.sb("mod1", [128, 16, 2], F32)
    mod_compute(P, adaw0_d, 1024, sct, lambda mc: sm[:, S_B0 + mc:S_B0 + mc + 1], banks[0], mod0, adaw_sb)
    mod_compute(P, adaw1_d, 2048, sct, lambda mc: sm[:, S_B1 + mc:S_B1 + mc + 1], banks[0], mod1, adaw_sb)
    gmod = P.sb("gmod", [128, 8, 2], F32)
    for kc in range(8):
        P.ts("pool", gmod[:, kc, :], mod1[:, 8 + kc, :], 1.0, sm[:, S_NG + kc:S_NG + kc + 1], ALU.add, ALU.mult)

    lamv = P.sb("lamv", [128, 256], F32)
    P.dma(lamv.v, lam_d.v)
    lt = P.sb("lt", [128, 128], F32)
    lsc = P.sb("lsc", [128, 8], F32)
    P.tt("pool", lt[:, 0:64], lamv[:, 0:64], lamv[:, 64:128], ALU.mult)
    P.tt("pool", lt[:, 64:128], lamv[:, 128:192], lamv[:, 192:256], ALU.mult)
    P.reduce(lsc[:, 0:1], lt[:, 0:64], AX.X, ALU.add)
    P.reduce(lsc[:, 1:2], lt[:, 64:128], AX.X, ALU.add)
    P.act(lsc[:, 2:4], lsc[:, 0:2], AF.Exp)
    P.tt("pool", lsc[:, 4:5], lsc[:, 2:3], lsc[:, 3:4], ALU.subtract)
    P.ts("pool", lsc[:, 5:6], lsc[:, 4:5], LAM_INIT, -1.0, ALU.add, ALU.mult)
    neglam = lsc[:, 5:6]
    subw = P.sb("subw", [128, 128], F32)
    P.dma(subw.v, subw_d.v)
    P.ts("pool", subw.v, subw.v, 1.0 - LAM_INIT, None, ALU.mult)

    Wo = P.sb("Wo", [128, 8, 1024], BF16)
    Wd = P.sb("Wd", [128, 8, 1024], BF16)
    wst = [P.sb("wst%d" % i, [128, 1024], F32) for i in range(1)] * 2
    n = 0
    for src, dst in ((wo_d, Wo), (wd_d, Wd)):
        for kc in range(8):
            P.dma(wst[n % 2].v, src[kc * 128:(kc + 1) * 128, :])
            P.copy("pool", dst[:, kc, :], wst[n % 2].v)
            n += 1

    xin = [P.sb("xin%d" % i, [128, 8, W], F32) for i in range(2)]
    yin = [P.sb("yin%d" % i, [128, 8, W], BF16) for i in range(2)]
    xn = P.sb("xn", [128, 8, W], F32)
    hT = P.sb("hT", [128, 8, W], BF16)
    sqb = [P.sb("sqb%d" % i, [128, W], F32) for i in range(2)]
    rstd = P.sb("rstd", [128, W], F32)
    htmp = [P.sb("htmp%d" % i, [128, W], F32) for i in range(2)]
    cosb = P.sb("cosb", [128, W], F32)
    sinb = P.sb("sinb", [128, W], F32)
    qraw = P.sb("qraw", [128, W], F32)
    qsq = P.sb("qsq", [128, W], F32)
    qrs = P.sb("qrs", [128, W], F32)
    qn_ = P.sb("qn_", [128, W], F32)
    qt1 = P.sb("qt1", [128, W], F32)
    qt2 = P.sb("qt2", [128, W], F32)
    KTb = P.sb("KTb", [128, TA], BF16)
    QTb = P.sb("QTb", [128, NQB * 512], BF16)
    P.memset("pool", QTb.v, 0.0)
    Vx = P.sb("Vx", [128, NKT, 130], BF16)
    P.memset("pool", Vx[:, :, 129:130], 0.0)
    Gs = P.sb("Gs", [128, TL // 128, 128], BF16)
    P.memset("pool", Vx[:, :, 128:129], 1.0)
    pT = [P.sb("pT%d" % i, [128, 512], BF16) for i in range(2)]
    vg_sb = [P.sb("vg_sb%d" % i, [128, 512], F32) for i in range(2)]
    o_sb = P.sb("o_sb", [128, 128], F32)
    o_sq = P.sb("o_sq", [128, 128], F32)
    ybs = [P.sb("yb%d" % i, [128, 256], BF16) for i in range(2)]
    zs = P.sb("zs", [128, 8], F32)
    ps_bf = P.ps("ps_bf_unused", [128, 2], F32) if False else None

    xa_r = xa_d.t[:].rearrange("(k p) c -> p k c", p=128)
    y0_r = None if isinstance(y0_d, list) else y0_d.t[:].rearrange("(k p) c -> p k c", p=128)
    xn_r = xn_d.t[:].rearrange("(k p) c -> p k c", p=128)

    def issue(bi, slot):
        t0 = tok0(bi)
        P.dma(xin[slot].v, View(xa_d, xa_r[:, :, t0:t0 + W]))
        if isinstance(y0_d, list):
            pi, off = (0, t0) if t0 < TC else (1 + (t0 - TC) // 2048, (t0 - TC) % 2048)
            yr = y0_d[pi].t[:].rearrange("(k p) c -> p k c", p=128)
            P.dma(yin[slot].v, View(y0_d[pi], yr[:, :, off:off + W]))
        else:
            P.dma(yin[slot].v, View(y0_d, y0_r[:, :, t0:t0 + W]))

    def stage_a(bi, slot, h, hp_quarter, first_pass):
        col = 1 if bi < 0 else 0
        t0 = tok0(bi)
        lat0 = t0 - TC
        for oc in range(8):
            pp = banks[oc % 2]
            for kc in range(8):
                P.matmul(pp[:, 0:W], Wo[:, kc, oc * 128:(oc + 1) * 128], yin[slot][:, kc, :],
                         start=(kc == 0), stop=(kc == 7))
            P.stt(xn[:, oc, :], pp[:, 0:W], mod0[:, oc, col:col + 1], xin[slot][:, oc, :], ALU.mult, ALU.add)
        if first_pass and bi >= 0 and stop != 'noxn':
            P.dma(View(xn_d, xn_r[:, :, lat0:lat0 + W]), xn.v)
        if stop == 'a1':
            return
        for kc in range(8):
            sq = sqb[kc % 2]
            P.act(sq.v, xn[:, kc, :], AF.Square)
            P.matmul(banks[2][:, 0:W], ones128.v, sq.v, start=(kc == 0), stop=(kc == 7))
        P.act(rstd.v, banks[2][:, 0:W], AF.Sqrt, scale=1.0 / D, bias=epsb[:, 0:1])
        P.recip(rstd.v, rstd.v)
        for kc in range(8):
            tmp = htmp[kc % 2]
            P.stt(tmp.v, xn[:, kc, :], gmod[:, kc, col:col + 1], rstd.v, ALU.mult, ALU.mult)
            P.act(hT[:, kc, :], tmp.v, AF.Identity, bias=mod1[:, kc, col:col + 1])
        if stop == 'a2':
            return
        if bi >= 0:
            P.dma(cosb.v, cos_d[:, lat0:lat0 + W])
            P.dma(sinb.v, sin_d[:, lat0:lat0 + W])
        for which in (("q", "k") if bi >= 0 else ("k",)):
            cc = h if which == "q" else 2 + h
            pp = banks[3]
            for kc in range(8):
                P.matmul(pp[:, 0:W], Wd[:, kc, cc * 128:(cc + 1) * 128], hT[:, kc, :],
                         start=(kc == 0), stop=(kc == 7))
            P.copy("act", qraw.v, pp[:, 0:W])
            P.tt("pool", qsq.v, qraw.v, qraw.v, ALU.mult)
            P.matmul(banks[4][:, 0:W], bones, qsq.v)
            P.act(qrs.v, banks[4][:, 0:W], AF.Sqrt, scale=1.0 / 64, bias=epsb[:, 0:1])
            P.recip(qrs.v, qrs.v)
            wcol = sm[:, S_QN:S_QN + 1] if which == "q" else sm[:, S_KN:S_KN + 1]
            P.stt(qn_.v, qraw.v, wcol, qrs.v, ALU.mult, ALU.mult)
            if bi >= 0:
                P.matmul(banks[4][:, W:2 * W], rrot, qn_.v)
                P.tt("dve", qt1.v, banks[4][:, W:2 * W], sinb.v, ALU.mult)
                P.tt("pool", qt2.v, qn_.v, cosb.v, ALU.mult)
                if which == "q":
                    qb_ = lat0 // 256
                    P.tt("pool", QTb[0:64, qb_ * 512:qb_ * 512 + 256], qt1[0:64, :], qt2[0:64, :], ALU.add)
                    P.tt("pool", QTb[64:128, qb_ * 512 + 256:qb_ * 512 + 512], qt1[64:128, :], qt2[64:128, :], ALU.add)
                else:
                    P.tt("pool", KTb[:, t0:t0 + W], qt1.v, qt2.v, ALU.add)
            else:
                P.copy("pool", KTb[:, t0:t0 + W], qn_.v)
        if stop == 'a3':
            return
        for sub in range(2):
            pp = banks[5 + sub]
            for kc in range(8):
                P.matmul(pp[:, 0:512], hT[:, kc, sub * 128:(sub + 1) * 128], Wd[:, kc, 512:1024],
                         start=(kc == 0), stop=(kc == 7))
            kt = (t0 + sub * 128) // 128
            vg = vg_sb[sub]
            P.copy("dve", vg.v, pp[:, 0:512])
            P.copy("pool", Vx[:, kt, 0:128], vg[:, h * 128:(h + 1) * 128])
            if bi >= 0:
                qt = (lat0 + sub * 128) // 128
                P.act(Gs[:, qt, :], vg[:, 256 + h * 128:256 + (h + 1) * 128], AF.Silu)

    def attention(h, nqb=NQB):
        sbank = [(banks[0], banks[1]), (banks[2], banks[3])]
        acc = [[banks[4], banks[5]], [banks[6], banks[7]]]
        n = 0

        def s_mm(qb_, kt_, n_):
            P.matmul(banks[n_ % 4][:, 0:512], KTb[:, kt_ * 128:(kt_ + 1) * 128], QTb[:, qb_ * 512:(qb_ + 1) * 512])

        s_mm(0, 0, 0)
        for qb in range(nqb):
            q0 = qb * 256
            for kt in range(NKT):
                sA = banks[n % 4]
                pt = pT[n % 2]
                if kt + 1 < NKT:
                    s_mm(qb, kt + 1, n + 1)
                elif qb + 1 < nqb:
                    s_mm(qb + 1, 0, n + 1)
                P.act(pt.v, sA[:, 0:512], AF.Exp, scale=QSCALE)
                if stop == 's1':
                    n += 1
                    continue
                for comp in range(2):
                    for qs in range(2):
                        P.matmul(acc[comp][qs][:, 0:130], pt[:, comp * 256 + qs * 128:comp * 256 + (qs + 1) * 128],
                                 Vx[:, kt, :], start=(kt == 0), stop=(kt == NKT - 1))
                n += 1
            if stop in ('s1', 's2'):
                continue
            ytile = ybs[qb % 2]
            for qs in range(2):
                a0 = acc[0][qs]
                a1 = acc[1][qs]
                P.recip(zs[:, 0:1], a0[:, 128:129])
                P.recip(zs[:, 1:2], a1[:, 128:129])
                P.tt("pool", zs[:, 2:3], zs[:, 1:2], neglam, ALU.mult)
                P.ts("dve", o_sb.v, a0[:, 0:128], zs[:, 0:1], None, ALU.mult)
                P.stt(o_sb.v, a1[:, 0:128], zs[:, 2:3], o_sb.v, ALU.mult, ALU.add)
                P.tt("pool", o_sq.v, o_sb.v, o_sb.v, ALU.mult)
                P.reduce(zs[:, 3:4], o_sq.v, AX.X, ALU.add)
                P.act(zs[:, 4:5], zs[:, 3:4], AF.Sqrt, scale=1.0 / 128, bias=epsb[:, 1:2])
                P.recip(zs[:, 4:5], zs[:, 4:5])
                P.stt(o_sb.v, o_sb.v, zs[:, 4:5], subw.v, ALU.mult, ALU.mult)
                qt = (q0 + qs * 128) // 128
                P.tt("pool", o_sq.v, o_sb.v, Gs[:, qt, :], ALU.mult)
                P.transpose(a0[:, 256:384], o_sq.v, ident)
                P.copy("act", ytile[:, qs * 128:(qs + 1) * 128], a0[:, 256:384])
            if isinstance(y1_d, list):
                P.dma(y1_d[q0 // 2048][h * 128:(h + 1) * 128, q0 % 2048:q0 % 2048 + 256], ytile.v)
            else:
                P.dma(y1_d[h * 128:(h + 1) * 128, q0:q0 + 256], ytile.v)

    return nc, P, issue, stage_a, attention


def build_l1_full(hp_quarter, do_attn=True, nheads=2, stop=None, nblocks=None, nqb=NQB, ctx=None):
    nc, P, issue, stage_a, attention = build_l1(stop, ctx)
    blocks = [-1] + list(range(NBLK_L))
    if nblocks:
        blocks = blocks[:nblocks]
    if stop == 'pre':
        P.emit()
        return nc
    for h in range(nheads):
        issue(blocks[0], 0)
        for i, bi in enumerate(blocks):
            if i + 1 < len(blocks):
                issue(blocks[i + 1], (i + 1) % 2)
            stage_a(bi, i % 2, h, hp_quarter, h == 0)
        if do_attn:
            attention(h, nqb)
    if ctx is None:
        P.emit()
    return nc


def l1_inputs(inp, b, hp, y0g_b):
    f32 = np.float32
    xa = np.ascontiguousarray(np.concatenate([inp["ctx"][b].T, inp["x"][b].T], axis=1))
    ct = np.zeros((128, 16), f32)
    ct[:, 0::2] = inp["c"][b].reshape(8, 128).T
    ct[:, 1::2] = inp["c_ctx"].reshape(8, 128).T
    sm = np.zeros((128, 48), f32)
    sm[:, 0:8] = inp["norm_g"][1].reshape(8, 128).T
    sm[:, 8:16] = inp["ada_b"][0][2048:3072].reshape(8, 128).T
    sm[:, 16:32] = inp["ada_b"][1][0:2048].reshape(16, 128).T
    sm[:, 32] = np.tile(inp["da_qn"][0], 2)
    sm[:, 33] = np.tile(inp["da_kn"][0], 2)
    da_in = inp["da_in"][0]
    cols = []
    for X in range(4):
        for hh in range(2):
            base = X * 1024 + (hp * 2 + hh) * 128
            cols.append(da_in[:, base:base + 128])
    wd = np.ascontiguousarray(np.concatenate(cols, axis=1))
    cosT, sinT, R = rope_tables()
    ident = np.eye(128, dtype=f32)
    bones = np.kron(np.eye(2, dtype=f32), np.ones((64, 64), f32))
    cst = np.ascontiguousarray(np.concatenate([ident, bones, R], axis=1))
    lamv = np.ascontiguousarray(np.broadcast_to(inp["da_lam"][0].reshape(1, 256), (128, 256))).astype(f32)
    subw = np.ascontiguousarray(np.broadcast_to(inp["da_subln"][0].reshape(1, 128), (128, 128))).astype(f32)
    return {"xa": xa, "y0g": y0g_b, "ct": ct,
            "adaw0g": np.ascontiguousarray(inp["ada_w"][0][:, 2048:3072]),
            "adaw1": np.ascontiguousarray(inp["ada_w"][1][:, 0:2048]),
            "smalls": sm, "wo": np.ascontiguousarray(inp["rw_out"][0]), "wd": wd, "cst": cst,
            "cosT": cosT, "sinT": sinT, "lamv": lamv, "subw": subw}


def build_l2(ctx=None):
    if ctx is None:
        nc = bass.Bass("TRN2", target_bir_lowering=False)
        P = Prog(nc)
        NT = 2048
        xn_d = P.dram_in("xn", [D, NT], F32)
        y1_d = P.dram_in("y1T", [D, NT], BF16)
        ct_d = P.dram_in("ct", [128, 16], F32)
        adaw_d = P.dram_in("adaw1g", [D, 1024], F32)
        sm_d = P.dram_in("smalls", [128, 8], F32)
        w_d = P.dram_in("wda", [D, D], F32)
        out_d = P.dram_out("outT", [D, NT], F32)
    else:
        nc, P = ctx["nc"], ctx["P"]
        NT = TL
        xn_d, y1_d, ct_d, adaw_d, sm_d, w_d, out_d = [ctx[k] for k in (
            "xn_s", "y1g", "ct", "c_adaw1g", "c_smalls", "c_wda", "outT")]
    sm = P.sb("sm", [128, 8], F32)
    P.dma(sm.v, sm_d.v)
    banks = [P.ps("bank%d" % i, [128, 512]) for i in range(3)]
    ct = P.sb("ct", [128, 16], F32)
    P.dma(ct.v, ct_d.v)
    sct = P.sb("sct", [128, 16], F32)
    P.act(sct.v, ct.v, AF.Silu)
    adaw_sb = P.sb("adaw_sb", [128, 8, 256], F32)
    modg = P.sb("modg", [128, 8, 2], F32)
    mod_compute(P, adaw_d, 1024, sct, lambda mc: sm[:, mc:mc + 1], banks[0], modg, adaw_sb)
    Wa = P.sb("Wa", [128, 8, 1024], BF16)
    wst = [P.sb("wst%d" % i, [128, 1024], F32) for i in range(2)]
    for kc in range(8):
        P.dma(wst[kc % 2].v, w_d[kc * 128:(kc + 1) * 128, :])
        P.copy("pool", Wa[:, kc, :], wst[kc % 2].v)
    xin = [P.sb("xin%d" % i, [128, 8, W], F32) for i in range(2)]
    yin = [P.sb("yin%d" % i, [128, 8, W], BF16) for i in range(2)]
    ob = [P.sb("ob%d" % i, [128, 8, W], F32) for i in range(2)]
    xn_r = xn_d.t[:].rearrange("(k p) c -> p k c", p=128)
    y1_r = None if isinstance(y1_d, list) else y1_d.t[:].rearrange("(k p) c -> p k c", p=128)
    out_r = out_d.t[:].rearrange("(k p) c -> p k c", p=128)
    for bi in range(NT // W):
        s_ = bi % 2
        P.dma(xin[s_].v, View(xn_d, xn_r[:, :, bi * W:(bi + 1) * W]))
        if isinstance(y1_d, list):
            c0 = bi * W
            yr = y1_d[c0 // 2048].t[:].rearrange("(k p) c -> p k c", p=128)
            P.dma(yin[s_].v, View(y1_d[c0 // 2048], yr[:, :, c0 % 2048:c0 % 2048 + W]))
        else:
            P.dma(yin[s_].v, View(y1_d, y1_r[:, :, bi * W:(bi + 1) * W]))
        for oc in range(8):
            pp = banks[1 + oc % 2]
            for kc in range(8):
                P.matmul(pp[:, 0:W], Wa[:, kc, oc * 128:(oc + 1) * 128], yin[s_][:, kc, :],
                         start=(kc == 0), stop=(kc == 7))
            P.stt(ob[s_][:, oc, :], pp[:, 0:W], modg[:, oc, 0:1], xin[s_][:, oc, :], ALU.mult, ALU.add)
        P.dma(View(out_d, out_r[:, :, bi * W:(bi + 1) * W]), ob[s_].v)
    if ctx is None:
        P.emit()
    return nc


def l2_inputs(inp, b, tq, xn, y1T):
    f32 = np.float32
    ct = np.zeros((128, 16), f32)
    ct[:, 0::2] = inp["c"][b].reshape(8, 128).T
    ct[:, 1::2] = inp["c_ctx"].reshape(8, 128).T
    sm = np.ascontiguousarray(inp["ada_b"][1][2048:3072].reshape(8, 128).T).astype(f32)
    return {"xn": xn, "y1T": y1T, "ct": ct, "adaw1g": np.ascontiguousarray(inp["ada_w"][1][:, 2048:3072]),
            "smalls": sm, "wda": np.ascontiguousarray(inp["da_out"][0])}


GROUPS = [[0, 1, 2, 3], [4, 5, 6, 7]]


def build_fused(upto=None):
    nc = bass.Bass("TRN2", target_bir_lowering=False)
    P = Prog(nc)
    ctx = {"nc": nc, "P": P}
    decl = [("a_xa", [D, XA_COLS], F32), ("ct", [128, 16], F32), ("a_adaw", [D, 2048], F32),
            ("a_smalls", [128, NSMALL], F32), ("a_win", [D, 1280], F32), ("a_w2", [128, 256], F32),
            ("a_a2", [128, 256], F32), ("a_cst", [128, C_TOT], F32),
            ("b_xa", [D, TA], F32), ("b_adaw0g", [D, 1024], F32), ("b_adaw1", [D, 2048], F32),
            ("b_smalls", [128, 48], F32), ("b_wo", [D, D], F32), ("b_wd", [D, 1024], F32),
            ("b_cst", [128, 384], F32), ("b_cosT", [128, TL], F32), ("b_sinT", [128, TL], F32),
            ("b_lamv", [128, 256], F32), ("b_subw", [128, 128], F32),
            ("c_adaw1g", [D, 1024], F32), ("c_smalls", [128, 8], F32), ("c_wda", [D, D], F32)]
    for name, shape, dt_ in decl:
        if upto == "A" and name[0] in "bc" and name[1] == "_":
            continue
        if upto == "B" and name[0] == "c" and name[1] == "_":
            continue
        ctx[name] = P.dram_in(name, shape, dt_)
    ctx["outT"] = P.dram_out("outT", [D, TL], F32)
    pw0 = [TC, 2048, 2048, 2048, 2048]
    ctx["y0loc"] = [P.dram_tmp("y0loc%d" % i, [256, w], BF16) for i, w in enumerate(pw0)]
    ctx["y0g"] = [P.dram_tmp("y0g%d" % i, [D, w], BF16) for i, w in enumerate(pw0)]
    ctx["of_scratch"] = P.dram_tmp("of_scratch", [256, TA], F32)
    ctx["xn_s"] = P.dram_tmp("xn_s", [D, TL], F32)
    ctx["y1loc"] = [P.dram_tmp("y1loc%d" % i, [256, 2048], BF16) for i in range(4)]
    ctx["y1g"] = [P.dram_tmp("y1g%d" % i, [D, 2048], BF16) for i in range(4)]
    build_l0(ctx=ctx)
    P.emit()
    P.end_phase()
    for i in range(5):
        P.collective("AllGather", ctx["y0g"][i].v, ctx["y0loc"][i].v, GROUPS)
    if upto == "A":
        tb = P.sb("dbg_b", [128, 8, 512], BF16)
        tf = P.sb("dbg_f", [128, 8, 512], F32)
        orr = ctx["outT"].t[:].rearrange("(k p) c -> p k c", p=128)
        for i in range(4):
            pi, off = ((1, 0), (1, 512), (2, 1536), (4, 1536))[i]
            yr = ctx["y0g"][pi].t[:].rearrange("(k p) c -> p k c", p=128)
            P.dma(tb.v, View(ctx["y0g"][pi], yr[:, :, off:off + 512]))
            P.copy("dve", tf.v, tb.v)
            P.dma(View(ctx["outT"], orr[:, :, i * 512:(i + 1) * 512]), tf.v)
        P.emit()
        P.end_phase()
        return nc
    build_l1_full(0, ctx=ctx)
    P.emit()
    P.end_phase()
    for i in range(4):
        P.collective("AllGather", ctx["y1g"][i].v, ctx["y1loc"][i].v, GROUPS)
    if upto == "B":
        tb = P.sb("dbg_b", [128, 8, 512], BF16)
        tf = P.sb("dbg_f", [128, 8, 512], F32)
        xr = ctx["xn_s"].t[:].rearrange("(k p) c -> p k c", p=128)
        orr = ctx["outT"].t[:].rearrange("(k p) c -> p k c", p=128)
        for i in range(2):
            c0 = (0, TL - 512)[i]
            yr = ctx["y1g"][c0 // 2048].t[:].rearrange("(k p) c -> p k c", p=128)
            P.dma(tb.v, View(ctx["y1g"][c0 // 2048], yr[:, :, c0 % 2048:c0 % 2048 + 512]))
            P.copy("dve", tf.v, tb.v)
            P.dma(View(ctx["outT"], orr[:, :, i * 512:(i + 1) * 512]), tf.v)
            P.dma(tf.v, View(ctx["xn_s"], xr[:, :, c0:c0 + 512]))
            P.dma(View(ctx["outT"], orr[:, :, (2 + i) * 512:(3 + i) * 512]), tf.v)
        P.emit()
        P.end_phase()
        return nc
    build_l2(ctx=ctx)
    P.emit()
    P.end_phase()
    return nc


def fused_inputs(inp, b, g):
    m = {}
    a = l0_inputs(inp, b, g)
    for k in ("xa", "adaw", "smalls", "win", "w2", "a2", "cst"):
        m["a_" + k] = a[k]
    m["ct"] = a["ct"]
    bb = l1_inputs(inp, b, g, None)
    for k in ("xa", "adaw0g", "adaw1", "smalls", "wo", "wd", "cst", "cosT", "sinT", "lamv", "subw"):
        m["b_" + k] = bb[k]
    c = l2_inputs(inp, b, g, None, None)
    for k in ("adaw1g", "smalls", "wda"):
        m["c_" + k] = c[k]
    return m


def kernel(**inp):
    inp = {k: np.asarray(v) for k, v in inp.items()}
    cores = list(range(8))
    nc = build_fused()
    maps = [fused_inputs(inp, c // 4, c % 4) for c in cores]
    res = run_bass_kernel_spmd(nc, maps, core_ids=cores).results
    out = np.zeros((2, TL, D), np.float32)
    for c in cores:
        b, tq = c // 4, c % 4
        out[b, tq * 2048:(tq + 1) * 2048] = np.asarray(res[c]["outT"])[:, tq * 2048:(tq + 1) * 2048].T
    return out
```

```python
import numpy as np
import concourse.bass as bass
import concourse.mybir as mybir
from concourse.bass_utils import run_bass_kernel_spmd

F32 = mybir.dt.float32
BF16 = mybir.dt.bfloat16
AF = mybir.ActivationFunctionType
ALU = mybir.AluOpType
AX = mybir.AxisListType

ENGS = ("pe", "act", "dve", "pool", "sp")


class Buf:
    def __init__(self, prog, t, name, space):
        self.prog = prog
        self.t = t
        self.name = name
        self.space = space
        self.last_writer = None
        self.readers = []
        self.dma_sem = None
        self.dma_count = 0

    def __getitem__(self, idx):
        return View(self, self.t[idx])

    @property
    def v(self):
        return View(self, self.t[:])


class View:
    def __init__(self, buf, ap):
        self.buf = buf
        self.ap = ap

    def __getitem__(self, idx):
        return View(self.buf, self.ap[idx])


class Op:
    __slots__ = ("eng", "fn", "deps", "needs_inc", "seq", "dma_buf", "dma_val", "idx", "dma_inc")

    def __init__(self, eng, fn):
        self.eng = eng
        self.fn = fn
        self.deps = []
        self.needs_inc = False
        self.seq = None
        self.dma_buf = None
        self.dma_val = None
        self.dma_inc = 16


def _ap(x):
    return x.ap if isinstance(x, View) else x


class Prog:
    def __init__(self, nc):
        import contextlib
        self.nc = nc
        self.ops = {e: [] for e in ENGS}
        self.bufs = []
        self.dram = {}
        self.same_engine_sync = True
        self.stack = contextlib.ExitStack()
        self.phase = 0
        self.phase_sem = None
        self.uid = 0

    def end_phase(self):
        import contextlib
        self.stack.close()
        self.stack = contextlib.ExitStack()
        self.bufs = []

    def sb(self, name, shape, dtype):
        self.uid += 1
        t = self.stack.enter_context(self.nc.sbuf_tensor("sb%d_%s" % (self.uid, name), list(shape), dtype))
        b = Buf(self, t, name, "sb")
        self.bufs.append(b)
        return b

    def ps(self, name, shape, dtype=F32):
        self.uid += 1
        t = self.stack.enter_context(self.nc.psum_tensor("pp%d_%s" % (self.uid, name), list(shape), dtype))
        b = Buf(self, t, name, "ps")
        self.bufs.append(b)
        return b

    def dram_in(self, name, shape, dtype):
        t = self.nc.dram_tensor(name, list(shape), dtype, kind="ExternalInput")
        b = Buf(self, t, name, "dram")
        self.dram[name] = b
        return b

    def dram_out(self, name, shape, dtype):
        t = self.nc.dram_tensor(name, list(shape), dtype, kind="ExternalOutput")
        b = Buf(self, t, name, "dram")
        self.dram[name] = b
        return b

    def dram_tmp(self, name, shape, dtype, shared=False):
        if shared:
            t = self.nc.dram_tensor(name, list(shape), dtype, addr_space="Shared")
        else:
            t = self.nc.dram_tensor(name, list(shape), dtype)
        b = Buf(self, t, name, "dram")
        self.dram[name] = b
        return b

    def _record(self, eng, fn, reads, writes):
        op = Op(eng, fn)
        deps = []
        for v in reads:
            b = v.buf if isinstance(v, View) else v
            if b.last_writer is not None:
                deps.append(b.last_writer)
            if b.space == "ps":
                deps.extend(r for r in b.readers if r.eng != eng)
        for v in writes:
            b = v.buf if isinstance(v, View) else v
            if b.last_writer is not None:
                deps.append(b.last_writer)
            deps.extend(b.readers)
        seen = set()
        for d in deps:
            if id(d) in seen or d is op:
                continue
            seen.add(id(d))
            if d.dma_buf is None and d.eng == eng and (eng == "pe" or not self.same_engine_sync):
                continue
            op.deps.append(d)
            if d.dma_buf is None:
                d.needs_inc = True
        for v in writes:
            b = v.buf if isinstance(v, View) else v
            b.last_writer = op
            b.readers = []
        for v in reads:
            b = v.buf if isinstance(v, View) else v
            if b.last_writer is not op:
                b.readers.append(op)
        self.ops[eng].append(op)
        return op

    def op(self, eng, fn, reads=(), writes=()):
        return self._record(eng, fn, list(reads), list(writes))

    def matmul(self, out, lhsT, rhs, start=True, stop=True, extra_reads=(), **kw):
        o, l, r = _ap(out), _ap(lhsT), _ap(rhs)
        return self._record("pe", lambda e: e.matmul(o, l, r, start=start, stop=stop, **kw),
                            [lhsT, rhs] + list(extra_reads), [out])

    def transpose(self, out, in_, ident):
        o, i, d = _ap(out), _ap(in_), _ap(ident)
        return self._record("pe", lambda e: e.transpose(o, i, d), [in_, ident], [out])

    def act(self, out, in_, func, bias=None, scale=None, eng="act", accum_out=None):
        o, i = _ap(out), _ap(in_)
        kw = {}
        reads = [in_]
        writes = [out]
        if bias is not None:
            kw["bias"] = _ap(bias)
            if isinstance(bias, View):
                reads.append(bias)
        if scale is not None:
            kw["scale"] = _ap(scale)
            if isinstance(scale, View):
                reads.append(scale)
        if accum_out is not None:
            kw["accum_out"] = _ap(accum_out)
            writes.append(accum_out)
        return self._record("act", lambda e: e.activation(o, i, func, **kw), reads, writes)

    def tt(self, eng, out, in0, in1, op):
        o, a, b = _ap(out), _ap(in0), _ap(in1)
        return self._record(eng, lambda e: e.tensor_tensor(o, a, b, op), [in0, in1], [out])

    def ts(self, eng, out, in0, s1, s2, op0, op1=None, accum_out=None):
        o, a = _ap(out), _ap(in0)
        reads = [in0]
        writes = [out]
        for s in (s1, s2):
            if isinstance(s, View):
                reads.append(s)
        x1, x2 = _ap(s1), _ap(s2)
        kw = {}
        if op1 is not None:
            kw["op1"] = op1
        if accum_out is not None:
            kw["accum_out"] = _ap(accum_out)
            writes.append(accum_out)
        return self._record(eng, lambda e: e.tensor_scalar(o, a, x1, x2, op0, **kw), reads, writes)

    def stt(self, out, in0, scalar, in1, op0, op1, eng="dve"):
        o, a, b = _ap(out), _ap(in0), _ap(in1)
        reads = [in0, in1]
        if isinstance(scalar, View):
            reads.append(scalar)
        s = _ap(scalar)
        return self._record(eng, lambda e: e.scalar_tensor_tensor(o, a, s, b, op0, op1), reads, [out])

    def copy(self, eng, out, in_):
        o, i = _ap(out), _ap(in_)
        if eng == "act":
            return self._record(eng, lambda e: e.copy(o, i), [in_], [out])
        return self._record(eng, lambda e: e.tensor_copy(o, i), [in_], [out])

    def scan(self, out, d0, d1, initial, op0, op1):
        o, a, b = _ap(out), _ap(d0), _ap(d1)
        reads = [d0, d1]
        if isinstance(initial, View):
            reads.append(initial)
        ini = _ap(initial)
        return self._record("dve", lambda e: e.tensor_tensor_scan(o, a, b, ini, op0, op1), reads, [out])

    def recip(self, out, in_):
        o, i = _ap(out), _ap(in_)
        return self._record("dve", lambda e: e.reciprocal(o, i), [in_], [out])

    def memset(self, eng, out, val):
        o = _ap(out)
        return self._record(eng, lambda e: e.memset(o, val), [], [out])

    def reduce(self, out, in_, axis, op, eng="dve"):
        o, i = _ap(out), _ap(in_)
        return self._record(eng, lambda e: e.tensor_reduce(o, i, axis, op), [in_], [out])

    def dma(self, out, in_, queue="sp", **kw):
        o, i = _ap(out), _ap(in_)
        ob = out.buf
        ib = in_.buf
        key = ob if ob.space != "dram" else ib
        op = self._record(queue, lambda e: e.dma_start(out=o, in_=i, **kw), [in_], [out])
        key.dma_count += 1
        op.dma_buf = key
        op.dma_val = 16 * key.dma_count
        return op

    def emit(self, final_waits=()):
        nc = self.nc
        for e in ENGS:
            n = 0
            for op in self.ops[e]:
                if op.dma_buf is None and op.needs_inc:
                    n += 1
                    op.seq = n
        SEMCAP = 30000
        nsem = {e: 1 + max([op.seq or 0 for op in self.ops[e]] + [0]) // SEMCAP for e in ENGS}
        ph = self.phase
        sems = {e: [nc.alloc_semaphore("s%d_%s_%d" % (ph, e, i)) for i in range(nsem[e])] for e in ENGS}
        if self.phase_sem is None:
            self.phase_sem = nc.alloc_semaphore("phase_done")
        phase_sem = self.phase_sem
        dummy_sb = self.sb("phdummy", [128, 8], F32)

        for b in self.bufs + list(self.dram.values()):
            if b.dma_count > 0:
                b.dma_sem = nc.alloc_semaphore("d%d_%s" % (ph, b.name))
        engmap = {"pe": "tensor", "act": "scalar", "dve": "vector", "pool": "gpsimd", "sp": "sync"}
        all_dma = []
        for e in ENGS:
            for op in self.ops[e]:
                if op.dma_buf is not None:
                    all_dma.append(op)

        def gen(ename):
            def body(eng):
                waited = {}
                if ph > 0:
                    eng.wait_ge(phase_sem, 4 * ph)
                for op in self.ops[ename]:
                    need = {}
                    for d in op.deps:
                        if d.dma_buf is not None:
                            k = ("dma", id(d.dma_buf))
                            sem = d.dma_buf.dma_sem
                            val = d.dma_val
                        else:
                            si = (d.seq - 1) // SEMCAP
                            k = ("eng", d.eng, si)
                            sem = sems[d.eng][si]
                            val = d.seq - si * SEMCAP
                        if k not in need or need[k][1] < val:
                            need[k] = (sem, val)
                    for k, (sem, val) in need.items():
                        if waited.get(k, 0) >= val:
                            continue
                        eng.wait_ge(sem, val)
                        waited[k] = val
                    inst = op.fn(eng)
                    if op.dma_buf is not None:
                        if op.dma_inc == 16:
                            inst.then_inc(op.dma_buf.dma_sem, 16)
                        else:
                            inst.then_inc(op.dma_buf.dma_sem)
                    elif op.needs_inc:
                        inst.then_inc(sems[ename][(op.seq - 1) // SEMCAP], 1)
                if ename == "sp":
                    finals = {}
                    for op in all_dma:
                        b = op.dma_buf
                        finals[id(b)] = (b.dma_sem, op.dma_val if op.dma_inc != 16 else 16 * b.dma_count)
                    for sem, val in finals.values():
                        eng.wait_ge(sem, val)
                    eng.sem_inc(phase_sem, 1)
                elif ename == "act":
                    eng.copy(dummy_sb.t[:, 2:3], dummy_sb.t[:, 3:4]).then_inc(phase_sem, 1)
                elif ename == "dve":
                    eng.memset(dummy_sb.t[:, 4:5], 0.0).then_inc(phase_sem, 1)
                elif ename == "pool":
                    eng.memset(dummy_sb.t[:, 6:7], 0.0).then_inc(phase_sem, 1)
            return body

        with nc.Block() as block:
            block.tensor(gen("pe"))
            block.scalar(gen("act"))
            block.vector(gen("dve"))
            block.gpsimd(gen("pool"))
            block.sync(gen("sp"))
        self.phase += 1
        self.ops = {e: [] for e in ENGS}
        for b in self.bufs + list(self.dram.values()):
            b.last_writer = None
            b.readers = []
            b.dma_count = 0
            b.dma_sem = None


def _collective(self, kind, out, in_, groups, op=None):
    o, i = _ap(out), _ap(in_)
    alu = op if op is not None else ALU.bypass
    rec = self._record("pool", lambda e: e.collective_compute(kind, alu, replica_groups=groups, ins=[i], outs=[o]),
                       [in_], [out])
    key = out.buf
    key.dma_count += 1
    rec.dma_buf = key
    rec.dma_inc = 1
    rec.dma_val = key.dma_count
    return rec


Prog.collective = _collective


D = 1024
TC = 256
TL = 8192
TA = TC + TL
W = 256
WH = W + 2
NBLK_L = TL // W
XA_COLS = 1 + TC + 1 + 1 + TL + 1
NSMALL = 50
EXPM05 = 0.6065306597126334
RMS_EPS = 1e-6
GN_EPS = 64e-5


def l0_consts():
    ident = np.eye(128, dtype=np.float32)
    bones = np.kron(np.eye(2, dtype=np.float32), np.ones((64, 64), np.float32))
    idx = np.arange(128)
    same = (idx[:, None] // 64) == (idx[None, :] // 64)
    strict_f = (same & (idx[None, :] < idx[:, None])).astype(np.float32)
    incl_f = (same & (idx[None, :] <= idx[:, None])).astype(np.float32)
    strict_b = (same & (idx[None, :] > idx[:, None])).astype(np.float32)
    incl_b = (same & (idx[None, :] >= idx[:, None])).astype(np.float32)
    out = {}
    for nm, st, inc in (("f", strict_f, incl_f), ("b", strict_b, incl_b)):
        m1 = np.concatenate([st, st], axis=1)
        m2h = np.concatenate([st.T, inc.T], axis=1)
        m2 = np.concatenate([m2h, m2h], axis=1)
        out["m1" + nm] = np.ascontiguousarray(m1)
        out["m2" + nm] = np.ascontiguousarray(m2)
    ident2 = np.concatenate([np.eye(64, dtype=np.float32)] * 2, axis=0)
    cst = np.concatenate([ident, bones, out["m1f"], out["m2f"], out["m1b"], out["m2b"], ident2,
                          np.ones((128, 64), np.float32)], axis=1)
    return np.ascontiguousarray(cst)


C_ID, C_BO, C_M1F, C_M2F, C_M1B, C_M2B, C_ID2, C_ONE = 0, 128, 256, 512, 1024, 1280, 1792, 1856
C_TOT = 1920


F32R = mybir.dt.float32r


def RR(view):
    return View(view.buf, view.ap.bitcast(F32R))


def V2(view):
    return View(view.buf, view.ap.rearrange("p a b -> p (a b)"))


def build_l0(debug_out=False, stop=None, ctx=None):
    if ctx is None:
        nc = bass.Bass("TRN2", target_bir_lowering=False)
        P = Prog(nc)
        xa_d = P.dram_in("xa", [D, XA_COLS], F32)
        ct_d = P.dram_in("ct", [128, 16], F32)
        adaw_d = P.dram_in("adaw", [D, 2048], F32)
        sm_d = P.dram_in("smalls", [128, NSMALL], F32)
        win_d = P.dram_in("win", [D, 1280], F32)
        w2_d = P.dram_in("w2", [128, 256], F32)
        a2_d = P.dram_in("a2", [128, 256], F32)
        cst_d = P.dram_in("cst", [128, C_TOT], F32)
        y0_d = P.dram_out("y0", [256, TA], BF16)
        of_d = P.dram_tmp("of_scratch", [256, TA], F32)
    else:
        nc, P = ctx["nc"], ctx["P"]
        xa_d, ct_d, adaw_d, sm_d, win_d, w2_d, a2_d, cst_d, y0_d, of_d = [ctx[k] for k in (
            "a_xa", "ct", "a_adaw", "a_smalls", "a_win", "a_w2", "a_a2", "a_cst", "y0loc", "of_scratch")]

    cst = P.sb("cst", [128, C_TOT], F32)
    P.dma(cst.v, cst_d.v)
    ident = cst[:, C_ID:C_ID + 128]
    bones = cst[:, C_BO:C_BO + 128]
    ident2 = cst[:, C_ID2:C_ID2 + 64]
    ones64 = cst[:, C_ONE:C_ONE + 64]
    masks = {0: (cst[:, C_M1F:C_M1F + 256], cst[:, C_M2F:C_M2F + 512]),
             1: (cst[:, C_M1B:C_M1B + 256], cst[:, C_M2B:C_M2B + 512])}
    sm = P.sb("sm", [128, NSMALL], F32)
    P.dma(sm.v, sm_d.v)
    S_NG, S_ADAB, S_MU, S_W0, S_A0, S_KK, S_KA, S_RK, S_GG, S_GB = 0, 8, 24, 32, 36, 40, 42, 44, 46, 48
    w2 = P.sb("w2", [128, 256], F32)
    a2 = P.sb("a2", [128, 256], F32)
    P.dma(w2.v, w2_d.v)
    P.dma(a2.v, a2_d.v)
    ones128 = P.sb("ones128", [128, 128], F32)
    P.memset("pool", ones128.v, 1.0)

    der = P.sb("der", [128, 32], F32)
    P.ts("pool", der[:, 0:8], sm[:, S_MU:S_MU + 8], -1.0, 1.0, ALU.mult, ALU.add)
    P.ts("pool", der[:, 8:16], sm[:, S_MU:S_MU + 8], 0.5, None, ALU.mult)
    P.ts("pool", der[:, 16:18], sm[:, S_KA:S_KA + 2], -1.0, 1.0, ALU.mult, ALU.add)
    omu = lambda ci: der[:, ci:ci + 1]
    hmu = lambda ci: der[:, 8 + ci:9 + ci]
    omka = lambda p: der[:, 16 + p:17 + p]

    def dbg_stop(views):
        tot = max(64, sum(n for _, n in views))
        dbg = P.dram_out("dbg", [128, tot], F32)
        dsb = P.sb("dsb", [128, tot], F32)
        P.memset("dve", dsb.v, 0.0)
        c = 0
        for v, n in views:
            P.copy("dve", dsb[:, c:c + n], v)
            c += n
        P.dma(dbg.v, dsb.v)
        P.emit()
        return nc
    if stop == "pre0":
        return dbg_stop([(der[:, 0:18], 18)])
    ct = P.sb("ct", [128, 16], F32)
    P.dma(ct.v, ct_d.v)
    sct = P.sb("sct", [128, 16], F32)
    P.act(sct.v, ct.v, AF.Silu)
    modT = P.sb("modT", [128, 16, 2], F32)
    adaw = P.sb("adaw", [128, 8, 512], F32)
    ps_misc = P.ps("ps_misc", [128, 512])
    adaw_r = adaw_d.t[:].rearrange("(k p) m -> p k m", p=128)
    for piece in range(4):
        P.dma(adaw.v, View(adaw_d, adaw_r[:, :, piece * 512:(piece + 1) * 512]))
        for mcl in range(4):
            mc = piece * 4 + mcl
            for kc in range(8):
                P.matmul(ps_misc[:, 0:2], adaw[:, kc, mcl * 128:(mcl + 1) * 128], sct[:, kc * 2:kc * 2 + 2],
                         start=(kc == 0), stop=(kc == 7))
            P.ts("dve", modT[:, mc, :], ps_misc[:, 0:2], sm[:, S_ADAB + mc:S_ADAB + mc + 1], None, ALU.add)
    if stop == "pre1":
        return dbg_stop([(V2(modT.v), 32)])
    gmod = P.sb("gmod", [128, 8, 2], F32)
    for kc in range(8):
        P.ts("pool", gmod[:, kc, :], modT[:, 8 + kc, :], 1.0, sm[:, S_NG + kc:S_NG + kc + 1], ALU.add, ALU.mult)

    if stop == "pre2":
        return dbg_stop([(V2(modT.v), 32), (V2(gmod.v), 16)])
    Wb = P.sb("Wb", [128, 8, 1280], BF16)
    wst = [P.sb("wst%d" % i, [128, 1280], F32) for i in range(2)]
    for kc in range(8):
        P.dma(wst[kc % 2].v, win_d[kc * 128:(kc + 1) * 128, :])
        P.copy("pool", Wb[:, kc, :], wst[kc % 2].v)

    if stop == "pre":
        dbg = P.dram_out("dbg", [128, 64], F32)
        dsb = P.sb("dsb", [128, 64], F32)
        P.copy("dve", dsb[:, 0:32], V2(modT.v))
        P.copy("dve", dsb[:, 32:48], V2(gmod.v))
        P.copy("dve", dsb[:, 48:64], Wb[:, 7, 0:16])
        P.dma(dbg.v, dsb.v)
        P.emit()
        return nc
    xin = [P.sb("xin%d" % i, [128, 8, WH], F32) for i in range(2)]
    hT = P.sb("hT", [128, 8, WH], BF16)
    sqb = [P.sb("sqb%d" % i, [128, WH], F32) for i in range(2)]
    rstd = P.sb("rstd", [128, WH], F32)
    htmp = [P.sb("htmp%d" % i, [128, WH], F32) for i in range(2)]
    ps_proj = [P.ps("ps_proj%d" % i, [128, 512]) for i in range(2)]
    ps_a = P.ps("ps_a", [128, 512])
    u_sb = [P.sb("u_sb%d" % i, [128, WH], F32) for i in range(2)]
    s_sb = [P.sb("s_sb%d" % i, [128, W], F32) for i in range(2)]
    t_sb = [P.sb("t_sb%d" % i, [128, W], F32) for i in range(2)]

    def blk(name):
        return P.sb(name, [128, W], F32)

    Rb = [blk("R%d" % p) for p in range(2)]
    Kb = [blk("K%d" % p) for p in range(2)]
    Vb = [blk("V%d" % p) for p in range(2)]
    SG = [blk("SG%d" % p) for p in range(2)]
    LW = blk("LW")
    LA = blk("LA")
    TLW = blk("TLW")
    LOGW = [blk("LOGW%d" % p) for p in range(2)]
    Ab = [blk("A%d" % p) for p in range(2)]
    KQ = [blk("KQ%d" % p) for p in range(2)]
    KK = [blk("KK%d" % p) for p in range(2)]
    KD = [blk("KD%d" % p) for p in range(2)]
    KD0 = [blk("KD0%d" % p) for p in range(2)]
    T1 = [blk("T1%d" % p) for p in range(2)]
    T2 = [blk("T2%d" % p) for p in range(2)]
    CL = [blk("CL%d" % p) for p in range(2)]
    PRE = [blk("PRE%d" % p) for p in range(2)]
    E1 = [blk("E1%d" % p) for p in range(2)]
    E2 = [blk("E2%d" % p) for p in range(2)]
    E3 = [blk("E3%d" % p) for p in range(2)]
    AT = [blk("AT%d" % p) for p in range(2)]
    RT = [blk("RT%d" % p) for p in range(2)]
    BT = [blk("BT%d" % p) for p in range(2)]
    KT = [blk("KT%d" % p) for p in range(2)]
    BH = [blk("BH%d" % p) for p in range(2)]
    KH = [blk("KH%d" % p) for p in range(2)]
    DG = [P.sb("DG%d" % p, [128, 256], F32) for p in range(2)]
    OB = [blk("OB%d" % p) for p in range(2)]
    OF = [blk("OF%d" % p) for p in range(2)]
    YB = [P.sb("YB%d" % p, [128, W], BF16) for p in range(2)]

    def sbp(name, shape):
        return [P.sb("%s%d" % (name, p), shape, F32) for p in range(2)]

    Lm = sbp("Lm", [128, 256])
    NM = sbp("NM", [128, 512])
    KM = sbp("KM", [128, 512])
    Lk = [sbp("Lk%d_" % i, [128, 256]) for i in range(2)]
    Nk = [sbp("Nk%d_" % i, [128, 256]) for i in range(2)]
    Xk = [sbp("Xk%d_" % i, [128, 256]) for i in range(2)]
    Zb = sbp("Zb", [128, 256])
    TZ = sbp("TZ", [128, 256])
    VT = sbp("VT", [128, 128])
    BHT = sbp("BHT", [128, 128])
    KHT = sbp("KHT", [128, 128])
    RPT = sbp("RPT", [128, 128])
    PT = sbp("PT", [128, 128])
    ST = [sbp("ST%d_" % i, [128, 128]) for i in range(3)]
    BTbd = sbp("BTbd", [128, 512])
    KTbd = sbp("KTbd", [128, 512])
    DGd = sbp("DGd", [128, 512])
    BHTc = [sbp("BHTc%d_" % i, [128, 128]) for i in range(2)]
    KHTc = [sbp("KHTc%d_" % i, [128, 128]) for i in range(2)]
    RPTm = [sbp("RPTm%d_" % i, [128, 128]) for i in range(2)]
    PTbd = [sbp("PTbd%d_" % i, [128, 128]) for i in range(2)]
    T3 = sbp("T3", [128, 128])
    OTK = sbp("OTK", [128, 128])
    for p in range(2):
        for bb in (BTbd[p], KTbd[p], BHTc[0][p], BHTc[1][p], KHTc[0][p], KHTc[1][p], RPTm[0][p], RPTm[1][p]):
            P.memset("pool", bb.v, 0.0)
    psB = [P.ps("psB%d" % p, [128, 512]) for p in range(2)]
    psC = [P.ps("psC%d" % p, [128, 512]) for p in range(2)]

    def HH(buf, h):
        return buf[:, h * 128:(h + 1) * 128]

    def NMa(p, h):
        return NM[p][:, h * 256:h * 256 + 128]

    def NMb(p, h):
        return NM[p][:, h * 256 + 128:h * 256 + 256]

    def KMa(p, h):
        return KM[p][:, h * 256:h * 256 + 128]

    def KMb(p, h):
        return KM[p][:, h * 256 + 128:h * 256 + 256]

    def V3(view, h):
        return View(view.buf, view.ap.rearrange("p (h c) -> p h c", h=h))


    def x_cols(blk_id):
        if blk_id < 0:
            return 0
        return 258 + 256 * blk_id

    def tok0(blk_id):
        return 0 if blk_id < 0 else TC + 256 * blk_id

    xa_r = xa_d.t[:].rearrange("(k p) c -> p k c", p=128)

    def issue_x(blk_id, slot):
        c0 = x_cols(blk_id)
        P.dma(xin[slot].v, View(xa_d, xa_r[:, :, c0:c0 + WH]))

    evac_flip = [0]

    def evac(out, in_):
        evac_flip[0] ^= 1
        P.copy("act" if evac_flip[0] else "dve", out, in_)


    epsb = P.sb("epsb", [128, 4], F32)
    P.memset("pool", epsb[:, 0:1], RMS_EPS)
    P.memset("pool", epsb[:, 1:2], 1e-12)
    P.memset("pool", epsb[:, 2:3], GN_EPS)

    def stage_a(blk_id, slot, d):
        col = 1 if blk_id < 0 else 0
        xs = xin[slot]
        for kc in range(8):
            sq = sqb[kc % 2]
            P.act(sq.v, xs[:, kc, :], AF.Square)
            P.matmul(ps_a[:, 0:WH], ones128.v, sq.v, start=(kc == 0), stop=(kc == 7))
        P.act(rstd.v, ps_a[:, 0:WH], AF.Sqrt, scale=1.0 / D, bias=epsb[:, 0:1])
        P.recip(rstd.v, rstd.v)
        for kc in range(8):
            tmp = htmp[kc % 2]
            P.stt(tmp.v, xs[:, kc, :], gmod[:, kc, col:col + 1], rstd.v, ALU.mult, ALU.mult)
            P.act(hT[:, kc, :], tmp.v, AF.Identity, bias=modT[:, kc, col:col + 1])
        if blk_id < 0 or blk_id == 0:
            P.memset("pool", hT[:, :, 0:1], 0.0)
        if blk_id < 0 or blk_id == NBLK_L - 1:
            P.memset("pool", hT[:, :, WH - 1:WH], 0.0)
        mixed_dst = [Rb[0], Rb[1], Kb[0], Kb[1], Vb[0], Vb[1], None, None, LW, LA]
        mix_ci = [0, 1, 2, 3, 4, 5, None, None, 6, 7]
        n = 0
        for cc in range(10):
            if cc in (6, 7) and d == 0:
                continue
            pp = ps_proj[n % 2]
            for kc in range(8):
                P.matmul(pp[:, 0:WH], Wb[:, kc, cc * 128:(cc + 1) * 128], hT[:, kc, :],
                         start=(kc == 0), stop=(kc == 7))
            if cc in (6, 7):
                P.act(SG[cc - 6].v, pp[:, 1:W + 1], AF.Silu)
            else:
                ci = mix_ci[cc]
                u = u_sb[n % 2]
                s_ = s_sb[n % 2]
                t_ = t_sb[n % 2]
                P.copy("act", u.v, pp[:, 0:WH])
                P.tt("dve", s_.v, u[:, 0:W], u[:, 2:W + 2], ALU.add)
                P.act(t_.v, u[:, 1:W + 1], AF.Identity, scale=omu(ci))
                P.stt(mixed_dst[cc].v, s_.v, hmu(ci), t_.v, ALU.mult, ALU.add)
            n += 1
        P.act(TLW.v, LW.v, AF.Tanh)
        def derive(p):
            pc = slice(p * 128, (p + 1) * 128)
            P.matmul(ps_a[:, p * W:(p + 1) * W], w2[64 * d:64 * d + 64, pc], TLW[64 * d:64 * d + 64, :])
            yield
            P.act(LOGW[p].v, ps_a[:, p * W:(p + 1) * W], AF.Sigmoid, bias=sm[:, S_W0 + 2 * d + p:S_W0 + 2 * d + p + 1])
            yield
            P.ts("dve", LOGW[p].v, LOGW[p].v, -EXPM05, None, ALU.mult)
            yield
            P.matmul(ps_a[:, p * W:(p + 1) * W], a2[64 * d:64 * d + 64, pc], LA[64 * d:64 * d + 64, :])
            yield
            P.act(Ab[p].v, ps_a[:, p * W:(p + 1) * W], AF.Sigmoid, bias=sm[:, S_A0 + 2 * d + p:S_A0 + 2 * d + p + 1])
            yield
            P.act(KQ[p].v, Kb[p].v, AF.Identity, scale=sm[:, S_KK + p:S_KK + p + 1])
            yield
            P.act(T1[p].v, KQ[p].v, AF.Square)
            yield
            P.matmul(ps_a[:, p * W:(p + 1) * W], bones, T1[p].v)
            yield
            P.act(T2[p].v, ps_a[:, p * W:(p + 1) * W], AF.Sqrt, bias=epsb[:, 1:2])
            yield
            P.recip(T2[p].v, T2[p].v)
            yield
            P.tt("pool", KK[p].v, KQ[p].v, T2[p].v, ALU.mult)
            yield
            P.act(T1[p].v, Ab[p].v, AF.Identity, scale=sm[:, S_KA + p:S_KA + p + 1], bias=omka(p))
            yield
            P.tt("pool", KD[p].v, Kb[p].v, T1[p].v, ALU.mult)
            yield
            if d == 1:
                P.matmul(ps_a[:, p * W:(p + 1) * W], a2[0:64, pc], LA[0:64, :])
                yield
                P.act(T2[p].v, ps_a[:, p * W:(p + 1) * W], AF.Sigmoid, bias=sm[:, S_A0 + p:S_A0 + p + 1])
                yield
                P.act(T2[p].v, T2[p].v, AF.Identity, scale=sm[:, S_KA + p:S_KA + p + 1], bias=omka(p))
                yield
                P.tt("pool", KD0[p].v, Kb[p].v, T2[p].v, ALU.mult)
                yield
            for ch in range(4):
                sl = slice(ch * 64, (ch + 1) * 64)
                P.scan(PRE[p][:, sl], ones64, LOGW[p][:, sl], 0.0, ALU.mult, ALU.add)
                yield
            if d == 0:
                clb = PRE[p]
            else:
                clb = CL[p]
                for ch in range(4):
                    sl = slice(ch * 64, (ch + 1) * 64)
                    P.act(CL[p][:, sl], PRE[p][:, sl], AF.Identity, scale=-1.0,
                          bias=PRE[p][:, ch * 64 + 63:ch * 64 + 64])
                    yield
                P.tt("pool", CL[p].v, CL[p].v, LOGW[p].v, ALU.add)
                yield
            P.act(E1[p].v, clb.v, AF.Exp)
            yield
            P.act(E2[p].v, clb.v, AF.Exp, scale=-1.0)
            yield
            P.tt("pool", T1[p].v, clb.v, LOGW[p].v, ALU.subtract)
            yield
            P.act(E3[p].v, T1[p].v, AF.Exp)
            yield
            P.stt(RR(AT[p].v), KK[p].v, -1.0, E3[p].v, ALU.mult, ALU.mult)
            yield
            P.tt("dve", RR(RT[p].v), Rb[p].v, E1[p].v, ALU.mult)
            yield
            P.tt("pool", T1[p].v, KK[p].v, Ab[p].v, ALU.mult)
            yield
            P.tt("pool", BT[p].v, T1[p].v, E2[p].v, ALU.mult)
            yield
            P.tt("pool", KT[p].v, KD[p].v, E2[p].v, ALU.mult)
            yield
            for ch in range(4):
                sl = slice(ch * 64, (ch + 1) * 64)
                gc = ch * 64 + 63 if d == 0 else ch * 64
                gcol = E1[p][:, gc:gc + 1]
                P.ts("dve", BH[p][:, sl], BT[p][:, sl], gcol, None, ALU.mult)
                yield
                P.act(KH[p][:, sl], KT[p][:, sl], AF.Identity, scale=gcol)
                yield
                P.ts("dve", DGd[p][:, ch * 128:(ch + 1) * 128], ident, gcol, None, ALU.mult)
                yield
            for h in range(2):
                hp = slice(64 * h, 64 * h + 64)
                for tl2 in range(2):
                    q = (tl2 * 2 + h) * 128
                    P.copy("dve", RR(BTbd[p][hp, q:q + 128]), BT[p][hp, tl2 * 128:(tl2 + 1) * 128])
                    yield
                    P.copy("act", RR(KTbd[p][hp, q:q + 128]), KT[p][hp, tl2 * 128:(tl2 + 1) * 128])
                    yield

        gens = [derive(p) for p in range(2)]
        while gens:
            for g in list(gens):
                try:
                    next(g)
                except StopIteration:
                    gens.remove(g)

    def stage_b(p, tl, d, sw_state, upto=99):
        m1, m2 = masks[d]
        cs = slice(tl * 128, (tl + 1) * 128)
        pc = psC[p]
        pb = psB[p]
        btbd = lambda h: BTbd[p][:, (tl * 2 + h) * 128:(tl * 2 + h + 1) * 128]
        ktbd = lambda h: KTbd[p][:, (tl * 2 + h) * 128:(tl * 2 + h + 1) * 128]
        P.matmul(pc[:, 0:256], RR(AT[p][:, cs]), RR(BTbd[p][:, tl * 256:(tl + 1) * 256]))
        P.tt("dve", RR(Lm[p].v), pc[:, 0:256], m1, ALU.mult)
        yield
        for h in range(2):
            P.matmul(pb[:, h * 256:h * 256 + 128], RR(btbd(h)), RR(AT[p][:, cs]))
            P.matmul(pb[:, h * 256 + 128:h * 256 + 256], RR(btbd(h)), RR(RT[p][:, cs]))
        P.tt("dve", RR(NM[p].v), pb[:, 0:512], m2, ALU.mult)
        yield
        for h in range(2):
            P.matmul(pb[:, h * 256:h * 256 + 128], RR(ktbd(h)), RR(AT[p][:, cs]))
            P.matmul(pb[:, h * 256 + 128:h * 256 + 256], RR(ktbd(h)), RR(RT[p][:, cs]))
        P.tt("dve", RR(KM[p].v), pb[:, 0:512], m2, ALU.mult)
        yield
        if upto <= 1:
            return
        X = Xk[0][p]
        for h in range(2):
            P.tt("dve", RR(HH(X, h)), NMa(p, h), ident, ALU.add)
        Lc = Lm[p]
        Nc_views = [NMa(p, h) for h in range(2)]
        xi = 0
        for k in range(1, 6):
            Ln = Lk[k % 2][p]
            for h in range(2):
                P.matmul(pc[:, h * 128:(h + 1) * 128], RR(Nc_views[h]), RR(HH(Lc, h)))
            P.copy("act", RR(Ln.v), pc[:, 0:256])
            yield
            if k < 5:
                Nn = Nk[k % 2][p]
                for h in range(2):
                    P.matmul(pc[:, 256 + h * 128:256 + (h + 1) * 128], RR(HH(Lc, h)), RR(Nc_views[h]))
                P.copy("act", RR(Nn.v), pc[:, 256:512])
            for h in range(2):
                P.matmul(pb[:, h * 128:(h + 1) * 128], RR(HH(Ln, h)), RR(HH(Xk[xi][p], h)))
            Xn = Xk[1 - xi][p]
            P.tt("dve", RR(Xn.v), pb[:, 0:256], Xk[xi][p].v, ALU.add)
            yield
            xi = 1 - xi
            Lc = Ln
            if k < 5:
                Nc_views = [HH(Nn, h) for h in range(2)]
        X = Xk[xi][p]
        if upto <= 2:
            return
        P.transpose(pc[:, 0:128], AT[p][:, cs], ident)
        P.copy("act", RR(Zb[p][:, 0:128]), pc[:, 0:128])
        yield
        P.transpose(pc[:, 128:256], Vb[p][:, cs], ident)
        P.copy("dve", RR(VT[p].v), pc[:, 128:256])
        yield
        P.transpose(pc[:, 256:384], BH[p][:, cs], ident)
        P.copy("act", RR(BHTc[0][p][0:64, :]), pc[0:64, 256:384])
        yield
        P.copy("dve", RR(BHTc[1][p][64:128, :]), pc[64:128, 256:384])
        yield
        P.transpose(pc[:, 384:512], KH[p][:, cs], ident)
        P.copy("act", KHTc[0][p][0:64, :], pc[0:64, 384:512])
        yield
        P.copy("dve", KHTc[1][p][64:128, :], pc[64:128, 384:512])
        yield
        for h in range(2):
            P.matmul(pb[:, h * 64:(h + 1) * 64], RR(KMa(p, h)), RR(VT[p][:, h * 64:(h + 1) * 64]))
        P.copy("act", RR(Zb[p][:, 128:256]), pb[:, 0:128])
        yield
        for part in range(2):
            for h in range(2):
                q = (part * 2 + h) * 64
                P.matmul(pb[:, 128 + q:128 + q + 64], RR(HH(X, h)), RR(Zb[p][:, q:q + 64]))
        P.copy("dve", RR(TZ[p].v), pb[:, 128:384])
        yield
        if upto <= 3:
            return
        for h in range(2):
            P.matmul(pc[:, h * 128:(h + 1) * 128], RR(TZ[p][:, 0:128]), RR(NMb(p, h)))
        for h in range(2):
            hp = slice(64 * h, 64 * h + 64)
            P.tt("dve", RPT[p][hp, :], pc[hp, h * 128:(h + 1) * 128], RT[p][hp, cs], ALU.add)
            yield
        P.copy("pool", RPTm[0][p][:, 0:64], RPT[p][:, 0:64])
        P.copy("pool", RPTm[1][p][:, 64:128], RPT[p][:, 64:128])
        for c in range(2):
            P.matmul(pc[:, 256 + c * 128:256 + (c + 1) * 128], RR(TZ[p][:, 0:128]), RR(BHTc[c][p].v))
        for c in range(2):
            ch = 2 * tl + c
            P.tt("dve", T3[p].v, pc[:, 256 + c * 128:256 + (c + 1) * 128], bones, ALU.mult)
            yield
            P.tt("pool", PTbd[c][p].v, T3[p].v, DGd[p][:, ch * 128:(ch + 1) * 128], ALU.add)
        if upto <= 5:
            return
        order = (0, 1) if d == 0 else (1, 0)
        s_at = {}
        for c in order:
            si = sw_state[p]
            S_in = ST[si][p]
            S_out = ST[(si + 1) % 3][p]
            s_at[c] = S_in
            P.matmul(pb[:, 0:128], PTbd[c][p].v, S_in.v, start=True, stop=False)
            P.matmul(pb[:, 0:128], BHTc[c][p].v, TZ[p][:, 128:256], start=False, stop=False)
            P.matmul(pb[:, 0:128], KHTc[c][p].v, VT[p].v, start=False, stop=True)
            P.tt("dve", S_out.v, pb[:, 0:128], bones, ALU.mult)
            yield
            sw_state[p] = (si + 1) % 3
        if upto <= 6:
            return
        P.matmul(pb[:, 128:256], RPTm[0][p].v, s_at[0].v, start=True, stop=False)
        P.matmul(pb[:, 128:256], RPTm[1][p].v, s_at[1].v, start=False, stop=False)
        for h in range(2):
            P.matmul(pb[:, 128 + h * 64:128 + (h + 1) * 64], RR(NMb(p, h)), RR(TZ[p][:, 128 + h * 64:128 + (h + 1) * 64]),
                     start=False, stop=False)
            P.matmul(pb[:, 128 + h * 64:128 + (h + 1) * 64], RR(KMb(p, h)), RR(VT[p][:, h * 64:(h + 1) * 64]),
                     start=False, stop=(h == 1))
        P.copy("act", OTK[p].v, pb[:, 128:256])
        yield
        P.transpose(pb[:, 256:384], OTK[p].v, ident)
        if d == 0:
            P.copy("dve", OB[p][:, cs], pb[:, 256:384])
            yield
        else:
            P.tt("dve", OB[p][:, cs], pb[:, 256:384], OF[p][:, cs], ALU.add)
            yield

    def readout(blk_id):
        t0 = tok0(blk_id)
        for p in range(2):
            P.matmul(ps_a[:, 0:W], bones, OB[p].v)
            P.stt(T1[p].v, ps_a[:, 0:W], -1.0 / 64, OB[p].v, ALU.mult, ALU.add)
            P.act(T2[p].v, T1[p].v, AF.Square)
            P.matmul(ps_a[:, W:2 * W], bones, T2[p].v)
            P.act(T2[p].v, ps_a[:, W:2 * W], AF.Sqrt, scale=1.0 / 64, bias=epsb[:, 2:3])
            P.recip(T2[p].v, T2[p].v)
            P.tt("pool", T1[p].v, T1[p].v, T2[p].v, ALU.mult)
            P.act(T1[p].v, T1[p].v, AF.Identity, scale=sm[:, S_GG + p:S_GG + p + 1],
                  bias=sm[:, S_GB + p:S_GB + p + 1])
            P.tt("pool", T2[p].v, KD[p].v, KD0[p].v, ALU.add)
            P.stt(T2[p].v, Rb[p].v, sm[:, S_RK + p:S_RK + p + 1], T2[p].v, ALU.mult, ALU.mult)
            P.matmul(ps_a[:, 0:W], bones, T2[p].v)
            P.tt("dve", T2[p].v, ps_a[:, 0:W], Vb[p].v, ALU.mult)
            P.tt("pool", T1[p].v, T1[p].v, T2[p].v, ALU.add)
            P.tt("pool", YB[p].v, T1[p].v, SG[p].v, ALU.mult)
            if isinstance(y0_d, list):
                pi, off = (0, t0) if t0 < TC else (1 + (t0 - TC) // 2048, (t0 - TC) % 2048)
                P.dma(y0_d[pi][p * 128:(p + 1) * 128, off:off + W], YB[p].v)
            else:
                P.dma(y0_d[p * 128:(p + 1) * 128, t0:t0 + W], YB[p].v)

    for p in range(2):
        P.memset("pool", ST[0][p].v, 0.0)
    for d in range(2):
        blocks = [-1] + (list(range(NBLK_L)) if d == 0 else list(range(NBLK_L - 1, -1, -1)))
        if debug_out and isinstance(debug_out, int) and debug_out > 1:
            blocks = blocks[:debug_out]
        sw_state = [0, 0]
        if d == 1:
            for p in range(2):
                P.memset("pool", ST[0][p].v, 0.0)
        issue_x(blocks[0], 0)
        for bi, b in enumerate(blocks):
            slot = bi % 2
            if bi + 1 < len(blocks):
                issue_x(blocks[bi + 1], 1 - slot)
            t0 = tok0(b)
            if d == 1:
                for p in range(2):
                    P.dma(OF[p].v, of_d[p * 128:(p + 1) * 128, t0:t0 + W])
            stage_a(b, slot, d)
            if stop == "a":
                return dbg_stop([(Rb[0].v, 256), (KK[1].v, 256), (LOGW[0].v, 256), (Ab[1].v, 256), (KD[0].v, 256),
                                 (AT[0].v, 256), (RT[0].v, 256), (BT[0].v, 256), (KT[0].v, 256), (BH[0].v, 256),
                                 (DG[0].v, 256), (Vb[1].v, 256)])
            tiles = (0, 1) if d == 0 else (1, 0)
            for tl in tiles:
                if not (stop and stop[0] == "b"):
                    gens = [stage_b(p, tl, d, sw_state) for p in range(2)]
                    while gens:
                        for g in list(gens):
                            try:
                                next(g)
                            except StopIteration:
                                gens.remove(g)
                    continue
                for p in range(2):
                    for _ in stage_b(p, tl, d, sw_state, upto=int(stop[1:]) if (stop and stop[0] == "b" and len(stop) > 1) else 99):
                        pass
                    if stop and stop[0] == "b":
                        return dbg_stop([(OB[0][:, 0:128], 128), (ST[sw_state[0]][0].v, 128), (TZ[0].v, 256),
                                         (Lm[0].v, 256), (NM[0].v, 512), (Xk[1][0].v, 256), (RPT[0].v, 128), (PTbd[0][0].v, 128)])
            if d == 0:
                for p in range(2):
                    P.dma(of_d[p * 128:(p + 1) * 128, t0:t0 + W], OB[p].v)
            else:
                readout(b)
    if ctx is None:
        P.emit()
    return nc


def l0_inputs(inp, b, hg):
    f32 = np.float32
    x, ctx = inp["x"], inp["ctx"]
    z1 = np.zeros((D, 1), f32)
    xa = np.concatenate([z1, ctx[b].T, z1, z1, x[b].T, z1], axis=1)
    ct = np.zeros((128, 16), f32)
    cb = inp["c"][b].reshape(8, 128).T
    cc = inp["c_ctx"].reshape(8, 128).T
    ct[:, 0::2] = cb
    ct[:, 1::2] = cc
    adaw = np.ascontiguousarray(inp["ada_w"][0][:, 0:2048])
    hc = slice(hg * 256, (hg + 1) * 256)
    rw_in = inp["rw_in"][0]
    cols = []
    for X in range(4):
        cols.append(rw_in[:, X * 1024 + hg * 256: X * 1024 + (hg + 1) * 256])
    cols.append(rw_in[:, 4096:4352])
    win = np.ascontiguousarray(np.concatenate(cols, axis=1))
    sm = np.zeros((128, NSMALL), f32)
    sm[:, 0:8] = inp["norm_g"][0].reshape(8, 128).T
    sm[:, 8:24] = inp["ada_b"][0][:2048].reshape(16, 128).T
    mu = inp["rw_mu"][0]
    mus = []
    for X in range(3):
        for p in range(2):
            mus.append(mu[X * 1024 + hg * 256 + p * 128: X * 1024 + hg * 256 + (p + 1) * 128])
    mus.append(mu[3072:3200])
    mus.append(mu[3200:3328])
    sm[:, 24:32] = np.stack(mus, axis=1)
    for d in range(2):
        for p in range(2):
            sm[:, 32 + 2 * d + p] = inp["rw_w0"][0][d, hg * 256 + p * 128: hg * 256 + (p + 1) * 128]
            sm[:, 36 + 2 * d + p] = inp["rw_a0"][0][d, hg * 256 + p * 128: hg * 256 + (p + 1) * 128]
    for p in range(2):
        sl = slice(hg * 256 + p * 128, hg * 256 + (p + 1) * 128)
        sm[:, 40 + p] = inp["rw_kk"][0][sl]
        sm[:, 42 + p] = inp["rw_ka"][0][sl]
        sm[:, 44 + p] = inp["rw_rk"][0].reshape(-1)[sl]
        sm[:, 46 + p] = inp["rw_gn_g"][0][sl]
        sm[:, 48 + p] = inp["rw_gn_b"][0][sl]
    w2 = np.ascontiguousarray(inp["rw_w2"][0][:, :, hc].reshape(128, 256))
    a2 = np.ascontiguousarray(inp["rw_a2"][0][:, :, hc].reshape(128, 256))
    return {"xa": np.ascontiguousarray(xa), "ct": ct, "adaw": adaw, "smalls": sm, "win": win,
            "w2": w2, "a2": a2, "cst": l0_consts()}


SUBLN_EPS = 1e-5
LAM_INIT = 0.8 - 0.6 * float(np.exp(-0.3 * 1))
QSCALE = 0.125
NKT = TA // 128
NQB = TL // 256


def tok0(bi):
    return 0 if bi < 0 else TC + 256 * bi


def mod_compute(P, adaw_d, ncol, sct, adab_view, ps, modT, adaw_sb):
    adaw_r = adaw_d.t[:].rearrange("(k p) m -> p k m", p=128)
    for piece in range(ncol // 256):
        P.dma(adaw_sb.v, View(adaw_d, adaw_r[:, :, piece * 256:(piece + 1) * 256]))
        for mcl in range(2):
            mc = piece * 2 + mcl
            for kc in range(8):
                P.matmul(ps[:, 0:2], adaw_sb[:, kc, mcl * 128:(mcl + 1) * 128], sct[:, kc * 2:kc * 2 + 2],
                         start=(kc == 0), stop=(kc == 7))
            P.ts("dve", modT[:, mc, :], ps[:, 0:2], adab_view(mc), None, ALU.add)


def rope_tables():
    rows = TL // 64
    t = np.arange(TL)
    row = (t // 64).astype(np.float32)
    colid = (t % 64).astype(np.float32)
    inv = (10000.0 ** (-np.arange(16, dtype=np.float32) / 16)).astype(np.float32)
    ang_r = row[None, :] * inv[:, None]
    ang_c = colid[None, :] * inv[:, None]
    cos64 = np.concatenate([np.cos(ang_r), np.cos(ang_r), np.cos(ang_c), np.cos(ang_c)], axis=0)
    sin64 = np.concatenate([np.sin(ang_r), np.sin(ang_r), np.sin(ang_c), np.sin(ang_c)], axis=0)
    cosT = np.concatenate([cos64, cos64], axis=0).astype(np.float32)
    sinT = np.concatenate([sin64, sin64], axis=0).astype(np.float32)
    R = np.zeros((128, 128), np.float32)
    for base in range(0, 128, 32):
        for f in range(16):
            R[base + 16 + f, base + f] = -1.0
            R[base + f, base + 16 + f] = 1.0
    return cosT, sinT, R


def build_l1(stop=None, ctx=None):
    if ctx is None:
        nc = bass.Bass("TRN2", target_bir_lowering=False)
        P = Prog(nc)
        xa_d = P.dram_in("xa", [D, TA], F32)
        y0_d = P.dram_in("y0g", [D, TA], BF16)
        ct_d = P.dram_in("ct", [128, 16], F32)
        adaw0_d = P.dram_in("adaw0g", [D, 1024], F32)
        adaw1_d = P.dram_in("adaw1", [D, 2048], F32)
        sm_d = P.dram_in("smalls", [128, 48], F32)
        wo_d = P.dram_in("wo", [D, D], F32)
        wd_d = P.dram_in("wd", [D, 1024], F32)
        cst_d = P.dram_in("cst", [128, 384], F32)
        cos_d = P.dram_in("cosT", [128, TL], F32)
        sin_d = P.dram_in("sinT", [128, TL], F32)
        lam_d = P.dram_in("lamv", [128, 256], F32)
        subw_d = P.dram_in("subw", [128, 128], F32)
        y1_d = P.dram_out("y1", [256, TL], BF16)
        xn_d = P.dram_out("xn", [D, TL], F32)
    else:
        nc, P = ctx["nc"], ctx["P"]
        (xa_d, y0_d, ct_d, adaw0_d, adaw1_d, sm_d, wo_d, wd_d, cst_d, cos_d, sin_d, lam_d, subw_d, y1_d, xn_d) = [
            ctx[k] for k in ("b_xa", "y0g", "ct", "b_adaw0g", "b_adaw1", "b_smalls", "b_wo", "b_wd", "b_cst",
                             "b_cosT", "b_sinT", "b_lamv", "b_subw", "y1loc", "xn_s")]

    cst = P.sb("cst", [128, 384], F32)
    P.dma(cst.v, cst_d.v)
    ident = cst[:, 0:128]
    bones = cst[:, 128:256]
    rrot = cst[:, 256:384]
    sm = P.sb("sm", [128, 48], F32)
    P.dma(sm.v, sm_d.v)
    S_NG, S_B0, S_B1, S_QN, S_KN = 0, 8, 16, 32, 33
    ones128 = P.sb("ones128", [128, 128], F32)
    P.memset("pool", ones128.v, 1.0)
    epsb = P.sb("epsb", [128, 4], F32)
    P.memset("pool", epsb[:, 0:1], RMS_EPS)
    P.memset("pool", epsb[:, 1:2], SUBLN_EPS)
    banks = [P.ps("bank%d" % i, [128, 512]) for i in range(8)]

    ct = P.sb("ct", [128, 16], F32)
    P.dma(ct.v, ct_d.v)
    sct = P.sb("sct", [128, 16], F32)
    P.act(sct.v, ct.v, AF.Silu)
    adaw_sb = P.sb("adaw_sb", [128, 8, 256], F32)
    mod0 = P.sb("mod0", [128, 8, 2], F32)
    mod1 = P.sb("mod1", [128, 16, 2], F32)
    mod_compute(P, adaw0_d, 1024, sct, lambda mc: sm[:, S_B0 + mc:S_B0 + mc + 1], banks[0], mod0, adaw_sb)
    mod_compute(P, adaw1_d, 2048, sct, lambda mc: sm[:, S_B1 + mc:S_B1 + mc + 1], banks[0], mod1, adaw_sb)
    gmod = P.sb("gmod", [128, 8, 2], F32)
    for kc in range(8):
        P.ts("pool", gmod[:, kc, :], mod1[:, 8 + kc, :], 1.0, sm[:, S_NG + kc:S_NG + kc + 1], ALU.add, ALU.mult)

    lamv = P.sb("lamv", [128, 256], F32)
    P.dma(lamv.v, lam_d.v)
    lt = P.sb("lt", [128, 128], F32)
    lsc = P.sb("lsc", [128, 8], F32)
    P.tt("pool", lt[:, 0:64], lamv[:, 0:64], lamv[:, 64:128], ALU.mult)
    P.tt("pool", lt[:, 64:128], lamv[:, 128:192], lamv[:, 192:256], ALU.mult)
    P.reduce(lsc[:, 0:1], lt[:, 0:64], AX.X, ALU.add)
    P.reduce(lsc[:, 1:2], lt[:, 64:128], AX.X, ALU.add)
    P.act(lsc[:, 2:4], lsc[:, 0:2], AF.Exp)
    P.tt("pool", lsc[:, 4:5], lsc[:, 2:3], lsc[:, 3:4], ALU.subtract)
    P.ts("pool", lsc[:, 5:6], lsc[:, 4:5], LAM_INIT, -1.0, ALU.add, ALU.mult)
    neglam = lsc[:, 5:6]
    subw = P.sb("subw", [128, 128], F32)
    P.dma(subw.v, subw_d.v)
    P.ts("pool", subw.v, subw.v, 1.0 - LAM_INIT, None, ALU.mult)

    Wo = P.sb("Wo", [128, 8, 1024], BF16)
    Wd = P.sb("Wd", [128, 8, 1024], BF16)
    wst = [P.sb("wst%d" % i, [128, 1024], F32) for i in range(1)] * 2
    n = 0
    for src, dst in ((wo_d, Wo), (wd_d, Wd)):
        for kc in range(8):
            P.dma(wst[n % 2].v, src[kc * 128:(kc + 1) * 128, :])
            P.copy("pool", dst[:, kc, :], wst[n % 2].v)
            n += 1

    xin = [P.sb("xin%d" % i, [128, 8, W], F32) for i in range(2)]
    yin = [P.sb("yin%d" % i, [128, 8, W], BF16) for i in range(2)]
    xn = P.sb("xn", [128, 8, W], F32)
    hT = P.sb("hT", [128, 8, W], BF16)
    sqb = [P.sb("sqb%d" % i, [128, W], F32) for i in range(2)]
    rstd = P.sb("rstd", [128, W], F32)
    htmp = [P.sb("htmp%d" % i, [128, W], F32) for i in range(2)]
    cosb = P.sb("cosb", [128, W], F32)
    sinb = P.sb("sinb", [128, W], F32)
    qraw = P.sb("qraw", [128, W], F32)
    qsq = P.sb("qsq", [128, W], F32)
    qrs = P.sb("qrs", [128, W], F32)
    qn_ = P.sb("qn_", [128, W], F32)
    qt1 = P.sb("qt1", [128, W], F32)
    qt2 = P.sb("qt2", [128, W], F32)
    KTb = P.sb("KTb", [128, TA], BF16)
    QTb = P.sb("QTb", [128, NQB * 512], BF16)
    P.memset("pool", QTb.v, 0.0)
    Vx = P.sb("Vx", [128, NKT, 130], BF16)
    P.memset("pool", Vx[:, :, 129:130], 0.0)
    Gs = P.sb("Gs", [128, TL // 128, 128], BF16)
    P.memset("pool", Vx[:, :, 128:129], 1.0)
    pT = [P.sb("pT%d" % i, [128, 512], BF16) for i in range(2)]
    vg_sb = [P.sb("vg_sb%d" % i, [128, 512], F32) for i in range(2)]
    o_sb = P.sb("o_sb", [128, 128], F32)
    o_sq = P.sb("o_sq", [128, 128], F32)
    ybs = [P.sb("yb%d" % i, [128, 256], BF16) for i in range(2)]
    zs = P.sb("zs", [128, 8], F32)
    ps_bf = P.ps("ps_bf_unused", [128, 2], F32) if False else None

    xa_r = xa_d.t[:].rearrange("(k p) c -> p k c", p=128)
    y0_r = None if isinstance(y0_d, list) else y0_d.t[:].rearrange("(k p) c -> p k c", p=128)
    xn_r = xn_d.t[:].rearrange("(k p) c -> p k c", p=128)

    def issue(bi, slot):
        t0 = tok0(bi)
        P.dma(xin[slot].v, View(xa_d, xa_r[:, :, t0:t0 + W]))
        if isinstance(y0_d, list):
            pi, off = (0, t0) if t0 < TC else (1 + (t0 - TC) // 2048, (t0 - TC) % 2048)
            yr = y0_d[pi].t[:].rearrange("(k p) c -> p k c", p=128)
            P.dma(yin[slot].v, View(y0_d[pi], yr[:, :, off:off + W]))
        else:
            P.dma(yin[slot].v, View(y0_d, y0_r[:, :, t0:t0 + W]))

    def stage_a(bi, slot, h, hp_quarter, first_pass):
        col = 1 if bi < 0 else 0
        t0 = tok0(bi)
        lat0 = t0 - TC
        for oc in range(8):
            pp = banks[oc % 2]
            for kc in range(8):
                P.matmul(pp[:, 0:W], Wo[:, kc, oc * 128:(oc + 1) * 128], yin[slot][:, kc, :],
                         start=(kc == 0), stop=(kc == 7))
            P.stt(xn[:, oc, :], pp[:, 0:W], mod0[:, oc, col:col + 1], xin[slot][:, oc, :], ALU.mult, ALU.add)
        if first_pass and bi >= 0 and stop != 'noxn':
            P.dma(View(xn_d, xn_r[:, :, lat0:lat0 + W]), xn.v)
        if stop == 'a1':
            return
        for kc in range(8):
            sq = sqb[kc % 2]
            P.act(sq.v, xn[:, kc, :], AF.Square)
            P.matmul(banks[2][:, 0:W], ones128.v, sq.v, start=(kc == 0), stop=(kc == 7))
        P.act(rstd.v, banks[2][:, 0:W], AF.Sqrt, scale=1.0 / D, bias=epsb[:, 0:1])
        P.recip(rstd.v, rstd.v)
        for kc in range(8):
            tmp = htmp[kc % 2]
            P.stt(tmp.v, xn[:, kc, :], gmod[:, kc, col:col + 1], rstd.v, ALU.mult, ALU.mult)
            P.act(hT[:, kc, :], tmp.v, AF.Identity, bias=mod1[:, kc, col:col + 1])
        if stop == 'a2':
            return
        if bi >= 0:
            P.dma(cosb.v, cos_d[:, lat0:lat0 + W])
            P.dma(sinb.v, sin_d[:, lat0:lat0 + W])
        for which in (("q", "k") if bi >= 0 else ("k",)):
            cc = h if which == "q" else 2 + h
            pp = banks[3]
            for kc in range(8):
                P.matmul(pp[:, 0:W], Wd[:, kc, cc * 128:(cc + 1) * 128], hT[:, kc, :],
                         start=(kc == 0), stop=(kc == 7))
            P.copy("act", qraw.v, pp[:, 0:W])
            P.tt("pool", qsq.v, qraw.v, qraw.v, ALU.mult)
            P.matmul(banks[4][:, 0:W], bones, qsq.v)
            P.act(qrs.v, banks[4][:, 0:W], AF.Sqrt, scale=1.0 / 64, bias=epsb[:, 0:1])
            P.recip(qrs.v, qrs.v)
            wcol = sm[:, S_QN:S_QN + 1] if which == "q" else sm[:, S_KN:S_KN + 1]
            P.stt(qn_.v, qraw.v, wcol, qrs.v, ALU.mult, ALU.mult)
            if bi >= 0:
                P.matmul(banks[4][:, W:2 * W], rrot, qn_.v)
                P.tt("dve", qt1.v, banks[4][:, W:2 * W], sinb.v, ALU.mult)
                P.tt("pool", qt2.v, qn_.v, cosb.v, ALU.mult)
                if which == "q":
                    qb_ = lat0 // 256
                    P.tt("pool", QTb[0:64, qb_ * 512:qb_ * 512 + 256], qt1[0:64, :], qt2[0:64, :], ALU.add)
                    P.tt("pool", QTb[64:128, qb_ * 512 + 256:qb_ * 512 + 512], qt1[64:128, :], qt2[64:128, :], ALU.add)
                else:
                    P.tt("pool", KTb[:, t0:t0 + W], qt1.v, qt2.v, ALU.add)
            else:
                P.copy("pool", KTb[:, t0:t0 + W], qn_.v)
        if stop == 'a3':
            return
        for sub in range(2):
            pp = banks[5 + sub]
            for kc in range(8):
                P.matmul(pp[:, 0:512], hT[:, kc, sub * 128:(sub + 1) * 128], Wd[:, kc, 512:1024],
                         start=(kc == 0), stop=(kc == 7))
            kt = (t0 + sub * 128) // 128
            vg = vg_sb[sub]
            P.copy("dve", vg.v, pp[:, 0:512])
            P.copy("pool", Vx[:, kt, 0:128], vg[:, h * 128:(h + 1) * 128])
            if bi >= 0:
                qt = (lat0 + sub * 128) // 128
                P.act(Gs[:, qt, :], vg[:, 256 + h * 128:256 + (h + 1) * 128], AF.Silu)

    def attention(h, nqb=NQB):
        sbank = [(banks[0], banks[1]), (banks[2], banks[3])]
        acc = [[banks[4], banks[5]], [banks[6], banks[7]]]
        n = 0

        def s_mm(qb_, kt_, n_):
            P.matmul(banks[n_ % 4][:, 0:512], KTb[:, kt_ * 128:(kt_ + 1) * 128], QTb[:, qb_ * 512:(qb_ + 1) * 512])

        s_mm(0, 0, 0)
        for qb in range(nqb):
            q0 = qb * 256
            for kt in range(NKT):
                sA = banks[n % 4]
                pt = pT[n % 2]
                if kt + 1 < NKT:
                    s_mm(qb, kt + 1, n + 1)
                elif qb + 1 < nqb:
                    s_mm(qb + 1, 0, n + 1)
                P.act(pt.v, sA[:, 0:512], AF.Exp, scale=QSCALE)
                if stop == 's1':
                    n += 1
                    continue
                for comp in range(2):
                    for qs in range(2):
                        P.matmul(acc[comp][qs][:, 0:130], pt[:, comp * 256 + qs * 128:comp * 256 + (qs + 1) * 128],
                                 Vx[:, kt, :], start=(kt == 0), stop=(kt == NKT - 1))
                n += 1
            if stop in ('s1', 's2'):
                continue
            ytile = ybs[qb % 2]
            for qs in range(2):
                a0 = acc[0][qs]
                a1 = acc[1][qs]
                P.recip(zs[:, 0:1], a0[:, 128:129])
                P.recip(zs[:, 1:2], a1[:, 128:129])
                P.tt("pool", zs[:, 2:3], zs[:, 1:2], neglam, ALU.mult)
                P.ts("dve", o_sb.v, a0[:, 0:128], zs[:, 0:1], None, ALU.mult)
                P.stt(o_sb.v, a1[:, 0:128], zs[:, 2:3], o_sb.v, ALU.mult, ALU.add)
                P.tt("pool", o_sq.v, o_sb.v, o_sb.v, ALU.mult)
                P.reduce(zs[:, 3:4], o_sq.v, AX.X, ALU.add)
                P.act(zs[:, 4:5], zs[:, 3:4], AF.Sqrt, scale=1.0 / 128, bias=epsb[:, 1:2])
                P.recip(zs[:, 4:5], zs[:, 4:5])
                P.stt(o_sb.v, o_sb.v, zs[:, 4:5], subw.v, ALU.mult, ALU.mult)
                qt = (q0 + qs * 128) // 128
                P.tt("pool", o_sq.v, o_sb.v, Gs[:, qt, :], ALU.mult)
                P.transpose(a0[:, 256:384], o_sq.v, ident)
                P.copy("act", ytile[:, qs * 128:(qs + 1) * 128], a0[:, 256:384])
            if isinstance(y1_d, list):
                P.dma(y1_d[q0 // 2048][h * 128:(h + 1) * 128, q0 % 2048:q0 % 2048 + 256], ytile.v)
            else:
                P.dma(y1_d[h * 128:(h + 1) * 128, q0:q0 + 256], ytile.v)

    return nc, P, issue, stage_a, attention


def build_l1_full(hp_quarter, do_attn=True, nheads=2, stop=None, nblocks=None, nqb=NQB, ctx=None):
    nc, P, issue, stage_a, attention = build_l1(stop, ctx)
    blocks = [-1] + list(range(NBLK_L))
    if nblocks:
        blocks = blocks[:nblocks]
    if stop == 'pre':
        P.emit()
        return nc
    for h in range(nheads):
        issue(blocks[0], 0)
        for i, bi in enumerate(blocks):
            if i + 1 < len(blocks):
                issue(blocks[i + 1], (i + 1) % 2)
            stage_a(bi, i % 2, h, hp_quarter, h == 0)
        if do_attn:
            attention(h, nqb)
    if ctx is None:
        P.emit()
    return nc


def l1_inputs(inp, b, hp, y0g_b):
    f32 = np.float32
    xa = np.ascontiguousarray(np.concatenate([inp["ctx"][b].T, inp["x"][b].T], axis=1))
    ct = np.zeros((128, 16), f32)
    ct[:, 0::2] = inp["c"][b].reshape(8, 128).T
    ct[:, 1::2] = inp["c_ctx"].reshape(8, 128).T
    sm = np.zeros((128, 48), f32)
    sm[:, 0:8] = inp["norm_g"][1].reshape(8, 128).T
    sm[:, 8:16] = inp["ada_b"][0][2048:3072].reshape(8, 128).T
    sm[:, 16:32] = inp["ada_b"][1][0:2048].reshape(16, 128).T
    sm[:, 32] = np.tile(inp["da_qn"][0], 2)
    sm[:, 33] = np.tile(inp["da_kn"][0], 2)
    da_in = inp["da_in"][0]
    cols = []
    for X in range(4):
        for hh in range(2):
            base = X * 1024 + (hp * 2 + hh) * 128
            cols.append(da_in[:, base:base + 128])
    wd = np.ascontiguousarray(np.concatenate(cols, axis=1))
    cosT, sinT, R = rope_tables()
    ident = np.eye(128, dtype=f32)
    bones = np.kron(np.eye(2, dtype=f32), np.ones((64, 64), f32))
    cst = np.ascontiguousarray(np.concatenate([ident, bones, R], axis=1))
    lamv = np.ascontiguousarray(np.broadcast_to(inp["da_lam"][0].reshape(1, 256), (128, 256))).astype(f32)
    subw = np.ascontiguousarray(np.broadcast_to(inp["da_subln"][0].reshape(1, 128), (128, 128))).astype(f32)
    return {"xa": xa, "y0g": y0g_b, "ct": ct,
            "adaw0g": np.ascontiguousarray(inp["ada_w"][0][:, 2048:3072]),
            "adaw1": np.ascontiguousarray(inp["ada_w"][1][:, 0:2048]),
            "smalls": sm, "wo": np.ascontiguousarray(inp["rw_out"][0]), "wd": wd, "cst": cst,
            "cosT": cosT, "sinT": sinT, "lamv": lamv, "subw": subw}


def build_l2(ctx=None):
    if ctx is None:
        nc = bass.Bass("TRN2", target_bir_lowering=False)
        P = Prog(nc)
        NT = 2048
        xn_d = P.dram_in("xn", [D, NT], F32)
        y1_d = P.dram_in("y1T", [D, NT], BF16)
        ct_d = P.dram_in("ct", [128, 16], F32)
        adaw_d = P.dram_in("adaw1g", [D, 1024], F32)
        sm_d = P.dram_in("smalls", [128, 8], F32)
        w_d = P.dram_in("wda", [D, D], F32)
        out_d = P.dram_out("outT", [D, NT], F32)
    else:
        nc, P = ctx["nc"], ctx["P"]
        NT = TL
        xn_d, y1_d, ct_d, adaw_d, sm_d, w_d, out_d = [ctx[k] for k in (
            "xn_s", "y1g", "ct", "c_adaw1g", "c_smalls", "c_wda", "outT")]
    sm = P.sb("sm", [128, 8], F32)
    P.dma(sm.v, sm_d.v)
    banks = [P.ps("bank%d" % i, [128, 512]) for i in range(3)]
    ct = P.sb("ct", [128, 16], F32)
    P.dma(ct.v, ct_d.v)
    sct = P.sb("sct", [128, 16], F32)
    P.act(sct.v, ct.v, AF.Silu)
    adaw_sb = P.sb("adaw_sb", [128, 8, 256], F32)
    modg = P.sb("modg", [128, 8, 2], F32)
    mod_compute(P, adaw_d, 1024, sct, lambda mc: sm[:, mc:mc + 1], banks[0], modg, adaw_sb)
    Wa = P.sb("Wa", [128, 8, 1024], BF16)
    wst = [P.sb("wst%d" % i, [128, 1024], F32) for i in range(2)]
    for kc in range(8):
        P.dma(wst[kc % 2].v, w_d[kc * 128:(kc + 1) * 128, :])
        P.copy("pool", Wa[:, kc, :], wst[kc % 2].v)
    xin = [P.sb("xin%d" % i, [128, 8, W], F32) for i in range(2)]
    yin = [P.sb("yin%d" % i, [128, 8, W], BF16) for i in range(2)]
    ob = [P.sb("ob%d" % i, [128, 8, W], F32) for i in range(2)]
    xn_r = xn_d.t[:].rearrange("(k p) c -> p k c", p=128)
    y1_r = None if isinstance(y1_d, list) else y1_d.t[:].rearrange("(k p) c -> p k c", p=128)
    out_r = out_d.t[:].rearrange("(k p) c -> p k c", p=128)
    for bi in range(NT // W):
        s_ = bi % 2
        P.dma(xin[s_].v, View(xn_d, xn_r[:, :, bi * W:(bi + 1) * W]))
        if isinstance(y1_d, list):
            c0 = bi * W
            yr = y1_d[c0 // 2048].t[:].rearrange("(k p) c -> p k c", p=128)
            P.dma(yin[s_].v, View(y1_d[c0 // 2048], yr[:, :, c0 % 2048:c0 % 2048 + W]))
        else:
            P.dma(yin[s_].v, View(y1_d, y1_r[:, :, bi * W:(bi + 1) * W]))
        for oc in range(8):
            pp = banks[1 + oc % 2]
            for kc in range(8):
                P.matmul(pp[:, 0:W], Wa[:, kc, oc * 128:(oc + 1) * 128], yin[s_][:, kc, :],
                         start=(kc == 0), stop=(kc == 7))
            P.stt(ob[s_][:, oc, :], pp[:, 0:W], modg[:, oc, 0:1], xin[s_][:, oc, :], ALU.mult, ALU.add)
        P.dma(View(out_d, out_r[:, :, bi * W:(bi + 1) * W]), ob[s_].v)
    if ctx is None:
        P.emit()
    return nc


def l2_inputs(inp, b, tq, xn, y1T):
    f32 = np.float32
    ct = np.zeros((128, 16), f32)
    ct[:, 0::2] = inp["c"][b].reshape(8, 128).T
    ct[:, 1::2] = inp["c_ctx"].reshape(8, 128).T
    sm = np.ascontiguousarray(inp["ada_b"][1][2048:3072].reshape(8, 128).T).astype(f32)
    return {"xn": xn, "y1T": y1T, "ct": ct, "adaw1g": np.ascontiguousarray(inp["ada_w"][1][:, 2048:3072]),
            "smalls": sm, "wda": np.ascontiguousarray(inp["da_out"][0])}


GROUPS = [[0, 1, 2, 3], [4, 5, 6, 7]]


def build_fused(upto=None):
    nc = bass.Bass("TRN2", target_bir_lowering=False)
    P = Prog(nc)
    ctx = {"nc": nc, "P": P}
    decl = [("a_xa", [D, XA_COLS], F32), ("ct", [128, 16], F32), ("a_adaw", [D, 2048], F32),
            ("a_smalls", [128, NSMALL], F32), ("a_win", [D, 1280], F32), ("a_w2", [128, 256], F32),
            ("a_a2", [128, 256], F32), ("a_cst", [128, C_TOT], F32),
            ("b_xa", [D, TA], F32), ("b_adaw0g", [D, 1024], F32), ("b_adaw1", [D, 2048], F32),
            ("b_smalls", [128, 48], F32), ("b_wo", [D, D], F32), ("b_wd", [D, 1024], F32),
            ("b_cst", [128, 384], F32), ("b_cosT", [128, TL], F32), ("b_sinT", [128, TL], F32),
            ("b_lamv", [128, 256], F32), ("b_subw", [128, 128], F32),
            ("c_adaw1g", [D, 1024], F32), ("c_smalls", [128, 8], F32), ("c_wda", [D, D], F32)]
    for name, shape, dt_ in decl:
        if upto == "A" and name[0] in "bc" and name[1] == "_":
            continue
        if upto == "B" and name[0] == "c" and name[1] == "_":
            continue
        ctx[name] = P.dram_in(name, shape, dt_)
    ctx["outT"] = P.dram_out("outT", [D, TL], F32)
    pw0 = [TC, 2048, 2048, 2048, 2048]
    ctx["y0loc"] = [P.dram_tmp("y0loc%d" % i, [256, w], BF16) for i, w in enumerate(pw0)]
    ctx["y0g"] = [P.dram_tmp("y0g%d" % i, [D, w], BF16) for i, w in enumerate(pw0)]
    ctx["of_scratch"] = P.dram_tmp("of_scratch", [256, TA], F32)
    ctx["xn_s"] = P.dram_tmp("xn_s", [D, TL], F32)
    ctx["y1loc"] = [P.dram_tmp("y1loc%d" % i, [256, 2048], BF16) for i in range(4)]
    ctx["y1g"] = [P.dram_tmp("y1g%d" % i, [D, 2048], BF16) for i in range(4)]
    build_l0(ctx=ctx)
    P.emit()
    P.end_phase()
    for i in range(5):
        P.collective("AllGather", ctx["y0g"][i].v, ctx["y0loc"][i].v, GROUPS)
    if upto == "A":
        tb = P.sb("dbg_b", [128, 8, 512], BF16)
        tf = P.sb("dbg_f", [128, 8, 512], F32)
        orr = ctx["outT"].t[:].rearrange("(k p) c -> p k c", p=128)
        for i in range(4):
            pi, off = ((1, 0), (1, 512), (2, 1536), (4, 1536))[i]
            yr = ctx["y0g"][pi].t[:].rearrange("(k p) c -> p k c", p=128)
            P.dma(tb.v, View(ctx["y0g"][pi], yr[:, :, off:off + 512]))
            P.copy("dve", tf.v, tb.v)
            P.dma(View(ctx["outT"], orr[:, :, i * 512:(i + 1) * 512]), tf.v)
        P.emit()
        P.end_phase()
        return nc
    build_l1_full(0, ctx=ctx)
    P.emit()
    P.end_phase()
    for i in range(4):
        P.collective("AllGather", ctx["y1g"][i].v, ctx["y1loc"][i].v, GROUPS)
    if upto == "B":
        tb = P.sb("dbg_b", [128, 8, 512], BF16)
        tf = P.sb("dbg_f", [128, 8, 512], F32)
        xr = ctx["xn_s"].t[:].rearrange("(k p) c -> p k c", p=128)
        orr = ctx["outT"].t[:].rearrange("(k p) c -> p k c", p=128)
        for i in range(2):
            c0 = (0, TL - 512)[i]
            yr = ctx["y1g"][c0 // 2048].t[:].rearrange("(k p) c -> p k c", p=128)
            P.dma(tb.v, View(ctx["y1g"][c0 // 2048], yr[:, :, c0 % 2048:c0 % 2048 + 512]))
            P.copy("dve", tf.v, tb.v)
            P.dma(View(ctx["outT"], orr[:, :, i * 512:(i + 1) * 512]), tf.v)
            P.dma(tf.v, View(ctx["xn_s"], xr[:, :, c0:c0 + 512]))
            P.dma(View(ctx["outT"], orr[:, :, (2 + i) * 512:(3 + i) * 512]), tf.v)
        P.emit()
        P.end_phase()
        return nc
    build_l2(ctx=ctx)
    P.emit()
    P.end_phase()
    return nc


def fused_inputs(inp, b, g):
    m = {}
    a = l0_inputs(inp, b, g)
    for k in ("xa", "adaw", "smalls", "win", "w2", "a2", "cst"):
        m["a_" + k] = a[k]
    m["ct"] = a["ct"]
    bb = l1_inputs(inp, b, g, None)
    for k in ("xa", "adaw0g", "adaw1", "smalls", "wo", "wd", "cst", "cosT", "sinT", "lamv", "subw"):
        m["b_" + k] = bb[k]
    c = l2_inputs(inp, b, g, None, None)
    for k in ("adaw1g", "smalls", "wda"):
        m["c_" + k] = c[k]
    return m


def kernel(**inp):
    inp = {k: np.asarray(v) for k, v in inp.items()}
    cores = list(range(8))
    nc = build_fused()
    maps = [fused_inputs(inp, c // 4, c % 4) for c in cores]
    res = run_bass_kernel_spmd(nc, maps, core_ids=cores).results
    out = np.zeros((2, TL, D), np.float32)
    for c in cores:
        b, tq = c // 4, c % 4
        out[b, tq * 2048:(tq + 1) * 2048] = np.asarray(res[c]["outT"])[:, tq * 2048:(tq + 1) * 2048].T
    return out
```

```python
import numpy as np
import concourse.bass as bass
import concourse.mybir as mybir
from concourse.bass_utils import run_bass_kernel_spmd

F32 = mybir.dt.float32
BF16 = mybir.dt.bfloat16
AF = mybir.ActivationFunctionType
ALU = mybir.AluOpType
AX = mybir.AxisListType

ENGS = ("pe", "act", "dve", "pool", "sp")


class Buf:
    def __init__(self, prog, t, name, space):
        self.prog = prog
        self.t = t
        self.name = name
        self.space = space
        self.last_writer = None
        self.readers = []
        self.dma_sem = None
        self.dma_count = 0

    def __getitem__(self, idx):
        return View(self, self.t[idx])

    @property
    def v(self):
        return View(self, self.t[:])


class View:
    def __init__(self, buf, ap):
        self.buf = buf
        self.ap = ap

    def __getitem__(self, idx):
        return View(self.buf, self.ap[idx])


class Op:
    __slots__ = ("eng", "fn", "deps", "needs_inc", "seq", "dma_buf", "dma_val", "idx", "dma_inc")

    def __init__(self, eng, fn):
        self.eng = eng
        self.fn = fn
        self.deps = []
        self.needs_inc = False
        self.seq = None
        self.dma_buf = None
        self.dma_val = None
        self.dma_inc = 16


def _ap(x):
    return x.ap if isinstance(x, View) else x


class Prog:
    def __init__(self, nc):
        import contextlib
        self.nc = nc
        self.ops = {e: [] for e in ENGS}
        self.bufs = []
        self.dram = {}
        self.same_engine_sync = True
        self.stack = contextlib.ExitStack()
        self.phase = 0
        self.phase_sem = None
        self.uid = 0

    def end_phase(self):
        import contextlib
        self.stack.close()
        self.stack = contextlib.ExitStack()
        self.bufs = []

    def sb(self, name, shape, dtype):
        self.uid += 1
        t = self.stack.enter_context(self.nc.sbuf_tensor("sb%d_%s" % (self.uid, name), list(shape), dtype))
        b = Buf(self, t, name, "sb")
        self.bufs.append(b)
        return b

    def ps(self, name, shape, dtype=F32):
        self.uid += 1
        t = self.stack.enter_context(self.nc.psum_tensor("pp%d_%s" % (self.uid, name), list(shape), dtype))
        b = Buf(self, t, name, "ps")
        self.bufs.append(b)
        return b

    def dram_in(self, name, shape, dtype):
        t = self.nc.dram_tensor(name, list(shape), dtype, kind="ExternalInput")
        b = Buf(self, t, name, "dram")
        self.dram[name] = b
        return b

    def dram_out(self, name, shape, dtype):
        t = self.nc.dram_tensor(name, list(shape), dtype, kind="ExternalOutput")
        b = Buf(self, t, name, "dram")
        self.dram[name] = b
        return b

    def dram_tmp(self, name, shape, dtype, shared=False):
        if shared:
            t = self.nc.dram_tensor(name, list(shape), dtype, addr_space="Shared")
        else:
            t = self.nc.dram_tensor(name, list(shape), dtype)
        b = Buf(self, t, name, "dram")
        self.dram[name] = b
        return b

    def _record(self, eng, fn, reads, writes):
        op = Op(eng, fn)
        deps = []
        for v in reads:
            b = v.buf if isinstance(v, View) else v
            if b.last_writer is not None:
                deps.append(b.last_writer)
            if b.space == "ps":
                deps.extend(r for r in b.readers if r.eng != eng)
        for v in writes:
            b = v.buf if isinstance(v, View) else v
            if b.last_writer is not None:
                deps.append(b.last_writer)
            deps.extend(b.readers)
        seen = set()
        for d in deps:
            if id(d) in seen or d is op:
                continue
            seen.add(id(d))
            if d.dma_buf is None and d.eng == eng and (eng == "pe" or not self.same_engine_sync):
                continue
            op.deps.append(d)
            if d.dma_buf is None:
                d.needs_inc = True
        for v in writes:
            b = v.buf if isinstance(v, View) else v
            b.last_writer = op
            b.readers = []
        for v in reads:
            b = v.buf if isinstance(v, View) else v
            if b.last_writer is not op:
                b.readers.append(op)
        self.ops[eng].append(op)
        return op

    def op(self, eng, fn, reads=(), writes=()):
        return self._record(eng, fn, list(reads), list(writes))

    def matmul(self, out, lhsT, rhs, start=True, stop=True, extra_reads=(), **kw):
        o, l, r = _ap(out), _ap(lhsT), _ap(rhs)
        return self._record("pe", lambda e: e.matmul(o, l, r, start=start, stop=stop, **kw),
                            [lhsT, rhs] + list(extra_reads), [out])

    def transpose(self, out, in_, ident):
        o, i, d = _ap(out), _ap(in_), _ap(ident)
        return self._record("pe", lambda e: e.transpose(o, i, d), [in_, ident], [out])

    def act(self, out, in_, func, bias=None, scale=None, eng="act", accum_out=None):
        o, i = _ap(out), _ap(in_)
        kw = {}
        reads = [in_]
        writes = [out]
        if bias is not None:
            kw["bias"] = _ap(bias)
            if isinstance(bias, View):
                reads.append(bias)
        if scale is not None:
            kw["scale"] = _ap(scale)
            if isinstance(scale, View):
                reads.append(scale)
        if accum_out is not None:
            kw["accum_out"] = _ap(accum_out)
            writes.append(accum_out)
        return self._record("act", lambda e: e.activation(o, i, func, **kw), reads, writes)

    def tt(self, eng, out, in0, in1, op):
        o, a, b = _ap(out), _ap(in0), _ap(in1)
        return self._record(eng, lambda e: e.tensor_tensor(o, a, b, op), [in0, in1], [out])

    def ts(self, eng, out, in0, s1, s2, op0, op1=None, accum_out=None):
        o, a = _ap(out), _ap(in0)
        reads = [in0]
        writes = [out]
        for s in (s1, s2):
            if isinstance(s, View):
                reads.append(s)
        x1, x2 = _ap(s1), _ap(s2)
        kw = {}
        if op1 is not None:
            kw["op1"] = op1
        if accum_out is not None:
            kw["accum_out"] = _ap(accum_out)
            writes.append(accum_out)
        return self._record(eng, lambda e: e.tensor_scalar(o, a, x1, x2, op0, **kw), reads, writes)

    def stt(self, out, in0, scalar, in1, op0, op1, eng="dve"):
        o, a, b = _ap(out), _ap(in0), _ap(in1)
        reads = [in0, in1]
        if isinstance(scalar, View):
            reads.append(scalar)
        s = _ap(scalar)
        return self._record(eng, lambda e: e.scalar_tensor_tensor(o, a, s, b, op0, op1), reads, [out])

    def copy(self, eng, out, in_):
        o, i = _ap(out), _ap(in_)
        if eng == "act":
            return self._record(eng, lambda e: e.copy(o, i), [in_], [out])
        return self._record(eng, lambda e: e.tensor_copy(o, i), [in_], [out])

    def scan(self, out, d0, d1, initial, op0, op1):
        o, a, b = _ap(out), _ap(d0), _ap(d1)
        reads = [d0, d1]
        if isinstance(initial, View):
            reads.append(initial)
        ini = _ap(initial)
        return self._record("dve", lambda e: e.tensor_tensor_scan(o, a, b, ini, op0, op1), reads, [out])

    def recip(self, out, in_):
        o, i = _ap(out), _ap(in_)
        return self._record("dve", lambda e: e.reciprocal(o, i), [in_], [out])

    def memset(self, eng, out, val):
        o = _ap(out)
        return self._record(eng, lambda e: e.memset(o, val), [], [out])

    def reduce(self, out, in_, axis, op, eng="dve"):
        o, i = _ap(out), _ap(in_)
        return self._record(eng, lambda e: e.tensor_reduce(o, i, axis, op), [in_], [out])

    def dma(self, out, in_, queue="sp", **kw):
        o, i = _ap(out), _ap(in_)
        ob = out.buf
        ib = in_.buf
        key = ob if ob.space != "dram" else ib
        op = self._record(queue, lambda e: e.dma_start(out=o, in_=i, **kw), [in_], [out])
        key.dma_count += 1
        op.dma_buf = key
        op.dma_val = 16 * key.dma_count
        return op

    def emit(self, final_waits=()):
        nc = self.nc
        for e in ENGS:
            n = 0
            for op in self.ops[e]:
                if op.dma_buf is None and op.needs_inc:
                    n += 1
                    op.seq = n
        SEMCAP = 30000
        nsem = {e: 1 + max([op.seq or 0 for op in self.ops[e]] + [0]) // SEMCAP for e in ENGS}
        ph = self.phase
        sems = {e: [nc.alloc_semaphore("s%d_%s_%d" % (ph, e, i)) for i in range(nsem[e])] for e in ENGS}
        if self.phase_sem is None:
            self.phase_sem = nc.alloc_semaphore("phase_done")
        phase_sem = self.phase_sem
        dummy_sb = self.sb("phdummy", [128, 8], F32)

        for b in self.bufs + list(self.dram.values()):
            if b.dma_count > 0:
                b.dma_sem = nc.alloc_semaphore("d%d_%s" % (ph, b.name))
        engmap = {"pe": "tensor", "act": "scalar", "dve": "vector", "pool": "gpsimd", "sp": "sync"}
        all_dma = []
        for e in ENGS:
            for op in self.ops[e]:
                if op.dma_buf is not None:
                    all_dma.append(op)

        def gen(ename):
            def body(eng):
                waited = {}
                if ph > 0:
                    eng.wait_ge(phase_sem, 4 * ph)
                for op in self.ops[ename]:
                    need = {}
                    for d in op.deps:
                        if d.dma_buf is not None:
                            k = ("dma", id(d.dma_buf))
                            sem = d.dma_buf.dma_sem
                            val = d.dma_val
                        else:
                            si = (d.seq - 1) // SEMCAP
                            k = ("eng", d.eng, si)
                            sem = sems[d.eng][si]
                            val = d.seq - si * SEMCAP
                        if k not in need or need[k][1] < val:
                            need[k] = (sem, val)
                    for k, (sem, val) in need.items():
                        if waited.get(k, 0) >= val:
                            continue
                        eng.wait_ge(sem, val)
                        waited[k] = val
                    inst = op.fn(eng)
                    if op.dma_buf is not None:
                        if op.dma_inc == 16:
                            inst.then_inc(op.dma_buf.dma_sem, 16)
                        else:
                            inst.then_inc(op.dma_buf.dma_sem)
                    elif op.needs_inc:
                        inst.then_inc(sems[ename][(op.seq - 1) // SEMCAP], 1)
                if ename == "sp":
                    finals = {}
                    for op in all_dma:
                        b = op.dma_buf
                        finals[id(b)] = (b.dma_sem, op.dma_val if op.dma_inc != 16 else 16 * b.dma_count)
                    for sem, val in finals.values():
                        eng.wait_ge(sem, val)
                    eng.sem_inc(phase_sem, 1)
                elif ename == "act":
                    eng.copy(dummy_sb.t[:, 2:3], dummy_sb.t[:, 3:4]).then_inc(phase_sem, 1)
                elif ename == "dve":
                    eng.memset(dummy_sb.t[:, 4:5], 0.0).then_inc(phase_sem, 1)
                elif ename == "pool":
                    eng.memset(dummy_sb.t[:, 6:7], 0.0).then_inc(phase_sem, 1)
            return body

        with nc.Block() as block:
            block.tensor(gen("pe"))
            block.scalar(gen("act"))
            block.vector(gen("dve"))
            block.gpsimd(gen("pool"))
            block.sync(gen("sp"))
        self.phase += 1
        self.ops = {e: [] for e in ENGS}
        for b in self.bufs + list(self.dram.values()):
            b.last_writer = None
            b.readers = []
            b.dma_count = 0
            b.dma_sem = None


def _collective(self, kind, out, in_, groups, op=None):
    o, i = _ap(out), _ap(in_)
    alu = op if op is not None else ALU.bypass
    rec = self._record("pool", lambda e: e.collective_compute(kind, alu, replica_groups=groups, ins=[i], outs=[o]),
                       [in_], [out])
    key = out.buf
    key.dma_count += 1
    rec.dma_buf = key
    rec.dma_inc = 1
    rec.dma_val = key.dma_count
    return rec


Prog.collective = _collective


D = 1024
TC = 256
TL = 8192
TA = TC + TL
W = 256
WH = W + 2
NBLK_L = TL // W
XA_COLS = 1 + TC + 1 + 1 + TL + 1
NSMALL = 50
EXPM05 = 0.6065306597126334
RMS_EPS = 1e-6
GN_EPS = 64e-5


def l0_consts():
    ident = np.eye(128, dtype=np.float32)
    bones = np.kron(np.eye(2, dtype=np.float32), np.ones((64, 64), np.float32))
    idx = np.arange(128)
    same = (idx[:, None] // 64) == (idx[None, :] // 64)
    strict_f = (same & (idx[None, :] < idx[:, None])).astype(np.float32)
    incl_f = (same & (idx[None, :] <= idx[:, None])).astype(np.float32)
    strict_b = (same & (idx[None, :] > idx[:, None])).astype(np.float32)
    incl_b = (same & (idx[None, :] >= idx[:, None])).astype(np.float32)
    out = {}
    for nm, st, inc in (("f", strict_f, incl_f), ("b", strict_b, incl_b)):
        m1 = np.concatenate([st, st], axis=1)
        m2h = np.concatenate([st.T, inc.T], axis=1)
        m2 = np.concatenate([m2h, m2h], axis=1)
        out["m1" + nm] = np.ascontiguousarray(m1)
        out["m2" + nm] = np.ascontiguousarray(m2)
    ident2 = np.concatenate([np.eye(64, dtype=np.float32)] * 2, axis=0)
    cst = np.concatenate([ident, bones, out["m1f"], out["m2f"], out["m1b"], out["m2b"], ident2,
                          np.ones((128, 64), np.float32)], axis=1)
    return np.ascontiguousarray(cst)


C_ID, C_BO, C_M1F, C_M2F, C_M1B, C_M2B, C_ID2, C_ONE = 0, 128, 256, 512, 1024, 1280, 1792, 1856
C_TOT = 1920


F32R = mybir.dt.float32r


def RR(view):
    return View(view.buf, view.ap.bitcast(F32R))


def V2(view):
    return View(view.buf, view.ap.rearrange("p a b -> p (a b)"))


def build_l0(debug_out=False, stop=None, ctx=None):
    if ctx is None:
        nc = bass.Bass("TRN2", target_bir_lowering=False)
        P = Prog(nc)
        xa_d = P.dram_in("xa", [D, XA_COLS], F32)
        ct_d = P.dram_in("ct", [128, 16], F32)
        adaw_d = P.dram_in("adaw", [D, 2048], F32)
        sm_d = P.dram_in("smalls", [128, NSMALL], F32)
        win_d = P.dram_in("win", [D, 1280], F32)
        w2_d = P.dram_in("w2", [128, 256], F32)
        a2_d = P.dram_in("a2", [128, 256], F32)
        cst_d = P.dram_in("cst", [128, C_TOT], F32)
        y0_d = P.dram_out("y0", [256, TA], BF16)
        of_d = P.dram_tmp("of_scratch", [256, TA], F32)
    else:
        nc, P = ctx["nc"], ctx["P"]
        xa_d, ct_d, adaw_d, sm_d, win_d, w2_d, a2_d, cst_d, y0_d, of_d = [ctx[k] for k in (
            "a_xa", "ct", "a_adaw", "a_smalls", "a_win", "a_w2", "a_a2", "a_cst", "y0loc", "of_scratch")]

    cst = P.sb("cst", [128, C_TOT], F32)
    P.dma(cst.v, cst_d.v)
    ident = cst[:, C_ID:C_ID + 128]
    bones = cst[:, C_BO:C_BO + 128]
    ident2 = cst[:, C_ID2:C_ID2 + 64]
    ones64 = cst[:, C_ONE:C_ONE + 64]
    masks = {0: (cst[:, C_M1F:C_M1F + 256], cst[:, C_M2F:C_M2F + 512]),
             1: (cst[:, C_M1B:C_M1B + 256], cst[:, C_M2B:C_M2B + 512])}
    sm = P.sb("sm", [128, NSMALL], F32)
    P.dma(sm.v, sm_d.v)
    S_NG, S_ADAB, S_MU, S_W0, S_A0, S_KK, S_KA, S_RK, S_GG, S_GB = 0, 8, 24, 32, 36, 40, 42, 44, 46, 48
    w2 = P.sb("w2", [128, 256], F32)
    a2 = P.sb("a2", [128, 256], F32)
    P.dma(w2.v, w2_d.v)
    P.dma(a2.v, a2_d.v)
    ones128 = P.sb("ones128", [128, 128], F32)
    P.ts("dve", RR(ones128.v), cst[:, C_ID:C_ID + 128], 0.0, 1.0, ALU.mult, ALU.add)

    der = P.sb("der", [128, 32], F32)
    P.ts("pool", der[:, 0:8], sm[:, S_MU:S_MU + 8], -1.0, 1.0, ALU.mult, ALU.add)
    P.ts("pool", der[:, 8:16], sm[:, S_MU:S_MU + 8], 0.5, None, ALU.mult)
    P.ts("pool", der[:, 16:18], sm[:, S_KA:S_KA + 2], -1.0, 1.0, ALU.mult, ALU.add)
    omu = lambda ci: der[:, ci:ci + 1]
    hmu = lambda ci: der[:, 8 + ci:9 + ci]
    omka = lambda p: der[:, 16 + p:17 + p]

    def dbg_stop(views):
        tot = max(64, sum(n for _, n in views))
        dbg = P.dram_out("dbg", [128, tot], F32)
        dsb = P.sb("dsb", [128, tot], F32)
        P.memset("dve", dsb.v, 0.0)
        c = 0
        for v, n in views:
            P.copy("dve", dsb[:, c:c + n], v)
            c += n
        P.dma(dbg.v, dsb.v)
        P.emit()
        return nc
    if stop == "pre0":
        return dbg_stop([(der[:, 0:18], 18)])
    ct = P.sb("ct", [128, 16], F32)
    P.dma(ct.v, ct_d.v)
    sct = P.sb("sct", [128, 16], F32)
    P.act(sct.v, ct.v, AF.Silu)
    modT = P.sb("modT", [128, 16, 2], F32)
    adaw = P.sb("adaw", [128, 8, 512], F32)
    ps_misc = P.ps("ps_misc", [128, 512])
    adaw_r = adaw_d.t[:].rearrange("(k p) m -> p k m", p=128)
    for piece in range(4):
        P.dma(adaw.v, View(adaw_d, adaw_r[:, :, piece * 512:(piece + 1) * 512]))
        for mcl in range(4):
            mc = piece * 4 + mcl
            for kc in range(8):
                P.matmul(ps_misc[:, 0:2], adaw[:, kc, mcl * 128:(mcl + 1) * 128], sct[:, kc * 2:kc * 2 + 2],
                         start=(kc == 0), stop=(kc == 7))
            P.ts("dve", modT[:, mc, :], ps_misc[:, 0:2], sm[:, S_ADAB + mc:S_ADAB + mc + 1], None, ALU.add)
    if stop == "pre1":
        return dbg_stop([(V2(modT.v), 32)])
    gmod = P.sb("gmod", [128, 8, 2], F32)
    for kc in range(8):
        P.ts("pool", gmod[:, kc, :], modT[:, 8 + kc, :], 1.0, sm[:, S_NG + kc:S_NG + kc + 1], ALU.add, ALU.mult)

    if stop == "pre2":
        return dbg_stop([(V2(modT.v), 32), (V2(gmod.v), 16)])
    Wb = P.sb("Wb", [128, 8, 1280], BF16)
    wst = [P.sb("wst%d" % i, [128, 1280], F32) for i in range(2)]
    for kc in range(8):
        P.dma(wst[kc % 2].v, win_d[kc * 128:(kc + 1) * 128, :])
        P.copy("pool", Wb[:, kc, :], wst[kc % 2].v)

    if stop == "pre":
        dbg = P.dram_out("dbg", [128, 64], F32)
        dsb = P.sb("dsb", [128, 64], F32)
        P.copy("dve", dsb[:, 0:32], V2(modT.v))
        P.copy("dve", dsb[:, 32:48], V2(gmod.v))
        P.copy("dve", dsb[:, 48:64], Wb[:, 7, 0:16])
        P.dma(dbg.v, dsb.v)
        P.emit()
        return nc
    xin = [P.sb("xin%d" % i, [128, 8, WH], F32) for i in range(2)]
    hT = P.sb("hT", [128, 8, WH], BF16)
    sqb = [P.sb("sqb%d" % i, [128, WH], F32) for i in range(2)]
    rstd = P.sb("rstd", [128, WH], F32)
    htmp = [P.sb("htmp%d" % i, [128, WH], F32) for i in range(2)]
    ps_proj = [P.ps("ps_proj%d" % i, [128, 512]) for i in range(2)]
    ps_a = P.ps("ps_a", [128, 512])
    u_sb = [P.sb("u_sb%d" % i, [128, WH], F32) for i in range(2)]
    s_sb = [P.sb("s_sb%d" % i, [128, W], F32) for i in range(2)]
    t_sb = [P.sb("t_sb%d" % i, [128, W], F32) for i in range(2)]

    def blk(name):
        return P.sb(name, [128, W], F32)

    Rb = [blk("R%d" % p) for p in range(2)]
    Kb = [blk("K%d" % p) for p in range(2)]
    Vb = [blk("V%d" % p) for p in range(2)]
    SG = [blk("SG%d" % p) for p in range(2)]
    LW = blk("LW")
    LA = blk("LA")
    TLW = blk("TLW")
    LOGW = [blk("LOGW%d" % p) for p in range(2)]
    Ab = [blk("A%d" % p) for p in range(2)]
    KQ = [blk("KQ%d" % p) for p in range(2)]
    KK = [blk("KK%d" % p) for p in range(2)]
    KD = [blk("KD%d" % p) for p in range(2)]
    KD0 = [blk("KD0%d" % p) for p in range(2)]
    T1 = [blk("T1%d" % p) for p in range(2)]
    T2 = [blk("T2%d" % p) for p in range(2)]
    CL = [blk("CL%d" % p) for p in range(2)]
    PRE = [blk("PRE%d" % p) for p in range(2)]
    E1 = [blk("E1%d" % p) for p in range(2)]
    E2 = [blk("E2%d" % p) for p in range(2)]
    E3 = [blk("E3%d" % p) for p in range(2)]
    AT = [blk("AT%d" % p) for p in range(2)]
    RT = [blk("RT%d" % p) for p in range(2)]
    BT = [blk("BT%d" % p) for p in range(2)]
    KT = [blk("KT%d" % p) for p in range(2)]
    BH = [blk("BH%d" % p) for p in range(2)]
    KH = [blk("KH%d" % p) for p in range(2)]
    DG = [P.sb("DG%d" % p, [128, 256], F32) for p in range(2)]
    OB = [blk("OB%d" % p) for p in range(2)]
    OF = [blk("OF%d" % p) for p in range(2)]
    YB = [P.sb("YB%d" % p, [128, W], BF16) for p in range(2)]

    def sbp(name, shape):
        return [P.sb("%s%d" % (name, p), shape, F32) for p in range(2)]

    Lm = sbp("Lm", [128, 256])
    NM = sbp("NM", [128, 512])
    KM = sbp("KM", [128, 512])
    Lk = [sbp("Lk%d_" % i, [128, 256]) for i in range(2)]
    Nk = [sbp("Nk%d_" % i, [128, 256]) for i in range(2)]
    Xk = [sbp("Xk%d_" % i, [128, 256]) for i in range(2)]
    Zb = sbp("Zb", [128, 256])
    TZ = sbp("TZ", [128, 256])
    VT = sbp("VT", [128, 128])
    BHT = sbp("BHT", [128, 128])
    KHT = sbp("KHT", [128, 128])
    RPT = sbp("RPT", [128, 128])
    PT = sbp("PT", [128, 128])
    ST = [sbp("ST%d_" % i, [128, 128]) for i in range(3)]
    BTbd = sbp("BTbd", [128, 512])
    KTbd = sbp("KTbd", [128, 512])
    DGd = sbp("DGd", [128, 512])
    BHTc = [sbp("BHTc%d_" % i, [128, 128]) for i in range(2)]
    KHTc = [sbp("KHTc%d_" % i, [128, 128]) for i in range(2)]
    RPTm = [sbp("RPTm%d_" % i, [128, 128]) for i in range(2)]
    PTbd = [sbp("PTbd%d_" % i, [128, 128]) for i in range(2)]
    T3 = sbp("T3", [128, 128])
    OTK = sbp("OTK", [128, 128])
    for p in range(2):
        for bb in (BTbd[p], KTbd[p], BHTc[0][p], BHTc[1][p], KHTc[0][p], KHTc[1][p], RPTm[0][p], RPTm[1][p]):
            P.memset("pool", bb.v, 0.0)
    psB = [P.ps("psB%d" % p, [128, 512]) for p in range(2)]
    psC = [P.ps("psC%d" % p, [128, 512]) for p in range(2)]

    def HH(buf, h):
        return buf[:, h * 128:(h + 1) * 128]

    def NMa(p, h):
        return NM[p][:, h * 256:h * 256 + 128]

    def NMb(p, h):
        return NM[p][:, h * 256 + 128:h * 256 + 256]

    def KMa(p, h):
        return KM[p][:, h * 256:h * 256 + 128]

    def KMb(p, h):
        return KM[p][:, h * 256 + 128:h * 256 + 256]

    def V3(view, h):
        return View(view.buf, view.ap.rearrange("p (h c) -> p h c", h=h))


    def x_cols(blk_id):
        if blk_id < 0:
            return 0
        return 258 + 256 * blk_id

    def tok0(blk_id):
        return 0 if blk_id < 0 else TC + 256 * blk_id

    xa_r = xa_d.t[:].rearrange("(k p) c -> p k c", p=128)

    def issue_x(blk_id, slot):
        c0 = x_cols(blk_id)
        P.dma(xin[slot].v, View(xa_d, xa_r[:, :, c0:c0 + WH]))

    evac_flip = [0]

    def evac(out, in_):
        evac_flip[0] ^= 1
        P.copy("act" if evac_flip[0] else "dve", out, in_)


    epsb = P.sb("epsb", [128, 4], F32)
    P.memset("pool", epsb[:, 0:1], RMS_EPS)
    P.memset("pool", epsb[:, 1:2], 1e-12)
    P.memset("pool", epsb[:, 2:3], GN_EPS)

    def stage_a(blk_id, slot, d):
        col = 1 if blk_id < 0 else 0
        xs = xin[slot]
        for kc in range(8):
            sq = sqb[kc % 2]
            P.act(RR(sq.v), xs[:, kc, :], AF.Square)
            P.matmul(ps_a[:, 0:WH], RR(ones128.v), RR(sq.v), start=(kc == 0), stop=(kc == 7))
        P.act(rstd.v, ps_a[:, 0:WH], AF.Sqrt, scale=1.0 / D, bias=epsb[:, 0:1])
        P.recip(rstd.v, rstd.v)
        for kc in range(8):
            tmp = htmp[kc % 2]
            P.stt(tmp.v, xs[:, kc, :], gmod[:, kc, col:col + 1], rstd.v, ALU.mult, ALU.mult)
            P.act(hT[:, kc, :], tmp.v, AF.Identity, bias=modT[:, kc, col:col + 1])
        if blk_id < 0 or blk_id == 0:
            P.memset("pool", hT[:, :, 0:1], 0.0)
        if blk_id < 0 or blk_id == NBLK_L - 1:
            P.memset("pool", hT[:, :, WH - 1:WH], 0.0)
        mixed_dst = [Rb[0], Rb[1], Kb[0], Kb[1], Vb[0], Vb[1], None, None, LW, LA]
        mix_ci = [0, 1, 2, 3, 4, 5, None, None, 6, 7]
        n = 0
        for cc in range(10):
            if cc in (6, 7) and d == 0:
                continue
            pp = ps_proj[n % 2]
            for kc in range(8):
                P.matmul(pp[:, 0:WH], Wb[:, kc, cc * 128:(cc + 1) * 128], hT[:, kc, :],
                         start=(kc == 0), stop=(kc == 7))
            if cc in (6, 7):
                P.act(SG[cc - 6].v, pp[:, 1:W + 1], AF.Silu)
            else:
                ci = mix_ci[cc]
                u = u_sb[n % 2]
                s_ = s_sb[n % 2]
                t_ = t_sb[n % 2]
                P.copy("act", u.v, pp[:, 0:WH])
                P.tt("dve", s_.v, u[:, 0:W], u[:, 2:W + 2], ALU.add)
                P.act(t_.v, u[:, 1:W + 1], AF.Identity, scale=omu(ci))
                P.stt(mixed_dst[cc].v, s_.v, hmu(ci), t_.v, ALU.mult, ALU.add)
            n += 1
        P.act(TLW.v, LW.v, AF.Tanh)
        def derive(p):
            pc = slice(p * 128, (p + 1) * 128)
            P.matmul(ps_a[:, p * W:(p + 1) * W], w2[64 * d:64 * d + 64, pc], TLW[64 * d:64 * d + 64, :])
            yield
            P.act(LOGW[p].v, ps_a[:, p * W:(p + 1) * W], AF.Sigmoid, bias=sm[:, S_W0 + 2 * d + p:S_W0 + 2 * d + p + 1])
            yield
            P.ts("dve", LOGW[p].v, LOGW[p].v, -EXPM05, None, ALU.mult)
            yield
            P.matmul(ps_a[:, p * W:(p + 1) * W], a2[64 * d:64 * d + 64, pc], LA[64 * d:64 * d + 64, :])
            yield
            P.act(Ab[p].v, ps_a[:, p * W:(p + 1) * W], AF.Sigmoid, bias=sm[:, S_A0 + 2 * d + p:S_A0 + 2 * d + p + 1])
            yield
            P.act(KQ[p].v, Kb[p].v, AF.Identity, scale=sm[:, S_KK + p:S_KK + p + 1])
            yield
            P.act(T1[p].v, KQ[p].v, AF.Square)
            yield
            P.matmul(ps_a[:, p * W:(p + 1) * W], bones, T1[p].v)
            yield
            P.act(T2[p].v, ps_a[:, p * W:(p + 1) * W], AF.Sqrt, bias=epsb[:, 1:2])
            yield
            P.recip(T2[p].v, T2[p].v)
            yield
            P.tt("pool", KK[p].v, KQ[p].v, T2[p].v, ALU.mult)
            yield
            P.act(T1[p].v, Ab[p].v, AF.Identity, scale=sm[:, S_KA + p:S_KA + p + 1], bias=omka(p))
            yield
            P.tt("pool", KD[p].v, Kb[p].v, T1[p].v, ALU.mult)
            yield
            if d == 1:
                P.matmul(ps_a[:, p * W:(p + 1) * W], a2[0:64, pc], LA[0:64, :])
                yield
                P.act(T2[p].v, ps_a[:, p * W:(p + 1) * W], AF.Sigmoid, bias=sm[:, S_A0 + p:S_A0 + p + 1])
                yield
                P.act(T2[p].v, T2[p].v, AF.Identity, scale=sm[:, S_KA + p:S_KA + p + 1], bias=omka(p))
                yield
                P.tt("pool", KD0[p].v, Kb[p].v, T2[p].v, ALU.mult)
                yield
            for ch in range(4):
                sl = slice(ch * 64, (ch + 1) * 64)
                P.scan(PRE[p][:, sl], ones64, LOGW[p][:, sl], 0.0, ALU.mult, ALU.add)
                yield
            if d == 0:
                clb = PRE[p]
            else:
                clb = CL[p]
                for ch in range(4):
                    sl = slice(ch * 64, (ch + 1) * 64)
                    P.act(CL[p][:, sl], PRE[p][:, sl], AF.Identity, scale=-1.0,
                          bias=PRE[p][:, ch * 64 + 63:ch * 64 + 64])
                    yield
                P.tt("pool", CL[p].v, CL[p].v, LOGW[p].v, ALU.add)
                yield
            P.act(E1[p].v, clb.v, AF.Exp)
            yield
            P.act(E2[p].v, clb.v, AF.Exp, scale=-1.0)
            yield
            P.tt("pool", T1[p].v, clb.v, LOGW[p].v, ALU.subtract)
            yield
            P.act(E3[p].v, T1[p].v, AF.Exp)
            yield
            P.stt(RR(AT[p].v), KK[p].v, -1.0, E3[p].v, ALU.mult, ALU.mult)
            yield
            P.tt("dve", RR(RT[p].v), Rb[p].v, E1[p].v, ALU.mult)
            yield
            P.tt("pool", T1[p].v, KK[p].v, Ab[p].v, ALU.mult)
            yield
            P.tt("pool", BT[p].v, T1[p].v, E2[p].v, ALU.mult)
            yield
            P.tt("pool", KT[p].v, KD[p].v, E2[p].v, ALU.mult)
            yield
            for ch in range(4):
                sl = slice(ch * 64, (ch + 1) * 64)
                gc = ch * 64 + 63 if d == 0 else ch * 64
                gcol = E1[p][:, gc:gc + 1]
                P.ts("dve", BH[p][:, sl], BT[p][:, sl], gcol, None, ALU.mult)
                yield
                P.act(KH[p][:, sl], KT[p][:, sl], AF.Identity, scale=gcol)
                yield
                P.ts("dve", DGd[p][:, ch * 128:(ch + 1) * 128], ident, gcol, None, ALU.mult)
                yield
            for h in range(2):
                hp = slice(64 * h, 64 * h + 64)
                for tl2 in range(2):
                    q = (tl2 * 2 + h) * 128
                    P.copy("dve", RR(BTbd[p][hp, q:q + 128]), BT[p][hp, tl2 * 128:(tl2 + 1) * 128])
                    yield
                    P.copy("act", RR(KTbd[p][hp, q:q + 128]), KT[p][hp, tl2 * 128:(tl2 + 1) * 128])
                    yield

        gens = [derive(p) for p in range(2)]
        while gens:
            for g in list(gens):
                try:
                    next(g)
                except StopIteration:
                    gens.remove(g)

    def stage_b(p, tl, d, sw_state, upto=99):
        m1, m2 = masks[d]
        cs = slice(tl * 128, (tl + 1) * 128)
        pc = psC[p]
        pb = psB[p]
        btbd = lambda h: BTbd[p][:, (tl * 2 + h) * 128:(tl * 2 + h + 1) * 128]
        ktbd = lambda h: KTbd[p][:, (tl * 2 + h) * 128:(tl * 2 + h + 1) * 128]
        P.matmul(pc[:, 0:256], RR(AT[p][:, cs]), RR(BTbd[p][:, tl * 256:(tl + 1) * 256]))
        P.tt("dve", RR(Lm[p].v), pc[:, 0:256], m1, ALU.mult)
        yield
        for h in range(2):
            P.matmul(pb[:, h * 256:h * 256 + 128], RR(btbd(h)), RR(AT[p][:, cs]))
            P.matmul(pb[:, h * 256 + 128:h * 256 + 256], RR(btbd(h)), RR(RT[p][:, cs]))
        P.tt("dve", RR(NM[p].v), pb[:, 0:512], m2, ALU.mult)
        yield
        for h in range(2):
            P.matmul(pb[:, h * 256:h * 256 + 128], RR(ktbd(h)), RR(AT[p][:, cs]))
            P.matmul(pb[:, h * 256 + 128:h * 256 + 256], RR(ktbd(h)), RR(RT[p][:, cs]))
        P.tt("dve", RR(KM[p].v), pb[:, 0:512], m2, ALU.mult)
        yield
        if upto <= 1:
            return
        X = Xk[0][p]
        for h in range(2):
            P.tt("dve", RR(HH(X, h)), NMa(p, h), ident, ALU.add)
        Lc = Lm[p]
        Nc_views = [NMa(p, h) for h in range(2)]
        xi = 0
        for k in range(1, 6):
            Ln = Lk[k % 2][p]
            for h in range(2):
                P.matmul(pc[:, h * 128:(h + 1) * 128], RR(Nc_views[h]), RR(HH(Lc, h)))
            P.copy("act", RR(Ln.v), pc[:, 0:256])
            yield
            if k < 5:
                Nn = Nk[k % 2][p]
                for h in range(2):
                    P.matmul(pc[:, 256 + h * 128:256 + (h + 1) * 128], RR(HH(Lc, h)), RR(Nc_views[h]))
                P.copy("act", RR(Nn.v), pc[:, 256:512])
            for h in range(2):
                P.matmul(pb[:, h * 128:(h + 1) * 128], RR(HH(Ln, h)), RR(HH(Xk[xi][p], h)))
            Xn = Xk[1 - xi][p]
            P.tt("dve", RR(Xn.v), pb[:, 0:256], Xk[xi][p].v, ALU.add)
            yield
            xi = 1 - xi
            Lc = Ln
            if k < 5:
                Nc_views = [HH(Nn, h) for h in range(2)]
        X = Xk[xi][p]
        if upto <= 2:
            return
        P.transpose(pc[:, 0:128], AT[p][:, cs], ident)
        P.copy("act", RR(Zb[p][:, 0:128]), pc[:, 0:128])
        yield
        P.transpose(pc[:, 128:256], Vb[p][:, cs], ident)
        P.copy("dve", RR(VT[p].v), pc[:, 128:256])
        yield
        P.transpose(pc[:, 256:384], BH[p][:, cs], ident)
        P.copy("act", RR(BHTc[0][p][0:64, :]), pc[0:64, 256:384])
        yield
        P.copy("dve", RR(BHTc[1][p][64:128, :]), pc[64:128, 256:384])
        yield
        P.transpose(pc[:, 384:512], KH[p][:, cs], ident)
        P.copy("act", KHTc[0][p][0:64, :], pc[0:64, 384:512])
        yield
        P.copy("dve", KHTc[1][p][64:128, :], pc[64:128, 384:512])
        yield
        for h in range(2):
            P.matmul(pb[:, h * 64:(h + 1) * 64], RR(KMa(p, h)), RR(VT[p][:, h * 64:(h + 1) * 64]))
        P.copy("act", RR(Zb[p][:, 128:256]), pb[:, 0:128])
        yield
        for part in range(2):
            for h in range(2):
                q = (part * 2 + h) * 64
                P.matmul(pb[:, 128 + q:128 + q + 64], RR(HH(X, h)), RR(Zb[p][:, q:q + 64]))
        P.copy("dve", RR(TZ[p].v), pb[:, 128:384])
        yield
        if upto <= 3:
            return
        for h in range(2):
            P.matmul(pc[:, h * 128:(h + 1) * 128], RR(TZ[p][:, 0:128]), RR(NMb(p, h)))
        for h in range(2):
            hp = slice(64 * h, 64 * h + 64)
            P.tt("dve", RPT[p][hp, :], pc[hp, h * 128:(h + 1) * 128], RT[p][hp, cs], ALU.add)
            yield
        P.copy("pool", RPTm[0][p][:, 0:64], RPT[p][:, 0:64])
        P.copy("pool", RPTm[1][p][:, 64:128], RPT[p][:, 64:128])
        for c in range(2):
            P.matmul(pc[:, 256 + c * 128:256 + (c + 1) * 128], RR(TZ[p][:, 0:128]), RR(BHTc[c][p].v))
        for c in range(2):
            ch = 2 * tl + c
            P.tt("dve", T3[p].v, pc[:, 256 + c * 128:256 + (c + 1) * 128], bones, ALU.mult)
            yield
            P.tt("pool", PTbd[c][p].v, T3[p].v, DGd[p][:, ch * 128:(ch + 1) * 128], ALU.add)
        if upto <= 5:
            return
        order = (0, 1) if d == 0 else (1, 0)
        s_at = {}
        for c in order:
            si = sw_state[p]
            S_in = ST[si][p]
            S_out = ST[(si + 1) % 3][p]
            s_at[c] = S_in
            P.matmul(pb[:, 0:128], PTbd[c][p].v, S_in.v, start=True, stop=False)
            P.matmul(pb[:, 0:128], BHTc[c][p].v, TZ[p][:, 128:256], start=False, stop=False)
            P.matmul(pb[:, 0:128], KHTc[c][p].v, VT[p].v, start=False, stop=True)
            P.tt("dve", S_out.v, pb[:, 0:128], bones, ALU.mult)
            yield
            sw_state[p] = (si + 1) % 3
        if upto <= 6:
            return
        P.matmul(pb[:, 128:256], RPTm[0][p].v, s_at[0].v, start=True, stop=False)
        P.matmul(pb[:, 128:256], RPTm[1][p].v, s_at[1].v, start=False, stop=False)
        for h in range(2):
            P.matmul(pb[:, 128 + h * 64:128 + (h + 1) * 64], RR(NMb(p, h)), RR(TZ[p][:, 128 + h * 64:128 + (h + 1) * 64]),
                     start=False, stop=False)
            P.matmul(pb[:, 128 + h * 64:128 + (h + 1) * 64], RR(KMb(p, h)), RR(VT[p][:, h * 64:(h + 1) * 64]),
                     start=False, stop=(h == 1))
        P.copy("act", OTK[p].v, pb[:, 128:256])
        yield
        P.transpose(pb[:, 256:384], OTK[p].v, ident)
        if d == 0:
            P.copy("dve", OB[p][:, cs], pb[:, 256:384])
            yield
        else:
            P.tt("dve", OB[p][:, cs], pb[:, 256:384], OF[p][:, cs], ALU.add)
            yield

    def readout(blk_id):
        t0 = tok0(blk_id)
        for p in range(2):
            P.matmul(ps_a[:, 0:W], bones, OB[p].v)
            P.stt(T1[p].v, ps_a[:, 0:W], -1.0 / 64, OB[p].v, ALU.mult, ALU.add)
            P.act(T2[p].v, T1[p].v, AF.Square)
            P.matmul(ps_a[:, W:2 * W], bones, T2[p].v)
            P.act(T2[p].v, ps_a[:, W:2 * W], AF.Sqrt, scale=1.0 / 64, bias=epsb[:, 2:3])
            P.recip(T2[p].v, T2[p].v)
            P.tt("pool", T1[p].v, T1[p].v, T2[p].v, ALU.mult)
            P.act(T1[p].v, T1[p].v, AF.Identity, scale=sm[:, S_GG + p:S_GG + p + 1],
                  bias=sm[:, S_GB + p:S_GB + p + 1])
            P.tt("pool", T2[p].v, KD[p].v, KD0[p].v, ALU.add)
            P.stt(T2[p].v, Rb[p].v, sm[:, S_RK + p:S_RK + p + 1], T2[p].v, ALU.mult, ALU.mult)
            P.matmul(ps_a[:, 0:W], bones, T2[p].v)
            P.tt("dve", T2[p].v, ps_a[:, 0:W], Vb[p].v, ALU.mult)
            P.tt("pool", T1[p].v, T1[p].v, T2[p].v, ALU.add)
            P.tt("pool", YB[p].v, T1[p].v, SG[p].v, ALU.mult)
            if isinstance(y0_d, list):
                pi, off = (0, t0) if t0 < TC else (1 + (t0 - TC) // 2048, (t0 - TC) % 2048)
                P.dma(y0_d[pi][p * 128:(p + 1) * 128, off:off + W], YB[p].v)
            else:
                P.dma(y0_d[p * 128:(p + 1) * 128, t0:t0 + W], YB[p].v)

    for p in range(2):
        P.memset("pool", ST[0][p].v, 0.0)
    for d in range(2):
        blocks = [-1] + (list(range(NBLK_L)) if d == 0 else list(range(NBLK_L - 1, -1, -1)))
        if debug_out and isinstance(debug_out, int) and debug_out > 1:
            blocks = blocks[:debug_out]
        sw_state = [0, 0]
        if d == 1:
            for p in range(2):
                P.memset("pool", ST[0][p].v, 0.0)
        issue_x(blocks[0], 0)
        for bi, b in enumerate(blocks):
            slot = bi % 2
            if bi + 1 < len(blocks):
                issue_x(blocks[bi + 1], 1 - slot)
            t0 = tok0(b)
            if d == 1:
                for p in range(2):
                    P.dma(OF[p].v, of_d[p * 128:(p + 1) * 128, t0:t0 + W])
            stage_a(b, slot, d)
            if stop == "a":
                return dbg_stop([(Rb[0].v, 256), (KK[1].v, 256), (LOGW[0].v, 256), (Ab[1].v, 256), (KD[0].v, 256),
                                 (AT[0].v, 256), (RT[0].v, 256), (BT[0].v, 256), (KT[0].v, 256), (BH[0].v, 256),
                                 (DG[0].v, 256), (Vb[1].v, 256)])
            tiles = (0, 1) if d == 0 else (1, 0)
            for tl in tiles:
                if not (stop and stop[0] == "b"):
                    gens = [stage_b(p, tl, d, sw_state) for p in range(2)]
                    while gens:
                        for g in list(gens):
                            try:
                                next(g)
                            except StopIteration:
                                gens.remove(g)
                    continue
                for p in range(2):
                    for _ in stage_b(p, tl, d, sw_state, upto=int(stop[1:]) if (stop and stop[0] == "b" and len(stop) > 1) else 99):
                        pass
                    if stop and stop[0] == "b":
                        return dbg_stop([(OB[0][:, 0:128], 128), (ST[sw_state[0]][0].v, 128), (TZ[0].v, 256),
                                         (Lm[0].v, 256), (NM[0].v, 512), (Xk[1][0].v, 256), (RPT[0].v, 128), (PTbd[0][0].v, 128)])
            if d == 0:
                for p in range(2):
                    P.dma(of_d[p * 128:(p + 1) * 128, t0:t0 + W], OB[p].v)
            else:
                readout(b)
    if ctx is None:
        P.emit()
    return nc


def l0_inputs(inp, b, hg):
    f32 = np.float32
    x, ctx = inp["x"], inp["ctx"]
    z1 = np.zeros((D, 1), f32)
    xa = np.concatenate([z1, ctx[b].T, z1, z1, x[b].T, z1], axis=1)
    ct = np.zeros((128, 16), f32)
    cb = inp["c"][b].reshape(8, 128).T
    cc = inp["c_ctx"].reshape(8, 128).T
    ct[:, 0::2] = cb
    ct[:, 1::2] = cc
    adaw = np.ascontiguousarray(inp["ada_w"][0][:, 0:2048])
    hc = slice(hg * 256, (hg + 1) * 256)
    rw_in = inp["rw_in"][0]
    cols = []
    for X in range(4):
        cols.append(rw_in[:, X * 1024 + hg * 256: X * 1024 + (hg + 1) * 256])
    cols.append(rw_in[:, 4096:4352])
    win = np.ascontiguousarray(np.concatenate(cols, axis=1))
    sm = np.zeros((128, NSMALL), f32)
    sm[:, 0:8] = inp["norm_g"][0].reshape(8, 128).T
    sm[:, 8:24] = inp["ada_b"][0][:2048].reshape(16, 128).T
    mu = inp["rw_mu"][0]
    mus = []
    for X in range(3):
        for p in range(2):
            mus.append(mu[X * 1024 + hg * 256 + p * 128: X * 1024 + hg * 256 + (p + 1) * 128])
    mus.append(mu[3072:3200])
    mus.append(mu[3200:3328])
    sm[:, 24:32] = np.stack(mus, axis=1)
    for d in range(2):
        for p in range(2):
            sm[:, 32 + 2 * d + p] = inp["rw_w0"][0][d, hg * 256 + p * 128: hg * 256 + (p + 1) * 128]
            sm[:, 36 + 2 * d + p] = inp["rw_a0"][0][d, hg * 256 + p * 128: hg * 256 + (p + 1) * 128]
    for p in range(2):
        sl = slice(hg * 256 + p * 128, hg * 256 + (p + 1) * 128)
        sm[:, 40 + p] = inp["rw_kk"][0][sl]
        sm[:, 42 + p] = inp["rw_ka"][0][sl]
        sm[:, 44 + p] = inp["rw_rk"][0].reshape(-1)[sl]
        sm[:, 46 + p] = inp["rw_gn_g"][0][sl]
        sm[:, 48 + p] = inp["rw_gn_b"][0][sl]
    w2 = np.ascontiguousarray(inp["rw_w2"][0][:, :, hc].reshape(128, 256))
    a2 = np.ascontiguousarray(inp["rw_a2"][0][:, :, hc].reshape(128, 256))
    return {"xa": np.ascontiguousarray(xa), "ct": ct, "adaw": adaw, "smalls": sm, "win": win,
            "w2": w2, "a2": a2, "cst": l0_consts()}


SUBLN_EPS = 1e-5
LAM_INIT = 0.8 - 0.6 * float(np.exp(-0.3 * 1))
QSCALE = 0.125
NKT = TA // 128
NQB = TL // 256


def tok0(bi):
    return 0 if bi < 0 else TC + 256 * bi


def mod_compute(P, adaw_d, ncol, sct, adab_view, ps, modT, adaw_sb):
    adaw_r = adaw_d.t[:].rearrange("(k p) m -> p k m", p=128)
    for piece in range(ncol // 256):
        P.dma(adaw_sb.v, View(adaw_d, adaw_r[:, :, piece * 256:(piece + 1) * 256]))
        for mcl in range(2):
            mc = piece * 2 + mcl
            for kc in range(8):
                P.matmul(ps[:, 0:2], adaw_sb[:, kc, mcl * 128:(mcl + 1) * 128], sct[:, kc * 2:kc * 2 + 2],
                         start=(kc == 0), stop=(kc == 7))
            P.ts("dve", modT[:, mc, :], ps[:, 0:2], adab_view(mc), None, ALU.add)


def rope_tables():
    rows = TL // 64
    t = np.arange(TL)
    row = (t // 64).astype(np.float32)
    colid = (t % 64).astype(np.float32)
    inv = (10000.0 ** (-np.arange(16, dtype=np.float32) / 16)).astype(np.float32)
    ang_r = row[None, :] * inv[:, None]
    ang_c = colid[None, :] * inv[:, None]
    cos64 = np.concatenate([np.cos(ang_r), np.cos(ang_r), np.cos(ang_c), np.cos(ang_c)], axis=0)
    sin64 = np.concatenate([np.sin(ang_r), np.sin(ang_r), np.sin(ang_c), np.sin(ang_c)], axis=0)
    cosT = np.concatenate([cos64, cos64], axis=0).astype(np.float32)
    sinT = np.concatenate([sin64, sin64], axis=0).astype(np.float32)
    R = np.zeros((128, 128), np.float32)
    for base in range(0, 128, 32):
        for f in range(16):
            R[base + 16 + f, base + f] = -1.0
            R[base + f, base + 16 + f] = 1.0
    return cosT, sinT, R


def build_l1(stop=None, ctx=None):
    if ctx is None:
        nc = bass.Bass("TRN2", target_bir_lowering=False)
        P = Prog(nc)
        xa_d = P.dram_in("xa", [D, TA], F32)
        y0_d = P.dram_in("y0g", [D, TA], BF16)
        ct_d = P.dram_in("ct", [128, 16], F32)
        adaw0_d = P.dram_in("adaw0g", [D, 1024], F32)
        adaw1_d = P.dram_in("adaw1", [D, 2048], F32)
        sm_d = P.dram_in("smalls", [128, 48], F32)
        wo_d = P.dram_in("wo", [D, D], F32)
        wd_d = P.dram_in("wd", [D, 1024], F32)
        cst_d = P.dram_in("cst", [128, 384], F32)
        cos_d = P.dram_in("cosT", [128, TL], F32)
        sin_d = P.dram_in("sinT", [128, TL], F32)
        lam_d = P.dram_in("lamv", [128, 256], F32)
        subw_d = P.dram_in("subw", [128, 128], F32)
        y1_d = P.dram_out("y1", [256, TL], BF16)
        xn_d = P.dram_out("xn", [D, TL], F32)
    else:
        nc, P = ctx["nc"], ctx["P"]
        (xa_d, y0_d, ct_d, adaw0_d, adaw1_d, sm_d, wo_d, wd_d, cst_d, cos_d, sin_d, lam_d, subw_d, y1_d, xn_d) = [
            ctx[k] for k in ("b_xa", "y0g", "ct", "b_adaw0g", "b_adaw1", "b_smalls", "b_wo", "b_wd", "b_cst",
                             "b_cosT", "b_sinT", "b_lamv", "b_subw", "y1loc", "xn_s")]

    cst = P.sb("cst", [128, 384], F32)
    P.dma(cst.v, cst_d.v)
    ident = cst[:, 0:128]
    bones = cst[:, 128:256]
    rrot = cst[:, 256:384]
    sm = P.sb("sm", [128, 48], F32)
    P.dma(sm.v, sm_d.v)
    S_NG, S_B0, S_B1, S_QN, S_KN = 0, 8, 16, 32, 33
    ones128 = P.sb("ones128", [128, 128], F32)
    P.memset("pool", ones128.v, 1.0)
    epsb = P.sb("epsb", [128, 4], F32)
    P.memset("pool", epsb[:, 0:1], RMS_EPS)
    P.memset("pool", epsb[:, 1:2], SUBLN_EPS)
    banks = [P.ps("bank%d" % i, [128, 512]) for i in range(8)]

    ct = P.sb("ct", [128, 16], F32)
    P.dma(ct.v, ct_d.v)
    sct = P.sb("sct", [128, 16], F32)
    P.act(sct.v, ct.v, AF.Silu)
    adaw_sb = P.sb("adaw_sb", [128, 8, 256], F32)
    mod0 = P.sb("mod0", [128, 8, 2], F32)
    mod1 = P.sb("mod1", [128, 16, 2], F32)
    mod_compute(P, adaw0_d, 1024, sct, lambda mc: sm[:, S_B0 + mc:S_B0 + mc + 1], banks[0], mod0, adaw_sb)
    mod_compute(P, adaw1_d, 2048, sct, lambda mc: sm[:, S_B1 + mc:S_B1 + mc + 1], banks[0], mod1, adaw_sb)
    gmod = P.sb("gmod", [128, 8, 2], F32)
    for kc in range(8):
        P.ts("pool", gmod[:, kc, :], mod1[:, 8 + kc, :], 1.0, sm[:, S_NG + kc:S_NG + kc + 1], ALU.add, ALU.mult)

    lamv = P.sb("lamv", [128, 256], F32)
    P.dma(lamv.v, lam_d.v)
    lt = P.sb("lt", [128, 128], F32)
    lsc = P.sb("lsc", [128, 8], F32)
    P.tt("pool", lt[:, 0:64], lamv[:, 0:64], lamv[:, 64:128], ALU.mult)
    P.tt("pool", lt[:, 64:128], lamv[:, 128:192], lamv[:, 192:256], ALU.mult)
    P.reduce(lsc[:, 0:1], lt[:, 0:64], AX.X, ALU.add)
    P.reduce(lsc[:, 1:2], lt[:, 64:128], AX.X, ALU.add)
    P.act(lsc[:, 2:4], lsc[:, 0:2], AF.Exp)
    P.tt("pool", lsc[:, 4:5], lsc[:, 2:3], lsc[:, 3:4], ALU.subtract)
    P.ts("pool", lsc[:, 5:6], lsc[:, 4:5], LAM_INIT, -1.0, ALU.add, ALU.mult)
    neglam = lsc[:, 5:6]
    subw = P.sb("subw", [128, 128], F32)
    P.dma(subw.v, subw_d.v)
    P.ts("pool", subw.v, subw.v, 1.0 - LAM_INIT, None, ALU.mult)

    Wo = P.sb("Wo", [128, 8, 1024], BF16)
    Wd = P.sb("Wd", [128, 8, 1024], BF16)
    wst = [P.sb("wst%d" % i, [128, 1024], F32) for i in range(1)] * 2
    n = 0
    for src, dst in ((wo_d, Wo), (wd_d, Wd)):
        for kc in range(8):
            P.dma(wst[n % 2].v, src[kc * 128:(kc + 1) * 128, :])
            P.copy("pool", dst[:, kc, :], wst[n % 2].v)
            n += 1

    xin = [P.sb("xin%d" % i, [128, 8, W], F32) for i in range(2)]
    yin = [P.sb("yin%d" % i, [128, 8, W], BF16) for i in range(2)]
    xn = P.sb("xn", [128, 8, W], F32)
    hT = P.sb("hT", [128, 8, W], BF16)
    sqb = [P.sb("sqb%d" % i, [128, W], F32) for i in range(2)]
    rstd = P.sb("rstd", [128, W], F32)
    htmp = [P.sb("htmp%d" % i, [128, W], F32) for i in range(2)]
    cosb = P.sb("cosb", [128, W], F32)
    sinb = P.sb("sinb", [128, W], F32)
    qraw = P.sb("qraw", [128, W], F32)
    qsq = P.sb("qsq", [128, W], F32)
    qrs = P.sb("qrs", [128, W], F32)
    qn_ = P.sb("qn_", [128, W], F32)
    qt1 = P.sb("qt1", [128, W], F32)
    qt2 = P.sb("qt2", [128, W], F32)
    KTb = P.sb("KTb", [128, TA], BF16)
    QTb = P.sb("QTb", [128, NQB * 512], BF16)
    P.memset("pool", QTb.v, 0.0)
    Vx = P.sb("Vx", [128, NKT, 130], BF16)
    P.memset("pool", Vx[:, :, 129:130], 0.0)
    Gs = P.sb("Gs", [128, TL // 128, 128], BF16)
    P.memset("pool", Vx[:, :, 128:129], 1.0)
    pT = [P.sb("pT%d" % i, [128, 512], BF16) for i in range(2)]
    vg_sb = [P.sb("vg_sb%d" % i, [128, 512], F32) for i in range(2)]
    o_sb = P.sb("o_sb", [128, 128], F32)
    o_sq = P.sb("o_sq", [128, 128], F32)
    ybs = [P.sb("yb%d" % i, [128, 256], BF16) for i in range(2)]
    zs = P.sb("zs", [128, 8], F32)
    ps_bf = P.ps("ps_bf_unused", [128, 2], F32) if False else None

    xa_r = xa_d.t[:].rearrange("(k p) c -> p k c", p=128)
    y0_r = None if isinstance(y0_d, list) else y0_d.t[:].rearrange("(k p) c -> p k c", p=128)
    xn_r = xn_d.t[:].rearrange("(k p) c -> p k c", p=128)

    def issue(bi, slot, first_pass=True):
        t0 = tok0(bi)
        if (not first_pass) and bi >= 0:
            return
        P.dma(xin[slot].v, View(xa_d, xa_r[:, :, t0:t0 + W]))
        if isinstance(y0_d, list):
            pi, off = (0, t0) if t0 < TC else (1 + (t0 - TC) // 2048, (t0 - TC) % 2048)
            yr = y0_d[pi].t[:].rearrange("(k p) c -> p k c", p=128)
            P.dma(yin[slot].v, View(y0_d[pi], yr[:, :, off:off + W]))
        else:
            P.dma(yin[slot].v, View(y0_d, y0_r[:, :, t0:t0 + W]))

    def stage_a(bi, slot, h, hp_quarter, first_pass):
        col = 1 if bi < 0 else 0
        t0 = tok0(bi)
        lat0 = t0 - TC
        reuse = (not first_pass) and bi >= 0
        if reuse:
            P.dma(xn.v, View(xn_d, xn_r[:, :, lat0:lat0 + W]))
        for oc in range(0 if reuse else 8):
            pp = banks[oc % 2]
            for kc in range(8):
                P.matmul(pp[:, 0:W], Wo[:, kc, oc * 128:(oc + 1) * 128], yin[slot][:, kc, :],
                         start=(kc == 0), stop=(kc == 7))
            P.stt(xn[:, oc, :], pp[:, 0:W], mod0[:, oc, col:col + 1], xin[slot][:, oc, :], ALU.mult, ALU.add)
        if first_pass and bi >= 0 and stop != 'noxn':
            P.dma(View(xn_d, xn_r[:, :, lat0:lat0 + W]), xn.v)
        if stop == 'a1':
            return
        for kc in range(8):
            sq = sqb[kc % 2]
            P.act(sq.v, xn[:, kc, :], AF.Square)
            P.matmul(banks[2][:, 0:W], ones128.v, sq.v, start=(kc == 0), stop=(kc == 7))
        P.act(rstd.v, banks[2][:, 0:W], AF.Sqrt, scale=1.0 / D, bias=epsb[:, 0:1])
        P.recip(rstd.v, rstd.v)
        for kc in range(8):
            tmp = htmp[kc % 2]
            P.stt(tmp.v, xn[:, kc, :], gmod[:, kc, col:col + 1], rstd.v, ALU.mult, ALU.mult)
            P.act(hT[:, kc, :], tmp.v, AF.Identity, bias=mod1[:, kc, col:col + 1])
        if stop == 'a2':
            return
        if bi >= 0:
            P.dma(cosb.v, cos_d[:, lat0:lat0 + W])
            P.dma(sinb.v, sin_d[:, lat0:lat0 + W])
        for which in (("q", "k") if bi >= 0 else ("k",)):
            cc = h if which == "q" else 2 + h
            pp = banks[3]
            for kc in range(8):
                P.matmul(pp[:, 0:W], Wd[:, kc, cc * 128:(cc + 1) * 128], hT[:, kc, :],
                         start=(kc == 0), stop=(kc == 7))
            P.copy("act", qraw.v, pp[:, 0:W])
            P.tt("pool", qsq.v, qraw.v, qraw.v, ALU.mult)
            P.matmul(banks[4][:, 0:W], bones, qsq.v)
            P.act(qrs.v, banks[4][:, 0:W], AF.Sqrt, scale=1.0 / 64, bias=epsb[:, 0:1])
            P.recip(qrs.v, qrs.v)
            wcol = sm[:, S_QN:S_QN + 1] if which == "q" else sm[:, S_KN:S_KN + 1]
            P.stt(qn_.v, qraw.v, wcol, qrs.v, ALU.mult, ALU.mult)
            if bi >= 0:
                P.matmul(banks[4][:, W:2 * W], rrot, qn_.v)
                P.tt("dve", qt1.v, banks[4][:, W:2 * W], sinb.v, ALU.mult)
                P.tt("pool", qt2.v, qn_.v, cosb.v, ALU.mult)
                if which == "q":
                    qb_ = lat0 // 256
                    P.tt("pool", QTb[0:64, qb_ * 512:qb_ * 512 + 256], qt1[0:64, :], qt2[0:64, :], ALU.add)
                    P.tt("pool", QTb[64:128, qb_ * 512 + 256:qb_ * 512 + 512], qt1[64:128, :], qt2[64:128, :], ALU.add)
                else:
                    P.tt("pool", KTb[:, t0:t0 + W], qt1.v, qt2.v, ALU.add)
            else:
                P.copy("pool", KTb[:, t0:t0 + W], qn_.v)
        if stop == 'a3':
            return
        for sub in range(2):
            pp = banks[5 + sub]
            for kc in range(8):
                P.matmul(pp[:, 0:512], hT[:, kc, sub * 128:(sub + 1) * 128], Wd[:, kc, 512:1024],
                         start=(kc == 0), stop=(kc == 7))
            kt = (t0 + sub * 128) // 128
            vg = vg_sb[sub]
            P.copy("dve", vg.v, pp[:, 0:512])
            P.copy("pool", Vx[:, kt, 0:128], vg[:, h * 128:(h + 1) * 128])
            if bi >= 0:
                qt = (lat0 + sub * 128) // 128
                P.act(Gs[:, qt, :], vg[:, 256 + h * 128:256 + (h + 1) * 128], AF.Silu)

    def attention(h, nqb=NQB):
        sbank = [(banks[0], banks[1]), (banks[2], banks[3])]
        acc = [[banks[4], banks[5]], [banks[6], banks[7]]]
        n = 0

        def s_mm(qb_, kt_, n_):
            P.matmul(banks[n_ % 4][:, 0:512], KTb[:, kt_ * 128:(kt_ + 1) * 128], QTb[:, qb_ * 512:(qb_ + 1) * 512])

        s_mm(0, 0, 0)
        for qb in range(nqb):
            q0 = qb * 256
            for kt in range(NKT):
                sA = banks[n % 4]
                pt = pT[n % 2]
                if kt + 1 < NKT:
                    s_mm(qb, kt + 1, n + 1)
                elif qb + 1 < nqb:
                    s_mm(qb + 1, 0, n + 1)
                P.act(pt.v, sA[:, 0:512], AF.Exp, scale=QSCALE)
                if stop == 's1':
                    n += 1
                    continue
                for comp in range(2):
                    for qs in range(2):
                        P.matmul(acc[comp][qs][:, 0:130], pt[:, comp * 256 + qs * 128:comp * 256 + (qs + 1) * 128],
                                 Vx[:, kt, :], start=(kt == 0), stop=(kt == NKT - 1))
                n += 1
            if stop in ('s1', 's2'):
                continue
            ytile = ybs[qb % 2]
            for qs in range(2):
                a0 = acc[0][qs]
                a1 = acc[1][qs]
                P.recip(zs[:, 0:1], a0[:, 128:129])
                P.recip(zs[:, 1:2], a1[:, 128:129])
                P.tt("pool", zs[:, 2:3], zs[:, 1:2], neglam, ALU.mult)
                P.ts("dve", o_sb.v, a0[:, 0:128], zs[:, 0:1], None, ALU.mult)
                P.stt(o_sb.v, a1[:, 0:128], zs[:, 2:3], o_sb.v, ALU.mult, ALU.add)
                P.tt("pool", o_sq.v, o_sb.v, o_sb.v, ALU.mult)
                P.reduce(zs[:, 3:4], o_sq.v, AX.X, ALU.add)
                P.act(zs[:, 4:5], zs[:, 3:4], AF.Sqrt, scale=1.0 / 128, bias=epsb[:, 1:2])
                P.recip(zs[:, 4:5], zs[:, 4:5])
                P.stt(o_sb.v, o_sb.v, zs[:, 4:5], subw.v, ALU.mult, ALU.mult)
                qt = (q0 + qs * 128) // 128
                P.tt("pool", o_sq.v, o_sb.v, Gs[:, qt, :], ALU.mult)
                P.transpose(a0[:, 256:384], o_sq.v, ident)
                P.copy("act", ytile[:, qs * 128:(qs + 1) * 128], a0[:, 256:384])
            if isinstance(y1_d, list):
                P.dma(y1_d[q0 // 2048][h * 128:(h + 1) * 128, q0 % 2048:q0 % 2048 + 256], ytile.v)
            else:
                P.dma(y1_d[h * 128:(h + 1) * 128, q0:q0 + 256], ytile.v)

    return nc, P, issue, stage_a, attention


def build_l1_full(hp_quarter, do_attn=True, nheads=2, stop=None, nblocks=None, nqb=NQB, ctx=None):
    nc, P, issue, stage_a, attention = build_l1(stop, ctx)
    blocks = [-1] + list(range(NBLK_L))
    if nblocks:
        blocks = blocks[:nblocks]
    if stop == 'pre':
        P.emit()
        return nc
    for h in range(nheads):
        issue(blocks[0], 0, h == 0)
        for i, bi in enumerate(blocks):
            if i + 1 < len(blocks):
                issue(blocks[i + 1], (i + 1) % 2, h == 0)
            stage_a(bi, i % 2, h, hp_quarter, h == 0)
        if do_attn:
            attention(h, nqb)
    if ctx is None:
        P.emit()
    return nc


def l1_inputs(inp, b, hp, y0g_b):
    f32 = np.float32
    xa = np.ascontiguousarray(np.concatenate([inp["ctx"][b].T, inp["x"][b].T], axis=1))
    ct = np.zeros((128, 16), f32)
    ct[:, 0::2] = inp["c"][b].reshape(8, 128).T
    ct[:, 1::2] = inp["c_ctx"].reshape(8, 128).T
    sm = np.zeros((128, 48), f32)
    sm[:, 0:8] = inp["norm_g"][1].reshape(8, 128).T
    sm[:, 8:16] = inp["ada_b"][0][2048:3072].reshape(8, 128).T
    sm[:, 16:32] = inp["ada_b"][1][0:2048].reshape(16, 128).T
    sm[:, 32] = np.tile(inp["da_qn"][0], 2)
    sm[:, 33] = np.tile(inp["da_kn"][0], 2)
    da_in = inp["da_in"][0]
    cols = []
    for X in range(4):
        for hh in range(2):
            base = X * 1024 + (hp * 2 + hh) * 128
            cols.append(da_in[:, base:base + 128])
    wd = np.ascontiguousarray(np.concatenate(cols, axis=1))
    cosT, sinT, R = rope_tables()
    ident = np.eye(128, dtype=f32)
    bones = np.kron(np.eye(2, dtype=f32), np.ones((64, 64), f32))
    cst = np.ascontiguousarray(np.concatenate([ident, bones, R], axis=1))
    lamv = np.ascontiguousarray(np.broadcast_to(inp["da_lam"][0].reshape(1, 256), (128, 256))).astype(f32)
    subw = np.ascontiguousarray(np.broadcast_to(inp["da_subln"][0].reshape(1, 128), (128, 128))).astype(f32)
    return {"xa": xa, "y0g": y0g_b, "ct": ct,
            "adaw0g": np.ascontiguousarray(inp["ada_w"][0][:, 2048:3072]),
            "adaw1": np.ascontiguousarray(inp["ada_w"][1][:, 0:2048]),
            "smalls": sm, "wo": np.ascontiguousarray(inp["rw_out"][0]), "wd": wd, "cst": cst,
            "cosT": cosT, "sinT": sinT, "lamv": lamv, "subw": subw}


def build_l2(ctx=None):
    if ctx is None:
        nc = bass.Bass("TRN2", target_bir_lowering=False)
        P = Prog(nc)
        NT = 2048
        xn_d = P.dram_in("xn", [D, NT], F32)
        y1_d = P.dram_in("y1T", [D, NT], BF16)
        ct_d = P.dram_in("ct", [128, 16], F32)
        adaw_d = P.dram_in("adaw1g", [D, 1024], F32)
        sm_d = P.dram_in("smalls", [128, 8], F32)
        w_d = P.dram_in("wda", [D, D], F32)
        out_d = P.dram_out("outT", [D, NT], F32)
    else:
        nc, P = ctx["nc"], ctx["P"]
        NT = TL
        xn_d, y1_d, ct_d, adaw_d, sm_d, w_d, out_d = [ctx[k] for k in (
            "xn_s", "y1g", "ct", "c_adaw1g", "c_smalls", "c_wda", "outT")]
    sm = P.sb("sm", [128, 8], F32)
    P.dma(sm.v, sm_d.v)
    banks = [P.ps("bank%d" % i, [128, 512]) for i in range(3)]
    ct = P.sb("ct", [128, 16], F32)
    P.dma(ct.v, ct_d.v)
    sct = P.sb("sct", [128, 16], F32)
    P.act(sct.v, ct.v, AF.Silu)
    adaw_sb = P.sb("adaw_sb", [128, 8, 256], F32)
    modg = P.sb("modg", [128, 8, 2], F32)
    mod_compute(P, adaw_d, 1024, sct, lambda mc: sm[:, mc:mc + 1], banks[0], modg, adaw_sb)
    Wa = P.sb("Wa", [128, 8, 1024], BF16)
    wst = [P.sb("wst%d" % i, [128, 1024], F32) for i in range(2)]
    for kc in range(8):
        P.dma(wst[kc % 2].v, w_d[kc * 128:(kc + 1) * 128, :])
        P.copy("pool", Wa[:, kc, :], wst[kc % 2].v)
    xin = [P.sb("xin%d" % i, [128, 8, W], F32) for i in range(2)]
    yin = [P.sb("yin%d" % i, [128, 8, W], BF16) for i in range(2)]
    ob = [P.sb("ob%d" % i, [128, 8, W], F32) for i in range(2)]
    xn_r = xn_d.t[:].rearrange("(k p) c -> p k c", p=128)
    y1_r = None if isinstance(y1_d, list) else y1_d.t[:].rearrange("(k p) c -> p k c", p=128)
    out_r = out_d.t[:].rearrange("(k p) c -> p k c", p=128)
    for bi in range(NT // W):
        s_ = bi % 2
        P.dma(xin[s_].v, View(xn_d, xn_r[:, :, bi * W:(bi + 1) * W]))
        if isinstance(y1_d, list):
            c0 = bi * W
            yr = y1_d[c0 // 2048].t[:].rearrange("(k p) c -> p k c", p=128)
            P.dma(yin[s_].v, View(y1_d[c0 // 2048], yr[:, :, c0 % 2048:c0 % 2048 + W]))
        else:
            P.dma(yin[s_].v, View(y1_d, y1_r[:, :, bi * W:(bi + 1) * W]))
        for oc in range(8):
            pp = banks[1 + oc % 2]
            for kc in range(8):
                P.matmul(pp[:, 0:W], Wa[:, kc, oc * 128:(oc + 1) * 128], yin[s_][:, kc, :],
                         start=(kc == 0), stop=(kc == 7))
            P.stt(ob[s_][:, oc, :], pp[:, 0:W], modg[:, oc, 0:1], xin[s_][:, oc, :], ALU.mult, ALU.add)
        P.dma(View(out_d, out_r[:, :, bi * W:(bi + 1) * W]), ob[s_].v)
    if ctx is None:
        P.emit()
    return nc


def l2_inputs(inp, b, tq, xn, y1T):
    f32 = np.float32
    ct = np.zeros((128, 16), f32)
    ct[:, 0::2] = inp["c"][b].reshape(8, 128).T
    ct[:, 1::2] = inp["c_ctx"].reshape(8, 128).T
    sm = np.ascontiguousarray(inp["ada_b"][1][2048:3072].reshape(8, 128).T).astype(f32)
    return {"xn": xn, "y1T": y1T, "ct": ct, "adaw1g": np.ascontiguousarray(inp["ada_w"][1][:, 2048:3072]),
            "smalls": sm, "wda": np.ascontiguousarray(inp["da_out"][0])}


GROUPS = [[0, 1, 2, 3], [4, 5, 6, 7]]


def build_fused(upto=None):
    nc = bass.Bass("TRN2", target_bir_lowering=False)
    P = Prog(nc)
    ctx = {"nc": nc, "P": P}
    decl = [("a_xa", [D, XA_COLS], F32), ("ct", [128, 16], F32), ("a_adaw", [D, 2048], F32),
            ("a_smalls", [128, NSMALL], F32), ("a_win", [D, 1280], F32), ("a_w2", [128, 256], F32),
            ("a_a2", [128, 256], F32), ("a_cst", [128, C_TOT], F32),
            ("b_xa", [D, TA], F32), ("b_adaw0g", [D, 1024], F32), ("b_adaw1", [D, 2048], F32),
            ("b_smalls", [128, 48], F32), ("b_wo", [D, D], F32), ("b_wd", [D, 1024], F32),
            ("b_cst", [128, 384], F32), ("b_cosT", [128, TL], F32), ("b_sinT", [128, TL], F32),
            ("b_lamv", [128, 256], F32), ("b_subw", [128, 128], F32),
            ("c_adaw1g", [D, 1024], F32), ("c_smalls", [128, 8], F32), ("c_wda", [D, D], F32)]
    for name, shape, dt_ in decl:
        if upto == "A" and name[0] in "bc" and name[1] == "_":
            continue
        if upto == "B" and name[0] == "c" and name[1] == "_":
            continue
        ctx[name] = P.dram_in(name, shape, dt_)
    ctx["outT"] = P.dram_out("outT", [D, TL], F32)
    pw0 = [TC, 2048, 2048, 2048, 2048]
    ctx["y0loc"] = [P.dram_tmp("y0loc%d" % i, [256, w], BF16) for i, w in enumerate(pw0)]
    ctx["y0g"] = [P.dram_tmp("y0g%d" % i, [D, w], BF16) for i, w in enumerate(pw0)]
    ctx["of_scratch"] = P.dram_tmp("of_scratch", [256, TA], F32)
    ctx["xn_s"] = P.dram_tmp("xn_s", [D, TL], F32)
    ctx["y1loc"] = [P.dram_tmp("y1loc%d" % i, [256, 2048], BF16) for i in range(4)]
    ctx["y1g"] = [P.dram_tmp("y1g%d" % i, [D, 2048], BF16) for i in range(4)]
    build_l0(ctx=ctx)
    P.emit()
    P.end_phase()
    for i in range(5):
        P.collective("AllGather", ctx["y0g"][i].v, ctx["y0loc"][i].v, GROUPS)
    if upto == "A":
        tb = P.sb("dbg_b", [128, 8, 512], BF16)
        tf = P.sb("dbg_f", [128, 8, 512], F32)
        orr = ctx["outT"].t[:].rearrange("(k p) c -> p k c", p=128)
        for i in range(4):
            pi, off = ((1, 0), (1, 512), (2, 1536), (4, 1536))[i]
            yr = ctx["y0g"][pi].t[:].rearrange("(k p) c -> p k c", p=128)
            P.dma(tb.v, View(ctx["y0g"][pi], yr[:, :, off:off + 512]))
            P.copy("dve", tf.v, tb.v)
            P.dma(View(ctx["outT"], orr[:, :, i * 512:(i + 1) * 512]), tf.v)
        P.emit()
        P.end_phase()
        return nc
    build_l1_full(0, ctx=ctx)
    P.emit()
    P.end_phase()
    for i in range(4):
        P.collective("AllGather", ctx["y1g"][i].v, ctx["y1loc"][i].v, GROUPS)
    if upto == "B":
        tb = P.sb("dbg_b", [128, 8, 512], BF16)
        tf = P.sb("dbg_f", [128, 8, 512], F32)
        xr = ctx["xn_s"].t[:].rearrange("(k p) c -> p k c", p=128)
        orr = ctx["outT"].t[:].rearrange("(k p) c -> p k c", p=128)
        for i in range(2):
            c0 = (0, TL - 512)[i]
            yr = ctx["y1g"][c0 // 2048].t[:].rearrange("(k p) c -> p k c", p=128)
            P.dma(tb.v, View(ctx["y1g"][c0 // 2048], yr[:, :, c0 % 2048:c0 % 2048 + 512]))
            P.copy("dve", tf.v, tb.v)
            P.dma(View(ctx["outT"], orr[:, :, i * 512:(i + 1) * 512]), tf.v)
            P.dma(tf.v, View(ctx["xn_s"], xr[:, :, c0:c0 + 512]))
            P.dma(View(ctx["outT"], orr[:, :, (2 + i) * 512:(3 + i) * 512]), tf.v)
        P.emit()
        P.end_phase()
        return nc
    build_l2(ctx=ctx)
    P.emit()
    P.end_phase()
    return nc


def fused_inputs(inp, b, g):
    m = {}
    a = l0_inputs(inp, b, g)
    for k in ("xa", "adaw", "smalls", "win", "w2", "a2", "cst"):
        m["a_" + k] = a[k]
    m["ct"] = a["ct"]
    bb = l1_inputs(inp, b, g, None)
    for k in ("xa", "adaw0g", "adaw1", "smalls", "wo", "wd", "cst", "cosT", "sinT", "lamv", "subw"):
        m["b_" + k] = bb[k]
    c = l2_inputs(inp, b, g, None, None)
    for k in ("adaw1g", "smalls", "wda"):
        m["c_" + k] = c[k]
    return m


def kernel(**inp):
    inp = {k: np.asarray(v) for k, v in inp.items()}
    cores = list(range(8))
    nc = build_fused()
    maps = [fused_inputs(inp, c // 4, c % 4) for c in cores]
    res = run_bass_kernel_spmd(nc, maps, core_ids=cores).results
    out = np.zeros((2, TL, D), np.float32)
    for c in cores:
        b, tq = c // 4, c % 4
        out[b, tq * 2048:(tq + 1) * 2048] = np.asarray(res[c]["outT"])[:, tq * 2048:(tq + 1) * 2048].T
    return out
```

```python
import numpy as np
import concourse.bass as bass
import concourse.mybir as mybir
from concourse.bass_utils import run_bass_kernel_spmd

F32 = mybir.dt.float32
BF16 = mybir.dt.bfloat16
AF = mybir.ActivationFunctionType
ALU = mybir.AluOpType
AX = mybir.AxisListType

ENGS = ("pe", "act", "dve", "pool", "sp")


class Buf:
    def __init__(self, prog, t, name, space):
        self.prog = prog
        self.t = t
        self.name = name
        self.space = space
        self.last_writer = None
        self.readers = []
        self.dma_sem = None
        self.dma_count = 0

    def __getitem__(self, idx):
        return View(self, self.t[idx])

    @property
    def v(self):
        return View(self, self.t[:])


class View:
    def __init__(self, buf, ap):
        self.buf = buf
        self.ap = ap

    def __getitem__(self, idx):
        return View(self.buf, self.ap[idx])


class Op:
    __slots__ = ("eng", "fn", "deps", "needs_inc", "seq", "dma_buf", "dma_val", "idx", "dma_inc")

    def __init__(self, eng, fn):
        self.eng = eng
        self.fn = fn
        self.deps = []
        self.needs_inc = False
        self.seq = None
        self.dma_buf = None
        self.dma_val = None
        self.dma_inc = 16


def _ap(x):
    return x.ap if isinstance(x, View) else x


class Prog:
    def __init__(self, nc):
        import contextlib
        self.nc = nc
        self.ops = {e: [] for e in ENGS}
        self.bufs = []
        self.dram = {}
        self.same_engine_sync = True
        self.stack = contextlib.ExitStack()
        self.phase = 0
        self.phase_sem = None
        self.uid = 0

    def end_phase(self):
        import contextlib
        self.stack.close()
        self.stack = contextlib.ExitStack()
        self.bufs = []

    def sb(self, name, shape, dtype):
        self.uid += 1
        t = self.stack.enter_context(self.nc.sbuf_tensor("sb%d_%s" % (self.uid, name), list(shape), dtype))
        b = Buf(self, t, name, "sb")
        self.bufs.append(b)
        return b

    def ps(self, name, shape, dtype=F32):
        self.uid += 1
        t = self.stack.enter_context(self.nc.psum_tensor("pp%d_%s" % (self.uid, name), list(shape), dtype))
        b = Buf(self, t, name, "ps")
        self.bufs.append(b)
        return b

    def dram_in(self, name, shape, dtype):
        t = self.nc.dram_tensor(name, list(shape), dtype, kind="ExternalInput")
        b = Buf(self, t, name, "dram")
        self.dram[name] = b
        return b

    def dram_out(self, name, shape, dtype):
        t = self.nc.dram_tensor(name, list(shape), dtype, kind="ExternalOutput")
        b = Buf(self, t, name, "dram")
        self.dram[name] = b
        return b

    def dram_tmp(self, name, shape, dtype, shared=False):
        if shared:
            t = self.nc.dram_tensor(name, list(shape), dtype, addr_space="Shared")
        else:
            t = self.nc.dram_tensor(name, list(shape), dtype)
        b = Buf(self, t, name, "dram")
        self.dram[name] = b
        return b

    def _record(self, eng, fn, reads, writes):
        op = Op(eng, fn)
        deps = []
        for v in reads:
            b = v.buf if isinstance(v, View) else v
            if b.last_writer is not None:
                deps.append(b.last_writer)
            if b.space == "ps":
                deps.extend(r for r in b.readers if r.eng != eng)
        for v in writes:
            b = v.buf if isinstance(v, View) else v
            if b.last_writer is not None:
                deps.append(b.last_writer)
            deps.extend(b.readers)
        seen = set()
        for d in deps:
            if id(d) in seen or d is op:
                continue
            seen.add(id(d))
            if d.dma_buf is None and d.eng == eng and (eng == "pe" or not self.same_engine_sync):
                continue
            op.deps.append(d)
            if d.dma_buf is None:
                d.needs_inc = True
        for v in writes:
            b = v.buf if isinstance(v, View) else v
            b.last_writer = op
            b.readers = []
        for v in reads:
            b = v.buf if isinstance(v, View) else v
            if b.last_writer is not op:
                b.readers.append(op)
        self.ops[eng].append(op)
        return op

    def op(self, eng, fn, reads=(), writes=()):
        return self._record(eng, fn, list(reads), list(writes))

    def matmul(self, out, lhsT, rhs, start=True, stop=True, extra_reads=(), **kw):
        o, l, r = _ap(out), _ap(lhsT), _ap(rhs)
        return self._record("pe", lambda e: e.matmul(o, l, r, start=start, stop=stop, **kw),
                            [lhsT, rhs] + list(extra_reads), [out])

    def transpose(self, out, in_, ident):
        o, i, d = _ap(out), _ap(in_), _ap(ident)
        return self._record("pe", lambda e: e.transpose(o, i, d), [in_, ident], [out])

    def act(self, out, in_, func, bias=None, scale=None, eng="act", accum_out=None):
        o, i = _ap(out), _ap(in_)
        kw = {}
        reads = [in_]
        writes = [out]
        if bias is not None:
            kw["bias"] = _ap(bias)
            if isinstance(bias, View):
                reads.append(bias)
        if scale is not None:
            kw["scale"] = _ap(scale)
            if isinstance(scale, View):
                reads.append(scale)
        if accum_out is not None:
            kw["accum_out"] = _ap(accum_out)
            writes.append(accum_out)
        return self._record("act", lambda e: e.activation(o, i, func, **kw), reads, writes)

    def tt(self, eng, out, in0, in1, op):
        o, a, b = _ap(out), _ap(in0), _ap(in1)
        return self._record(eng, lambda e: e.tensor_tensor(o, a, b, op), [in0, in1], [out])

    def ts(self, eng, out, in0, s1, s2, op0, op1=None, accum_out=None):
        o, a = _ap(out), _ap(in0)
        reads = [in0]
        writes = [out]
        for s in (s1, s2):
            if isinstance(s, View):
                reads.append(s)
        x1, x2 = _ap(s1), _ap(s2)
        kw = {}
        if op1 is not None:
            kw["op1"] = op1
        if accum_out is not None:
            kw["accum_out"] = _ap(accum_out)
            writes.append(accum_out)
        return self._record(eng, lambda e: e.tensor_scalar(o, a, x1, x2, op0, **kw), reads, writes)

    def stt(self, out, in0, scalar, in1, op0, op1, eng="dve"):
        o, a, b = _ap(out), _ap(in0), _ap(in1)
        reads = [in0, in1]
        if isinstance(scalar, View):
            reads.append(scalar)
        s = _ap(scalar)
        return self._record(eng, lambda e: e.scalar_tensor_tensor(o, a, s, b, op0, op1), reads, [out])

    def copy(self, eng, out, in_):
        o, i = _ap(out), _ap(in_)
        if eng == "act":
            return self._record(eng, lambda e: e.copy(o, i), [in_], [out])
        return self._record(eng, lambda e: e.tensor_copy(o, i), [in_], [out])

    def scan(self, out, d0, d1, initial, op0, op1):
        o, a, b = _ap(out), _ap(d0), _ap(d1)
        reads = [d0, d1]
        if isinstance(initial, View):
            reads.append(initial)
        ini = _ap(initial)
        return self._record("dve", lambda e: e.tensor_tensor_scan(o, a, b, ini, op0, op1), reads, [out])

    def recip(self, out, in_):
        o, i = _ap(out), _ap(in_)
        return self._record("dve", lambda e: e.reciprocal(o, i), [in_], [out])

    def memset(self, eng, out, val):
        o = _ap(out)
        return self._record(eng, lambda e: e.memset(o, val), [], [out])

    def reduce(self, out, in_, axis, op, eng="dve"):
        o, i = _ap(out), _ap(in_)
        return self._record(eng, lambda e: e.tensor_reduce(o, i, axis, op), [in_], [out])

    def dma(self, out, in_, queue="sp", **kw):
        o, i = _ap(out), _ap(in_)
        ob = out.buf
        ib = in_.buf
        key = ob if ob.space != "dram" else ib
        op = self._record(queue, lambda e: e.dma_start(out=o, in_=i, **kw), [in_], [out])
        key.dma_count += 1
        op.dma_buf = key
        op.dma_val = 16 * key.dma_count
        return op

    def emit(self, final_waits=()):
        nc = self.nc
        for e in ENGS:
            n = 0
            for op in self.ops[e]:
                if op.dma_buf is None and op.needs_inc:
                    n += 1
                    op.seq = n
        SEMCAP = 30000
        nsem = {e: 1 + max([op.seq or 0 for op in self.ops[e]] + [0]) // SEMCAP for e in ENGS}
        ph = self.phase
        sems = {e: [nc.alloc_semaphore("s%d_%s_%d" % (ph, e, i)) for i in range(nsem[e])] for e in ENGS}
        if self.phase_sem is None:
            self.phase_sem = nc.alloc_semaphore("phase_done")
        phase_sem = self.phase_sem
        dummy_sb = self.sb("phdummy", [128, 8], F32)

        for b in self.bufs + list(self.dram.values()):
            if b.dma_count > 0:
                b.dma_sem = nc.alloc_semaphore("d%d_%s" % (ph, b.name))
        engmap = {"pe": "tensor", "act": "scalar", "dve": "vector", "pool": "gpsimd", "sp": "sync"}
        all_dma = []
        for e in ENGS:
            for op in self.ops[e]:
                if op.dma_buf is not None:
                    all_dma.append(op)

        def gen(ename):
            def body(eng):
                waited = {}
                if ph > 0:
                    eng.wait_ge(phase_sem, 4 * ph)
                for op in self.ops[ename]:
                    need = {}
                    for d in op.deps:
                        if d.dma_buf is not None:
                            k = ("dma", id(d.dma_buf))
                            sem = d.dma_buf.dma_sem
                            val = d.dma_val
                        else:
                            si = (d.seq - 1) // SEMCAP
                            k = ("eng", d.eng, si)
                            sem = sems[d.eng][si]
                            val = d.seq - si * SEMCAP
                        if k not in need or need[k][1] < val:
                            need[k] = (sem, val)
                    for k, (sem, val) in need.items():
                        if waited.get(k, 0) >= val:
                            continue
                        eng.wait_ge(sem, val)
                        waited[k] = val
                    inst = op.fn(eng)
                    if op.dma_buf is not None:
                        if op.dma_inc == 16:
                            inst.then_inc(op.dma_buf.dma_sem, 16)
                        else:
                            inst.then_inc(op.dma_buf.dma_sem)
                    elif op.needs_inc:
                        inst.then_inc(sems[ename][(op.seq - 1) // SEMCAP], 1)
                if ename == "sp":
                    finals = {}
                    for op in all_dma:
                        b = op.dma_buf
                        finals[id(b)] = (b.dma_sem, op.dma_val if op.dma_inc != 16 else 16 * b.dma_count)
                    for sem, val in finals.values():
                        eng.wait_ge(sem, val)
                    eng.sem_inc(phase_sem, 1)
                elif ename == "act":
                    eng.copy(dummy_sb.t[:, 2:3], dummy_sb.t[:, 3:4]).then_inc(phase_sem, 1)
                elif ename == "dve":
                    eng.memset(dummy_sb.t[:, 4:5], 0.0).then_inc(phase_sem, 1)
                elif ename == "pool":
                    eng.memset(dummy_sb.t[:, 6:7], 0.0).then_inc(phase_sem, 1)
            return body

        with nc.Block() as block:
            block.tensor(gen("pe"))
            block.scalar(gen("act"))
            block.vector(gen("dve"))
            block.gpsimd(gen("pool"))
            block.sync(gen("sp"))
        self.phase += 1
        self.ops = {e: [] for e in ENGS}
        for b in self.bufs + list(self.dram.values()):
            b.last_writer = None
            b.readers = []
            b.dma_count = 0
            b.dma_sem = None


def _collective(self, kind, out, in_, groups, op=None):
    o, i = _ap(out), _ap(in_)
    alu = op if op is not None else ALU.bypass
    rec = self._record("pool", lambda e: e.collective_compute(kind, alu, replica_groups=groups, ins=[i], outs=[o]),
                       [in_], [out])
    key = out.buf
    key.dma_count += 1
    rec.dma_buf = key
    rec.dma_inc = 1
    rec.dma_val = key.dma_count
    return rec


Prog.collective = _collective


D = 1024
TC = 256
TL = 8192
TA = TC + TL
W = 256
WH = W + 2
NBLK_L = TL // W
XA_COLS = 1 + TC + 1 + 1 + TL + 1
NSMALL = 50
EXPM05 = 0.6065306597126334
RMS_EPS = 1e-6
GN_EPS = 64e-5


def l0_consts():
    ident = np.eye(128, dtype=np.float32)
    bones = np.kron(np.eye(2, dtype=np.float32), np.ones((64, 64), np.float32))
    idx = np.arange(128)
    same = (idx[:, None] // 64) == (idx[None, :] // 64)
    strict_f = (same & (idx[None, :] < idx[:, None])).astype(np.float32)
    incl_f = (same & (idx[None, :] <= idx[:, None])).astype(np.float32)
    strict_b = (same & (idx[None, :] > idx[:, None])).astype(np.float32)
    incl_b = (same & (idx[None, :] >= idx[:, None])).astype(np.float32)
    out = {}
    for nm, st, inc in (("f", strict_f, incl_f), ("b", strict_b, incl_b)):
        m1 = np.concatenate([st, st], axis=1)
        m2h = np.concatenate([st.T, inc.T], axis=1)
        m2 = np.concatenate([m2h, m2h], axis=1)
        out["m1" + nm] = np.ascontiguousarray(m1)
        out["m2" + nm] = np.ascontiguousarray(m2)
    ident2 = np.concatenate([np.eye(64, dtype=np.float32)] * 2, axis=0)
    cst = np.concatenate([ident, bones, out["m1f"], out["m2f"], out["m1b"], out["m2b"], ident2,
                          np.ones((128, 64), np.float32)], axis=1)
    return np.ascontiguousarray(cst)


C_ID, C_BO, C_M1F, C_M2F, C_M1B, C_M2B, C_ID2, C_ONE = 0, 128, 256, 512, 1024, 1280, 1792, 1856
C_TOT = 1920


F32R = mybir.dt.float32r


def RR(view):
    return View(view.buf, view.ap.bitcast(F32R))


def V2(view):
    return View(view.buf, view.ap.rearrange("p a b -> p (a b)"))


def build_l0(debug_out=False, stop=None, ctx=None):
    if ctx is None:
        nc = bass.Bass("TRN2", target_bir_lowering=False)
        P = Prog(nc)
        xa_d = P.dram_in("xa", [D, XA_COLS], F32)
        ct_d = P.dram_in("ct", [128, 16], F32)
        adaw_d = P.dram_in("adaw", [D, 2048], F32)
        sm_d = P.dram_in("smalls", [128, NSMALL], F32)
        win_d = P.dram_in("win", [D, 1280], F32)
        w2_d = P.dram_in("w2", [128, 256], F32)
        a2_d = P.dram_in("a2", [128, 256], F32)
        cst_d = P.dram_in("cst", [128, C_TOT], F32)
        y0_d = P.dram_out("y0", [256, TA], BF16)
        of_d = P.dram_tmp("of_scratch", [256, TA], F32)
    else:
        nc, P = ctx["nc"], ctx["P"]
        xa_d, ct_d, adaw_d, sm_d, win_d, w2_d, a2_d, cst_d, y0_d, of_d = [ctx[k] for k in (
            "a_xa", "ct", "a_adaw", "a_smalls", "a_win", "a_w2", "a_a2", "a_cst", "y0loc", "of_scratch")]

    cst = P.sb("cst", [128, C_TOT], F32)
    P.dma(cst.v, cst_d.v)
    ident = cst[:, C_ID:C_ID + 128]
    bones = cst[:, C_BO:C_BO + 128]
    ident2 = cst[:, C_ID2:C_ID2 + 64]
    ones64 = cst[:, C_ONE:C_ONE + 64]
    masks = {0: (cst[:, C_M1F:C_M1F + 256], cst[:, C_M2F:C_M2F + 512]),
             1: (cst[:, C_M1B:C_M1B + 256], cst[:, C_M2B:C_M2B + 512])}
    sm = P.sb("sm", [128, NSMALL], F32)
    P.dma(sm.v, sm_d.v)
    S_NG, S_ADAB, S_MU, S_W0, S_A0, S_KK, S_KA, S_RK, S_GG, S_GB = 0, 8, 24, 32, 36, 40, 42, 44, 46, 48
    w2 = P.sb("w2", [128, 256], F32)
    a2 = P.sb("a2", [128, 256], F32)
    P.dma(w2.v, w2_d.v)
    P.dma(a2.v, a2_d.v)
    ones128 = P.sb("ones128", [128, 128], F32)
    P.ts("dve", RR(ones128.v), cst[:, C_ID:C_ID + 128], 0.0, 1.0, ALU.mult, ALU.add)

    der = P.sb("der", [128, 32], F32)
    P.ts("pool", der[:, 0:8], sm[:, S_MU:S_MU + 8], -1.0, 1.0, ALU.mult, ALU.add)
    P.ts("pool", der[:, 8:16], sm[:, S_MU:S_MU + 8], 0.5, None, ALU.mult)
    P.ts("pool", der[:, 16:18], sm[:, S_KA:S_KA + 2], -1.0, 1.0, ALU.mult, ALU.add)
    omu = lambda ci: der[:, ci:ci + 1]
    hmu = lambda ci: der[:, 8 + ci:9 + ci]
    omka = lambda p: der[:, 16 + p:17 + p]

    def dbg_stop(views):
        tot = max(64, sum(n for _, n in views))
        dbg = P.dram_out("dbg", [128, tot], F32)
        dsb = P.sb("dsb", [128, tot], F32)
        P.memset("dve", dsb.v, 0.0)
        c = 0
        for v, n in views:
            P.copy("dve", dsb[:, c:c + n], v)
            c += n
        P.dma(dbg.v, dsb.v)
        P.emit()
        return nc
    if stop == "pre0":
        return dbg_stop([(der[:, 0:18], 18)])
    ct = P.sb("ct", [128, 16], F32)
    P.dma(ct.v, ct_d.v)
    sct = P.sb("sct", [128, 16], F32)
    P.act(sct.v, ct.v, AF.Silu)
    modT = P.sb("modT", [128, 16, 2], F32)
    adaw = P.sb("adaw", [128, 8, 512], F32)
    ps_misc = P.ps("ps_misc", [128, 512])
    adaw_r = adaw_d.t[:].rearrange("(k p) m -> p k m", p=128)
    for piece in range(4):
        P.dma(adaw.v, View(adaw_d, adaw_r[:, :, piece * 512:(piece + 1) * 512]))
        for mcl in range(4):
            mc = piece * 4 + mcl
            for kc in range(8):
                P.matmul(ps_misc[:, 0:2], adaw[:, kc, mcl * 128:(mcl + 1) * 128], sct[:, kc * 2:kc * 2 + 2],
                         start=(kc == 0), stop=(kc == 7))
            P.ts("dve", modT[:, mc, :], ps_misc[:, 0:2], sm[:, S_ADAB + mc:S_ADAB + mc + 1], None, ALU.add)
    if stop == "pre1":
        return dbg_stop([(V2(modT.v), 32)])
    gmod = P.sb("gmod", [128, 8, 2], F32)
    for kc in range(8):
        P.ts("pool", gmod[:, kc, :], modT[:, 8 + kc, :], 1.0, sm[:, S_NG + kc:S_NG + kc + 1], ALU.add, ALU.mult)

    if stop == "pre2":
        return dbg_stop([(V2(modT.v), 32), (V2(gmod.v), 16)])
    Wb = P.sb("Wb", [128, 8, 1280], BF16)
    wst = [P.sb("wst%d" % i, [128, 1280], F32) for i in range(2)]
    for kc in range(8):
        P.dma(wst[kc % 2].v, win_d[kc * 128:(kc + 1) * 128, :])
        P.copy("pool", Wb[:, kc, :], wst[kc % 2].v)

    if stop == "pre":
        dbg = P.dram_out("dbg", [128, 64], F32)
        dsb = P.sb("dsb", [128, 64], F32)
        P.copy("dve", dsb[:, 0:32], V2(modT.v))
        P.copy("dve", dsb[:, 32:48], V2(gmod.v))
        P.copy("dve", dsb[:, 48:64], Wb[:, 7, 0:16])
        P.dma(dbg.v, dsb.v)
        P.emit()
        return nc
    xin = [P.sb("xin%d" % i, [128, 8, WH], F32) for i in range(2)]
    hT = P.sb("hT", [128, 8, WH], BF16)
    sqb = [P.sb("sqb%d" % i, [128, WH], F32) for i in range(2)]
    rstd = P.sb("rstd", [128, WH], F32)
    htmp = [P.sb("htmp%d" % i, [128, WH], F32) for i in range(2)]
    ps_proj = [P.ps("ps_proj%d" % i, [128, 512]) for i in range(2)]
    ps_a = P.ps("ps_a", [128, 512])
    u_sb = [P.sb("u_sb%d" % i, [128, WH], F32) for i in range(2)]
    s_sb = [P.sb("s_sb%d" % i, [128, W], F32) for i in range(2)]
    t_sb = [P.sb("t_sb%d" % i, [128, W], F32) for i in range(2)]

    def blk(name):
        return P.sb(name, [128, W], F32)

    Rb = [blk("R%d" % p) for p in range(2)]
    Kb = [blk("K%d" % p) for p in range(2)]
    Vb = [blk("V%d" % p) for p in range(2)]
    SG = [blk("SG%d" % p) for p in range(2)]
    LW = blk("LW")
    LA = blk("LA")
    TLW = blk("TLW")
    LOGW = [blk("LOGW%d" % p) for p in range(2)]
    Ab = [blk("A%d" % p) for p in range(2)]
    KQ = [blk("KQ%d" % p) for p in range(2)]
    KK = [blk("KK%d" % p) for p in range(2)]
    KD = [blk("KD%d" % p) for p in range(2)]
    KD0 = [blk("KD0%d" % p) for p in range(2)]
    T1 = [blk("T1%d" % p) for p in range(2)]
    T2 = [blk("T2%d" % p) for p in range(2)]
    CL = [blk("CL%d" % p) for p in range(2)]
    PRE = [blk("PRE%d" % p) for p in range(2)]
    E1 = [blk("E1%d" % p) for p in range(2)]
    E2 = [blk("E2%d" % p) for p in range(2)]
    E3 = [blk("E3%d" % p) for p in range(2)]
    AT = [blk("AT%d" % p) for p in range(2)]
    RT = [blk("RT%d" % p) for p in range(2)]
    BT = [blk("BT%d" % p) for p in range(2)]
    KT = [blk("KT%d" % p) for p in range(2)]
    BH = [blk("BH%d" % p) for p in range(2)]
    KH = [blk("KH%d" % p) for p in range(2)]
    DG = [P.sb("DG%d" % p, [128, 256], F32) for p in range(2)]
    OB = [blk("OB%d" % p) for p in range(2)]
    OF = [blk("OF%d" % p) for p in range(2)]
    YB = [P.sb("YB%d" % p, [128, W], BF16) for p in range(2)]

    def sbp(name, shape):
        return [P.sb("%s%d" % (name, p), shape, F32) for p in range(2)]

    Lm = sbp("Lm", [128, 256])
    NM = sbp("NM", [128, 512])
    KM = sbp("KM", [128, 512])
    Lk = [sbp("Lk%d_" % i, [128, 256]) for i in range(2)]
    Nk = [sbp("Nk%d_" % i, [128, 256]) for i in range(2)]
    Xk = [sbp("Xk%d_" % i, [128, 256]) for i in range(2)]
    Zb = sbp("Zb", [128, 256])
    TZ = sbp("TZ", [128, 256])
    VT = sbp("VT", [128, 128])
    BHT = sbp("BHT", [128, 128])
    KHT = sbp("KHT", [128, 128])
    RPT = sbp("RPT", [128, 128])
    PT = sbp("PT", [128, 128])
    ST = [sbp("ST%d_" % i, [128, 128]) for i in range(3)]
    BTbd = sbp("BTbd", [128, 512])
    KTbd = sbp("KTbd", [128, 512])
    DGd = sbp("DGd", [128, 512])
    BHTc = [sbp("BHTc%d_" % i, [128, 128]) for i in range(2)]
    KHTc = [sbp("KHTc%d_" % i, [128, 128]) for i in range(2)]
    RPTm = [sbp("RPTm%d_" % i, [128, 128]) for i in range(2)]
    PTbd = [sbp("PTbd%d_" % i, [128, 128]) for i in range(2)]
    T3 = sbp("T3", [128, 128])
    OTK = sbp("OTK", [128, 128])
    for p in range(2):
        for bb in (BTbd[p], KTbd[p], BHTc[0][p], BHTc[1][p], KHTc[0][p], KHTc[1][p], RPTm[0][p], RPTm[1][p]):
            P.memset("pool", bb.v, 0.0)
    psB = [P.ps("psB%d" % p, [128, 512]) for p in range(2)]
    psC = [P.ps("psC%d" % p, [128, 512]) for p in range(2)]

    def HH(buf, h):
        return buf[:, h * 128:(h + 1) * 128]

    def NMa(p, h):
        return NM[p][:, h * 256:h * 256 + 128]

    def NMb(p, h):
        return NM[p][:, h * 256 + 128:h * 256 + 256]

    def KMa(p, h):
        return KM[p][:, h * 256:h * 256 + 128]

    def KMb(p, h):
        return KM[p][:, h * 256 + 128:h * 256 + 256]

    def V3(view, h):
        return View(view.buf, view.ap.rearrange("p (h c) -> p h c", h=h))


    def x_cols(blk_id):
        if blk_id < 0:
            return 0
        return 258 + 256 * blk_id

    def tok0(blk_id):
        return 0 if blk_id < 0 else TC + 256 * blk_id

    xa_r = xa_d.t[:].rearrange("(k p) c -> p k c", p=128)

    def issue_x(blk_id, slot):
        c0 = x_cols(blk_id)
        P.dma(xin[slot].v, View(xa_d, xa_r[:, :, c0:c0 + WH]))

    evac_flip = [0]

    def evac(out, in_):
        evac_flip[0] ^= 1
        P.copy("act" if evac_flip[0] else "dve", out, in_)


    epsb = P.sb("epsb", [128, 4], F32)
    P.memset("pool", epsb[:, 0:1], RMS_EPS)
    P.memset("pool", epsb[:, 1:2], 1e-12)
    P.memset("pool", epsb[:, 2:3], GN_EPS)

    def stage_a(blk_id, slot, d):
        col = 1 if blk_id < 0 else 0
        xs = xin[slot]
        for kc in range(8):
            sq = sqb[kc % 2]
            P.act(RR(sq.v), xs[:, kc, :], AF.Square)
            P.matmul(ps_a[:, 0:WH], RR(ones128.v), RR(sq.v), start=(kc == 0), stop=(kc == 7))
        P.act(rstd.v, ps_a[:, 0:WH], AF.Sqrt, scale=1.0 / D, bias=epsb[:, 0:1])
        P.recip(rstd.v, rstd.v)
        for kc in range(8):
            tmp = htmp[kc % 2]
            P.stt(tmp.v, xs[:, kc, :], gmod[:, kc, col:col + 1], rstd.v, ALU.mult, ALU.mult)
            P.act(hT[:, kc, :], tmp.v, AF.Identity, bias=modT[:, kc, col:col + 1])
        if blk_id < 0 or blk_id == 0:
            P.memset("pool", hT[:, :, 0:1], 0.0)
        if blk_id < 0 or blk_id == NBLK_L - 1:
            P.memset("pool", hT[:, :, WH - 1:WH], 0.0)
        mixed_dst = [Rb[0], Rb[1], Kb[0], Kb[1], Vb[0], Vb[1], None, None, LW, LA]
        mix_ci = [0, 1, 2, 3, 4, 5, None, None, 6, 7]
        n = 0
        for cc in range(10):
            if cc in (6, 7) and d == 0:
                continue
            pp = ps_proj[n % 2]
            for kc in range(8):
                P.matmul(pp[:, 0:WH], Wb[:, kc, cc * 128:(cc + 1) * 128], hT[:, kc, :],
                         start=(kc == 0), stop=(kc == 7))
            if cc in (6, 7):
                P.act(SG[cc - 6].v, pp[:, 1:W + 1], AF.Silu)
            else:
                ci = mix_ci[cc]
                u = u_sb[n % 2]
                s_ = s_sb[n % 2]
                t_ = t_sb[n % 2]
                P.copy("act", u.v, pp[:, 0:WH])
                P.tt("dve", s_.v, u[:, 0:W], u[:, 2:W + 2], ALU.add)
                P.act(t_.v, u[:, 1:W + 1], AF.Identity, scale=omu(ci))
                P.stt(mixed_dst[cc].v, s_.v, hmu(ci), t_.v, ALU.mult, ALU.add)
            n += 1
        P.act(TLW.v, LW.v, AF.Tanh)
        def derive(p):
            pc = slice(p * 128, (p + 1) * 128)
            P.matmul(ps_a[:, p * W:(p + 1) * W], w2[64 * d:64 * d + 64, pc], TLW[64 * d:64 * d + 64, :])
            yield
            P.act(LOGW[p].v, ps_a[:, p * W:(p + 1) * W], AF.Sigmoid, bias=sm[:, S_W0 + 2 * d + p:S_W0 + 2 * d + p + 1])
            yield
            P.ts("dve", LOGW[p].v, LOGW[p].v, -EXPM05, None, ALU.mult)
            yield
            P.matmul(ps_a[:, p * W:(p + 1) * W], a2[64 * d:64 * d + 64, pc], LA[64 * d:64 * d + 64, :])
            yield
            P.act(Ab[p].v, ps_a[:, p * W:(p + 1) * W], AF.Sigmoid, bias=sm[:, S_A0 + 2 * d + p:S_A0 + 2 * d + p + 1])
            yield
            P.act(KQ[p].v, Kb[p].v, AF.Identity, scale=sm[:, S_KK + p:S_KK + p + 1])
            yield
            P.act(T1[p].v, KQ[p].v, AF.Square)
            yield
            P.matmul(ps_a[:, p * W:(p + 1) * W], bones, T1[p].v)
            yield
            P.act(T2[p].v, ps_a[:, p * W:(p + 1) * W], AF.Sqrt, bias=epsb[:, 1:2])
            yield
            P.recip(T2[p].v, T2[p].v)
            yield
            P.tt("pool", KK[p].v, KQ[p].v, T2[p].v, ALU.mult)
            yield
            P.act(T1[p].v, Ab[p].v, AF.Identity, scale=sm[:, S_KA + p:S_KA + p + 1], bias=omka(p))
            yield
            P.tt("pool", KD[p].v, Kb[p].v, T1[p].v, ALU.mult)
            yield
            if d == 1:
                P.matmul(ps_a[:, p * W:(p + 1) * W], a2[0:64, pc], LA[0:64, :])
                yield
                P.act(T2[p].v, ps_a[:, p * W:(p + 1) * W], AF.Sigmoid, bias=sm[:, S_A0 + p:S_A0 + p + 1])
                yield
                P.act(T2[p].v, T2[p].v, AF.Identity, scale=sm[:, S_KA + p:S_KA + p + 1], bias=omka(p))
                yield
                P.tt("pool", KD0[p].v, Kb[p].v, T2[p].v, ALU.mult)
                yield
            for ch in range(4):
                sl = slice(ch * 64, (ch + 1) * 64)
                P.scan(PRE[p][:, sl], ones64, LOGW[p][:, sl], 0.0, ALU.mult, ALU.add)
                yield
            if d == 0:
                clb = PRE[p]
            else:
                clb = CL[p]
                for ch in range(4):
                    sl = slice(ch * 64, (ch + 1) * 64)
                    P.act(CL[p][:, sl], PRE[p][:, sl], AF.Identity, scale=-1.0,
                          bias=PRE[p][:, ch * 64 + 63:ch * 64 + 64])
                    yield
                P.tt("pool", CL[p].v, CL[p].v, LOGW[p].v, ALU.add)
                yield
            P.act(E1[p].v, clb.v, AF.Exp)
            yield
            P.act(E2[p].v, clb.v, AF.Exp, scale=-1.0)
            yield
            P.tt("pool", T1[p].v, clb.v, LOGW[p].v, ALU.subtract)
            yield
            P.act(E3[p].v, T1[p].v, AF.Exp)
            yield
            P.stt(RR(AT[p].v), KK[p].v, -1.0, E3[p].v, ALU.mult, ALU.mult)
            yield
            P.tt("dve", RR(RT[p].v), Rb[p].v, E1[p].v, ALU.mult)
            yield
            P.tt("pool", T1[p].v, KK[p].v, Ab[p].v, ALU.mult)
            yield
            P.tt("pool", BT[p].v, T1[p].v, E2[p].v, ALU.mult)
            yield
            P.tt("pool", KT[p].v, KD[p].v, E2[p].v, ALU.mult)
            yield
            for ch in range(4):
                sl = slice(ch * 64, (ch + 1) * 64)
                gc = ch * 64 + 63 if d == 0 else ch * 64
                gcol = E1[p][:, gc:gc + 1]
                P.ts("dve", BH[p][:, sl], BT[p][:, sl], gcol, None, ALU.mult)
                yield
                P.act(KH[p][:, sl], KT[p][:, sl], AF.Identity, scale=gcol)
                yield
                P.ts("dve", DGd[p][:, ch * 128:(ch + 1) * 128], ident, gcol, None, ALU.mult)
                yield
            for h in range(2):
                hp = slice(64 * h, 64 * h + 64)
                for tl2 in range(2):
                    q = (tl2 * 2 + h) * 128
                    P.copy("dve", RR(BTbd[p][hp, q:q + 128]), BT[p][hp, tl2 * 128:(tl2 + 1) * 128])
                    yield
                    P.copy("act", RR(KTbd[p][hp, q:q + 128]), KT[p][hp, tl2 * 128:(tl2 + 1) * 128])
                    yield

        gens = [derive(p) for p in range(2)]
        while gens:
            for g in list(gens):
                try:
                    next(g)
                except StopIteration:
                    gens.remove(g)

    def stage_b(p, tl, d, sw_state, upto=99):
        m1, m2 = masks[d]
        cs = slice(tl * 128, (tl + 1) * 128)
        pc = psC[p]
        pb = psB[p]
        btbd = lambda h: BTbd[p][:, (tl * 2 + h) * 128:(tl * 2 + h + 1) * 128]
        ktbd = lambda h: KTbd[p][:, (tl * 2 + h) * 128:(tl * 2 + h + 1) * 128]
        P.matmul(pc[:, 0:256], RR(AT[p][:, cs]), RR(BTbd[p][:, tl * 256:(tl + 1) * 256]))
        P.tt("dve", RR(Lm[p].v), pc[:, 0:256], m1, ALU.mult)
        yield
        for h in range(2):
            P.matmul(pb[:, h * 256:h * 256 + 128], RR(btbd(h)), RR(AT[p][:, cs]))
            P.matmul(pb[:, h * 256 + 128:h * 256 + 256], RR(btbd(h)), RR(RT[p][:, cs]))
        P.tt("dve", RR(NM[p].v), pb[:, 0:512], m2, ALU.mult)
        yield
        for h in range(2):
            P.matmul(pb[:, h * 256:h * 256 + 128], RR(ktbd(h)), RR(AT[p][:, cs]))
            P.matmul(pb[:, h * 256 + 128:h * 256 + 256], RR(ktbd(h)), RR(RT[p][:, cs]))
        P.tt("dve", RR(KM[p].v), pb[:, 0:512], m2, ALU.mult)
        yield
        if upto <= 1:
            return
        X = Xk[0][p]
        for h in range(2):
            P.tt("dve", RR(HH(X, h)), NMa(p, h), ident, ALU.add)
        Lc = Lm[p]
        Nc_views = [NMa(p, h) for h in range(2)]
        xi = 0
        for k in range(1, 6):
            Ln = Lk[k % 2][p]
            for h in range(2):
                P.matmul(pc[:, h * 128:(h + 1) * 128], RR(Nc_views[h]), RR(HH(Lc, h)))
            P.copy("act", RR(Ln.v), pc[:, 0:256])
            yield
            if k < 5:
                Nn = Nk[k % 2][p]
                for h in range(2):
                    P.matmul(pc[:, 256 + h * 128:256 + (h + 1) * 128], RR(HH(Lc, h)), RR(Nc_views[h]))
                P.copy("act", RR(Nn.v), pc[:, 256:512])
            for h in range(2):
                P.matmul(pb[:, h * 128:(h + 1) * 128], RR(HH(Ln, h)), RR(HH(Xk[xi][p], h)))
            Xn = Xk[1 - xi][p]
            P.tt("dve", RR(Xn.v), pb[:, 0:256], Xk[xi][p].v, ALU.add)
            yield
            xi = 1 - xi
            Lc = Ln
            if k < 5:
                Nc_views = [HH(Nn, h) for h in range(2)]
        X = Xk[xi][p]
        if upto <= 2:
            return
        P.transpose(pc[:, 0:128], AT[p][:, cs], ident)
        P.copy("act", RR(Zb[p][:, 0:128]), pc[:, 0:128])
        yield
        P.transpose(pc[:, 128:256], Vb[p][:, cs], ident)
        P.copy("dve", RR(VT[p].v), pc[:, 128:256])
        yield
        P.transpose(pc[:, 256:384], BH[p][:, cs], ident)
        P.copy("act", RR(BHTc[0][p][0:64, :]), pc[0:64, 256:384])
        yield
        P.copy("dve", RR(BHTc[1][p][64:128, :]), pc[64:128, 256:384])
        yield
        P.transpose(pc[:, 384:512], KH[p][:, cs], ident)
        P.copy("act", KHTc[0][p][0:64, :], pc[0:64, 384:512])
        yield
        P.copy("dve", KHTc[1][p][64:128, :], pc[64:128, 384:512])
        yield
        for h in range(2):
            P.matmul(pb[:, h * 64:(h + 1) * 64], RR(KMa(p, h)), RR(VT[p][:, h * 64:(h + 1) * 64]))
        P.copy("act", RR(Zb[p][:, 128:256]), pb[:, 0:128])
        yield
        for part in range(2):
            for h in range(2):
                q = (part * 2 + h) * 64
                P.matmul(pb[:, 128 + q:128 + q + 64], RR(HH(X, h)), RR(Zb[p][:, q:q + 64]))
        P.copy("dve", RR(TZ[p].v), pb[:, 128:384])
        yield
        if upto <= 3:
            return
        for h in range(2):
            P.matmul(pc[:, h * 128:(h + 1) * 128], RR(TZ[p][:, 0:128]), RR(NMb(p, h)))
        for h in range(2):
            hp = slice(64 * h, 64 * h + 64)
            P.tt("dve", RPT[p][hp, :], pc[hp, h * 128:(h + 1) * 128], RT[p][hp, cs], ALU.add)
            yield
        P.copy("pool", RPTm[0][p][:, 0:64], RPT[p][:, 0:64])
        P.copy("pool", RPTm[1][p][:, 64:128], RPT[p][:, 64:128])
        for c in range(2):
            P.matmul(pc[:, 256 + c * 128:256 + (c + 1) * 128], RR(TZ[p][:, 0:128]), RR(BHTc[c][p].v))
        for c in range(2):
            ch = 2 * tl + c
            P.tt("dve", T3[p].v, pc[:, 256 + c * 128:256 + (c + 1) * 128], bones, ALU.mult)
            yield
            P.tt("pool", PTbd[c][p].v, T3[p].v, DGd[p][:, ch * 128:(ch + 1) * 128], ALU.add)
        if upto <= 5:
            return
        order = (0, 1) if d == 0 else (1, 0)
        s_at = {}
        for c in order:
            si = sw_state[p]
            S_in = ST[si][p]
            S_out = ST[(si + 1) % 3][p]
            s_at[c] = S_in
            P.matmul(pb[:, 0:128], PTbd[c][p].v, S_in.v, start=True, stop=False)
            P.matmul(pb[:, 0:128], BHTc[c][p].v, TZ[p][:, 128:256], start=False, stop=False)
            P.matmul(pb[:, 0:128], KHTc[c][p].v, VT[p].v, start=False, stop=True)
            P.tt("dve", S_out.v, pb[:, 0:128], bones, ALU.mult)
            yield
            sw_state[p] = (si + 1) % 3
        if upto <= 6:
            return
        P.matmul(pb[:, 128:256], RPTm[0][p].v, s_at[0].v, start=True, stop=False)
        P.matmul(pb[:, 128:256], RPTm[1][p].v, s_at[1].v, start=False, stop=False)
        for h in range(2):
            P.matmul(pb[:, 128 + h * 64:128 + (h + 1) * 64], RR(NMb(p, h)), RR(TZ[p][:, 128 + h * 64:128 + (h + 1) * 64]),
                     start=False, stop=False)
            P.matmul(pb[:, 128 + h * 64:128 + (h + 1) * 64], RR(KMb(p, h)), RR(VT[p][:, h * 64:(h + 1) * 64]),
                     start=False, stop=(h == 1))
        P.copy("act", OTK[p].v, pb[:, 128:256])
        yield
        P.transpose(pb[:, 256:384], OTK[p].v, ident)
        if d == 0:
            P.copy("dve", OB[p][:, cs], pb[:, 256:384])
            yield
        else:
            P.tt("dve", OB[p][:, cs], pb[:, 256:384], OF[p][:, cs], ALU.add)
            yield

    def readout(blk_id):
        t0 = tok0(blk_id)
        for p in range(2):
            P.matmul(ps_a[:, 0:W], bones, OB[p].v)
            P.stt(T1[p].v, ps_a[:, 0:W], -1.0 / 64, OB[p].v, ALU.mult, ALU.add)
            P.act(T2[p].v, T1[p].v, AF.Square)
            P.matmul(ps_a[:, W:2 * W], bones, T2[p].v)
            P.act(T2[p].v, ps_a[:, W:2 * W], AF.Sqrt, scale=1.0 / 64, bias=epsb[:, 2:3])
            P.recip(T2[p].v, T2[p].v)
            P.tt("pool", T1[p].v, T1[p].v, T2[p].v, ALU.mult)
            P.act(T1[p].v, T1[p].v, AF.Identity, scale=sm[:, S_GG + p:S_GG + p + 1],
                  bias=sm[:, S_GB + p:S_GB + p + 1])
            P.tt("pool", T2[p].v, KD[p].v, KD0[p].v, ALU.add)
            P.stt(T2[p].v, Rb[p].v, sm[:, S_RK + p:S_RK + p + 1], T2[p].v, ALU.mult, ALU.mult)
            P.matmul(ps_a[:, 0:W], bones, T2[p].v)
            P.tt("dve", T2[p].v, ps_a[:, 0:W], Vb[p].v, ALU.mult)
            P.tt("pool", T1[p].v, T1[p].v, T2[p].v, ALU.add)
            P.tt("pool", YB[p].v, T1[p].v, SG[p].v, ALU.mult)
            if isinstance(y0_d, list):
                pi, off = (0, t0) if t0 < TC else (1 + (t0 - TC) // 2048, (t0 - TC) % 2048)
                P.dma(y0_d[pi][p * 128:(p + 1) * 128, off:off + W], YB[p].v)
            else:
                P.dma(y0_d[p * 128:(p + 1) * 128, t0:t0 + W], YB[p].v)

    for p in range(2):
        P.memset("pool", ST[0][p].v, 0.0)
    for d in range(2):
        blocks = [-1] + (list(range(NBLK_L)) if d == 0 else list(range(NBLK_L - 1, -1, -1)))
        if debug_out and isinstance(debug_out, int) and debug_out > 1:
            blocks = blocks[:debug_out]
        sw_state = [0, 0]
        if d == 1:
            for p in range(2):
                P.memset("pool", ST[0][p].v, 0.0)
        issue_x(blocks[0], 0)
        for bi, b in enumerate(blocks):
            slot = bi % 2
            if bi + 1 < len(blocks):
                issue_x(blocks[bi + 1], 1 - slot)
            t0 = tok0(b)
            if d == 1:
                for p in range(2):
                    P.dma(OF[p].v, of_d[p * 128:(p + 1) * 128, t0:t0 + W])
            stage_a(b, slot, d)
            if stop == "a":
                return dbg_stop([(Rb[0].v, 256), (KK[1].v, 256), (LOGW[0].v, 256), (Ab[1].v, 256), (KD[0].v, 256),
                                 (AT[0].v, 256), (RT[0].v, 256), (BT[0].v, 256), (KT[0].v, 256), (BH[0].v, 256),
                                 (DG[0].v, 256), (Vb[1].v, 256)])
            tiles = (0, 1) if d == 0 else (1, 0)
            for tl in tiles:
                if not (stop and stop[0] == "b"):
                    gens = [stage_b(p, tl, d, sw_state) for p in range(2)]
                    while gens:
                        for g in list(gens):
                            try:
                                next(g)
                            except StopIteration:
                                gens.remove(g)
                    continue
                for p in range(2):
                    for _ in stage_b(p, tl, d, sw_state, upto=int(stop[1:]) if (stop and stop[0] == "b" and len(stop) > 1) else 99):
                        pass
                    if stop and stop[0] == "b":
                        return dbg_stop([(OB[0][:, 0:128], 128), (ST[sw_state[0]][0].v, 128), (TZ[0].v, 256),
                                         (Lm[0].v, 256), (NM[0].v, 512), (Xk[1][0].v, 256), (RPT[0].v, 128), (PTbd[0][0].v, 128)])
            if d == 0:
                for p in range(2):
                    P.dma(of_d[p * 128:(p + 1) * 128, t0:t0 + W], OB[p].v)
            else:
                readout(b)
    if ctx is None:
        P.emit()
    return nc


def l0_inputs(inp, b, hg):
    f32 = np.float32
    x, ctx = inp["x"], inp["ctx"]
    z1 = np.zeros((D, 1), f32)
    xa = np.concatenate([z1, ctx[b].T, z1, z1, x[b].T, z1], axis=1)
    ct = np.zeros((128, 16), f32)
    cb = inp["c"][b].reshape(8, 128).T
    cc = inp["c_ctx"].reshape(8, 128).T
    ct[:, 0::2] = cb
    ct[:, 1::2] = cc
    adaw = np.ascontiguousarray(inp["ada_w"][0][:, 0:2048])
    hc = slice(hg * 256, (hg + 1) * 256)
    rw_in = inp["rw_in"][0]
    cols = []
    for X in range(4):
        cols.append(rw_in[:, X * 1024 + hg * 256: X * 1024 + (hg + 1) * 256])
    cols.append(rw_in[:, 4096:4352])
    win = np.ascontiguousarray(np.concatenate(cols, axis=1))
    sm = np.zeros((128, NSMALL), f32)
    sm[:, 0:8] = inp["norm_g"][0].reshape(8, 128).T
    sm[:, 8:24] = inp["ada_b"][0][:2048].reshape(16, 128).T
    mu = inp["rw_mu"][0]
    mus = []
    for X in range(3):
        for p in range(2):
            mus.append(mu[X * 1024 + hg * 256 + p * 128: X * 1024 + hg * 256 + (p + 1) * 128])
    mus.append(mu[3072:3200])
    mus.append(mu[3200:3328])
    sm[:, 24:32] = np.stack(mus, axis=1)
    for d in range(2):
        for p in range(2):
            sm[:, 32 + 2 * d + p] = inp["rw_w0"][0][d, hg * 256 + p * 128: hg * 256 + (p + 1) * 128]
            sm[:, 36 + 2 * d + p] = inp["rw_a0"][0][d, hg * 256 + p * 128: hg * 256 + (p + 1) * 128]
    for p in range(2):
        sl = slice(hg * 256 + p * 128, hg * 256 + (p + 1) * 128)
        sm[:, 40 + p] = inp["rw_kk"][0][sl]
        sm[:, 42 + p] = inp["rw_ka"][0][sl]
        sm[:, 44 + p] = inp["rw_rk"][0].reshape(-1)[sl]
        sm[:, 46 + p] = inp["rw_gn_g"][0][sl]
        sm[:, 48 + p] = inp["rw_gn_b"][0][sl]
    w2 = np.ascontiguousarray(inp["rw_w2"][0][:, :, hc].reshape(128, 256))
    a2 = np.ascontiguousarray(inp["rw_a2"][0][:, :, hc].reshape(128, 256))
    return {"xa": np.ascontiguousarray(xa), "ct": ct, "adaw": adaw, "smalls": sm, "win": win,
            "w2": w2, "a2": a2, "cst": l0_consts()}


SUBLN_EPS = 1e-5
LAM_INIT = 0.8 - 0.6 * float(np.exp(-0.3 * 1))
QSCALE = 0.125
NKT = TA // 128
NQB = TL // 256


def tok0(bi):
    return 0 if bi < 0 else TC + 256 * bi


def mod_compute(P, adaw_d, ncol, sct, adab_view, ps, modT, adaw_sb):
    adaw_r = adaw_d.t[:].rearrange("(k p) m -> p k m", p=128)
    for piece in range(ncol // 256):
        P.dma(adaw_sb.v, View(adaw_d, adaw_r[:, :, piece * 256:(piece + 1) * 256]))
        for mcl in range(2):
            mc = piece * 2 + mcl
            for kc in range(8):
                P.matmul(ps[:, 0:2], adaw_sb[:, kc, mcl * 128:(mcl + 1) * 128], sct[:, kc * 2:kc * 2 + 2],
                         start=(kc == 0), stop=(kc == 7))
            P.ts("dve", modT[:, mc, :], ps[:, 0:2], adab_view(mc), None, ALU.add)


def rope_tables():
    rows = TL // 64
    t = np.arange(TL)
    row = (t // 64).astype(np.float32)
    colid = (t % 64).astype(np.float32)
    inv = (10000.0 ** (-np.arange(16, dtype=np.float32) / 16)).astype(np.float32)
    ang_r = row[None, :] * inv[:, None]
    ang_c = colid[None, :] * inv[:, None]
    cos64 = np.concatenate([np.cos(ang_r), np.cos(ang_r), np.cos(ang_c), np.cos(ang_c)], axis=0)
    sin64 = np.concatenate([np.sin(ang_r), np.sin(ang_r), np.sin(ang_c), np.sin(ang_c)], axis=0)
    cosT = np.concatenate([cos64, cos64], axis=0).astype(np.float32)
    sinT = np.concatenate([sin64, sin64], axis=0).astype(np.float32)
    R = np.zeros((128, 128), np.float32)
    for base in range(0, 128, 32):
        for f in range(16):
            R[base + 16 + f, base + f] = -1.0
            R[base + f, base + 16 + f] = 1.0
    return cosT, sinT, R


def build_l1(stop=None, ctx=None):
    if ctx is None:
        nc = bass.Bass("TRN2", target_bir_lowering=False)
        P = Prog(nc)
        xa_d = P.dram_in("xa", [D, TA], F32)
        y0_d = P.dram_in("y0g", [D, TA], BF16)
        ct_d = P.dram_in("ct", [128, 16], F32)
        adaw0_d = P.dram_in("adaw0g", [D, 1024], F32)
        adaw1_d = P.dram_in("adaw1", [D, 2048], F32)
        sm_d = P.dram_in("smalls", [128, 48], F32)
        wo_d = P.dram_in("wo", [D, D], F32)
        wd_d = P.dram_in("wd", [D, 1024], F32)
        cst_d = P.dram_in("cst", [128, 384], F32)
        cos_d = P.dram_in("cosT", [128, TL], F32)
        sin_d = P.dram_in("sinT", [128, TL], F32)
        lam_d = P.dram_in("lamv", [128, 256], F32)
        subw_d = P.dram_in("subw", [128, 128], F32)
        y1_d = P.dram_out("y1", [256, TL], BF16)
        xn_d = P.dram_out("xn", [D, TL], F32)
    else:
        nc, P = ctx["nc"], ctx["P"]
        (xa_d, y0_d, ct_d, adaw0_d, adaw1_d, sm_d, wo_d, wd_d, cst_d, cos_d, sin_d, lam_d, subw_d, y1_d, xn_d) = [
            ctx[k] for k in ("b_xa", "y0g", "ct", "b_adaw0g", "b_adaw1", "b_smalls", "b_wo", "b_wd", "b_cst",
                             "b_cosT", "b_sinT", "b_lamv", "b_subw", "y1loc", "xn_s")]

    cst = P.sb("cst", [128, 384], F32)
    P.dma(cst.v, cst_d.v)
    ident = cst[:, 0:128]
    bones = cst[:, 128:256]
    rrot = cst[:, 256:384]
    sm = P.sb("sm", [128, 48], F32)
    P.dma(sm.v, sm_d.v)
    S_NG, S_B0, S_B1, S_QN, S_KN = 0, 8, 16, 32, 33
    ones128 = P.sb("ones128", [128, 128], F32)
    P.memset("pool", ones128.v, 1.0)
    epsb = P.sb("epsb", [128, 4], F32)
    P.memset("pool", epsb[:, 0:1], RMS_EPS)
    P.memset("pool", epsb[:, 1:2], SUBLN_EPS)
    banks = [P.ps("bank%d" % i, [128, 512]) for i in range(8)]

    ct = P.sb("ct", [128, 16], F32)
    P.dma(ct.v, ct_d.v)
    sct = P.sb("sct", [128, 16], F32)
    P.act(sct.v, ct.v, AF.Silu)
    adaw_sb = P.sb("adaw_sb", [128, 8, 256], F32)
    mod0 = P.sb("mod0", [128, 8, 2], F32)
    mod1 = P.sb("mod1", [128, 16, 2], F32)
    mod_compute(P, adaw0_d, 1024, sct, lambda mc: sm[:, S_B0 + mc:S_B0 + mc + 1], banks[0], mod0, adaw_sb)
    mod_compute(P, adaw1_d, 2048, sct, lambda mc: sm[:, S_B1 + mc:S_B1 + mc + 1], banks[0], mod1, adaw_sb)
    gmod = P.sb("gmod", [128, 8, 2], F32)
    for kc in range(8):
        P.ts("pool", gmod[:, kc, :], mod1[:, 8 + kc, :], 1.0, sm[:, S_NG + kc:S_NG + kc + 1], ALU.add, ALU.mult)

    lamv = P.sb("lamv", [128, 256], F32)
    P.dma(lamv.v, lam_d.v)
    lt = P.sb("lt", [128, 128], F32)
    lsc = P.sb("lsc", [128, 8], F32)
    P.tt("pool", lt[:, 0:64], lamv[:, 0:64], lamv[:, 64:128], ALU.mult)
    P.tt("pool", lt[:, 64:128], lamv[:, 128:192], lamv[:, 192:256], ALU.mult)
    P.reduce(lsc[:, 0:1], lt[:, 0:64], AX.X, ALU.add)
    P.reduce(lsc[:, 1:2], lt[:, 64:128], AX.X, ALU.add)
    P.act(lsc[:, 2:4], lsc[:, 0:2], AF.Exp)
    P.tt("pool", lsc[:, 4:5], lsc[:, 2:3], lsc[:, 3:4], ALU.subtract)
    P.ts("pool", lsc[:, 5:6], lsc[:, 4:5], LAM_INIT, -1.0, ALU.add, ALU.mult)
    neglam = lsc[:, 5:6]
    subw = P.sb("subw", [128, 128], F32)
    P.dma(subw.v, subw_d.v)
    P.ts("pool", subw.v, subw.v, 1.0 - LAM_INIT, None, ALU.mult)

    Wo = P.sb("Wo", [128, 8, 1024], BF16)
    Wd = P.sb("Wd", [128, 8, 1024], BF16)
    wst = [P.sb("wst%d" % i, [128, 1024], F32) for i in range(1)] * 2
    n = 0
    for src, dst in ((wo_d, Wo), (wd_d, Wd)):
        for kc in range(8):
            P.dma(wst[n % 2].v, src[kc * 128:(kc + 1) * 128, :])
            P.copy("pool", dst[:, kc, :], wst[n % 2].v)
            n += 1

    xin = [P.sb("xin%d" % i, [128, 8, W], F32) for i in range(2)]
    yin = [P.sb("yin%d" % i, [128, 8, W], BF16) for i in range(2)]
    xn = P.sb("xn", [128, 8, W], F32)
    hT = P.sb("hT", [128, 8, W], BF16)
    sqb = [P.sb("sqb%d" % i, [128, W], F32) for i in range(2)]
    rstd = P.sb("rstd", [128, W], F32)
    htmp = [P.sb("htmp%d" % i, [128, W], F32) for i in range(2)]
    cosb = P.sb("cosb", [128, W], F32)
    sinb = P.sb("sinb", [128, W], F32)
    qraw = P.sb("qraw", [128, W], F32)
    qsq = P.sb("qsq", [128, W], F32)
    qrs = P.sb("qrs", [128, W], F32)
    qn_ = P.sb("qn_", [128, W], F32)
    qt1 = P.sb("qt1", [128, W], F32)
    qt2 = P.sb("qt2", [128, W], F32)
    KTb = P.sb("KTb", [128, TA], BF16)
    QTb = P.sb("QTb", [128, NQB * 512], BF16)
    P.memset("pool", QTb.v, 0.0)
    Vx = P.sb("Vx", [128, NKT, 130], BF16)
    P.memset("pool", Vx[:, :, 129:130], 0.0)
    Gs = P.sb("Gs", [128, TL // 128, 128], BF16)
    P.memset("pool", Vx[:, :, 128:129], 1.0)
    pT = [P.sb("pT%d" % i, [128, 512], BF16) for i in range(4)]
    vg_sb = [P.sb("vg_sb%d" % i, [128, 512], F32) for i in range(2)]
    o_sb = P.sb("o_sb", [128, 128], F32)
    o_sq = P.sb("o_sq", [128, 128], F32)
    ybs = [P.sb("yb%d" % i, [128, 256], BF16) for i in range(2)]
    zs = P.sb("zs", [128, 8], F32)
    ps_bf = P.ps("ps_bf_unused", [128, 2], F32) if False else None

    xa_r = xa_d.t[:].rearrange("(k p) c -> p k c", p=128)
    y0_r = None if isinstance(y0_d, list) else y0_d.t[:].rearrange("(k p) c -> p k c", p=128)
    xn_r = xn_d.t[:].rearrange("(k p) c -> p k c", p=128)

    def issue(bi, slot, first_pass=True):
        t0 = tok0(bi)
        if (not first_pass) and bi >= 0:
            return
        P.dma(xin[slot].v, View(xa_d, xa_r[:, :, t0:t0 + W]))
        if isinstance(y0_d, list):
            pi, off = (0, t0) if t0 < TC else (1 + (t0 - TC) // 2048, (t0 - TC) % 2048)
            yr = y0_d[pi].t[:].rearrange("(k p) c -> p k c", p=128)
            P.dma(yin[slot].v, View(y0_d[pi], yr[:, :, off:off + W]))
        else:
            P.dma(yin[slot].v, View(y0_d, y0_r[:, :, t0:t0 + W]))

    def stage_a(bi, slot, h, hp_quarter, first_pass):
        col = 1 if bi < 0 else 0
        t0 = tok0(bi)
        lat0 = t0 - TC
        reuse = (not first_pass) and bi >= 0
        if reuse:
            P.dma(xn.v, View(xn_d, xn_r[:, :, lat0:lat0 + W]))
        for oc in range(0 if reuse else 8):
            pp = banks[oc % 2]
            for kc in range(8):
                P.matmul(pp[:, 0:W], Wo[:, kc, oc * 128:(oc + 1) * 128], yin[slot][:, kc, :],
                         start=(kc == 0), stop=(kc == 7))
            P.stt(xn[:, oc, :], pp[:, 0:W], mod0[:, oc, col:col + 1], xin[slot][:, oc, :], ALU.mult, ALU.add)
        if first_pass and bi >= 0 and stop != 'noxn':
            P.dma(View(xn_d, xn_r[:, :, lat0:lat0 + W]), xn.v)
        if stop == 'a1':
            return
        for kc in range(8):
            sq = sqb[kc % 2]
            P.act(sq.v, xn[:, kc, :], AF.Square)
            P.matmul(banks[2][:, 0:W], ones128.v, sq.v, start=(kc == 0), stop=(kc == 7))
        P.act(rstd.v, banks[2][:, 0:W], AF.Sqrt, scale=1.0 / D, bias=epsb[:, 0:1])
        P.recip(rstd.v, rstd.v)
        for kc in range(8):
            tmp = htmp[kc % 2]
            P.stt(tmp.v, xn[:, kc, :], gmod[:, kc, col:col + 1], rstd.v, ALU.mult, ALU.mult)
            P.act(hT[:, kc, :], tmp.v, AF.Identity, bias=mod1[:, kc, col:col + 1])
        if stop == 'a2':
            return
        if bi >= 0:
            P.dma(cosb.v, cos_d[:, lat0:lat0 + W])
            P.dma(sinb.v, sin_d[:, lat0:lat0 + W])
        for which in (("q", "k") if bi >= 0 else ("k",)):
            cc = h if which == "q" else 2 + h
            pp = banks[3]
            for kc in range(8):
                P.matmul(pp[:, 0:W], Wd[:, kc, cc * 128:(cc + 1) * 128], hT[:, kc, :],
                         start=(kc == 0), stop=(kc == 7))
            P.copy("act", qraw.v, pp[:, 0:W])
            P.tt("pool", qsq.v, qraw.v, qraw.v, ALU.mult)
            P.matmul(banks[4][:, 0:W], bones, qsq.v)
            P.act(qrs.v, banks[4][:, 0:W], AF.Sqrt, scale=1.0 / 64, bias=epsb[:, 0:1])
            P.recip(qrs.v, qrs.v)
            wcol = sm[:, S_QN:S_QN + 1] if which == "q" else sm[:, S_KN:S_KN + 1]
            P.stt(qn_.v, qraw.v, wcol, qrs.v, ALU.mult, ALU.mult)
            if bi >= 0:
                P.matmul(banks[4][:, W:2 * W], rrot, qn_.v)
                P.tt("dve", qt1.v, banks[4][:, W:2 * W], sinb.v, ALU.mult)
                P.tt("pool", qt2.v, qn_.v, cosb.v, ALU.mult)
                if which == "q":
                    qb_ = lat0 // 256
                    P.tt("pool", QTb[0:64, qb_ * 512:qb_ * 512 + 256], qt1[0:64, :], qt2[0:64, :], ALU.add)
                    P.tt("pool", QTb[64:128, qb_ * 512 + 256:qb_ * 512 + 512], qt1[64:128, :], qt2[64:128, :], ALU.add)
                else:
                    P.tt("pool", KTb[:, t0:t0 + W], qt1.v, qt2.v, ALU.add)
            else:
                P.copy("pool", KTb[:, t0:t0 + W], qn_.v)
        if stop == 'a3':
            return
        for sub in range(2):
            pp = banks[5 + sub]
            for kc in range(8):
                P.matmul(pp[:, 0:512], hT[:, kc, sub * 128:(sub + 1) * 128], Wd[:, kc, 512:1024],
                         start=(kc == 0), stop=(kc == 7))
            kt = (t0 + sub * 128) // 128
            vg = vg_sb[sub]
            P.copy("dve", vg.v, pp[:, 0:512])
            P.copy("pool", Vx[:, kt, 0:128], vg[:, h * 128:(h + 1) * 128])
            if bi >= 0:
                qt = (lat0 + sub * 128) // 128
                P.act(Gs[:, qt, :], vg[:, 256 + h * 128:256 + (h + 1) * 128], AF.Silu)

    def attention(h, nqb=NQB):
        sbank = [(banks[0], banks[1]), (banks[2], banks[3])]
        acc = [[banks[4], banks[5]], [banks[6], banks[7]]]
        n = 0

        def s_mm(qb_, kt_, n_):
            P.matmul(banks[n_ % 4][:, 0:512], KTb[:, kt_ * 128:(kt_ + 1) * 128], QTb[:, qb_ * 512:(qb_ + 1) * 512])

        tiles = [(qb_, kt_) for qb_ in range(nqb) for kt_ in range(NKT)]
        LOOK = 3
        for j in range(min(LOOK, len(tiles))):
            s_mm(tiles[j][0], tiles[j][1], j)
        for qb in range(nqb):
            q0 = qb * 256
            for kt in range(NKT):
                sA = banks[n % 4]
                pt = pT[n % 4]
                if n + LOOK < len(tiles):
                    s_mm(tiles[n + LOOK][0], tiles[n + LOOK][1], n + LOOK)
                P.act(pt.v, sA[:, 0:512], AF.Exp, scale=QSCALE)
                if stop == 's1':
                    n += 1
                    continue
                for comp in range(2):
                    for qs in range(2):
                        P.matmul(acc[comp][qs][:, 0:130], pt[:, comp * 256 + qs * 128:comp * 256 + (qs + 1) * 128],
                                 Vx[:, kt, :], start=(kt == 0), stop=(kt == NKT - 1))
                n += 1
            if stop in ('s1', 's2'):
                continue
            ytile = ybs[qb % 2]
            for qs in range(2):
                a0 = acc[0][qs]
                a1 = acc[1][qs]
                P.recip(zs[:, 0:1], a0[:, 128:129])
                P.recip(zs[:, 1:2], a1[:, 128:129])
                P.tt("pool", zs[:, 2:3], zs[:, 1:2], neglam, ALU.mult)
                P.ts("dve", o_sb.v, a0[:, 0:128], zs[:, 0:1], None, ALU.mult)
                P.stt(o_sb.v, a1[:, 0:128], zs[:, 2:3], o_sb.v, ALU.mult, ALU.add)
                P.tt("pool", o_sq.v, o_sb.v, o_sb.v, ALU.mult)
                P.reduce(zs[:, 3:4], o_sq.v, AX.X, ALU.add)
                P.act(zs[:, 4:5], zs[:, 3:4], AF.Sqrt, scale=1.0 / 128, bias=epsb[:, 1:2])
                P.recip(zs[:, 4:5], zs[:, 4:5])
                P.stt(o_sb.v, o_sb.v, zs[:, 4:5], subw.v, ALU.mult, ALU.mult)
                qt = (q0 + qs * 128) // 128
                P.tt("pool", o_sq.v, o_sb.v, Gs[:, qt, :], ALU.mult)
                P.transpose(a0[:, 256:384], o_sq.v, ident)
                P.copy("act", ytile[:, qs * 128:(qs + 1) * 128], a0[:, 256:384])
            if isinstance(y1_d, list):
                P.dma(y1_d[q0 // 2048][h * 128:(h + 1) * 128, q0 % 2048:q0 % 2048 + 256], ytile.v)
            else:
                P.dma(y1_d[h * 128:(h + 1) * 128, q0:q0 + 256], ytile.v)

    return nc, P, issue, stage_a, attention


def build_l1_full(hp_quarter, do_attn=True, nheads=2, stop=None, nblocks=None, nqb=NQB, ctx=None):
    nc, P, issue, stage_a, attention = build_l1(stop, ctx)
    blocks = [-1] + list(range(NBLK_L))
    if nblocks:
        blocks = blocks[:nblocks]
    if stop == 'pre':
        P.emit()
        return nc
    for h in range(nheads):
        issue(blocks[0], 0, h == 0)
        for i, bi in enumerate(blocks):
            if i + 1 < len(blocks):
                issue(blocks[i + 1], (i + 1) % 2, h == 0)
            stage_a(bi, i % 2, h, hp_quarter, h == 0)
        if do_attn:
            attention(h, nqb)
    if ctx is None:
        P.emit()
    return nc


def l1_inputs(inp, b, hp, y0g_b):
    f32 = np.float32
    xa = np.ascontiguousarray(np.concatenate([inp["ctx"][b].T, inp["x"][b].T], axis=1))
    ct = np.zeros((128, 16), f32)
    ct[:, 0::2] = inp["c"][b].reshape(8, 128).T
    ct[:, 1::2] = inp["c_ctx"].reshape(8, 128).T
    sm = np.zeros((128, 48), f32)
    sm[:, 0:8] = inp["norm_g"][1].reshape(8, 128).T
    sm[:, 8:16] = inp["ada_b"][0][2048:3072].reshape(8, 128).T
    sm[:, 16:32] = inp["ada_b"][1][0:2048].reshape(16, 128).T
    sm[:, 32] = np.tile(inp["da_qn"][0], 2)
    sm[:, 33] = np.tile(inp["da_kn"][0], 2)
    da_in = inp["da_in"][0]
    cols = []
    for X in range(4):
        for hh in range(2):
            base = X * 1024 + (hp * 2 + hh) * 128
            cols.append(da_in[:, base:base + 128])
    wd = np.ascontiguousarray(np.concatenate(cols, axis=1))
    cosT, sinT, R = rope_tables()
    ident = np.eye(128, dtype=f32)
    bones = np.kron(np.eye(2, dtype=f32), np.ones((64, 64), f32))
    cst = np.ascontiguousarray(np.concatenate([ident, bones, R], axis=1))
    lamv = np.ascontiguousarray(np.broadcast_to(inp["da_lam"][0].reshape(1, 256), (128, 256))).astype(f32)
    subw = np.ascontiguousarray(np.broadcast_to(inp["da_subln"][0].reshape(1, 128), (128, 128))).astype(f32)
    return {"xa": xa, "y0g": y0g_b, "ct": ct,
            "adaw0g": np.ascontiguousarray(inp["ada_w"][0][:, 2048:3072]),
            "adaw1": np.ascontiguousarray(inp["ada_w"][1][:, 0:2048]),
            "smalls": sm, "wo": np.ascontiguousarray(inp["rw_out"][0]), "wd": wd, "cst": cst,
            "cosT": cosT, "sinT": sinT, "lamv": lamv, "subw": subw}


def build_l2(ctx=None):
    if ctx is None:
        nc = bass.Bass("TRN2", target_bir_lowering=False)
        P = Prog(nc)
        NT = 2048
        xn_d = P.dram_in("xn", [D, NT], F32)
        y1_d = P.dram_in("y1T", [D, NT], BF16)
        ct_d = P.dram_in("ct", [128, 16], F32)
        adaw_d = P.dram_in("adaw1g", [D, 1024], F32)
        sm_d = P.dram_in("smalls", [128, 8], F32)
        w_d = P.dram_in("wda", [D, D], F32)
        out_d = P.dram_out("outT", [D, NT], F32)
    else:
        nc, P = ctx["nc"], ctx["P"]
        NT = TL
        xn_d, y1_d, ct_d, adaw_d, sm_d, w_d, out_d = [ctx[k] for k in (
            "xn_s", "y1g", "ct", "c_adaw1g", "c_smalls", "c_wda", "outT")]
    sm = P.sb("sm", [128, 8], F32)
    P.dma(sm.v, sm_d.v)
    banks = [P.ps("bank%d" % i, [128, 512]) for i in range(4)]
    ct = P.sb("ct", [128, 16], F32)
    P.dma(ct.v, ct_d.v)
    sct = P.sb("sct", [128, 16], F32)
    P.act(sct.v, ct.v, AF.Silu)
    adaw_sb = P.sb("adaw_sb", [128, 8, 256], F32)
    modg = P.sb("modg", [128, 8, 2], F32)
    mod_compute(P, adaw_d, 1024, sct, lambda mc: sm[:, mc:mc + 1], banks[0], modg, adaw_sb)
    Wa = P.sb("Wa", [128, 8, 1024], BF16)
    wst = [P.sb("wst%d" % i, [128, 1024], F32) for i in range(2)]
    for kc in range(8):
        P.dma(wst[kc % 2].v, w_d[kc * 128:(kc + 1) * 128, :])
        P.copy("pool", Wa[:, kc, :], wst[kc % 2].v)
    xin = [P.sb("xin%d" % i, [128, 8, W], F32) for i in range(2)]
    yin = [P.sb("yin%d" % i, [128, 8, W], BF16) for i in range(2)]
    ob = [P.sb("ob%d" % i, [128, 8, W], F32) for i in range(2)]
    xn_r = xn_d.t[:].rearrange("(k p) c -> p k c", p=128)
    y1_r = None if isinstance(y1_d, list) else y1_d.t[:].rearrange("(k p) c -> p k c", p=128)
    out_r = out_d.t[:].rearrange("(k p) c -> p k c", p=128)
    for bi in range(NT // W):
        s_ = bi % 2
        P.dma(xin[s_].v, View(xn_d, xn_r[:, :, bi * W:(bi + 1) * W]))
        if isinstance(y1_d, list):
            c0 = bi * W
            yr = y1_d[c0 // 2048].t[:].rearrange("(k p) c -> p k c", p=128)
            P.dma(yin[s_].v, View(y1_d[c0 // 2048], yr[:, :, c0 % 2048:c0 % 2048 + W]))
        else:
            P.dma(yin[s_].v, View(y1_d, y1_r[:, :, bi * W:(bi + 1) * W]))
        for oc in range(8):
            pp = banks[1 + oc % 2]
            for kc in range(8):
                P.matmul(pp[:, 0:W], Wa[:, kc, oc * 128:(oc + 1) * 128], yin[s_][:, kc, :],
                         start=(kc == 0), stop=(kc == 7))
            P.stt(ob[s_][:, oc, :], pp[:, 0:W], modg[:, oc, 0:1], xin[s_][:, oc, :], ALU.mult, ALU.add)
        P.dma(View(out_d, out_r[:, :, bi * W:(bi + 1) * W]), ob[s_].v)
    if ctx is None:
        P.emit()
    return nc


def l2_inputs(inp, b, tq, xn, y1T):
    f32 = np.float32
    ct = np.zeros((128, 16), f32)
    ct[:, 0::2] = inp["c"][b].reshape(8, 128).T
    ct[:, 1::2] = inp["c_ctx"].reshape(8, 128).T
    sm = np.ascontiguousarray(inp["ada_b"][1][2048:3072].reshape(8, 128).T).astype(f32)
    return {"xn": xn, "y1T": y1T, "ct": ct, "adaw1g": np.ascontiguousarray(inp["ada_w"][1][:, 2048:3072]),
            "smalls": sm, "wda": np.ascontiguousarray(inp["da_out"][0])}


GROUPS = [[0, 1, 2, 3], [4, 5, 6, 7]]


def build_fused(upto=None):
    nc = bass.Bass("TRN2", target_bir_lowering=False)
    P = Prog(nc)
    ctx = {"nc": nc, "P": P}
    decl = [("a_xa", [D, XA_COLS], F32), ("ct", [128, 16], F32), ("a_adaw", [D, 2048], F32),
            ("a_smalls", [128, NSMALL], F32), ("a_win", [D, 1280], F32), ("a_w2", [128, 256], F32),
            ("a_a2", [128, 256], F32), ("a_cst", [128, C_TOT], F32),
            ("b_xa", [D, TA], F32), ("b_adaw0g", [D, 1024], F32), ("b_adaw1", [D, 2048], F32),
            ("b_smalls", [128, 48], F32), ("b_wo", [D, D], F32), ("b_wd", [D, 1024], F32),
            ("b_cst", [128, 384], F32), ("b_cosT", [128, TL], F32), ("b_sinT", [128, TL], F32),
            ("b_lamv", [128, 256], F32), ("b_subw", [128, 128], F32),
            ("c_adaw1g", [D, 1024], F32), ("c_smalls", [128, 8], F32), ("c_wda", [D, D], F32)]
    for name, shape, dt_ in decl:
        if upto == "A" and name[0] in "bc" and name[1] == "_":
            continue
        if upto == "B" and name[0] == "c" and name[1] == "_":
            continue
        ctx[name] = P.dram_in(name, shape, dt_)
    ctx["outT"] = P.dram_out("outT", [D, TL], F32)
    pw0 = [TC, 2048, 2048, 2048, 2048]
    ctx["y0loc"] = [P.dram_tmp("y0loc%d" % i, [256, w], BF16) for i, w in enumerate(pw0)]
    ctx["y0g"] = [P.dram_tmp("y0g%d" % i, [D, w], BF16) for i, w in enumerate(pw0)]
    ctx["of_scratch"] = P.dram_tmp("of_scratch", [256, TA], F32)
    ctx["xn_s"] = P.dram_tmp("xn_s", [D, TL], F32)
    ctx["y1loc"] = [P.dram_tmp("y1loc%d" % i, [256, 2048], BF16) for i in range(4)]
    ctx["y1g"] = [P.dram_tmp("y1g%d" % i, [D, 2048], BF16) for i in range(4)]
    build_l0(ctx=ctx)
    P.emit()
    P.end_phase()
    for i in range(5):
        P.collective("AllGather", ctx["y0g"][i].v, ctx["y0loc"][i].v, GROUPS)
    if upto == "A":
        tb = P.sb("dbg_b", [128, 8, 512], BF16)
        tf = P.sb("dbg_f", [128, 8, 512], F32)
        orr = ctx["outT"].t[:].rearrange("(k p) c -> p k c", p=128)
        for i in range(4):
            pi, off = ((1, 0), (1, 512), (2, 1536), (4, 1536))[i]
            yr = ctx["y0g"][pi].t[:].rearrange("(k p) c -> p k c", p=128)
            P.dma(tb.v, View(ctx["y0g"][pi], yr[:, :, off:off + 512]))
            P.copy("dve", tf.v, tb.v)
            P.dma(View(ctx["outT"], orr[:, :, i * 512:(i + 1) * 512]), tf.v)
        P.emit()
        P.end_phase()
        return nc
    build_l1_full(0, ctx=ctx)
    P.emit()
    P.end_phase()
    for i in range(4):
        P.collective("AllGather", ctx["y1g"][i].v, ctx["y1loc"][i].v, GROUPS)
    if upto == "B":
        tb = P.sb("dbg_b", [128, 8, 512], BF16)
        tf = P.sb("dbg_f", [128, 8, 512], F32)
        xr = ctx["xn_s"].t[:].rearrange("(k p) c -> p k c", p=128)
        orr = ctx["outT"].t[:].rearrange("(k p) c -> p k c", p=128)
        for i in range(2):
            c0 = (0, TL - 512)[i]
            yr = ctx["y1g"][c0 // 2048].t[:].rearrange("(k p) c -> p k c", p=128)
            P.dma(tb.v, View(ctx["y1g"][c0 // 2048], yr[:, :, c0 % 2048:c0 % 2048 + 512]))
            P.copy("dve", tf.v, tb.v)
            P.dma(View(ctx["outT"], orr[:, :, i * 512:(i + 1) * 512]), tf.v)
            P.dma(tf.v, View(ctx["xn_s"], xr[:, :, c0:c0 + 512]))
            P.dma(View(ctx["outT"], orr[:, :, (2 + i) * 512:(3 + i) * 512]), tf.v)
        P.emit()
        P.end_phase()
        return nc
    build_l2(ctx=ctx)
    P.emit()
    P.end_phase()
    return nc


def fused_inputs(inp, b, g):
    m = {}
    a = l0_inputs(inp, b, g)
    for k in ("xa", "adaw", "smalls", "win", "w2", "a2", "cst"):
        m["a_" + k] = a[k]
    m["ct"] = a["ct"]
    bb = l1_inputs(inp, b, g, None)
    for k in ("xa", "adaw0g", "adaw1", "smalls", "wo", "wd", "cst", "cosT", "sinT", "lamv", "subw"):
        m["b_" + k] = bb[k]
    c = l2_inputs(inp, b, g, None, None)
    for k in ("adaw1g", "smalls", "wda"):
        m["c_" + k] = c[k]
    return m


def kernel(**inp):
    inp = {k: np.asarray(v) for k, v in inp.items()}
    cores = list(range(8))
    nc = build_fused()
    maps = [fused_inputs(inp, c // 4, c % 4) for c in cores]
    res = run_bass_kernel_spmd(nc, maps, core_ids=cores).results
    out = np.zeros((2, TL, D), np.float32)
    for c in cores:
        b, tq = c // 4, c % 4
        out[b, tq * 2048:(tq + 1) * 2048] = np.asarray(res[c]["outT"])[:, tq * 2048:(tq + 1) * 2048].T
    return out
```

```python
import numpy as np
import concourse.bass as bass
import concourse.mybir as mybir
from concourse.bass_utils import run_bass_kernel_spmd

F32 = mybir.dt.float32
BF16 = mybir.dt.bfloat16
AF = mybir.ActivationFunctionType
ALU = mybir.AluOpType
AX = mybir.AxisListType

ENGS = ("pe", "act", "dve", "pool", "sp")


class Buf:
    def __init__(self, prog, t, name, space):
        self.prog = prog
        self.t = t
        self.name = name
        self.space = space
        self.last_writer = None
        self.readers = []
        self.dma_sem = None
        self.dma_count = 0

    def __getitem__(self, idx):
        return View(self, self.t[idx])

    @property
    def v(self):
        return View(self, self.t[:])


class View:
    def __init__(self, buf, ap):
        self.buf = buf
        self.ap = ap

    def __getitem__(self, idx):
        return View(self.buf, self.ap[idx])


class Op:
    __slots__ = ("eng", "fn", "deps", "needs_inc", "seq", "dma_buf", "dma_val", "idx", "dma_inc")

    def __init__(self, eng, fn):
        self.eng = eng
        self.fn = fn
        self.deps = []
        self.needs_inc = False
        self.seq = None
        self.dma_buf = None
        self.dma_val = None
        self.dma_inc = 16


def _ap(x):
    return x.ap if isinstance(x, View) else x


class Prog:
    def __init__(self, nc):
        import contextlib
        self.nc = nc
        self.ops = {e: [] for e in ENGS}
        self.bufs = []
        self.dram = {}
        self.same_engine_sync = True
        self.stack = contextlib.ExitStack()
        self.phase = 0
        self.phase_sem = None
        self.uid = 0

    def end_phase(self):
        import contextlib
        self.stack.close()
        self.stack = contextlib.ExitStack()
        self.bufs = []

    def sb(self, name, shape, dtype):
        self.uid += 1
        t = self.stack.enter_context(self.nc.sbuf_tensor("sb%d_%s" % (self.uid, name), list(shape), dtype))
        b = Buf(self, t, name, "sb")
        self.bufs.append(b)
        return b

    def ps(self, name, shape, dtype=F32):
        self.uid += 1
        t = self.stack.enter_context(self.nc.psum_tensor("pp%d_%s" % (self.uid, name), list(shape), dtype))
        b = Buf(self, t, name, "ps")
        self.bufs.append(b)
        return b

    def dram_in(self, name, shape, dtype):
        t = self.nc.dram_tensor(name, list(shape), dtype, kind="ExternalInput")
        b = Buf(self, t, name, "dram")
        self.dram[name] = b
        return b

    def dram_out(self, name, shape, dtype):
        t = self.nc.dram_tensor(name, list(shape), dtype, kind="ExternalOutput")
        b = Buf(self, t, name, "dram")
        self.dram[name] = b
        return b

    def dram_tmp(self, name, shape, dtype, shared=False):
        if shared:
            t = self.nc.dram_tensor(name, list(shape), dtype, addr_space="Shared")
        else:
            t = self.nc.dram_tensor(name, list(shape), dtype)
        b = Buf(self, t, name, "dram")
        self.dram[name] = b
        return b

    def _record(self, eng, fn, reads, writes):
        op = Op(eng, fn)
        deps = []
        for v in reads:
            b = v.buf if isinstance(v, View) else v
            if b.last_writer is not None:
                deps.append(b.last_writer)
            if b.space == "ps":
                deps.extend(r for r in b.readers if r.eng != eng)
        for v in writes:
            b = v.buf if isinstance(v, View) else v
            if b.last_writer is not None:
                deps.append(b.last_writer)
            deps.extend(b.readers)
        seen = set()
        for d in deps:
            if id(d) in seen or d is op:
                continue
            seen.add(id(d))
            if d.dma_buf is None and d.eng == eng and (eng == "pe" or not self.same_engine_sync):
                continue
            op.deps.append(d)
            if d.dma_buf is None:
                d.needs_inc = True
        for v in writes:
            b = v.buf if isinstance(v, View) else v
            b.last_writer = op
            b.readers = []
        for v in reads:
            b = v.buf if isinstance(v, View) else v
            if b.last_writer is not op:
                b.readers.append(op)
        self.ops[eng].append(op)
        return op

    def op(self, eng, fn, reads=(), writes=()):
        return self._record(eng, fn, list(reads), list(writes))

    def matmul(self, out, lhsT, rhs, start=True, stop=True, extra_reads=(), **kw):
        o, l, r = _ap(out), _ap(lhsT), _ap(rhs)
        return self._record("pe", lambda e: e.matmul(o, l, r, start=start, stop=stop, **kw),
                            [lhsT, rhs] + list(extra_reads), [out])

    def transpose(self, out, in_, ident):
        o, i, d = _ap(out), _ap(in_), _ap(ident)
        return self._record("pe", lambda e: e.transpose(o, i, d), [in_, ident], [out])

    def act(self, out, in_, func, bias=None, scale=None, eng="act", accum_out=None):
        o, i = _ap(out), _ap(in_)
        kw = {}
        reads = [in_]
        writes = [out]
        if bias is not None:
            kw["bias"] = _ap(bias)
            if isinstance(bias, View):
                reads.append(bias)
        if scale is not None:
            kw["scale"] = _ap(scale)
            if isinstance(scale, View):
                reads.append(scale)
        if accum_out is not None:
            kw["accum_out"] = _ap(accum_out)
            writes.append(accum_out)
        return self._record("act", lambda e: e.activation(o, i, func, **kw), reads, writes)

    def tt(self, eng, out, in0, in1, op):
        o, a, b = _ap(out), _ap(in0), _ap(in1)
        return self._record(eng, lambda e: e.tensor_tensor(o, a, b, op), [in0, in1], [out])

    def ts(self, eng, out, in0, s1, s2, op0, op1=None, accum_out=None):
        o, a = _ap(out), _ap(in0)
        reads = [in0]
        writes = [out]
        for s in (s1, s2):
            if isinstance(s, View):
                reads.append(s)
        x1, x2 = _ap(s1), _ap(s2)
        kw = {}
        if op1 is not None:
            kw["op1"] = op1
        if accum_out is not None:
            kw["accum_out"] = _ap(accum_out)
            writes.append(accum_out)
        return self._record(eng, lambda e: e.tensor_scalar(o, a, x1, x2, op0, **kw), reads, writes)

    def stt(self, out, in0, scalar, in1, op0, op1, eng="dve"):
        o, a, b = _ap(out), _ap(in0), _ap(in1)
        reads = [in0, in1]
        if isinstance(scalar, View):
            reads.append(scalar)
        s = _ap(scalar)
        return self._record(eng, lambda e: e.scalar_tensor_tensor(o, a, s, b, op0, op1), reads, [out])

    def copy(self, eng, out, in_):
        o, i = _ap(out), _ap(in_)
        if eng == "act":
            return self._record(eng, lambda e: e.copy(o, i), [in_], [out])
        return self._record(eng, lambda e: e.tensor_copy(o, i), [in_], [out])

    def scan(self, out, d0, d1, initial, op0, op1):
        o, a, b = _ap(out), _ap(d0), _ap(d1)
        reads = [d0, d1]
        if isinstance(initial, View):
            reads.append(initial)
        ini = _ap(initial)
        return self._record("dve", lambda e: e.tensor_tensor_scan(o, a, b, ini, op0, op1), reads, [out])

    def recip(self, out, in_):
        o, i = _ap(out), _ap(in_)
        return self._record("dve", lambda e: e.reciprocal(o, i), [in_], [out])

    def memset(self, eng, out, val):
        o = _ap(out)
        return self._record(eng, lambda e: e.memset(o, val), [], [out])

    def reduce(self, out, in_, axis, op, eng="dve"):
        o, i = _ap(out), _ap(in_)
        return self._record(eng, lambda e: e.tensor_reduce(o, i, axis, op), [in_], [out])

    def dma(self, out, in_, queue="sp", **kw):
        o, i = _ap(out), _ap(in_)
        ob = out.buf
        ib = in_.buf
        key = ob if ob.space != "dram" else ib
        op = self._record(queue, lambda e: e.dma_start(out=o, in_=i, **kw), [in_], [out])
        key.dma_count += 1
        op.dma_buf = key
        op.dma_val = 16 * key.dma_count
        return op

    def emit(self, final_waits=()):
        nc = self.nc
        for e in ENGS:
            n = 0
            for op in self.ops[e]:
                if op.dma_buf is None and op.needs_inc:
                    n += 1
                    op.seq = n
        SEMCAP = 30000
        nsem = {e: 1 + max([op.seq or 0 for op in self.ops[e]] + [0]) // SEMCAP for e in ENGS}
        ph = self.phase
        sems = {e: [nc.alloc_semaphore("s%d_%s_%d" % (ph, e, i)) for i in range(nsem[e])] for e in ENGS}
        if self.phase_sem is None:
            self.phase_sem = nc.alloc_semaphore("phase_done")
        phase_sem = self.phase_sem
        dummy_sb = self.sb("phdummy", [128, 8], F32)

        for b in self.bufs + list(self.dram.values()):
            if b.dma_count > 0:
                b.dma_sem = nc.alloc_semaphore("d%d_%s" % (ph, b.name))
        engmap = {"pe": "tensor", "act": "scalar", "dve": "vector", "pool": "gpsimd", "sp": "sync"}
        all_dma = []
        for e in ENGS:
            for op in self.ops[e]:
                if op.dma_buf is not None:
                    all_dma.append(op)

        def gen(ename):
            def body(eng):
                waited = {}
                if ph > 0:
                    eng.wait_ge(phase_sem, 4 * ph)
                for op in self.ops[ename]:
                    need = {}
                    for d in op.deps:
                        if d.dma_buf is not None:
                            k = ("dma", id(d.dma_buf))
                            sem = d.dma_buf.dma_sem
                            val = d.dma_val
                        else:
                            si = (d.seq - 1) // SEMCAP
                            k = ("eng", d.eng, si)
                            sem = sems[d.eng][si]
                            val = d.seq - si * SEMCAP
                        if k not in need or need[k][1] < val:
                            need[k] = (sem, val)
                    for k, (sem, val) in need.items():
                        if waited.get(k, 0) >= val:
                            continue
                        eng.wait_ge(sem, val)
                        waited[k] = val
                    inst = op.fn(eng)
                    if op.dma_buf is not None:
                        if op.dma_inc == 16:
                            inst.then_inc(op.dma_buf.dma_sem, 16)
                        else:
                            inst.then_inc(op.dma_buf.dma_sem)
                    elif op.needs_inc:
                        inst.then_inc(sems[ename][(op.seq - 1) // SEMCAP], 1)
                if ename == "sp":
                    finals = {}
                    for op in all_dma:
                        b = op.dma_buf
                        finals[id(b)] = (b.dma_sem, op.dma_val if op.dma_inc != 16 else 16 * b.dma_count)
                    for sem, val in finals.values():
                        eng.wait_ge(sem, val)
                    eng.sem_inc(phase_sem, 1)
                elif ename == "act":
                    eng.copy(dummy_sb.t[:, 2:3], dummy_sb.t[:, 3:4]).then_inc(phase_sem, 1)
                elif ename == "dve":
                    eng.memset(dummy_sb.t[:, 4:5], 0.0).then_inc(phase_sem, 1)
                elif ename == "pool":
                    eng.memset(dummy_sb.t[:, 6:7], 0.0).then_inc(phase_sem, 1)
            return body

        with nc.Block() as block:
            block.tensor(gen("pe"))
            block.scalar(gen("act"))
            block.vector(gen("dve"))
            block.gpsimd(gen("pool"))
            block.sync(gen("sp"))
        self.phase += 1
        self.ops = {e: [] for e in ENGS}
        for b in self.bufs + list(self.dram.values()):
            b.last_writer = None
            b.readers = []
            b.dma_count = 0
            b.dma_sem = None


def _collective(self, kind, out, in_, groups, op=None):
    o, i = _ap(out), _ap(in_)
    alu = op if op is not None else ALU.bypass
    rec = self._record("pool", lambda e: e.collective_compute(kind, alu, replica_groups=groups, ins=[i], outs=[o]),
                       [in_], [out])
    key = out.buf
    key.dma_count += 1
    rec.dma_buf = key
    rec.dma_inc = 1
    rec.dma_val = key.dma_count
    return rec


Prog.collective = _collective


D = 1024
TC = 256
TL = 8192
TA = TC + TL
W = 256
WH = W + 2
NBLK_L = TL // W
XA_COLS = 1 + TC + 1 + 1 + TL + 1
NSMALL = 50
EXPM05 = 0.6065306597126334
RMS_EPS = 1e-6
GN_EPS = 64e-5


def l0_consts():
    ident = np.eye(128, dtype=np.float32)
    bones = np.kron(np.eye(2, dtype=np.float32), np.ones((64, 64), np.float32))
    idx = np.arange(128)
    same = (idx[:, None] // 64) == (idx[None, :] // 64)
    strict_f = (same & (idx[None, :] < idx[:, None])).astype(np.float32)
    incl_f = (same & (idx[None, :] <= idx[:, None])).astype(np.float32)
    strict_b = (same & (idx[None, :] > idx[:, None])).astype(np.float32)
    incl_b = (same & (idx[None, :] >= idx[:, None])).astype(np.float32)
    out = {}
    for nm, st, inc in (("f", strict_f, incl_f), ("b", strict_b, incl_b)):
        m1 = np.concatenate([st, st], axis=1)
        m2h = np.concatenate([st.T, inc.T], axis=1)
        m2 = np.concatenate([m2h, m2h], axis=1)
        out["m1" + nm] = np.ascontiguousarray(m1)
        out["m2" + nm] = np.ascontiguousarray(m2)
    ident2 = np.concatenate([np.eye(64, dtype=np.float32)] * 2, axis=0)
    cst = np.concatenate([ident, bones, out["m1f"], out["m2f"], out["m1b"], out["m2b"], ident2,
                          np.ones((128, 64), np.float32)], axis=1)
    return np.ascontiguousarray(cst)


C_ID, C_BO, C_M1F, C_M2F, C_M1B, C_M2B, C_ID2, C_ONE = 0, 128, 256, 512, 1024, 1280, 1792, 1856
C_TOT = 1920


F32R = mybir.dt.float32r


def RR(view):
    return View(view.buf, view.ap.bitcast(F32R))


def V2(view):
    return View(view.buf, view.ap.rearrange("p a b -> p (a b)"))


def build_l0(debug_out=False, stop=None, ctx=None):
    if ctx is None:
        nc = bass.Bass("TRN2", target_bir_lowering=False)
        P = Prog(nc)
        xa_d = P.dram_in("xa", [D, XA_COLS], F32)
        ct_d = P.dram_in("ct", [128, 16], F32)
        adaw_d = P.dram_in("adaw", [D, 2048], F32)
        sm_d = P.dram_in("smalls", [128, NSMALL], F32)
        win_d = P.dram_in("win", [D, 1280], F32)
        w2_d = P.dram_in("w2", [128, 256], F32)
        a2_d = P.dram_in("a2", [128, 256], F32)
        cst_d = P.dram_in("cst", [128, C_TOT], F32)
        y0_d = P.dram_out("y0", [256, TA], BF16)
        of_d = P.dram_tmp("of_scratch", [256, TA], F32)
    else:
        nc, P = ctx["nc"], ctx["P"]
        xa_d, ct_d, adaw_d, sm_d, win_d, w2_d, a2_d, cst_d, y0_d, of_d = [ctx[k] for k in (
            "a_xa", "ct", "a_adaw", "a_smalls", "a_win", "a_w2", "a_a2", "a_cst", "y0loc", "of_scratch")]

    cst = P.sb("cst", [128, C_TOT], F32)
    P.dma(cst.v, cst_d.v)
    ident = cst[:, C_ID:C_ID + 128]
    bones = cst[:, C_BO:C_BO + 128]
    ident2 = cst[:, C_ID2:C_ID2 + 64]
    ones64 = cst[:, C_ONE:C_ONE + 64]
    masks = {0: (cst[:, C_M1F:C_M1F + 256], cst[:, C_M2F:C_M2F + 512]),
             1: (cst[:, C_M1B:C_M1B + 256], cst[:, C_M2B:C_M2B + 512])}
    sm = P.sb("sm", [128, NSMALL], F32)
    P.dma(sm.v, sm_d.v)
    S_NG, S_ADAB, S_MU, S_W0, S_A0, S_KK, S_KA, S_RK, S_GG, S_GB = 0, 8, 24, 32, 36, 40, 42, 44, 46, 48
    w2 = P.sb("w2", [128, 256], F32)
    a2 = P.sb("a2", [128, 256], F32)
    P.dma(w2.v, w2_d.v)
    P.dma(a2.v, a2_d.v)
    ones128 = P.sb("ones128", [128, 128], F32)
    P.ts("dve", RR(ones128.v), cst[:, C_ID:C_ID + 128], 0.0, 1.0, ALU.mult, ALU.add)

    der = P.sb("der", [128, 32], F32)
    P.ts("pool", der[:, 0:8], sm[:, S_MU:S_MU + 8], -1.0, 1.0, ALU.mult, ALU.add)
    P.ts("pool", der[:, 8:16], sm[:, S_MU:S_MU + 8], 0.5, None, ALU.mult)
    P.ts("pool", der[:, 16:18], sm[:, S_KA:S_KA + 2], -1.0, 1.0, ALU.mult, ALU.add)
    omu = lambda ci: der[:, ci:ci + 1]
    hmu = lambda ci: der[:, 8 + ci:9 + ci]
    omka = lambda p: der[:, 16 + p:17 + p]

    def dbg_stop(views):
        tot = max(64, sum(n for _, n in views))
        dbg = P.dram_out("dbg", [128, tot], F32)
        dsb = P.sb("dsb", [128, tot], F32)
        P.memset("dve", dsb.v, 0.0)
        c = 0
        for v, n in views:
            P.copy("dve", dsb[:, c:c + n], v)
            c += n
        P.dma(dbg.v, dsb.v)
        P.emit()
        return nc
    if stop == "pre0":
        return dbg_stop([(der[:, 0:18], 18)])
    ct = P.sb("ct", [128, 16], F32)
    P.dma(ct.v, ct_d.v)
    sct = P.sb("sct", [128, 16], F32)
    P.act(sct.v, ct.v, AF.Silu)
    modT = P.sb("modT", [128, 16, 2], F32)
    adaw = P.sb("adaw", [128, 8, 512], F32)
    ps_misc = P.ps("ps_misc", [128, 512])
    adaw_r = adaw_d.t[:].rearrange("(k p) m -> p k m", p=128)
    for piece in range(4):
        P.dma(adaw.v, View(adaw_d, adaw_r[:, :, piece * 512:(piece + 1) * 512]))
        for mcl in range(4):
            mc = piece * 4 + mcl
            for kc in range(8):
                P.matmul(ps_misc[:, 0:2], adaw[:, kc, mcl * 128:(mcl + 1) * 128], sct[:, kc * 2:kc * 2 + 2],
                         start=(kc == 0), stop=(kc == 7))
            P.ts("dve", modT[:, mc, :], ps_misc[:, 0:2], sm[:, S_ADAB + mc:S_ADAB + mc + 1], None, ALU.add)
    if stop == "pre1":
        return dbg_stop([(V2(modT.v), 32)])
    gmod = P.sb("gmod", [128, 8, 2], F32)
    for kc in range(8):
        P.ts("pool", gmod[:, kc, :], modT[:, 8 + kc, :], 1.0, sm[:, S_NG + kc:S_NG + kc + 1], ALU.add, ALU.mult)

    if stop == "pre2":
        return dbg_stop([(V2(modT.v), 32), (V2(gmod.v), 16)])
    Wb = P.sb("Wb", [128, 8, 1280], BF16)
    wst = [P.sb("wst%d" % i, [128, 1280], F32) for i in range(2)]
    for kc in range(8):
        P.dma(wst[kc % 2].v, win_d[kc * 128:(kc + 1) * 128, :])
        P.copy("pool", Wb[:, kc, :], wst[kc % 2].v)

    if stop == "pre":
        dbg = P.dram_out("dbg", [128, 64], F32)
        dsb = P.sb("dsb", [128, 64], F32)
        P.copy("dve", dsb[:, 0:32], V2(modT.v))
        P.copy("dve", dsb[:, 32:48], V2(gmod.v))
        P.copy("dve", dsb[:, 48:64], Wb[:, 7, 0:16])
        P.dma(dbg.v, dsb.v)
        P.emit()
        return nc
    xin = [P.sb("xin%d" % i, [128, 8, WH], F32) for i in range(2)]
    hT = P.sb("hT", [128, 8, WH], BF16)
    sqb = [P.sb("sqb%d" % i, [128, WH], F32) for i in range(2)]
    rstd = P.sb("rstd", [128, WH], F32)
    htmp = [P.sb("htmp%d" % i, [128, WH], F32) for i in range(2)]
    ps_proj = [P.ps("ps_proj%d" % i, [128, 512]) for i in range(2)]
    ps_a = P.ps("ps_a", [128, 512])
    u_sb = [P.sb("u_sb%d" % i, [128, WH], F32) for i in range(2)]
    s_sb = [P.sb("s_sb%d" % i, [128, W], F32) for i in range(2)]
    t_sb = [P.sb("t_sb%d" % i, [128, W], F32) for i in range(2)]

    def blk(name):
        return P.sb(name, [128, W], F32)

    Rb = [blk("R%d" % p) for p in range(2)]
    Kb = [blk("K%d" % p) for p in range(2)]
    Vb = [blk("V%d" % p) for p in range(2)]
    SG = [blk("SG%d" % p) for p in range(2)]
    LW = blk("LW")
    LA = blk("LA")
    TLW = blk("TLW")
    LOGW = [blk("LOGW%d" % p) for p in range(2)]
    Ab = [blk("A%d" % p) for p in range(2)]
    KQ = [blk("KQ%d" % p) for p in range(2)]
    KK = [blk("KK%d" % p) for p in range(2)]
    KD = [blk("KD%d" % p) for p in range(2)]
    KD0 = [blk("KD0%d" % p) for p in range(2)]
    T1 = [blk("T1%d" % p) for p in range(2)]
    T2 = [blk("T2%d" % p) for p in range(2)]
    CL = [blk("CL%d" % p) for p in range(2)]
    PRE = [blk("PRE%d" % p) for p in range(2)]
    E1 = [blk("E1%d" % p) for p in range(2)]
    E2 = [blk("E2%d" % p) for p in range(2)]
    E3 = [blk("E3%d" % p) for p in range(2)]
    AT = [blk("AT%d" % p) for p in range(2)]
    RT = [blk("RT%d" % p) for p in range(2)]
    BT = [blk("BT%d" % p) for p in range(2)]
    KT = [blk("KT%d" % p) for p in range(2)]
    BH = [blk("BH%d" % p) for p in range(2)]
    KH = [blk("KH%d" % p) for p in range(2)]
    DG = [P.sb("DG%d" % p, [128, 256], F32) for p in range(2)]
    OB = [blk("OB%d" % p) for p in range(2)]
    OF = [blk("OF%d" % p) for p in range(2)]
    YB = [P.sb("YB%d" % p, [128, W], BF16) for p in range(2)]

    def sbp(name, shape):
        return [P.sb("%s%d" % (name, p), shape, F32) for p in range(2)]

    Lm = sbp("Lm", [128, 256])
    NM = sbp("NM", [128, 512])
    KM = sbp("KM", [128, 512])
    Lk = [sbp("Lk%d_" % i, [128, 256]) for i in range(2)]
    Nk = [sbp("Nk%d_" % i, [128, 256]) for i in range(2)]
    Xk = [sbp("Xk%d_" % i, [128, 256]) for i in range(2)]
    Zb = sbp("Zb", [128, 256])
    TZ = sbp("TZ", [128, 256])
    VT = sbp("VT", [128, 128])
    BHT = sbp("BHT", [128, 128])
    KHT = sbp("KHT", [128, 128])
    RPT = sbp("RPT", [128, 128])
    PT = sbp("PT", [128, 128])
    ST = [sbp("ST%d_" % i, [128, 128]) for i in range(3)]
    BTbd = sbp("BTbd", [128, 512])
    KTbd = sbp("KTbd", [128, 512])
    DGd = sbp("DGd", [128, 512])
    BHTc = [sbp("BHTc%d_" % i, [128, 128]) for i in range(2)]
    KHTc = [sbp("KHTc%d_" % i, [128, 128]) for i in range(2)]
    RPTm = [sbp("RPTm%d_" % i, [128, 128]) for i in range(2)]
    PTbd = [sbp("PTbd%d_" % i, [128, 128]) for i in range(2)]
    T3 = sbp("T3", [128, 128])
    OTK = sbp("OTK", [128, 128])
    for p in range(2):
        for bb in (BTbd[p], KTbd[p], BHTc[0][p], BHTc[1][p], KHTc[0][p], KHTc[1][p], RPTm[0][p], RPTm[1][p]):
            P.memset("pool", bb.v, 0.0)
    psB = [P.ps("psB%d" % p, [128, 512]) for p in range(2)]
    psC = [P.ps("psC%d" % p, [128, 512]) for p in range(2)]

    def HH(buf, h):
        return buf[:, h * 128:(h + 1) * 128]

    def NMa(p, h):
        return NM[p][:, h * 256:h * 256 + 128]

    def NMb(p, h):
        return NM[p][:, h * 256 + 128:h * 256 + 256]

    def KMa(p, h):
        return KM[p][:, h * 256:h * 256 + 128]

    def KMb(p, h):
        return KM[p][:, h * 256 + 128:h * 256 + 256]

    def V3(view, h):
        return View(view.buf, view.ap.rearrange("p (h c) -> p h c", h=h))


    def x_cols(blk_id):
        if blk_id < 0:
            return 0
        return 258 + 256 * blk_id

    def tok0(blk_id):
        return 0 if blk_id < 0 else TC + 256 * blk_id

    xa_r = xa_d.t[:].rearrange("(k p) c -> p k c", p=128)

    def issue_x(blk_id, slot):
        c0 = x_cols(blk_id)
        P.dma(xin[slot].v, View(xa_d, xa_r[:, :, c0:c0 + WH]))

    evac_flip = [0]

    def evac(out, in_):
        evac_flip[0] ^= 1
        P.copy("act" if evac_flip[0] else "dve", out, in_)


    epsb = P.sb("epsb", [128, 4], F32)
    P.memset("pool", epsb[:, 0:1], RMS_EPS)
    P.memset("pool", epsb[:, 1:2], 1e-12)
    P.memset("pool", epsb[:, 2:3], GN_EPS)

    def stage_a(blk_id, slot, d):
        col = 1 if blk_id < 0 else 0
        xs = xin[slot]
        for kc in range(8):
            sq = sqb[kc % 2]
            P.act(RR(sq.v), xs[:, kc, :], AF.Square)
            P.matmul(ps_a[:, 0:WH], RR(ones128.v), RR(sq.v), start=(kc == 0), stop=(kc == 7))
        P.act(rstd.v, ps_a[:, 0:WH], AF.Sqrt, scale=1.0 / D, bias=epsb[:, 0:1])
        P.recip(rstd.v, rstd.v)
        for kc in range(8):
            tmp = htmp[kc % 2]
            P.stt(tmp.v, xs[:, kc, :], gmod[:, kc, col:col + 1], rstd.v, ALU.mult, ALU.mult)
            P.act(hT[:, kc, :], tmp.v, AF.Identity, bias=modT[:, kc, col:col + 1])
        if blk_id < 0 or blk_id == 0:
            P.memset("pool", hT[:, :, 0:1], 0.0)
        if blk_id < 0 or blk_id == NBLK_L - 1:
            P.memset("pool", hT[:, :, WH - 1:WH], 0.0)
        mixed_dst = [Rb[0], Rb[1], Kb[0], Kb[1], Vb[0], Vb[1], None, None, LW, LA]
        mix_ci = [0, 1, 2, 3, 4, 5, None, None, 6, 7]
        n = 0
        for cc in range(10):
            if cc in (6, 7) and d == 0:
                continue
            pp = ps_proj[n % 2]
            for kc in range(8):
                P.matmul(pp[:, 0:WH], Wb[:, kc, cc * 128:(cc + 1) * 128], hT[:, kc, :],
                         start=(kc == 0), stop=(kc == 7))
            if cc in (6, 7):
                P.act(SG[cc - 6].v, pp[:, 1:W + 1], AF.Silu)
            else:
                ci = mix_ci[cc]
                u = u_sb[n % 2]
                s_ = s_sb[n % 2]
                t_ = t_sb[n % 2]
                P.copy("act", u.v, pp[:, 0:WH])
                P.tt("dve", s_.v, u[:, 0:W], u[:, 2:W + 2], ALU.add)
                P.act(t_.v, u[:, 1:W + 1], AF.Identity, scale=omu(ci))
                P.stt(mixed_dst[cc].v, s_.v, hmu(ci), t_.v, ALU.mult, ALU.add)
            n += 1
        P.act(TLW.v, LW.v, AF.Tanh)
        def derive(p):
            pc = slice(p * 128, (p + 1) * 128)
            P.matmul(ps_a[:, p * W:(p + 1) * W], w2[64 * d:64 * d + 64, pc], TLW[64 * d:64 * d + 64, :])
            yield
            P.act(LOGW[p].v, ps_a[:, p * W:(p + 1) * W], AF.Sigmoid, bias=sm[:, S_W0 + 2 * d + p:S_W0 + 2 * d + p + 1])
            yield
            P.ts("dve", LOGW[p].v, LOGW[p].v, -EXPM05, None, ALU.mult)
            yield
            P.matmul(ps_a[:, p * W:(p + 1) * W], a2[64 * d:64 * d + 64, pc], LA[64 * d:64 * d + 64, :])
            yield
            P.act(Ab[p].v, ps_a[:, p * W:(p + 1) * W], AF.Sigmoid, bias=sm[:, S_A0 + 2 * d + p:S_A0 + 2 * d + p + 1])
            yield
            P.act(KQ[p].v, Kb[p].v, AF.Identity, scale=sm[:, S_KK + p:S_KK + p + 1])
            yield
            P.act(T1[p].v, KQ[p].v, AF.Square)
            yield
            P.matmul(ps_a[:, p * W:(p + 1) * W], bones, T1[p].v)
            yield
            P.act(T2[p].v, ps_a[:, p * W:(p + 1) * W], AF.Sqrt, bias=epsb[:, 1:2])
            yield
            P.recip(T2[p].v, T2[p].v)
            yield
            P.tt("pool", KK[p].v, KQ[p].v, T2[p].v, ALU.mult)
            yield
            P.act(T1[p].v, Ab[p].v, AF.Identity, scale=sm[:, S_KA + p:S_KA + p + 1], bias=omka(p))
            yield
            P.tt("pool", KD[p].v, Kb[p].v, T1[p].v, ALU.mult)
            yield
            if d == 1:
                P.matmul(ps_a[:, p * W:(p + 1) * W], a2[0:64, pc], LA[0:64, :])
                yield
                P.act(T2[p].v, ps_a[:, p * W:(p + 1) * W], AF.Sigmoid, bias=sm[:, S_A0 + p:S_A0 + p + 1])
                yield
                P.act(T2[p].v, T2[p].v, AF.Identity, scale=sm[:, S_KA + p:S_KA + p + 1], bias=omka(p))
                yield
                P.tt("pool", KD0[p].v, Kb[p].v, T2[p].v, ALU.mult)
                yield
            for ch in range(4):
                sl = slice(ch * 64, (ch + 1) * 64)
                P.scan(PRE[p][:, sl], ones64, LOGW[p][:, sl], 0.0, ALU.mult, ALU.add)
                yield
            if d == 0:
                clb = PRE[p]
            else:
                clb = CL[p]
                for ch in range(4):
                    sl = slice(ch * 64, (ch + 1) * 64)
                    P.act(CL[p][:, sl], PRE[p][:, sl], AF.Identity, scale=-1.0,
                          bias=PRE[p][:, ch * 64 + 63:ch * 64 + 64])
                    yield
                P.tt("pool", CL[p].v, CL[p].v, LOGW[p].v, ALU.add)
                yield
            P.act(E1[p].v, clb.v, AF.Exp)
            yield
            P.act(E2[p].v, clb.v, AF.Exp, scale=-1.0)
            yield
            P.tt("pool", T1[p].v, clb.v, LOGW[p].v, ALU.subtract)
            yield
            P.act(E3[p].v, T1[p].v, AF.Exp)
            yield
            P.stt(RR(AT[p].v), KK[p].v, -1.0, E3[p].v, ALU.mult, ALU.mult)
            yield
            P.tt("dve", RR(RT[p].v), Rb[p].v, E1[p].v, ALU.mult)
            yield
            P.tt("pool", T1[p].v, KK[p].v, Ab[p].v, ALU.mult)
            yield
            P.tt("pool", BT[p].v, T1[p].v, E2[p].v, ALU.mult)
            yield
            P.tt("pool", KT[p].v, KD[p].v, E2[p].v, ALU.mult)
            yield
            for ch in range(4):
                sl = slice(ch * 64, (ch + 1) * 64)
                gc = ch * 64 + 63 if d == 0 else ch * 64
                gcol = E1[p][:, gc:gc + 1]
                P.ts("dve", BH[p][:, sl], BT[p][:, sl], gcol, None, ALU.mult)
                yield
                P.act(KH[p][:, sl], KT[p][:, sl], AF.Identity, scale=gcol)
                yield
                P.ts("dve", DGd[p][:, ch * 128:(ch + 1) * 128], ident, gcol, None, ALU.mult)
                yield
            for h in range(2):
                hp = slice(64 * h, 64 * h + 64)
                for tl2 in range(2):
                    q = (tl2 * 2 + h) * 128
                    P.copy("dve", RR(BTbd[p][hp, q:q + 128]), BT[p][hp, tl2 * 128:(tl2 + 1) * 128])
                    yield
                    P.copy("act", RR(KTbd[p][hp, q:q + 128]), KT[p][hp, tl2 * 128:(tl2 + 1) * 128])
                    yield

        gens = [derive(p) for p in range(2)]
        while gens:
            for g in list(gens):
                try:
                    next(g)
                except StopIteration:
                    gens.remove(g)

    def stage_b(p, tl, d, sw_state, upto=99):
        m1, m2 = masks[d]
        cs = slice(tl * 128, (tl + 1) * 128)
        pc = psC[p]
        pb = psB[p]
        btbd = lambda h: BTbd[p][:, (tl * 2 + h) * 128:(tl * 2 + h + 1) * 128]
        ktbd = lambda h: KTbd[p][:, (tl * 2 + h) * 128:(tl * 2 + h + 1) * 128]
        P.matmul(pc[:, 0:256], RR(AT[p][:, cs]), RR(BTbd[p][:, tl * 256:(tl + 1) * 256]))
        P.tt("dve", RR(Lm[p].v), pc[:, 0:256], m1, ALU.mult)
        yield
        for h in range(2):
            P.matmul(pb[:, h * 256:h * 256 + 128], RR(btbd(h)), RR(AT[p][:, cs]))
            P.matmul(pb[:, h * 256 + 128:h * 256 + 256], RR(btbd(h)), RR(RT[p][:, cs]))
        P.tt("dve", RR(NM[p].v), pb[:, 0:512], m2, ALU.mult)
        yield
        for h in range(2):
            P.matmul(pb[:, h * 256:h * 256 + 128], RR(ktbd(h)), RR(AT[p][:, cs]))
            P.matmul(pb[:, h * 256 + 128:h * 256 + 256], RR(ktbd(h)), RR(RT[p][:, cs]))
        P.tt("dve", RR(KM[p].v), pb[:, 0:512], m2, ALU.mult)
        yield
        if upto <= 1:
            return
        X = Xk[0][p]
        for h in range(2):
            P.tt("dve", RR(HH(X, h)), NMa(p, h), ident, ALU.add)
        Lc = Lm[p]
        Nc_views = [NMa(p, h) for h in range(2)]
        xi = 0
        for k in range(1, 6):
            Ln = Lk[k % 2][p]
            for h in range(2):
                P.matmul(pc[:, h * 128:(h + 1) * 128], RR(Nc_views[h]), RR(HH(Lc, h)))
            P.copy("act", RR(Ln.v), pc[:, 0:256])
            yield
            if k < 5:
                Nn = Nk[k % 2][p]
                for h in range(2):
                    P.matmul(pc[:, 256 + h * 128:256 + (h + 1) * 128], RR(HH(Lc, h)), RR(Nc_views[h]))
                P.copy("act", RR(Nn.v), pc[:, 256:512])
            for h in range(2):
                P.matmul(pb[:, h * 128:(h + 1) * 128], RR(HH(Ln, h)), RR(HH(Xk[xi][p], h)))
            Xn = Xk[1 - xi][p]
            P.tt("dve", RR(Xn.v), pb[:, 0:256], Xk[xi][p].v, ALU.add)
            yield
            xi = 1 - xi
            Lc = Ln
            if k < 5:
                Nc_views = [HH(Nn, h) for h in range(2)]
        X = Xk[xi][p]
        if upto <= 2:
            return
        P.transpose(pc[:, 0:128], AT[p][:, cs], ident)
        P.copy("act", RR(Zb[p][:, 0:128]), pc[:, 0:128])
        yield
        P.transpose(pc[:, 128:256], Vb[p][:, cs], ident)
        P.copy("dve", RR(VT[p].v), pc[:, 128:256])
        yield
        P.transpose(pc[:, 256:384], BH[p][:, cs], ident)
        P.copy("act", RR(BHTc[0][p][0:64, :]), pc[0:64, 256:384])
        yield
        P.copy("dve", RR(BHTc[1][p][64:128, :]), pc[64:128, 256:384])
        yield
        P.transpose(pc[:, 384:512], KH[p][:, cs], ident)
        P.copy("act", KHTc[0][p][0:64, :], pc[0:64, 384:512])
        yield
        P.copy("dve", KHTc[1][p][64:128, :], pc[64:128, 384:512])
        yield
        for h in range(2):
            P.matmul(pb[:, h * 64:(h + 1) * 64], RR(KMa(p, h)), RR(VT[p][:, h * 64:(h + 1) * 64]))
        P.copy("act", RR(Zb[p][:, 128:256]), pb[:, 0:128])
        yield
        for part in range(2):
            for h in range(2):
                q = (part * 2 + h) * 64
                P.matmul(pb[:, 128 + q:128 + q + 64], RR(HH(X, h)), RR(Zb[p][:, q:q + 64]))
        P.copy("dve", RR(TZ[p].v), pb[:, 128:384])
        yield
        if upto <= 3:
            return
        for h in range(2):
            P.matmul(pc[:, h * 128:(h + 1) * 128], RR(TZ[p][:, 0:128]), RR(NMb(p, h)))
        for h in range(2):
            hp = slice(64 * h, 64 * h + 64)
            P.tt("dve", RPT[p][hp, :], pc[hp, h * 128:(h + 1) * 128], RT[p][hp, cs], ALU.add)
            yield
        P.copy("pool", RPTm[0][p][:, 0:64], RPT[p][:, 0:64])
        P.copy("pool", RPTm[1][p][:, 64:128], RPT[p][:, 64:128])
        for c in range(2):
            P.matmul(pc[:, 256 + c * 128:256 + (c + 1) * 128], RR(TZ[p][:, 0:128]), RR(BHTc[c][p].v))
        for c in range(2):
            ch = 2 * tl + c
            P.tt("dve", T3[p].v, pc[:, 256 + c * 128:256 + (c + 1) * 128], bones, ALU.mult)
            yield
            P.tt("pool", PTbd[c][p].v, T3[p].v, DGd[p][:, ch * 128:(ch + 1) * 128], ALU.add)
        if upto <= 5:
            return
        order = (0, 1) if d == 0 else (1, 0)
        s_at = {}
        for c in order:
            si = sw_state[p]
            S_in = ST[si][p]
            S_out = ST[(si + 1) % 3][p]
            s_at[c] = S_in
            P.matmul(pb[:, 0:128], PTbd[c][p].v, S_in.v, start=True, stop=False)
            P.matmul(pb[:, 0:128], BHTc[c][p].v, TZ[p][:, 128:256], start=False, stop=False)
            P.matmul(pb[:, 0:128], KHTc[c][p].v, VT[p].v, start=False, stop=True)
            P.tt("dve", S_out.v, pb[:, 0:128], bones, ALU.mult)
            yield
            sw_state[p] = (si + 1) % 3
        if upto <= 6:
            return
        P.matmul(pb[:, 128:256], RPTm[0][p].v, s_at[0].v, start=True, stop=False)
        P.matmul(pb[:, 128:256], RPTm[1][p].v, s_at[1].v, start=False, stop=False)
        for h in range(2):
            P.matmul(pb[:, 128 + h * 64:128 + (h + 1) * 64], RR(NMb(p, h)), RR(TZ[p][:, 128 + h * 64:128 + (h + 1) * 64]),
                     start=False, stop=False)
            P.matmul(pb[:, 128 + h * 64:128 + (h + 1) * 64], RR(KMb(p, h)), RR(VT[p][:, h * 64:(h + 1) * 64]),
                     start=False, stop=(h == 1))
        P.copy("act", OTK[p].v, pb[:, 128:256])
        yield
        P.transpose(pb[:, 256:384], OTK[p].v, ident)
        if d == 0:
            P.copy("dve", OB[p][:, cs], pb[:, 256:384])
            yield
        else:
            P.tt("dve", OB[p][:, cs], pb[:, 256:384], OF[p][:, cs], ALU.add)
            yield

    def readout(blk_id):
        t0 = tok0(blk_id)
        for p in range(2):
            P.matmul(ps_a[:, 0:W], bones, OB[p].v)
            P.stt(T1[p].v, ps_a[:, 0:W], -1.0 / 64, OB[p].v, ALU.mult, ALU.add)
            P.act(T2[p].v, T1[p].v, AF.Square)
            P.matmul(ps_a[:, W:2 * W], bones, T2[p].v)
            P.act(T2[p].v, ps_a[:, W:2 * W], AF.Sqrt, scale=1.0 / 64, bias=epsb[:, 2:3])
            P.recip(T2[p].v, T2[p].v)
            P.tt("pool", T1[p].v, T1[p].v, T2[p].v, ALU.mult)
            P.act(T1[p].v, T1[p].v, AF.Identity, scale=sm[:, S_GG + p:S_GG + p + 1],
                  bias=sm[:, S_GB + p:S_GB + p + 1])
            P.tt("pool", T2[p].v, KD[p].v, KD0[p].v, ALU.add)
            P.stt(T2[p].v, Rb[p].v, sm[:, S_RK + p:S_RK + p + 1], T2[p].v, ALU.mult, ALU.mult)
            P.matmul(ps_a[:, 0:W], bones, T2[p].v)
            P.tt("dve", T2[p].v, ps_a[:, 0:W], Vb[p].v, ALU.mult)
            P.tt("pool", T1[p].v, T1[p].v, T2[p].v, ALU.add)
            P.tt("pool", YB[p].v, T1[p].v, SG[p].v, ALU.mult)
            if isinstance(y0_d, list):
                pi, off = (0, t0) if t0 < TC else (1 + (t0 - TC) // 2048, (t0 - TC) % 2048)
                P.dma(y0_d[pi][p * 128:(p + 1) * 128, off:off + W], YB[p].v)
            else:
                P.dma(y0_d[p * 128:(p + 1) * 128, t0:t0 + W], YB[p].v)

    for p in range(2):
        P.memset("pool", ST[0][p].v, 0.0)
    for d in range(2):
        blocks = [-1] + (list(range(NBLK_L)) if d == 0 else list(range(NBLK_L - 1, -1, -1)))
        if debug_out and isinstance(debug_out, int) and debug_out > 1:
            blocks = blocks[:debug_out]
        sw_state = [0, 0]
        if d == 1:
            for p in range(2):
                P.memset("pool", ST[0][p].v, 0.0)
        issue_x(blocks[0], 0)
        for bi, b in enumerate(blocks):
            slot = bi % 2
            if bi + 1 < len(blocks):
                issue_x(blocks[bi + 1], 1 - slot)
            t0 = tok0(b)
            if d == 1:
                for p in range(2):
                    P.dma(OF[p].v, of_d[p * 128:(p + 1) * 128, t0:t0 + W])
            stage_a(b, slot, d)
            if stop == "a":
                return dbg_stop([(Rb[0].v, 256), (KK[1].v, 256), (LOGW[0].v, 256), (Ab[1].v, 256), (KD[0].v, 256),
                                 (AT[0].v, 256), (RT[0].v, 256), (BT[0].v, 256), (KT[0].v, 256), (BH[0].v, 256),
                                 (DG[0].v, 256), (Vb[1].v, 256)])
            tiles = (0, 1) if d == 0 else (1, 0)
            for tl in tiles:
                if not (stop and stop[0] == "b"):
                    gens = [stage_b(p, tl, d, sw_state) for p in range(2)]
                    while gens:
                        for g in list(gens):
                            try:
                                next(g)
                            except StopIteration:
                                gens.remove(g)
                    continue
                for p in range(2):
                    for _ in stage_b(p, tl, d, sw_state, upto=int(stop[1:]) if (stop and stop[0] == "b" and len(stop) > 1) else 99):
                        pass
                    if stop and stop[0] == "b":
                        return dbg_stop([(OB[0][:, 0:128], 128), (ST[sw_state[0]][0].v, 128), (TZ[0].v, 256),
                                         (Lm[0].v, 256), (NM[0].v, 512), (Xk[1][0].v, 256), (RPT[0].v, 128), (PTbd[0][0].v, 128)])
            if d == 0:
                for p in range(2):
                    P.dma(of_d[p * 128:(p + 1) * 128, t0:t0 + W], OB[p].v)
            else:
                readout(b)
    if ctx is None:
        P.emit()
    return nc


def l0_inputs(inp, b, hg):
    f32 = np.float32
    x, ctx = inp["x"], inp["ctx"]
    z1 = np.zeros((D, 1), f32)
    xa = np.concatenate([z1, ctx[b].T, z1, z1, x[b].T, z1], axis=1)
    ct = np.zeros((128, 16), f32)
    cb = inp["c"][b].reshape(8, 128).T
    cc = inp["c_ctx"].reshape(8, 128).T
    ct[:, 0::2] = cb
    ct[:, 1::2] = cc
    adaw = np.ascontiguousarray(inp["ada_w"][0][:, 0:2048])
    hc = slice(hg * 256, (hg + 1) * 256)
    rw_in = inp["rw_in"][0]
    cols = []
    for X in range(4):
        cols.append(rw_in[:, X * 1024 + hg * 256: X * 1024 + (hg + 1) * 256])
    cols.append(rw_in[:, 4096:4352])
    win = np.ascontiguousarray(np.concatenate(cols, axis=1))
    sm = np.zeros((128, NSMALL), f32)
    sm[:, 0:8] = inp["norm_g"][0].reshape(8, 128).T
    sm[:, 8:24] = inp["ada_b"][0][:2048].reshape(16, 128).T
    mu = inp["rw_mu"][0]
    mus = []
    for X in range(3):
        for p in range(2):
            mus.append(mu[X * 1024 + hg * 256 + p * 128: X * 1024 + hg * 256 + (p + 1) * 128])
    mus.append(mu[3072:3200])
    mus.append(mu[3200:3328])
    sm[:, 24:32] = np.stack(mus, axis=1)
    for d in range(2):
        for p in range(2):
            sm[:, 32 + 2 * d + p] = inp["rw_w0"][0][d, hg * 256 + p * 128: hg * 256 + (p + 1) * 128]
            sm[:, 36 + 2 * d + p] = inp["rw_a0"][0][d, hg * 256 + p * 128: hg * 256 + (p + 1) * 128]
    for p in range(2):
        sl = slice(hg * 256 + p * 128, hg * 256 + (p + 1) * 128)
        sm[:, 40 + p] = inp["rw_kk"][0][sl]
        sm[:, 42 + p] = inp["rw_ka"][0][sl]
        sm[:, 44 + p] = inp["rw_rk"][0].reshape(-1)[sl]
        sm[:, 46 + p] = inp["rw_gn_g"][0][sl]
        sm[:, 48 + p] = inp["rw_gn_b"][0][sl]
    w2 = np.ascontiguousarray(inp["rw_w2"][0][:, :, hc].reshape(128, 256))
    a2 = np.ascontiguousarray(inp["rw_a2"][0][:, :, hc].reshape(128, 256))
    return {"xa": np.ascontiguousarray(xa), "ct": ct, "adaw": adaw, "smalls": sm, "win": win,
            "w2": w2, "a2": a2, "cst": l0_consts()}


SUBLN_EPS = 1e-5
LAM_INIT = 0.8 - 0.6 * float(np.exp(-0.3 * 1))
QSCALE = 0.125
NKT = TA // 128
NQB = TL // 256


def tok0(bi):
    return 0 if bi < 0 else TC + 256 * bi


def mod_compute(P, adaw_d, ncol, sct, adab_view, ps, modT, adaw_sb):
    adaw_r = adaw_d.t[:].rearrange("(k p) m -> p k m", p=128)
    for piece in range(ncol // 256):
        P.dma(adaw_sb.v, View(adaw_d, adaw_r[:, :, piece * 256:(piece + 1) * 256]))
        for mcl in range(2):
            mc = piece * 2 + mcl
            for kc in range(8):
                P.matmul(ps[:, 0:2], adaw_sb[:, kc, mcl * 128:(mcl + 1) * 128], sct[:, kc * 2:kc * 2 + 2],
                         start=(kc == 0), stop=(kc == 7))
            P.ts("dve", modT[:, mc, :], ps[:, 0:2], adab_view(mc), None, ALU.add)


def rope_tables():
    rows = TL // 64
    t = np.arange(TL)
    row = (t // 64).astype(np.float32)
    colid = (t % 64).astype(np.float32)
    inv = (10000.0 ** (-np.arange(16, dtype=np.float32) / 16)).astype(np.float32)
    ang_r = row[None, :] * inv[:, None]
    ang_c = colid[None, :] * inv[:, None]
    cos64 = np.concatenate([np.cos(ang_r), np.cos(ang_r), np.cos(ang_c), np.cos(ang_c)], axis=0)
    sin64 = np.concatenate([np.sin(ang_r), np.sin(ang_r), np.sin(ang_c), np.sin(ang_c)], axis=0)
    cosT = np.concatenate([cos64, cos64], axis=0).astype(np.float32)
    sinT = np.concatenate([sin64, sin64], axis=0).astype(np.float32)
    R = np.zeros((128, 128), np.float32)
    for base in range(0, 128, 32):
        for f in range(16):
            R[base + 16 + f, base + f] = -1.0
            R[base + f, base + 16 + f] = 1.0
    return cosT, sinT, R


def build_l1(stop=None, ctx=None):
    if ctx is None:
        nc = bass.Bass("TRN2", target_bir_lowering=False)
        P = Prog(nc)
        xa_d = P.dram_in("xa", [D, TA], F32)
        y0_d = P.dram_in("y0g", [D, TA], BF16)
        ct_d = P.dram_in("ct", [128, 16], F32)
        adaw0_d = P.dram_in("adaw0g", [D, 1024], F32)
        adaw1_d = P.dram_in("adaw1", [D, 2048], F32)
        sm_d = P.dram_in("smalls", [128, 48], F32)
        wo_d = P.dram_in("wo", [D, D], F32)
        wd_d = P.dram_in("wd", [D, 1024], F32)
        cst_d = P.dram_in("cst", [128, 384], F32)
        cos_d = P.dram_in("cosT", [128, TL], F32)
        sin_d = P.dram_in("sinT", [128, TL], F32)
        lam_d = P.dram_in("lamv", [128, 256], F32)
        subw_d = P.dram_in("subw", [128, 128], F32)
        y1_d = P.dram_out("y1", [256, TL], BF16)
        xn_d = P.dram_out("xn", [D, TL], F32)
    else:
        nc, P = ctx["nc"], ctx["P"]
        (xa_d, y0_d, ct_d, adaw0_d, adaw1_d, sm_d, wo_d, wd_d, cst_d, cos_d, sin_d, lam_d, subw_d, y1_d, xn_d) = [
            ctx[k] for k in ("b_xa", "y0g", "ct", "b_adaw0g", "b_adaw1", "b_smalls", "b_wo", "b_wd", "b_cst",
                             "b_cosT", "b_sinT", "b_lamv", "b_subw", "y1loc", "xn_s")]

    cst = P.sb("cst", [128, 384], F32)
    P.dma(cst.v, cst_d.v)
    ident = cst[:, 0:128]
    bones = cst[:, 128:256]
    rrot = cst[:, 256:384]
    sm = P.sb("sm", [128, 48], F32)
    P.dma(sm.v, sm_d.v)
    S_NG, S_B0, S_B1, S_QN, S_KN = 0, 8, 16, 32, 33
    ones128 = P.sb("ones128", [128, 128], F32)
    P.memset("pool", ones128.v, 1.0)
    epsb = P.sb("epsb", [128, 4], F32)
    P.memset("pool", epsb[:, 0:1], RMS_EPS)
    P.memset("pool", epsb[:, 1:2], SUBLN_EPS)
    banks = [P.ps("bank%d" % i, [128, 512]) for i in range(8)]

    ct = P.sb("ct", [128, 16], F32)
    P.dma(ct.v, ct_d.v)
    sct = P.sb("sct", [128, 16], F32)
    P.act(sct.v, ct.v, AF.Silu)
    adaw_sb = P.sb("adaw_sb", [128, 8, 256], F32)
    mod0 = P.sb("mod0", [128, 8, 2], F32)
    mod1 = P.sb("mod1", [128, 16, 2], F32)
    mod_compute(P, adaw0_d, 1024, sct, lambda mc: sm[:, S_B0 + mc:S_B0 + mc + 1], banks[0], mod0, adaw_sb)
    mod_compute(P, adaw1_d, 2048, sct, lambda mc: sm[:, S_B1 + mc:S_B1 + mc + 1], banks[0], mod1, adaw_sb)
    gmod = P.sb("gmod", [128, 8, 2], F32)
    for kc in range(8):
        P.ts("pool", gmod[:, kc, :], mod1[:, 8 + kc, :], 1.0, sm[:, S_NG + kc:S_NG + kc + 1], ALU.add, ALU.mult)

    lamv = P.sb("lamv", [128, 256], F32)
    P.dma(lamv.v, lam_d.v)
    lt = P.sb("lt", [128, 128], F32)
    lsc = P.sb("lsc", [128, 8], F32)
    P.tt("pool", lt[:, 0:64], lamv[:, 0:64], lamv[:, 64:128], ALU.mult)
    P.tt("pool", lt[:, 64:128], lamv[:, 128:192], lamv[:, 192:256], ALU.mult)
    P.reduce(lsc[:, 0:1], lt[:, 0:64], AX.X, ALU.add)
    P.reduce(lsc[:, 1:2], lt[:, 64:128], AX.X, ALU.add)
    P.act(lsc[:, 2:4], lsc[:, 0:2], AF.Exp)
    P.tt("pool", lsc[:, 4:5], lsc[:, 2:3], lsc[:, 3:4], ALU.subtract)
    P.ts("pool", lsc[:, 5:6], lsc[:, 4:5], LAM_INIT, -1.0, ALU.add, ALU.mult)
    neglam = lsc[:, 5:6]
    subw = P.sb("subw", [128, 128], F32)
    P.dma(subw.v, subw_d.v)
    P.ts("pool", subw.v, subw.v, 1.0 - LAM_INIT, None, ALU.mult)

    Wo = P.sb("Wo", [128, 8, 1024], BF16)
    Wd = P.sb("Wd", [128, 8, 1024], BF16)
    wst = [P.sb("wst%d" % i, [128, 1024], F32) for i in range(1)] * 2
    n = 0
    for src, dst in ((wo_d, Wo), (wd_d, Wd)):
        for kc in range(8):
            P.dma(wst[n % 2].v, src[kc * 128:(kc + 1) * 128, :])
            P.copy("pool", dst[:, kc, :], wst[n % 2].v)
            n += 1

    xin = [P.sb("xin%d" % i, [128, 8, W], F32) for i in range(2)]
    yin = [P.sb("yin%d" % i, [128, 8, W], BF16) for i in range(2)]
    xn = P.sb("xn", [128, 8, W], F32)
    hT = P.sb("hT", [128, 8, W], BF16)
    sqb = [P.sb("sqb%d" % i, [128, W], F32) for i in range(2)]
    rstd = P.sb("rstd", [128, W], F32)
    htmp = [P.sb("htmp%d" % i, [128, W], F32) for i in range(2)]
    cosb = P.sb("cosb", [128, W], F32)
    sinb = P.sb("sinb", [128, W], F32)
    qraw = P.sb("qraw", [128, W], F32)
    qsq = P.sb("qsq", [128, W], F32)
    qrs = P.sb("qrs", [128, W], F32)
    qn_ = P.sb("qn_", [128, W], F32)
    qt1 = P.sb("qt1", [128, W], F32)
    qt2 = P.sb("qt2", [128, W], F32)
    KTb = P.sb("KTb", [128, TA], BF16)
    QTb = P.sb("QTb", [128, NQB * 512], BF16)
    P.memset("pool", QTb.v, 0.0)
    Vx = P.sb("Vx", [128, NKT, 130], BF16)
    P.memset("pool", Vx[:, :, 129:130], 0.0)
    Gs = P.sb("Gs", [128, TL // 128, 128], BF16)
    P.memset("pool", Vx[:, :, 128:129], 1.0)
    pT = [P.sb("pT%d" % i, [128, 512], BF16) for i in range(4)]
    vg_sb = [P.sb("vg_sb%d" % i, [128, 512], F32) for i in range(2)]
    o_sb = P.sb("o_sb", [128, 128], F32)
    o_sq = P.sb("o_sq", [128, 128], F32)
    ybs = [P.sb("yb%d" % i, [128, 256], BF16) for i in range(2)]
    zs = P.sb("zs", [128, 8], F32)
    ps_bf = P.ps("ps_bf_unused", [128, 2], F32) if False else None

    xa_r = xa_d.t[:].rearrange("(k p) c -> p k c", p=128)
    y0_r = None if isinstance(y0_d, list) else y0_d.t[:].rearrange("(k p) c -> p k c", p=128)
    xn_r = xn_d.t[:].rearrange("(k p) c -> p k c", p=128)

    def issue(bi, slot, first_pass=True):
        t0 = tok0(bi)
        if (not first_pass) and bi >= 0:
            return
        P.dma(xin[slot].v, View(xa_d, xa_r[:, :, t0:t0 + W]))
        if isinstance(y0_d, list):
            pi, off = (0, t0) if t0 < TC else (1 + (t0 - TC) // 2048, (t0 - TC) % 2048)
            yr = y0_d[pi].t[:].rearrange("(k p) c -> p k c", p=128)
            P.dma(yin[slot].v, View(y0_d[pi], yr[:, :, off:off + W]))
        else:
            P.dma(yin[slot].v, View(y0_d, y0_r[:, :, t0:t0 + W]))

    def stage_a(bi, slot, h, hp_quarter, first_pass):
        col = 1 if bi < 0 else 0
        t0 = tok0(bi)
        lat0 = t0 - TC
        reuse = (not first_pass) and bi >= 0
        if reuse:
            P.dma(xn.v, View(xn_d, xn_r[:, :, lat0:lat0 + W]))
        for oc in range(0 if reuse else 8):
            pp = banks[(0, 1, 7)[oc % 3]]
            for kc in range(8):
                P.matmul(pp[:, 0:W], Wo[:, kc, oc * 128:(oc + 1) * 128], yin[slot][:, kc, :],
                         start=(kc == 0), stop=(kc == 7))
            P.stt(xn[:, oc, :], pp[:, 0:W], mod0[:, oc, col:col + 1], xin[slot][:, oc, :], ALU.mult, ALU.add)
        if first_pass and bi >= 0 and stop != 'noxn':
            P.dma(View(xn_d, xn_r[:, :, lat0:lat0 + W]), xn.v)
        if stop == 'a1':
            return
        for kc in range(8):
            sq = sqb[kc % 2]
            P.act(sq.v, xn[:, kc, :], AF.Square)
            P.matmul(banks[2][:, 0:W], ones128.v, sq.v, start=(kc == 0), stop=(kc == 7))
        P.act(rstd.v, banks[2][:, 0:W], AF.Sqrt, scale=1.0 / D, bias=epsb[:, 0:1])
        P.recip(rstd.v, rstd.v)
        for kc in range(8):
            tmp = htmp[kc % 2]
            P.stt(tmp.v, xn[:, kc, :], gmod[:, kc, col:col + 1], rstd.v, ALU.mult, ALU.mult)
            P.act(hT[:, kc, :], tmp.v, AF.Identity, bias=mod1[:, kc, col:col + 1])
        if stop == 'a2':
            return
        if bi >= 0:
            P.dma(cosb.v, cos_d[:, lat0:lat0 + W])
            P.dma(sinb.v, sin_d[:, lat0:lat0 + W])
        for which in (("q", "k") if bi >= 0 else ("k",)):
            cc = h if which == "q" else 2 + h
            pp = banks[3]
            for kc in range(8):
                P.matmul(pp[:, 0:W], Wd[:, kc, cc * 128:(cc + 1) * 128], hT[:, kc, :],
                         start=(kc == 0), stop=(kc == 7))
            P.copy("act", qraw.v, pp[:, 0:W])
            P.tt("pool", qsq.v, qraw.v, qraw.v, ALU.mult)
            P.matmul(banks[4][:, 0:W], bones, qsq.v)
            P.act(qrs.v, banks[4][:, 0:W], AF.Sqrt, scale=1.0 / 64, bias=epsb[:, 0:1])
            P.recip(qrs.v, qrs.v)
            wcol = sm[:, S_QN:S_QN + 1] if which == "q" else sm[:, S_KN:S_KN + 1]
            P.stt(qn_.v, qraw.v, wcol, qrs.v, ALU.mult, ALU.mult)
            if bi >= 0:
                P.matmul(banks[4][:, W:2 * W], rrot, qn_.v)
                P.tt("dve", qt1.v, banks[4][:, W:2 * W], sinb.v, ALU.mult)
                P.tt("pool", qt2.v, qn_.v, cosb.v, ALU.mult)
                if which == "q":
                    qb_ = lat0 // 256
                    P.tt("pool", QTb[0:64, qb_ * 512:qb_ * 512 + 256], qt1[0:64, :], qt2[0:64, :], ALU.add)
                    P.tt("pool", QTb[64:128, qb_ * 512 + 256:qb_ * 512 + 512], qt1[64:128, :], qt2[64:128, :], ALU.add)
                else:
                    P.tt("pool", KTb[:, t0:t0 + W], qt1.v, qt2.v, ALU.add)
            else:
                P.copy("pool", KTb[:, t0:t0 + W], qn_.v)
        if stop == 'a3':
            return
        for sub in range(2):
            pp = banks[5 + sub]
            for kc in range(8):
                P.matmul(pp[:, 0:512], hT[:, kc, sub * 128:(sub + 1) * 128], Wd[:, kc, 512:1024],
                         start=(kc == 0), stop=(kc == 7))
            kt = (t0 + sub * 128) // 128
            vg = vg_sb[sub]
            P.copy("dve", vg.v, pp[:, 0:512])
            P.copy("pool", Vx[:, kt, 0:128], vg[:, h * 128:(h + 1) * 128])
            if bi >= 0:
                qt = (lat0 + sub * 128) // 128
                P.act(Gs[:, qt, :], vg[:, 256 + h * 128:256 + (h + 1) * 128], AF.Silu)

    def attention(h, nqb=NQB):
        sbank = [(banks[0], banks[1]), (banks[2], banks[3])]
        acc = [[banks[4], banks[5]], [banks[6], banks[7]]]
        n = 0

        def s_mm(qb_, kt_, n_):
            P.matmul(banks[n_ % 4][:, 0:512], KTb[:, kt_ * 128:(kt_ + 1) * 128], QTb[:, qb_ * 512:(qb_ + 1) * 512])

        tiles = [(qb_, kt_) for qb_ in range(nqb) for kt_ in range(NKT)]
        LOOK = 3
        for j in range(min(LOOK, len(tiles))):
            s_mm(tiles[j][0], tiles[j][1], j)
        for qb in range(nqb):
            q0 = qb * 256
            for kt in range(NKT):
                sA = banks[n % 4]
                pt = pT[n % 4]
                if n + LOOK < len(tiles):
                    s_mm(tiles[n + LOOK][0], tiles[n + LOOK][1], n + LOOK)
                P.act(pt.v, sA[:, 0:512], AF.Exp, scale=QSCALE)
                if stop == 's1':
                    n += 1
                    continue
                for comp in range(2):
                    for qs in range(2):
                        P.matmul(acc[comp][qs][:, 0:130], pt[:, comp * 256 + qs * 128:comp * 256 + (qs + 1) * 128],
                                 Vx[:, kt, :], start=(kt == 0), stop=(kt == NKT - 1))
                n += 1
            if stop in ('s1', 's2'):
                continue
            ytile = ybs[qb % 2]
            for qs in range(2):
                a0 = acc[0][qs]
                a1 = acc[1][qs]
                P.recip(zs[:, 0:1], a0[:, 128:129])
                P.recip(zs[:, 1:2], a1[:, 128:129])
                P.tt("pool", zs[:, 2:3], zs[:, 1:2], neglam, ALU.mult)
                P.ts("dve", o_sb.v, a0[:, 0:128], zs[:, 0:1], None, ALU.mult)
                P.stt(o_sb.v, a1[:, 0:128], zs[:, 2:3], o_sb.v, ALU.mult, ALU.add)
                P.tt("pool", o_sq.v, o_sb.v, o_sb.v, ALU.mult)
                P.reduce(zs[:, 3:4], o_sq.v, AX.X, ALU.add)
                P.act(zs[:, 4:5], zs[:, 3:4], AF.Sqrt, scale=1.0 / 128, bias=epsb[:, 1:2])
                P.recip(zs[:, 4:5], zs[:, 4:5])
                P.stt(o_sb.v, o_sb.v, zs[:, 4:5], subw.v, ALU.mult, ALU.mult)
                qt = (q0 + qs * 128) // 128
                P.tt("pool", o_sq.v, o_sb.v, Gs[:, qt, :], ALU.mult)
                P.transpose(a0[:, 256:384], o_sq.v, ident)
                P.copy("act", ytile[:, qs * 128:(qs + 1) * 128], a0[:, 256:384])
            if isinstance(y1_d, list):
                P.dma(y1_d[q0 // 2048][h * 128:(h + 1) * 128, q0 % 2048:q0 % 2048 + 256], ytile.v)
            else:
                P.dma(y1_d[h * 128:(h + 1) * 128, q0:q0 + 256], ytile.v)

    return nc, P, issue, stage_a, attention


def build_l1_full(hp_quarter, do_attn=True, nheads=2, stop=None, nblocks=None, nqb=NQB, ctx=None):
    nc, P, issue, stage_a, attention = build_l1(stop, ctx)
    blocks = [-1] + list(range(NBLK_L))
    if nblocks:
        blocks = blocks[:nblocks]
    if stop == 'pre':
        P.emit()
        return nc
    for h in range(nheads):
        issue(blocks[0], 0, h == 0)
        for i, bi in enumerate(blocks):
            if i + 1 < len(blocks):
                issue(blocks[i + 1], (i + 1) % 2, h == 0)
            stage_a(bi, i % 2, h, hp_quarter, h == 0)
        if do_attn:
            attention(h, nqb)
    if ctx is None:
        P.emit()
    return nc


def l1_inputs(inp, b, hp, y0g_b):
    f32 = np.float32
    xa = np.ascontiguousarray(np.concatenate([inp["ctx"][b].T, inp["x"][b].T], axis=1))
    ct = np.zeros((128, 16), f32)
    ct[:, 0::2] = inp["c"][b].reshape(8, 128).T
    ct[:, 1::2] = inp["c_ctx"].reshape(8, 128).T
    sm = np.zeros((128, 48), f32)
    sm[:, 0:8] = inp["norm_g"][1].reshape(8, 128).T
    sm[:, 8:16] = inp["ada_b"][0][2048:3072].reshape(8, 128).T
    sm[:, 16:32] = inp["ada_b"][1][0:2048].reshape(16, 128).T
    sm[:, 32] = np.tile(inp["da_qn"][0], 2)
    sm[:, 33] = np.tile(inp["da_kn"][0], 2)
    da_in = inp["da_in"][0]
    cols = []
    for X in range(4):
        for hh in range(2):
            base = X * 1024 + (hp * 2 + hh) * 128
            cols.append(da_in[:, base:base + 128])
    wd = np.ascontiguousarray(np.concatenate(cols, axis=1))
    cosT, sinT, R = rope_tables()
    ident = np.eye(128, dtype=f32)
    bones = np.kron(np.eye(2, dtype=f32), np.ones((64, 64), f32))
    cst = np.ascontiguousarray(np.concatenate([ident, bones, R], axis=1))
    lamv = np.ascontiguousarray(np.broadcast_to(inp["da_lam"][0].reshape(1, 256), (128, 256))).astype(f32)
    subw = np.ascontiguousarray(np.broadcast_to(inp["da_subln"][0].reshape(1, 128), (128, 128))).astype(f32)
    return {"xa": xa, "y0g": y0g_b, "ct": ct,
            "adaw0g": np.ascontiguousarray(inp["ada_w"][0][:, 2048:3072]),
            "adaw1": np.ascontiguousarray(inp["ada_w"][1][:, 0:2048]),
            "smalls": sm, "wo": np.ascontiguousarray(inp["rw_out"][0]), "wd": wd, "cst": cst,
            "cosT": cosT, "sinT": sinT, "lamv": lamv, "subw": subw}


def build_l2(ctx=None):
    if ctx is None:
        nc = bass.Bass("TRN2", target_bir_lowering=False)
        P = Prog(nc)
        NT = 2048
        xn_d = P.dram_in("xn", [D, NT], F32)
        y1_d = P.dram_in("y1T", [D, NT], BF16)
        ct_d = P.dram_in("ct", [128, 16], F32)
        adaw_d = P.dram_in("adaw1g", [D, 1024], F32)
        sm_d = P.dram_in("smalls", [128, 8], F32)
        w_d = P.dram_in("wda", [D, D], F32)
        out_d = P.dram_out("outT", [D, NT], F32)
    else:
        nc, P = ctx["nc"], ctx["P"]
        NT = TL
        xn_d, y1_d, ct_d, adaw_d, sm_d, w_d, out_d = [ctx[k] for k in (
            "xn_s", "y1g", "ct", "c_adaw1g", "c_smalls", "c_wda", "outT")]
    sm = P.sb("sm", [128, 8], F32)
    P.dma(sm.v, sm_d.v)
    banks = [P.ps("bank%d" % i, [128, 512]) for i in range(5)]
    ct = P.sb("ct", [128, 16], F32)
    P.dma(ct.v, ct_d.v)
    sct = P.sb("sct", [128, 16], F32)
    P.act(sct.v, ct.v, AF.Silu)
    adaw_sb = P.sb("adaw_sb", [128, 8, 256], F32)
    modg = P.sb("modg", [128, 8, 2], F32)
    mod_compute(P, adaw_d, 1024, sct, lambda mc: sm[:, mc:mc + 1], banks[0], modg, adaw_sb)
    Wa = P.sb("Wa", [128, 8, 1024], BF16)
    wst = [P.sb("wst%d" % i, [128, 1024], F32) for i in range(2)]
    for kc in range(8):
        P.dma(wst[kc % 2].v, w_d[kc * 128:(kc + 1) * 128, :])
        P.copy("pool", Wa[:, kc, :], wst[kc % 2].v)
    xin = [P.sb("xin%d" % i, [128, 8, W], F32) for i in range(2)]
    yin = [P.sb("yin%d" % i, [128, 8, W], BF16) for i in range(2)]
    ob = [P.sb("ob%d" % i, [128, 8, W], F32) for i in range(2)]
    xn_r = xn_d.t[:].rearrange("(k p) c -> p k c", p=128)
    y1_r = None if isinstance(y1_d, list) else y1_d.t[:].rearrange("(k p) c -> p k c", p=128)
    out_r = out_d.t[:].rearrange("(k p) c -> p k c", p=128)
    for bi in range(NT // W):
        s_ = bi % 2
        P.dma(xin[s_].v, View(xn_d, xn_r[:, :, bi * W:(bi + 1) * W]))
        if isinstance(y1_d, list):
            c0 = bi * W
            yr = y1_d[c0 // 2048].t[:].rearrange("(k p) c -> p k c", p=128)
            P.dma(yin[s_].v, View(y1_d[c0 // 2048], yr[:, :, c0 % 2048:c0 % 2048 + W]))
        else:
            P.dma(yin[s_].v, View(y1_d, y1_r[:, :, bi * W:(bi + 1) * W]))
        for oc in range(8):
            pp = banks[1 + oc % 4]
            for kc in range(8):
                P.matmul(pp[:, 0:W], Wa[:, kc, oc * 128:(oc + 1) * 128], yin[s_][:, kc, :],
                         start=(kc == 0), stop=(kc == 7))
            P.stt(ob[s_][:, oc, :], pp[:, 0:W], modg[:, oc, 0:1], xin[s_][:, oc, :], ALU.mult, ALU.add)
        P.dma(View(out_d, out_r[:, :, bi * W:(bi + 1) * W]), ob[s_].v)
    if ctx is None:
        P.emit()
    return nc


def l2_inputs(inp, b, tq, xn, y1T):
    f32 = np.float32
    ct = np.zeros((128, 16), f32)
    ct[:, 0::2] = inp["c"][b].reshape(8, 128).T
    ct[:, 1::2] = inp["c_ctx"].reshape(8, 128).T
    sm = np.ascontiguousarray(inp["ada_b"][1][2048:3072].reshape(8, 128).T).astype(f32)
    return {"xn": xn, "y1T": y1T, "ct": ct, "adaw1g": np.ascontiguousarray(inp["ada_w"][1][:, 2048:3072]),
            "smalls": sm, "wda": np.ascontiguousarray(inp["da_out"][0])}


GROUPS = [[0, 1, 2, 3], [4, 5, 6, 7]]


def build_fused(upto=None):
    nc = bass.Bass("TRN2", target_bir_lowering=False)
    P = Prog(nc)
    ctx = {"nc": nc, "P": P}
    decl = [("a_xa", [D, XA_COLS], F32), ("ct", [128, 16], F32), ("a_adaw", [D, 2048], F32),
            ("a_smalls", [128, NSMALL], F32), ("a_win", [D, 1280], F32), ("a_w2", [128, 256], F32),
            ("a_a2", [128, 256], F32), ("a_cst", [128, C_TOT], F32),
            ("b_xa", [D, TA], F32), ("b_adaw0g", [D, 1024], F32), ("b_adaw1", [D, 2048], F32),
            ("b_smalls", [128, 48], F32), ("b_wo", [D, D], F32), ("b_wd", [D, 1024], F32),
            ("b_cst", [128, 384], F32), ("b_cosT", [128, TL], F32), ("b_sinT", [128, TL], F32),
            ("b_lamv", [128, 256], F32), ("b_subw", [128, 128], F32),
            ("c_adaw1g", [D, 1024], F32), ("c_smalls", [128, 8], F32), ("c_wda", [D, D], F32)]
    for name, shape, dt_ in decl:
        if upto == "A" and name[0] in "bc" and name[1] == "_":
            continue
        if upto == "B" and name[0] == "c" and name[1] == "_":
            continue
        ctx[name] = P.dram_in(name, shape, dt_)
    ctx["outT"] = P.dram_out("outT", [D, TL], F32)
    pw0 = [TC, 2048, 2048, 2048, 2048]
    ctx["y0loc"] = [P.dram_tmp("y0loc%d" % i, [256, w], BF16) for i, w in enumerate(pw0)]
    ctx["y0g"] = [P.dram_tmp("y0g%d" % i, [D, w], BF16) for i, w in enumerate(pw0)]
    ctx["of_scratch"] = P.dram_tmp("of_scratch", [256, TA], F32)
    ctx["xn_s"] = P.dram_tmp("xn_s", [D, TL], F32)
    ctx["y1loc"] = [P.dram_tmp("y1loc%d" % i, [256, 2048], BF16) for i in range(4)]
    ctx["y1g"] = [P.dram_tmp("y1g%d" % i, [D, 2048], BF16) for i in range(4)]
    build_l0(ctx=ctx)
    P.emit()
    P.end_phase()
    for i in range(5):
        P.collective("AllGather", ctx["y0g"][i].v, ctx["y0loc"][i].v, GROUPS)
    if upto == "A":
        tb = P.sb("dbg_b", [128, 8, 512], BF16)
        tf = P.sb("dbg_f", [128, 8, 512], F32)
        orr = ctx["outT"].t[:].rearrange("(k p) c -> p k c", p=128)
        for i in range(4):
            pi, off = ((1, 0), (1, 512), (2, 1536), (4, 1536))[i]
            yr = ctx["y0g"][pi].t[:].rearrange("(k p) c -> p k c", p=128)
            P.dma(tb.v, View(ctx["y0g"][pi], yr[:, :, off:off + 512]))
            P.copy("dve", tf.v, tb.v)
            P.dma(View(ctx["outT"], orr[:, :, i * 512:(i + 1) * 512]), tf.v)
        P.emit()
        P.end_phase()
        return nc
    build_l1_full(0, ctx=ctx)
    P.emit()
    P.end_phase()
    for i in range(4):
        P.collective("AllGather", ctx["y1g"][i].v, ctx["y1loc"][i].v, GROUPS)
    if upto == "B":
        tb = P.sb("dbg_b", [128, 8, 512], BF16)
        tf = P.sb("dbg_f", [128, 8, 512], F32)
        xr = ctx["xn_s"].t[:].rearrange("(k p) c -> p k c", p=128)
        orr = ctx["outT"].t[:].rearrange("(k p) c -> p k c", p=128)
        for i in range(2):
            c0 = (0, TL - 512)[i]
            yr = ctx["y1g"][c0 // 2048].t[:].rearrange("(k p) c -> p k c", p=128)
            P.dma(tb.v, View(ctx["y1g"][c0 // 2048], yr[:, :, c0 % 2048:c0 % 2048 + 512]))
            P.copy("dve", tf.v, tb.v)
            P.dma(View(ctx["outT"], orr[:, :, i * 512:(i + 1) * 512]), tf.v)
            P.dma(tf.v, View(ctx["xn_s"], xr[:, :, c0:c0 + 512]))
            P.dma(View(ctx["outT"], orr[:, :, (2 + i) * 512:(3 + i) * 512]), tf.v)
        P.emit()
        P.end_phase()
        return nc
    build_l2(ctx=ctx)
    P.emit()
    P.end_phase()
    return nc


def fused_inputs(inp, b, g):
    m = {}
    a = l0_inputs(inp, b, g)
    for k in ("xa", "adaw", "smalls", "win", "w2", "a2", "cst"):
        m["a_" + k] = a[k]
    m["ct"] = a["ct"]
    bb = l1_inputs(inp, b, g, None)
    for k in ("xa", "adaw0g", "adaw1", "smalls", "wo", "wd", "cst", "cosT", "sinT", "lamv", "subw"):
        m["b_" + k] = bb[k]
    c = l2_inputs(inp, b, g, None, None)
    for k in ("adaw1g", "smalls", "wda"):
        m["c_" + k] = c[k]
    return m


def kernel(**inp):
    inp = {k: np.asarray(v) for k, v in inp.items()}
    cores = list(range(8))
    nc = build_fused()
    maps = [fused_inputs(inp, c // 4, c % 4) for c in cores]
    res = run_bass_kernel_spmd(nc, maps, core_ids=cores).results
    out = np.zeros((2, TL, D), np.float32)
    for c in cores:
        b, tq = c // 4, c % 4
        out[b, tq * 2048:(tq + 1) * 2048] = np.asarray(res[c]["outT"])[:, tq * 2048:(tq + 1) * 2048].T
    return out
```
